# Optimizing a Trainium2 kernel written in Bass

```python
import math
import jax, jax.numpy as jnp
from jax import lax
import numpy as np

D_MODEL = 1024
BATCH = 16
SEQ = 256
DEPTH = 4
DEC_BATCH = 2
DEC_SEQ = 4096
PAST_LEN = 256

GRID_W = 64
D_MIX = D_MODEL
BR = D_MIX // 4
POOL_WINDOWS = (2, 4, 8, 16)
POOL_GROUPS = len(POOL_WINDOWS)
POOL_GD = BR // POOL_GROUPS
DN_HEADS = 4
DN_HEAD_DIM = BR // DN_HEADS
CONV_K = 5
CHUNK = 64
S5_P = 16
S5_G = BR // S5_P
S5_N = 64
FT_HEADS = 4
FT_HD = BR // FT_HEADS
SPLIT_SIZES = (BR, BR, 3 * BR, BR, 2 * DN_HEADS, 2 * DN_HEADS, BR, BR, BR, BR)
D_IN_PROJ = sum(SPLIT_SIZES)
EPS = 1e-6
F32 = jnp.float32

kernel_name = 'hybrid_pool_deltanet_s5_fourier_diffusion_step'


def rmsnorm(x, g):
    x32 = x.astype(F32)
    return x32 * lax.rsqrt(jnp.mean(x32 * x32, axis=-1, keepdims=True) + EPS) * g


def l2norm(x):
    return x * lax.rsqrt(jnp.sum(x * x, axis=-1, keepdims=True) + EPS)


def box_mean(x, w, axis):
    L = x.shape[axis]
    pos = np.arange(L)
    lo = np.clip(pos - w // 2, 0, L)
    hi = np.clip(pos - w // 2 + w, 0, L)
    pad = [(0, 0)] * x.ndim
    pad[axis] = (1, 0)
    cs = jnp.pad(jnp.cumsum(x, axis=axis), pad)
    cnt_shape = [1] * x.ndim
    cnt_shape[axis] = L
    cnt = (hi - lo).astype(np.float32).reshape(cnt_shape)
    return (jnp.take(cs, hi, axis=axis) - jnp.take(cs, lo, axis=axis)) / cnt


def pool_branch(u, pool_w, pool_scale, rows):
    B, T, _ = u.shape
    outs = []
    for gi, w in enumerate(POOL_WINDOWS):
        ug = u[..., gi * POOL_GD:(gi + 1) * POOL_GD]
        if rows is None:
            pooled = box_mean(ug, w, 1)
        else:
            ug2 = ug.reshape(B, rows, GRID_W, POOL_GD)
            pooled = box_mean(box_mean(ug2, w, 1), w, 2).reshape(B, T, POOL_GD)
        outs.append(pooled - ug)
    p = jnp.stack(outs, axis=2)
    p = jnp.einsum('btgc,gcd->btgd', p, pool_w).reshape(B, T, BR)
    return p * pool_scale


def short_conv(u, w):
    C = u.shape[-1]
    return lax.conv_general_dilated(
        u, w.astype(u.dtype)[:, None, :], window_strides=(1,),
        padding=[(CONV_K // 2, CONV_K // 2)],
        dimension_numbers=('NWC', 'WIO', 'NWC'), feature_group_count=C)


def gated_delta_chunked(q, k, v, g, beta, s0):
    B, T, H, DK = q.shape
    DV = v.shape[-1]
    n = T // CHUNK

    def chunks(a):
        a = a.reshape((B, n, CHUNK, H) + a.shape[3:])
        return jnp.moveaxis(a, (1, 3), (0, 2))

    qc = chunks(q) * (DK ** -0.5)
    kc = chunks(k)
    vc = chunks(v)
    bc = chunks(beta)
    gc = jnp.cumsum(chunks(g), axis=-1)
    incl = np.tril(np.ones((CHUNK, CHUNK), dtype=bool))
    strict = np.tril(np.ones((CHUNK, CHUNK), dtype=bool), -1)
    diff = gc[..., :, None] - gc[..., None, :]
    decay = jnp.where(incl, jnp.exp(jnp.where(incl, diff, 0.0)), 0.0)
    kk = jnp.einsum('nbhik,nbhjk->nbhij', kc, kc)
    a_mat = jnp.where(strict, kk * decay * bc[..., :, None], 0.0)
    rhs = jnp.concatenate([kc * (bc * jnp.exp(gc))[..., None], vc * bc[..., None]], axis=-1)
    sol = lax.linalg.triangular_solve(a_mat, rhs, left_side=True, lower=True, unit_diagonal=True)
    w_c, u_c = sol[..., :DK], sol[..., DK:]
    aqk = jnp.where(incl, jnp.einsum('nbhik,nbhjk->nbhij', qc, kc) * decay, 0.0)

    def step(S, xs):
        q_i, k_i, w_i, u_i, g_i, a_i = xs
        v_new = u_i - jnp.einsum('bhck,bhkv->bhcv', w_i, S)
        o = (jnp.einsum('bhck,bhkv->bhcv', q_i * jnp.exp(g_i)[..., None], S)
             + jnp.einsum('bhij,bhjv->bhiv', a_i, v_new))
        g_last = g_i[..., -1:]
        S = (S * jnp.exp(g_last)[..., None]
             + jnp.einsum('bhck,bhcv->bhkv', k_i * jnp.exp(g_last - g_i)[..., None], v_new))
        return S, o

    s_fin, o = lax.scan(step, s0, (qc, kc, w_c, u_c, gc, aqk))
    o = jnp.moveaxis(o, (0, 2), (1, 3)).reshape(B, T, H, DV)
    return o, s_fin


def delta_branch(qkv, b_raw, a_raw, lp, s0):
    B, T, _ = qkv.shape
    qkv = jax.nn.silu(short_conv(qkv, lp['dn_conv']))
    q, k, v = jnp.split(qkv, 3, axis=-1)
    shp = (B, T, DN_HEADS, DN_HEAD_DIM)
    q = l2norm(q.reshape(shp))
    k = l2norm(k.reshape(shp))
    v = v.reshape(shp)
    beta = jax.nn.sigmoid(b_raw).reshape(B, T, 2, DN_HEADS)
    g = -jnp.exp(lp['dn_a_log']) * jax.nn.softplus(a_raw.reshape(B, T, 2, DN_HEADS) + lp['dn_dt_bias'])
    o_f, s_f = gated_delta_chunked(q, k, v, g[:, :, 0], beta[:, :, 0], s0[:, 0])
    rev = lambda a: jnp.flip(a, axis=1)
    o_b, s_b = gated_delta_chunked(rev(q), rev(k), rev(v), rev(g[:, :, 1]), rev(beta[:, :, 1]), s0[:, 1])
    o = rmsnorm(o_f + rev(o_b), lp['dn_norm_g']).reshape(B, T, BR)
    return o, jnp.stack([s_f, s_b], axis=1)


def s5_scan(u, a_re, a_im, log_dt, b_re, b_im, c_re, c_im, s0_re, s0_im, reverse):
    dt = jnp.exp(log_dt)[:, None]
    mag = jnp.exp(a_re * dt)
    ab_re = mag * jnp.cos(a_im * dt)
    ab_im = mag * jnp.sin(a_im * dt)
    den = a_re * a_re + a_im * a_im
    nr = ab_re - 1.0
    coef_re = (nr * a_re + ab_im * a_im) / den
    coef_im = (ab_im * a_re - nr * a_im) / den
    bb_re = coef_re[..., None] * b_re - coef_im[..., None] * b_im
    bb_im = coef_re[..., None] * b_im + coef_im[..., None] * b_re
    bu_re = jnp.einsum('btgp,gnp->btgn', u, bb_re)
    bu_im = jnp.einsum('btgp,gnp->btgn', u, bb_im)
    if reverse:
        bu_re = jnp.flip(bu_re, axis=1)
        bu_im = jnp.flip(bu_im, axis=1)
    bu_re = bu_re.at[:, 0].add(ab_re * s0_re - ab_im * s0_im)
    bu_im = bu_im.at[:, 0].add(ab_re * s0_im + ab_im * s0_re)
    a_seq_re = jnp.broadcast_to(ab_re, bu_re.shape)
    a_seq_im = jnp.broadcast_to(ab_im, bu_im.shape)

    def combine(e1, e2):
        a1r, a1i, b1r, b1i = e1
        a2r, a2i, b2r, b2i = e2
        return (a2r * a1r - a2i * a1i, a2r * a1i + a2i * a1r,
                a2r * b1r - a2i * b1i + b2r, a2r * b1i + a2i * b1r + b2i)

    _, _, s_re, s_im = lax.associative_scan(combine, (a_seq_re, a_seq_im, bu_re, bu_im), axis=1)
    fin = (s_re[:, -1], s_im[:, -1])
    if reverse:
        s_re = jnp.flip(s_re, axis=1)
        s_im = jnp.flip(s_im, axis=1)
    y = jnp.einsum('btgn,gpn->btgp', s_re, c_re) - jnp.einsum('btgn,gpn->btgp', s_im, c_im)
    return y, fin


def s5_branch(u, lp, s0):
    B, T, _ = u.shape
    ug = u.reshape(B, T, S5_G, S5_P)
    ys, fins = [], []
    for d in range(2):
        y_d, fin_d = s5_scan(ug, lp['s5_a_re'][d], lp['s5_a_im'][d], lp['s5_log_dt'][d],
                             lp['s5_b_re'][d], lp['s5_b_im'][d], lp['s5_c_re'][d], lp['s5_c_im'][d],
                             s0[:, d, 0], s0[:, d, 1], reverse=(d == 1))
        ys.append(y_d)
        fins.append(jnp.stack(fin_d, axis=1))
    y = ys[0] + ys[1] + lp['s5_d'].reshape(S5_G, S5_P) * ug
    y = jax.nn.gelu(y.reshape(B, T, BR))
    y = y * jax.nn.sigmoid(y @ lp['s5_glu_w'] + lp['s5_glu_b'])
    return y, jnp.stack(fins, axis=1)


def fourier_branch(u):
    B, T, _ = u.shape
    f = jnp.fft.fftn(u.reshape(B, T, FT_HEADS, FT_HD), axes=(1, 3), norm='ortho').real
    return f.reshape(B, T, BR)


def trunk_layer(x, cond, lp, s_dn0, s_s50, rows):
    B = x.shape[0]
    if s_dn0 is None:
        s_dn0 = jnp.zeros((B, 2, DN_HEADS, DN_HEAD_DIM, DN_HEAD_DIM), F32)
        s_s50 = jnp.zeros((B, 2, 2, S5_G, S5_N), F32)
    ada = jax.nn.silu(cond) @ lp['w_ada'] + lp['b_ada']
    shift, scale, gate = jnp.split(ada[:, None, :], 3, axis=-1)
    h = rmsnorm(x, lp['norm_g']) * (1.0 + scale) + shift
    idx = [int(i) for i in np.cumsum(SPLIT_SIZES)[:-1]]
    pool_u, pool_z, dn_qkv, dn_z, dn_b, dn_a, s5_u, s5_z, ft_u, ft_z = jnp.split(h @ lp['w_in'], idx, axis=-1)
    y_pool = pool_branch(pool_u, lp['pool_w'], lp['pool_scale'], rows) * jax.nn.silu(pool_z)
    y_dn, st_dn = delta_branch(dn_qkv, dn_b, dn_a, lp, s_dn0)
    y_dn = y_dn * jax.nn.silu(dn_z)
    y_s5, st_s5 = s5_branch(s5_u, lp, s_s50)
    y_s5 = y_s5 * jax.nn.silu(s5_z)
    y_ft = (fourier_branch(ft_u) @ lp['ft_w']) * jax.nn.silu(ft_z)
    y = jnp.concatenate([y_pool, y_dn, y_s5, y_ft], axis=-1) @ lp['w_out']
    return x + gate * y, st_dn, st_s5


def setup_inputs(seed: int = 0) -> dict:
    key = jax.random.key(seed)
    ks = iter(jax.random.split(key, 40))
    nrm = lambda shape, s: jax.random.normal(next(ks), shape, jnp.float32) * s
    uni = lambda shape, lo, hi: jax.random.uniform(next(ks), shape, jnp.float32, minval=lo, maxval=hi)
    L = DEPTH
    x_prompt = nrm((BATCH, SEQ, D_MODEL), 1.0)
    x_sample = nrm((DEC_BATCH, DEC_SEQ, D_MODEL), 1.0)
    c = nrm((DEC_BATCH, D_MODEL), 1.0)
    state_delta = nrm((DEC_BATCH, DEPTH, 2, DN_HEADS, DN_HEAD_DIM, DN_HEAD_DIM), 0.1)
    state_s5 = nrm((DEC_BATCH, DEPTH, 2, 2, S5_G, S5_N), 0.3)
    c_ctx = nrm((D_MODEL,), 1.0)
    w_ada = nrm((L, D_MODEL, 3 * D_MODEL), 0.2 * D_MODEL ** -0.5)
    b_ada = nrm((L, 3 * D_MODEL), 0.02)
    norm_g = 1.0 + nrm((L, D_MODEL), 0.02)
    w_in = nrm((L, D_MODEL, D_IN_PROJ), D_MODEL ** -0.5)
    pool_w = nrm((L, POOL_GROUPS, POOL_GD, POOL_GD), POOL_GD ** -0.5)
    pool_scale = 1.0 + nrm((L, BR), 0.02)
    dn_conv = nrm((L, CONV_K, 3 * BR), CONV_K ** -0.5)
    dn_a_log = jnp.log(uni((L, 2, DN_HEADS), 1.0, 16.0))
    dn_dt = jnp.exp(uni((L, 2, DN_HEADS), math.log(1e-3), math.log(1e-1)))
    dn_dt_bias = dn_dt + jnp.log(-jnp.expm1(-dn_dt))
    dn_norm_g = 1.0 + nrm((L, DN_HEAD_DIM), 0.02)
    s5_a_re = -0.5 + nrm((L, 2, S5_G, S5_N), 0.01)
    s5_a_im = math.pi * jnp.arange(S5_N, dtype=jnp.float32) + nrm((L, 2, S5_G, S5_N), 0.01)
    s5_log_dt = uni((L, 2, S5_G), math.log(1e-3), math.log(1e-1))
    s5_b_re = nrm((L, 2, S5_G, S5_N, S5_P), (2 * S5_P) ** -0.5)
    s5_b_im = nrm((L, 2, S5_G, S5_N, S5_P), (2 * S5_P) ** -0.5)
    s5_c_re = nrm((L, 2, S5_G, S5_P, S5_N), (2 * S5_N) ** -0.5)
    s5_c_im = nrm((L, 2, S5_G, S5_P, S5_N), (2 * S5_N) ** -0.5)
    s5_d = nrm((L, BR), 1.0)
    s5_glu_w = nrm((L, BR, BR), BR ** -0.5)
    s5_glu_b = nrm((L, BR), 0.02)
    ft_w = nrm((L, BR, BR), BR ** -0.5)
    w_out = nrm((L, D_MIX, D_MODEL), D_MIX ** -0.5)
    final_g = 1.0 + nrm((D_MODEL,), 0.02)
    return {'x_prompt': x_prompt, 'x_sample': x_sample, 'c': c,
            'state_delta': state_delta, 'state_s5': state_s5, 'c_ctx': c_ctx,
            'w_ada': w_ada, 'b_ada': b_ada, 'norm_g': norm_g, 'w_in': w_in,
            'pool_w': pool_w, 'pool_scale': pool_scale,
            'dn_conv': dn_conv, 'dn_a_log': dn_a_log, 'dn_dt_bias': dn_dt_bias, 'dn_norm_g': dn_norm_g,
            's5_a_re': s5_a_re, 's5_a_im': s5_a_im, 's5_log_dt': s5_log_dt,
            's5_b_re': s5_b_re, 's5_b_im': s5_b_im, 's5_c_re': s5_c_re, 's5_c_im': s5_c_im,
            's5_d': s5_d, 's5_glu_w': s5_glu_w, 's5_glu_b': s5_glu_b,
            'ft_w': ft_w, 'w_out': w_out, 'final_g': final_g}


def reference(x_prompt, x_sample, c, state_delta, state_s5, c_ctx, w_ada, b_ada, norm_g, w_in,
              pool_w, pool_scale, dn_conv, dn_a_log, dn_dt_bias, dn_norm_g,
              s5_a_re, s5_a_im, s5_log_dt, s5_b_re, s5_b_im, s5_c_re, s5_c_im,
              s5_d, s5_glu_w, s5_glu_b, ft_w, w_out, final_g):
    params = {'w_ada': w_ada, 'b_ada': b_ada, 'norm_g': norm_g, 'w_in': w_in,
              'pool_w': pool_w, 'pool_scale': pool_scale,
              'dn_conv': dn_conv, 'dn_a_log': dn_a_log, 'dn_dt_bias': dn_dt_bias, 'dn_norm_g': dn_norm_g,
              's5_a_re': s5_a_re, 's5_a_im': s5_a_im, 's5_log_dt': s5_log_dt,
              's5_b_re': s5_b_re, 's5_b_im': s5_b_im, 's5_c_re': s5_c_re, 's5_c_im': s5_c_im,
              's5_d': s5_d, 's5_glu_w': s5_glu_w, 's5_glu_b': s5_glu_b,
              'ft_w': ft_w, 'w_out': w_out}
    xp = x_prompt.astype(F32)
    xs = x_sample.astype(F32)
    rows = xs.shape[1] // GRID_W
    ctx_cond = c_ctx.astype(F32)[None]
    lat_cond = c.astype(F32)
    new_dn, new_s5 = [], []
    for l in range(DEPTH):
        lp = {name: arr[l] for name, arr in params.items()}
        xp, st_dn, st_s5 = trunk_layer(xp, ctx_cond, lp, None, None, None)
        new_dn.append(st_dn)
        new_s5.append(st_s5)
        xs, _, _ = trunk_layer(xs, lat_cond, lp, state_delta[:, l].astype(F32),
                               state_s5[:, l].astype(F32), rows)
    y_prompt = rmsnorm(xp, final_g).astype(x_prompt.dtype)
    y_sample = rmsnorm(xs, final_g).astype(x_sample.dtype)
    new_state_delta = jnp.stack(new_dn, axis=1).astype(state_delta.dtype)
    new_state_s5 = jnp.stack(new_s5, axis=1).astype(state_s5.dtype)
    return (y_prompt, y_sample, new_state_delta, new_state_s5)
```

```python
from contextlib import ExitStack
import numpy as np
import ml_dtypes
import concourse.bass as bass
import concourse.mybir as mybir
from concourse.bass_utils import run_bass_kernel_spmd

F32 = mybir.dt.float32
BF16 = mybir.dt.bfloat16
AF = mybir.ActivationFunctionType
ALU = mybir.AluOpType
AX = mybir.AxisListType

SEM_LIMIT = 30000


class Sem:
    def __init__(self, handle, is_dma):
        self.h = handle
        self.count = 0
        self.is_dma = is_dma


class Res:
    def __init__(self, name, t=None):
        self.name = name
        self.t = t
        self.w = None
        self.r = {}
        self.dsem = None
        self.psum = False

    def __getitem__(self, key):
        return self.t[key]


class KB:
    def __init__(self, nc, stack):
        self.nc = nc
        self.stack = stack
        self.eng = {'pe': nc.tensor, 'act': nc.scalar, 'dve': nc.vector,
                    'pool': nc.gpsimd, 'sp': nc.sync}
        self.nsem = 0
        self.esem = {e: self.new_sem(e, False) for e in ['pe', 'act', 'dve', 'pool']}
        self.known = {e: {} for e in self.eng}
        self.ninstr = 0
        self.all_dma_sems = []

    def new_sem(self, name, is_dma):
        self.nsem += 1
        h = self.stack.enter_context(self.nc.semaphore(f"s{self.nsem}_{name}"))
        return Sem(h, is_dma)

    def sb(self, name, shape, dtype):
        t = self.stack.enter_context(self.nc.sbuf_tensor(name, list(shape), dtype))
        return Res(name, t)

    def ps(self, name, shape, dtype):
        t = self.stack.enter_context(self.nc.psum_tensor(name, list(shape), dtype))
        r = Res(name, t)
        r.psum = True
        return r

    def dram(self, name, shape, dtype, kind="Internal"):
        t = self.nc.dram_tensor(name, list(shape), dtype, kind=kind)
        return t

    def _waits(self, eng, reads, writes):
        waits = {}

        def need(t):
            if t is None:
                return
            sem, val = t
            if sem.is_dma:
                val = sem.count
            if waits.get(sem, 0) < val:
                waits[sem] = val

        for r in reads:
            need(r.w)
            if r.psum:
                for s, v in r.r.items():
                    need((s, v))
        for w in writes:
            need(w.w)
            for s, v in w.r.items():
                need((s, v))
        E = self.eng[eng]
        kn = self.known[eng]
        own = self.esem.get(eng)
        for sem, val in waits.items():
            if eng == 'pe' and sem is own:
                continue
            if kn.get(sem, 0) >= val:
                continue
            E.wait_ge(sem.h, val)
            kn[sem] = val

    def _commit(self, ticket, reads, writes):
        sem, val = ticket
        for r in reads:
            if r.r.get(sem, 0) < val:
                r.r[sem] = val
        for w in writes:
            w.w = ticket
            w.r = {}

    def op(self, eng, emit, reads=(), writes=()):
        self._waits(eng, reads, writes)
        ins = emit()
        sem = self.esem[eng]
        sem.count += 1
        ins.then_inc(sem.h, 1)
        self._commit((sem, sem.count), reads, writes)
        if sem.count >= SEM_LIMIT:
            self.esem[eng] = self.new_sem(eng, False)
        self.ninstr += 1

    def dma(self, q, out, in_, sbres, reads=(), writes=(), **kw):
        self._waits(q, reads, writes)
        if sbres.dsem is None:
            sbres.dsem = {}
        qk = 'sw' if q == 'pool' else 'hw'
        if qk not in sbres.dsem or sbres.dsem[qk].count >= SEM_LIMIT:
            sbres.dsem[qk] = self.new_sem("d" + qk + "_" + sbres.name, True)
            self.all_dma_sems.append(sbres.dsem[qk])
        sem = sbres.dsem[qk]
        ins = self.eng[q].dma_start(out=out, in_=in_, **kw)
        sem.count += 16
        ins.then_inc(sem.h, 16)
        self._commit((sem, sem.count), reads, writes)
        self.ninstr += 1

    def alias_begin(self, src, aliases):
        for a in aliases:
            a.w = src.w
            a.r = dict(src.r)

    def alias_end(self, src, aliases):
        for a in aliases:
            items = list(a.r.items()) + ([a.w] if a.w is not None else [])
            for s, v in items:
                if src.r.get(s, 0) < v:
                    src.r[s] = v

    def finish(self, q='sp'):
        E = self.eng[q]
        for sem in self.all_dma_sems:
            if sem.count > 0 and self.known[q].get(sem, 0) < sem.count:
                E.wait_ge(sem.h, sem.count)
        for e, sem in self.esem.items():
            if sem.count > 0:
                E.wait_ge(sem.h, sem.count)


D = 1024
DEPTH = 4
TS = 4096
TP = 512
NT = TS + TP
NTT = NT // 128
NG = NT // 512
DIN = 2576
C_POOL_U, C_POOL_Z, C_QKV, C_DNZ, C_BETA, C_ALPHA, C_S5U, C_S5Z, C_FTU, C_FTZ = \
    0, 256, 512, 1280, 1536, 1544, 1552, 1808, 2064, 2320
COL_TILES = [(c, 128) for c in range(0, 1536, 128)] + [(1536, 16)] + \
            [(c, 128) for c in range(1552, 2576, 128)]
EPS = 1e-6


def cond_of_tile(tt):
    return 0 if tt * 128 < TS else 1


class Model:
    def __init__(self, nc, st, cfg):
        self.nc = nc
        self.cfg = cfg
        self.depth = cfg.get('depth', DEPTH)
        kb = self.kb = KB(nc, st)
        dt_in = lambda name, shape, dt=F32: nc.dram_tensor(name, list(shape), dt, kind="ExternalInput").ap()
        L = DEPTH
        self.x_in = dt_in("x_all", [NT, D])
        self.condT = dt_in("condT", [128, 8, 2])
        self.w_ada = dt_in("w_ada", [L, D, 3 * D])
        self.b_adaT = dt_in("b_adaT", [L, 128, 24])
        self.b_ada = dt_in("b_ada", [L, 3 * D])
        self.norm_gT = dt_in("norm_gT", [L, 128, 8])
        self.w_in = dt_in("w_in", [L, D, DIN])
        self.w_out = dt_in("w_out", [L, D, D])
        self.final_g = dt_in("final_g", [1, D])
        self.ident_in = dt_in("ident", [128, 128], BF16)
        self.y_out = nc.dram_tensor("y_all", [NT, D], F32, kind="ExternalOutput").ap()
        self.X = nc.dram_tensor("X_scr", [NT, D], F32, kind="Internal").ap()
        self.PT = nc.dram_tensor("PT_scr", [DIN, NT], F32, kind="Internal").ap()
        self.PTOK = nc.dram_tensor("PTOK_scr", [NT, 256], BF16, kind="Internal").ap()
        self.YT = nc.dram_tensor("YT_scr", [D, NT], BF16, kind="Internal").ap()
        self.rX = [Res(f"X{t}") for t in range(NTT)]
        self.rPT = [Res(f"PT{g}") for g in range(NG)]
        self.rPTOK = [Res(f"PTOK{t}") for t in range(NTT)]
        self.rYT = [[Res(f"YT{k}_{g}") for g in range(NG)] for k in range(8)]
        self.ident = kb.sb("ident_sb", [128, 128], BF16)
        self.w_in_bf = kb.sb("w_in_bf", [128, 8, DIN], BF16)
        self.w_out_bf = kb.sb("w_out_bf", [128, 8, D], BF16)
        self.wstage = [kb.sb(f"wstage{i}", [128, DIN], F32) for i in range(2)]
        self.wstage_i = 0
        self.scT = kb.sb("scT", [128, 8, 2], F32)
        self.scbc = kb.sb("scbc", [128, 8, 2, 128], F32)
        self.b_col = kb.sb("b_col", [128, 24], F32)
        self.ada_acc = kb.sb("ada_acc", [128, 32], F32)
        self.ng_col = kb.sb("ng_col", [128, 8], F32)
        self.sh_col = kb.sb("sh_col", [128, 8, 2], F32)
        self.gs_col = kb.sb("gs_col", [128, 8, 2], F32)
        self.gate_bc = [kb.sb(f"gate_bc{c}", [128, D], F32) for c in range(2)]
        self.fg_bc = kb.sb("fg_bc", [128, D], F32)
        self.eps_col = kb.sb("eps_col", [128, 1], F32)
        self.xt = [kb.sb(f"xt{i}", [128, D], F32) for i in range(2)]
        self.xn = [kb.sb(f"xn{i}", [128, D], BF16) for i in range(2)]
        self.ss = [kb.sb(f"ss{i}", [128, 1], F32) for i in range(2)]
        self.rstd = [kb.sb(f"rstd{i}", [128, 1], F32) for i in range(2)]
        self.junk = kb.sb("junk", [128, D], BF16)
        self.hT = [kb.sb(f"hT{i}", [128, 8, 512], BF16) for i in range(2)]
        self.ss4 = [kb.sb(f"ss4_{i}", [128, 1], F32) for i in range(4)]
        self.rstd4 = [kb.sb(f"rstd4_{i}", [128, 1], F32) for i in range(4)]
        self.pstage = [kb.sb(f"pstage{i}", [128, 512], F32) for i in range(3)]
        self.pstage_i = 0
        self.tmstage = [kb.sb(f"tmstage{i}", [128, 512], BF16) for i in range(2)]
        self.yT = [kb.sb(f"yT{i}", [128, 8, 128], BF16) for i in range(2)]
        self.otmp = [kb.sb(f"otmp{i}", [128, D], F32) for i in range(2)]
        self.xt4 = [self.xt[0], self.xt[1], self.otmp[0], self.otmp[1]]
        self.bank = [kb.ps(f"bank{i}", [128, 512], F32) for i in range(8)]
        self.bank_i = 0
        wo_ = self.w_out_bf[:].rearrange("p a b -> p (a b)")
        self.xn4 = [Res(f"xn4_{i}", wo_[:, i * 1024:(i + 1) * 1024]) for i in range(8)]
        self.mixer_init()
        self.ft_init()
        self.s5_init()
        self.dn_init()

    def next_bank(self, lo=2, hi=8):
        b = self.bank[lo + self.bank_i % (hi - lo)]
        self.bank_i += 1
        return b

    def setup(self):
        kb, nc = self.kb, self.nc
        kb.dma('sp', self.ident[:], self.ident_in, self.ident, writes=[self.ident])
        kb.dma('sp', self.scT[:], self.condT, self.scT, writes=[self.scT])
        kb.dma('sp', self.fg_bc[:], self.final_g.partition_broadcast(128), self.fg_bc, writes=[self.fg_bc])
        kb.op('dve', lambda: nc.vector.memset(self.eps_col[:], EPS), writes=[self.eps_col])
        kb.op('act', lambda: nc.scalar.activation(out=self.scT[:], in_=self.scT[:], func=AF.Silu),
              reads=[self.scT], writes=[self.scT])
        for kt in range(8):
            for ci in range(2):
                kb.op('dve', lambda kt=kt, ci=ci: nc.vector.tensor_copy(
                    out=self.scbc[:, kt, ci, :],
                    in_=self.scT[:, kt, ci:ci + 1].to_broadcast([128, 128])),
                    reads=[self.scT], writes=[self.scbc])

    def load_weight_rows(self, dst_views, src_ap, ncols, cast_eng='pool'):
        kb, nc = self.kb, self.nc
        stg = self.wstage[self.wstage_i % 2]
        self.wstage_i += 1
        kb.dma('sp', stg[:, 0:ncols], src_ap, stg, writes=[stg])
        return stg

    def adaln(self, l):
        kb, nc = self.kb, self.nc
        kb.dma('sp', self.b_col[:], self.b_adaT[l], self.b_col, writes=[self.b_col])
        kb.dma('sp', self.ng_col[:], self.norm_gT[l], self.ng_col, writes=[self.ng_col])
        for ci in range(2):
            kb.dma('sp', self.gate_bc[ci][:], self.b_ada[l:l + 1, 2 * D:3 * D].partition_broadcast(128),
                   self.gate_bc[ci], writes=[self.gate_bc[ci]])
        pcol = self.bank[2]
        pg = [self.bank[3], self.bank[4], self.bank[5], self.bank[6]]
        for kt in range(8):
            stg = self.wstage[self.wstage_i % 2]
            self.wstage_i += 1
            kb.dma('sp', stg[:, 0:2 * D], self.w_ada[l, kt * 128:(kt + 1) * 128, 0:2 * D], stg, writes=[stg])
            stg2 = self.wstage[self.wstage_i % 2]
            self.wstage_i += 1
            kb.dma('sp', stg2[:, 0:D], self.w_ada[l, kt * 128:(kt + 1) * 128, 2 * D:3 * D], stg2, writes=[stg2])
            for mt in range(16):
                kb.op('pe', lambda mt=mt, kt=kt, stg=stg: nc.tensor.matmul(
                    pcol[:, mt * 2:mt * 2 + 2], lhsT=stg[:, mt * 128:(mt + 1) * 128], rhs=self.scT[:, kt, :],
                    start=True, stop=True), reads=[stg, self.scT], writes=[pcol])
            if kt == 0:
                kb.op('dve', lambda: nc.vector.tensor_copy(out=self.ada_acc[:], in_=pcol[:, 0:32]),
                      reads=[pcol], writes=[self.ada_acc])
            else:
                kb.op('dve', lambda: nc.vector.tensor_tensor(out=self.ada_acc[:], in0=pcol[:, 0:32], in1=self.ada_acc[:], op=ALU.add),
                      reads=[pcol, self.ada_acc], writes=[self.ada_acc])
            for ci in range(2):
                for hf in range(2):
                    kb.op('pe', lambda ci=ci, hf=hf, kt=kt, stg2=stg2: nc.tensor.matmul(
                        pg[ci * 2 + hf][:], lhsT=self.scbc[:, kt, ci, :],
                        rhs=stg2[:, hf * 512:(hf + 1) * 512],
                        start=(kt == 0), stop=(kt == 7)), reads=[stg2, self.scbc], writes=[pg[ci * 2 + hf]])
        pc3 = self.ada_acc[:].rearrange("p (m c) -> p m c", c=2)
        for ci in range(2):
            kb.op('dve', lambda ci=ci: nc.vector.tensor_tensor(
                out=self.sh_col[:, :, ci], in0=pc3[:, 0:8, ci], in1=self.b_col[:, 0:8], op=ALU.add),
                reads=[self.ada_acc, self.b_col], writes=[self.sh_col])
            kb.op('dve', lambda ci=ci: nc.vector.tensor_tensor(
                out=self.gs_col[:, :, ci], in0=pc3[:, 8:16, ci], in1=self.b_col[:, 8:16], op=ALU.add),
                reads=[self.ada_acc, self.b_col], writes=[self.gs_col])
            kb.op('dve', lambda ci=ci: nc.vector.scalar_tensor_tensor(
                out=self.gs_col[:, :, ci], in0=self.gs_col[:, :, ci], scalar=1.0, in1=self.ng_col[:],
                op0=ALU.add, op1=ALU.mult), reads=[self.gs_col, self.ng_col], writes=[self.gs_col])
            for hf in range(2):
                kb.op('dve', lambda ci=ci, hf=hf: nc.vector.tensor_tensor(
                    out=self.gate_bc[ci][:, hf * 512:(hf + 1) * 512], in0=pg[ci * 2 + hf][:],
                    in1=self.gate_bc[ci][:, hf * 512:(hf + 1) * 512], op=ALU.add),
                    reads=[pg[ci * 2 + hf], self.gate_bc[ci]], writes=[self.gate_bc[ci]])

    def load_layer_weights(self, l):
        kb, nc = self.kb, self.nc
        for kt in range(8):
            stg = self.wstage[self.wstage_i % 2]
            self.wstage_i += 1
            kb.dma('sp', stg[:, 0:DIN], self.w_in[l, kt * 128:(kt + 1) * 128, :], stg, writes=[stg])
            kb.op('act' if kt % 2 == 0 else 'dve', (lambda kt=kt, stg=stg: nc.scalar.copy(out=self.w_in_bf[:, kt, :], in_=stg[:, 0:DIN])) if kt % 2 == 0 else
                  (lambda kt=kt, stg=stg: nc.vector.tensor_copy(out=self.w_in_bf[:, kt, :], in_=stg[:, 0:DIN])),
                  reads=[stg], writes=[self.w_in_bf])

    def load_w_out(self, l):
        kb, nc = self.kb, self.nc
        for kt in range(8):
            stg = self.wstage[self.wstage_i % 2]
            self.wstage_i += 1
            kb.dma('sp', stg[:, 0:D], self.w_out[l, kt * 128:(kt + 1) * 128, :], stg, writes=[stg])
            kb.op('act' if kt % 2 == 0 else 'dve', (lambda kt=kt, stg=stg: nc.scalar.copy(out=self.w_out_bf[:, kt, :], in_=stg[:, 0:D])) if kt % 2 == 0 else
                  (lambda kt=kt, stg=stg: nc.vector.tensor_copy(out=self.w_out_bf[:, kt, :], in_=stg[:, 0:D])),
                  reads=[stg], writes=[self.w_out_bf])

    def phase_a(self, l):
        kb, nc = self.kb, self.nc
        src = self.x_in if l == 0 else self.X

        def stage1(g):
            for j in range(4):
                tt = g * 4 + j
                xt, xn, ss, rstd = self.xt4[j], self.xn4[(g % 2) * 4 + j], self.ss4[j], self.rstd4[j]
                kb.dma('sp', xt[:], src[tt * 128:(tt + 1) * 128, :], xt,
                       reads=([self.rX[tt]] if l > 0 else []), writes=[xt])
                kb.op('act', lambda xt=xt, ss=ss: nc.scalar.activation(out=self.junk[:], in_=xt[:], func=AF.Square, accum_out=ss[:]),
                      reads=[xt], writes=[self.junk, ss])
                kb.op('act', lambda ss=ss, rstd=rstd: nc.scalar.activation(out=rstd[:], in_=ss[:], func=AF.Sqrt, scale=1.0 / D, bias=self.eps_col[:]),
                      reads=[ss, self.eps_col], writes=[rstd])
                kb.op('dve', lambda rstd=rstd: nc.vector.reciprocal(out=rstd[:], in_=rstd[:]), reads=[rstd], writes=[rstd])
                kb.op('dve', lambda xt=xt, xn=xn, rstd=rstd: nc.vector.tensor_scalar(out=xn[:], in0=xt[:], scalar1=rstd[:], scalar2=None, op0=ALU.mult),
                      reads=[xt, rstd], writes=[xn])

        def stage2(g):
            hT = self.hT[g % 2]
            for j in range(4):
                tt = g * 4 + j
                ci = cond_of_tile(tt)
                xn = self.xn4[(g % 2) * 4 + j]
                pA, pB = self.bank[0], self.bank[1]
                pAv, pBv = pA[:].bitcast(BF16), pB[:].bitcast(BF16)
                for kt in range(8):
                    pv, pr = (pAv, pA) if kt < 4 else (pBv, pB)
                    kk = kt % 4
                    kb.op('pe', lambda kt=kt, kk=kk, pv=pv, xn=xn: nc.tensor.transpose(pv[:, kk * 128:(kk + 1) * 128], xn[:, kt * 128:(kt + 1) * 128], self.ident[:]),
                          reads=[xn, self.ident], writes=[pr])
                for kt in range(4):
                    kb.op('act', lambda kt=kt, j=j, ci=ci: nc.scalar.activation(
                        out=hT[:, kt, j * 128:(j + 1) * 128], in_=pAv[:, kt * 128:(kt + 1) * 128], func=AF.Identity,
                        scale=self.gs_col[:, kt, ci:ci + 1], bias=self.sh_col[:, kt, ci:ci + 1]),
                        reads=[pA, self.gs_col, self.sh_col], writes=[hT])
                pB3 = pBv[:, 0:512].rearrange("p (k t) -> p k t", k=4)
                ho = hT[:, 4:8, j * 128:(j + 1) * 128]
                kb.op('dve', lambda pB3=pB3, ho=ho, ci=ci: nc.vector.tensor_tensor(out=ho, in0=pB3, in1=self.gs_col[:, 4:8, ci:ci + 1].to_broadcast([128, 4, 128]), op=ALU.mult),
                      reads=[pB, self.gs_col], writes=[hT])
                kb.op('dve', lambda ho=ho, ci=ci: nc.vector.tensor_tensor(out=ho, in0=ho, in1=self.sh_col[:, 4:8, ci:ci + 1].to_broadcast([128, 4, 128]), op=ALU.add),
                      reads=[hT, self.sh_col], writes=[hT])

        def proj(g):
            hT = self.hT[g % 2]
            for i, (c0, cw) in enumerate(COL_TILES):
                acc = self.next_bank()
                for kt in range(8):
                    kb.op('pe', lambda kt=kt, acc=acc, c0=c0, cw=cw: nc.tensor.matmul(acc[0:cw, :], lhsT=self.w_in_bf[:, kt, c0:c0 + cw], rhs=hT[:, kt, :],
                                                                                      start=(kt == 0), stop=(kt == 7)),
                          reads=[self.w_in_bf, hT], writes=[acc])
                slots = self.pstage + self.wk
                stg = slots[self.pstage_i % len(slots)]
                self.pstage_i += 1
                if i % 2 == 0:
                    kb.op('act', lambda stg=stg, acc=acc, cw=cw: nc.scalar.copy(out=stg[0:cw, :], in_=acc[0:cw, :]), reads=[acc], writes=[stg])
                else:
                    kb.op('dve', lambda stg=stg, acc=acc, cw=cw: nc.vector.tensor_copy(out=stg[0:cw, :], in_=acc[0:cw, :]), reads=[acc], writes=[stg])
                kb.dma('pool', self.PT[c0:c0 + cw, g * 512:(g + 1) * 512], stg[0:cw, :], stg,
                       reads=[stg], writes=[self.rPT[g]])
            for j in range(4):
                tt = g * 4 + j
                acc = self.next_bank()
                for kt in range(8):
                    kb.op('pe', lambda kt=kt, acc=acc, j=j: nc.tensor.matmul(acc[:, 0:256], lhsT=hT[:, kt, j * 128:(j + 1) * 128], rhs=self.w_in_bf[:, kt, C_POOL_U:C_POOL_U + 256],
                                                                        start=(kt == 0), stop=(kt == 7)),
                          reads=[self.w_in_bf, hT], writes=[acc])
                stg = self.tmstage[tt % 2]
                kb.op('dve', lambda stg=stg, acc=acc: nc.vector.tensor_copy(out=stg[:, 0:256], in_=acc[:, 0:256]), reads=[acc], writes=[stg])
                kb.dma('pool', self.PTOK[tt * 128:(tt + 1) * 128, :], stg[:, 0:256], stg, reads=[stg], writes=[self.rPTOK[tt]])

        stage1(0)
        stage2(0)
        for g in range(NG):
            if g + 1 < NG:
                stage1(g + 1)
            proj(g)
            if g + 1 < NG:
                stage2(g + 1)

    def phase_b_stub(self, l):
        kb, nc = self.kb, self.nc
        for g in range(NG):
            for kt in range(8):
                stg = self.pstage[self.pstage_i % 3]
                self.pstage_i += 1
                kb.dma('sp', stg[:], self.PT[kt * 128:(kt + 1) * 128, g * 512:(g + 1) * 512], stg,
                       reads=[self.rPT[g]], writes=[stg])
                o = self.tmstage[kt % 2]
                kb.op('act', lambda: nc.scalar.copy(out=o[:], in_=stg[:]), reads=[stg], writes=[o])
                kb.dma('pool', self.YT[kt * 128:(kt + 1) * 128, g * 512:(g + 1) * 512], o[:], o,
                       reads=[o], writes=[self.rYT[kt][g]])

    def phase_c(self, l):
        kb, nc = self.kb, self.nc
        last = (l == self.depth - 1)
        src = self.x_in if l == 0 else self.X
        YTv = self.YT.rearrange("(k p) t -> p k t", p=128)
        for tt in range(NTT):
            ci = cond_of_tile(tt)
            s = tt % 2
            xt, yT, ot = self.xt[s], self.yT[s], self.otmp[s]
            kb.dma('sp', yT[:], YTv[:, :, tt * 128:(tt + 1) * 128], yT, reads=[self.rYT[k][tt // 4] for k in range(8)], writes=[yT])
            kb.dma('sp', xt[:], src[tt * 128:(tt + 1) * 128, :], xt,
                   reads=([self.rX[tt]] if l > 0 else []), writes=[xt])
            for hf in range(2):
                acc = self.next_bank()
                for kt in range(8):
                    kb.op('pe', lambda kt=kt: nc.tensor.matmul(acc[:], lhsT=yT[:, kt, :], rhs=self.w_out_bf[:, kt, hf * 512:(hf + 1) * 512],
                                                               start=(kt == 0), stop=(kt == 7)),
                          reads=[yT, self.w_out_bf], writes=[acc])
                sl = slice(hf * 512, (hf + 1) * 512)
                kb.op('dve', lambda: nc.vector.tensor_tensor(out=ot[:, sl], in0=acc[:], in1=self.gate_bc[ci][:, sl], op=ALU.mult),
                      reads=[acc, self.gate_bc[ci]], writes=[ot])
                kb.op('dve', lambda: nc.vector.tensor_tensor(out=xt[:, sl], in0=ot[:, sl], in1=xt[:, sl], op=ALU.add),
                      reads=[ot, xt], writes=[xt])
            if not last:
                kb.dma('pool', self.X[tt * 128:(tt + 1) * 128, :], xt[:], xt, reads=[xt], writes=[self.rX[tt]])
            else:
                ss, rstd = self.ss[s], self.rstd[s]
                kb.op('act', lambda: nc.scalar.activation(out=self.junk[:], in_=xt[:], func=AF.Square, accum_out=ss[:]),
                      reads=[xt], writes=[self.junk, ss])
                kb.op('act', lambda: nc.scalar.activation(out=rstd[:], in_=ss[:], func=AF.Sqrt, scale=1.0 / D, bias=self.eps_col[:]),
                      reads=[ss, self.eps_col], writes=[rstd])
                kb.op('dve', lambda: nc.vector.reciprocal(out=rstd[:], in_=rstd[:]), reads=[rstd], writes=[rstd])
                kb.op('dve', lambda: nc.vector.scalar_tensor_tensor(out=ot[:], in0=xt[:], scalar=rstd[:], in1=self.fg_bc[:],
                                                                    op0=ALU.mult, op1=ALU.mult),
                      reads=[xt, rstd, self.fg_bc], writes=[ot])
                kb.dma('pool', self.y_out[tt * 128:(tt + 1) * 128, :], ot[:], ot, reads=[ot])

    def build(self):
        self.setup()
        self.mixer_setup()
        self.ft_setup()
        self.s5_setup()
        self.dn_setup()
        for l in range(self.depth):
            self.adaln(l)
            self.load_layer_weights(l)
            self.kb.alias_begin(self.w_out_bf, self.xn4)
            self.phase_a(l)
            self.kb.alias_end(self.w_out_bf, self.xn4)
            if self.cfg.get('stub', False):
                self.phase_b_stub(l)
            else:
                self.phase_b(l)
            self.load_w_out(l)
            self.phase_c(l)
        self.kb.finish('sp')


def build_nc(cfg):
    nc = bass.Bass("TRN2", target_bir_lowering=False)
    with ExitStack() as st:
        m = Model(nc, st, cfg)
        m.build()
        print("instructions:", m.kb.ninstr, "sems:", m.kb.nsem)
    return nc


def make_in_maps(inputs, ncores=8):
    f32 = lambda a: np.ascontiguousarray(np.asarray(a, dtype=np.float32))
    x_prompt = f32(inputs['x_prompt']); x_sample = f32(inputs['x_sample'])
    c = f32(inputs['c']); c_ctx = f32(inputs['c_ctx'])
    L = DEPTH
    b_ada = f32(inputs['b_ada'])
    shared = dict(
        w_ada=f32(inputs['w_ada']), b_ada=b_ada,
        b_adaT=np.ascontiguousarray(b_ada.reshape(L, 24, 128).transpose(0, 2, 1)),
        norm_gT=np.ascontiguousarray(f32(inputs['norm_g']).reshape(L, 8, 128).transpose(0, 2, 1)),
        w_in=f32(inputs['w_in']), w_out=f32(inputs['w_out']),
        final_g=f32(inputs['final_g']).reshape(1, D),
        ident=np.eye(128).astype(ml_dtypes.bfloat16),
        pool_w=f32(inputs['pool_w']),
        pool_scaleT=np.ascontiguousarray(f32(inputs['pool_scale']).reshape(L, 2, 128).transpose(0, 2, 1)),
    )
    bm, inv = make_pool_consts()
    shared.update(bmats=bm, inv_cnt=inv)
    shared.update(make_ft_consts())
    shared.update(make_dn_consts())
    shared.update(ft_w=f32(inputs['ft_w']))
    maps = []
    for k in range(ncores):
        sb = k // 4
        x_all = np.concatenate([x_sample[sb], x_prompt[2 * k], x_prompt[2 * k + 1]], axis=0)
        cond = np.stack([c[sb], c_ctx], axis=0)
        condT = np.ascontiguousarray(cond.reshape(2, 8, 128).transpose(2, 1, 0))
        m = dict(shared)
        m.update(x_all=np.ascontiguousarray(x_all), condT=condT)
        m.update(make_s5_inputs(inputs, k))
        m.update(make_dn_inputs(inputs, k))
        maps.append(m)
    return maps


POOL_WINDOWS = (2, 4, 8, 16)
POOL_D2 = {2: (-1, 0), 4: (-1, 1), 8: (-2, 2), 16: (-4, 4)}
POOL_D1 = (-1, 1)


def pool_block_index():
    idx = {}
    n = 0
    for w in POOL_WINDOWS:
        lo, hi = POOL_D2[w]
        for d in range(lo, hi + 1):
            idx[('s', w, d)] = n
            n += 1
    for w in POOL_WINDOWS:
        for d in range(POOL_D1[0], POOL_D1[1] + 1):
            idx[('p', w, d)] = n
            n += 1
    return idx, n


def make_pool_consts():
    idx, n = pool_block_index()
    B = np.zeros((n, 128, 128), np.float32)
    i = np.arange(128)
    ril, cin = i // 64, i % 64
    for w in POOL_WINDOWS:
        lo, hi = POOL_D2[w]
        for d in range(lo, hi + 1):
            dr = 2 * d + ril[:, None] - ril[None, :]
            dc = cin[:, None] - cin[None, :]
            B[idx[('s', w, d)]] = ((dr >= -(w // 2)) & (dr < w - w // 2) & (dc >= -(w // 2)) & (dc < w - w // 2))
        for d in range(-1, 2):
            dt_ = 128 * d + i[:, None] - i[None, :]
            B[idx[('p', w, d)]] = ((dt_ >= -(w // 2)) & (dt_ < w - w // 2))
    inv = np.zeros((4, NT), np.float32)

    def cnt(Ln, w):
        pos = np.arange(Ln)
        lo = np.clip(pos - w // 2, 0, Ln)
        hi = np.clip(pos - w // 2 + w, 0, Ln)
        return (hi - lo).astype(np.float64)
    for gi, w in enumerate(POOL_WINDOWS):
        c64 = cnt(64, w)
        inv[gi, :TS] = (1.0 / (c64[:, None] * c64[None, :])).reshape(-1)
        c256 = 1.0 / cnt(256, w)
        inv[gi, TS:] = np.concatenate([c256, c256])
    return B.astype(ml_dtypes.bfloat16), inv


def seq_tile_range(tt):
    if tt < TS // 128:
        return 0, TS // 128
    lo = tt - (tt - TS // 128) % 2
    return lo, lo + 2


def _mixer_init(self):
    kb, nc = self.kb, self.nc
    L = DEPTH
    dt_in = lambda name, shape, dt=F32: nc.dram_tensor(name, list(shape), dt, kind="ExternalInput").ap()
    _, nblk = pool_block_index()
    self.bmats_in = dt_in("bmats", [nblk, 128, 128], BF16)
    self.inv_cnt = dt_in("inv_cnt", [4, NT])
    self.pool_w = dt_in("pool_w", [L, 4, 64, 64])
    self.pool_scaleT = dt_in("pool_scaleT", [L, 128, 2])
    self.bmats = kb.sb("bmats_sb", [128, nblk, 128], BF16)
    self.pwstage = kb.sb("pwstage", [128, 2, 64], F32)
    self.pwblk = kb.sb("pwblk", [128, 2, 128], BF16)
    self.pscale = kb.sb("pscale", [128, 2], F32)
    self.utok = kb.sb("utok", [128, 12, 256], BF16)
    self.wk = [kb.sb(f"wk{i}", [128, 512], F32) for i in range(6)]
    self.wkb = [kb.sb(f"wkb{i}", [128, 512], BF16) for i in range(4)]
    self.ybuf = [kb.sb(f"ybuf{i}", [128, 512], BF16) for i in range(2)]
    self.ybuf_i = 0
    self.zero_bf = kb.sb("zero_bf", [128, 512], BF16)


def _mixer_setup(self):
    kb, nc = self.kb, self.nc
    kb.dma('sp', self.bmats[:], self.bmats_in.rearrange("n p q -> p n q"), self.bmats, writes=[self.bmats])
    kb.op('dve', lambda: nc.vector.memset(self.zero_bf[:], 0.0), writes=[self.zero_bf])


def zero_rows(self, kts):
    kb = self.kb
    for kt in kts:
        for g in range(NG):
            kb.dma('pool', self.YT[kt * 128:(kt + 1) * 128, g * 512:(g + 1) * 512], self.zero_bf[:], self.zero_bf,
                   reads=[self.zero_bf], writes=[self.rYT[kt][g]])


def store_y(self, yb, row0, g):
    kb = self.kb
    kb.dma('pool', self.YT[row0:row0 + 128, g * 512:(g + 1) * 512], yb[:], yb,
           reads=[yb], writes=[self.rYT[row0 // 128][g]])


def pool_phase(self, l):
    kb, nc = self.kb, self.nc
    bidx, _ = pool_block_index()
    kb.dma('sp', self.pwstage[:], self.pool_w[l].rearrange("(a gl) c d -> (gl c) a d", gl=2), self.pwstage,
           writes=[self.pwstage])
    kb.dma('sp', self.pscale[:], self.pool_scaleT[l], self.pscale, writes=[self.pscale])
    kb.op('dve', lambda: nc.vector.memset(self.pwblk[:], 0.0), writes=[self.pwblk])
    for a in range(2):
        kb.op('dve', lambda a=a: nc.vector.tensor_copy(out=self.pwblk[0:64, a, 0:64], in_=self.pwstage[0:64, a, :]),
              reads=[self.pwstage], writes=[self.pwblk])
        kb.op('dve', lambda a=a: nc.vector.tensor_copy(out=self.pwblk[64:128, a, 64:128], in_=self.pwstage[64:128, a, :]),
              reads=[self.pwstage], writes=[self.pwblk])
    for g in range(NG):
        sample = (g * 512 < TS)
        t0 = 4 * g
        lo_l = max(seq_tile_range(t0)[0], t0 - 4) if sample else t0
        hi_l = min(seq_tile_range(t0)[1], t0 + 8) if sample else t0 + 4
        for m in range(lo_l, hi_l):
            kb.dma('sp', self.utok[:, m - (t0 - 4), :], self.PTOK[m * 128:(m + 1) * 128, 0:256], self.utok,
                   reads=[self.rPTOK[m]], writes=[self.utok])
        for a in range(2):
            uT, zT, inv, tmp, sz = self.wk[0], self.wk[1], self.wk[2], self.wk[3], self.wk[4]
            diffb = self.wkb[0]
            tsl = slice(g * 512, (g + 1) * 512)
            kb.dma('sp', uT[:], self.PT[C_POOL_U + a * 128:C_POOL_U + (a + 1) * 128, tsl], uT, reads=[self.rPT[g]], writes=[uT])
            kb.dma('sp', zT[:], self.PT[C_POOL_Z + a * 128:C_POOL_Z + (a + 1) * 128, tsl], zT, reads=[self.rPT[g]], writes=[zT])
            for gl in range(2):
                kb.dma('sp', inv[gl * 64:(gl + 1) * 64, :], self.inv_cnt[2 * a + gl:2 * a + gl + 1, tsl].partition_broadcast(64),
                       inv, writes=[inv])
            acc = self.next_bank()
            for gl in range(2):
                gi = 2 * a + gl
                w = POOL_WINDOWS[gi]
                for j in range(4):
                    m = t0 + j
                    slo, shi = seq_tile_range(m)
                    dlo, dhi = POOL_D2[w] if sample else POOL_D1
                    ds = [d for d in range(dlo, dhi + 1) if slo <= m + d < shi]
                    for ii, d in enumerate(ds):
                        bi = bidx[('s' if sample else 'p', w, d)]
                        kb.op('pe', lambda ii=ii, d=d, bi=bi, m=m, j=j, gl=gl, gi=gi, ds=ds: nc.tensor.matmul(
                            acc[gl * 64:(gl + 1) * 64, j * 128:(j + 1) * 128],
                            lhsT=self.utok[:, m + d - (t0 - 4), gi * 64:(gi + 1) * 64], rhs=self.bmats[:, bi, :],
                            start=(ii == 0), stop=(ii == len(ds) - 1)),
                            reads=[self.utok, self.bmats], writes=[acc])
            kb.op('dve', lambda: nc.vector.tensor_tensor(out=tmp[:], in0=acc[:], in1=inv[:], op=ALU.mult),
                  reads=[acc, inv], writes=[tmp])
            kb.op('dve', lambda: nc.vector.tensor_tensor(out=diffb[:], in0=tmp[:], in1=uT[:], op=ALU.subtract),
                  reads=[tmp, uT], writes=[diffb])
            acc2 = self.next_bank()
            kb.op('pe', lambda: nc.tensor.matmul(acc2[:], lhsT=self.pwblk[:, a, :], rhs=diffb[:], start=True, stop=True),
                  reads=[self.pwblk, diffb], writes=[acc2])
            kb.op('act', lambda: nc.scalar.activation(out=sz[:], in_=zT[:], func=AF.Silu), reads=[zT], writes=[sz])
            yb = self.ybuf[self.ybuf_i % 2]
            self.ybuf_i += 1
            kb.op('dve', lambda: nc.vector.scalar_tensor_tensor(out=yb[:], in0=acc2[:], scalar=self.pscale[:, a:a + 1], in1=sz[:],
                                                                op0=ALU.mult, op1=ALU.mult),
                  reads=[acc2, self.pscale, sz], writes=[yb])
            self.store_y(yb, a * 128, g)


def phase_b(self, l):
    br = self.cfg.get('branches', ('pool', 'dn', 's5', 'ft'))
    if 'pool' in br:
        self.pool_phase(l)
    else:
        self.zero_rows([0, 1])
    if 'dn' in br:
        self.dn_phase(l)
    else:
        self.zero_rows([2, 3])
    if 's5' in br:
        self.s5_phase(l)
    else:
        self.zero_rows([4, 5])
    if 'ft' in br:
        self.ft_phase(l)
    else:
        self.zero_rows([6, 7])
    if self.cfg.get('dump_yt', False) and l == 0:
        dbg = self.nc.dram_tensor("yt_dbg", [D, NT], BF16, kind="ExternalOutput").ap()
        r = Res("ytdbg")
        self.kb.dma('sp', dbg, self.YT, r, reads=[self.rYT[k][g] for k in range(8) for g in range(NG)])


Model.mixer_init = _mixer_init
Model.mixer_setup = _mixer_setup
Model.zero_rows = zero_rows
Model.store_y = store_y
Model.pool_phase = pool_phase
Model.phase_b = phase_b


def make_ft_consts():
    c = np.arange(64, dtype=np.float64)
    ang = 2 * np.pi * np.outer(c, c) / 64
    wdft = np.zeros((256, 512), np.float64)
    for h in range(4):
        wdft[h * 64:(h + 1) * 64, h * 64:(h + 1) * 64] = np.cos(ang) / 8.0
        wdft[h * 64:(h + 1) * 64, 256 + h * 64:256 + (h + 1) * 64] = -np.sin(ang) / 8.0
    C, S = np.cos(ang) / 64.0, np.sin(ang) / 64.0
    ma = np.zeros((128, 128), np.float64)
    ma[0:64, 0:64] = C
    ma[64:128, 0:64] = S
    ma[0:64, 64:128] = -S
    ma[64:128, 64:128] = C
    t2 = np.arange(64, dtype=np.float64)
    tp = np.arange(4096, dtype=np.float64)
    th = 2 * np.pi * (np.outer(t2, tp) % 4096) / 4096
    g = np.zeros((128, 64, 64), np.float64)
    g[0:64] = np.cos(th).reshape(64, 64, 64).transpose(0, 2, 1)
    g[64:128] = np.sin(th).reshape(64, 64, 64).transpose(0, 2, 1)
    t = np.arange(256, dtype=np.float64)
    a256 = 2 * np.pi * (np.outer(t, t) % 256) / 256
    cs256 = np.stack([np.cos(a256) / 16.0, np.sin(a256) / 16.0], axis=0)
    cs256 = cs256.reshape(2, 2, 128, 256).transpose(2, 0, 1, 3)
    f = lambda a: np.ascontiguousarray(a.astype(np.float32))
    return dict(ft_wdft=f(wdft.reshape(2, 128, 512).transpose(1, 0, 2)), ft_ma=f(ma),
                ft_g=f(g.reshape(128, 4096)), ft_cs256=f(cs256.reshape(128, 1024)))


def _ft_init(self):
    kb, nc = self.kb, self.nc
    L = DEPTH
    dt_in = lambda name, shape, dt=F32: nc.dram_tensor(name, list(shape), dt, kind="ExternalInput").ap()
    self.ft_wdft_in = dt_in("ft_wdft", [128, 2, 512])
    self.ft_ma_in = dt_in("ft_ma", [128, 128])
    self.ft_g_in = dt_in("ft_g", [128, 4096])
    self.ft_cs256_in = dt_in("ft_cs256", [128, 1024])
    self.ft_w_in = dt_in("ft_w", [L, 256, 256])
    self.wdft_bf = kb.sb("wdft_bf", [128, 2, 512], BF16)
    self.ma_bf = kb.sb("ma_bf", [128, 128], BF16)
    self.g_bf = kb.sb("g_bf", [128, 64, 64], BF16)
    self.cs256_bf = kb.sb("cs256_bf", [128, 2, 2, 256], BF16)
    self.ftw_bf = kb.sb("ftw_bf", [128, 2, 256], BF16)
    self.VS = nc.dram_tensor("VS_scr", [NT, 512], BF16, kind="Internal").ap()
    self.ZS = nc.dram_tensor("ZS_scr", [2, 64, 64, 256], BF16, kind="Internal").ap()
    self.FS = nc.dram_tensor("FS_scr", [NT, 256], BF16, kind="Internal").ap()
    self.rVS = [Res(f"VS{g}") for g in range(NG)]
    self.rZS = Res("ZS")
    self.rFS = [Res(f"FS{g}") for g in range(NG)]


def _ft_setup(self):
    kb, nc = self.kb, self.nc

    def load_cast(dst_ap, dst_res, src_ap, ncols):
        stg = self.wstage[self.wstage_i % 2]
        self.wstage_i += 1
        kb.dma('sp', stg[:, 0:ncols], src_ap, stg, writes=[stg])
        kb.op('dve', lambda: nc.vector.tensor_copy(out=dst_ap, in_=stg[:, 0:ncols]), reads=[stg], writes=[dst_res])
    load_cast(self.wdft_bf[:].rearrange("p a b -> p (a b)"), self.wdft_bf, self.ft_wdft_in.rearrange("p a b -> p (a b)"), 1024)
    load_cast(self.ma_bf[:], self.ma_bf, self.ft_ma_in, 128)
    gv = self.g_bf[:].rearrange("p a b -> p (a b)")
    load_cast(gv[:, 0:2048], self.g_bf, self.ft_g_in[:, 0:2048], 2048)
    load_cast(gv[:, 2048:4096], self.g_bf, self.ft_g_in[:, 2048:4096], 2048)
    load_cast(self.cs256_bf[:].rearrange("p a b c -> p (a b c)"), self.cs256_bf, self.ft_cs256_in, 1024)


def ft_phase(self, l):
    kb, nc = self.kb, self.nc
    stg = self.wstage[self.wstage_i % 2]
    self.wstage_i += 1
    kb.dma('sp', stg[:, 0:512].rearrange("p (k d) -> p k d", k=2), self.ft_w_in[l].rearrange("(k p) d -> p k d", p=128), stg, writes=[stg])
    kb.op('dve', lambda: nc.vector.tensor_copy(out=self.ftw_bf[:].rearrange("p k d -> p (k d)"), in_=stg[:, 0:512]),
          reads=[stg], writes=[self.ftw_bf])
    for g in range(NG):
        tsl = slice(g * 512, (g + 1) * 512)
        uTb = self.wkb[0:2]
        for ct in range(2):
            uf = self.wk[ct]
            kb.dma('sp', uf[:], self.PT[C_FTU + ct * 128:C_FTU + (ct + 1) * 128, tsl], uf, reads=[self.rPT[g]], writes=[uf])
            kb.op('act', lambda ct=ct, uf=uf: nc.scalar.copy(out=uTb[ct][:], in_=uf[:]), reads=[uf], writes=[uTb[ct]])
        for j in range(4):
            acc = self.next_bank()
            for ct in range(2):
                kb.op('pe', lambda ct=ct: nc.tensor.matmul(acc[:], lhsT=uTb[ct][:, j * 128:(j + 1) * 128], rhs=self.wdft_bf[:, ct, :],
                                                           start=(ct == 0), stop=(ct == 1)),
                      reads=[uTb[ct], self.wdft_bf], writes=[acc])
            vb = self.tmstage[j % 2]
            kb.op('dve', lambda: nc.vector.tensor_copy(out=vb[:], in_=acc[:]), reads=[acc], writes=[vb])
            tt = g * 4 + j
            kb.dma('pool', self.VS[tt * 128:(tt + 1) * 128, :], vb[:], vb, reads=[vb], writes=[self.rVS[g]])
    VSv = self.VS[0:TS, :].rearrange("(a b) (r c) -> r a b c", b=64, r=2)
    for ch in range(8):
        va = self.hT[ch % 2]
        vav = va[:].rearrange("p a b -> p (a b)")[:, 0:2048].rearrange("p (t c) -> p t c", c=256)
        for ri in range(2):
            kb.dma('sp', vav[ri * 64:(ri + 1) * 64, :, :], VSv[ri, :, ch * 8:(ch + 1) * 8, :], va,
                   reads=self.rVS[0:8], writes=[va])
        zo = va[:].rearrange("p a b -> p (a b)")[:, 2048:4096]
        vflat = va[:].rearrange("p a b -> p (a b)")
        for q in range(4):
            acc = self.next_bank()
            kb.op('pe', lambda q=q: nc.tensor.matmul(acc[:], lhsT=self.ma_bf[:], rhs=vflat[:, q * 512:(q + 1) * 512], start=True, stop=True),
                  reads=[self.ma_bf, va], writes=[acc])
            if q % 2 == 0:
                kb.op('act', lambda q=q: nc.scalar.copy(out=zo[:, q * 512:(q + 1) * 512], in_=acc[:]), reads=[acc], writes=[va])
            else:
                kb.op('dve', lambda q=q: nc.vector.tensor_copy(out=zo[:, q * 512:(q + 1) * 512], in_=acc[:]), reads=[acc], writes=[va])
        for ri in range(2):
            kb.dma('pool', self.ZS[ri, :, ch * 8:(ch + 1) * 8, :], zo[ri * 64:(ri + 1) * 64, :].rearrange("p (t c) -> p t c", c=256), va,
                   reads=[va], writes=[self.rZS])
    ZSv = self.ZS.rearrange("r k t c -> r t k c")
    FSv = self.FS[0:TS, :].rearrange("(k2 k1) c -> k2 k1 c", k1=64)
    for ch in range(4):
        zb = self.hT[ch % 2]
        zflat = zb[:].rearrange("p a b -> p (a b)")
        z2 = zflat.rearrange("p (k c) -> p k c", c=256)
        for ri in range(2):
            kb.dma('sp', z2[ri * 64:(ri + 1) * 64, :, :], ZSv[ri, :, ch * 16:(ch + 1) * 16, :], zb,
                   reads=[self.rZS], writes=[zb])
        fo_res = self.wstage[1]
        fo = fo_res[:].bitcast(BF16)
        for kp in range(8):
            acc = self.next_bank()
            for e in range(2):
                k1l = kp * 2 + e
                k1 = ch * 16 + k1l
                kb.op('pe', lambda e=e, k1=k1, k1l=k1l: nc.tensor.matmul(acc[0:64, e * 256:(e + 1) * 256], lhsT=self.g_bf[:, k1, :], rhs=z2[:, k1l, :],
                                                                         start=True, stop=True),
                      reads=[self.g_bf, zb], writes=[acc])
            if kp % 2 == 0:
                kb.op('act', lambda kp=kp: nc.scalar.copy(out=fo[0:64, kp * 512:(kp + 1) * 512], in_=acc[0:64, :]), reads=[acc], writes=[fo_res])
            else:
                kb.op('dve', lambda kp=kp: nc.vector.tensor_copy(out=fo[0:64, kp * 512:(kp + 1) * 512], in_=acc[0:64, :]), reads=[acc], writes=[fo_res])
        kb.dma('pool', FSv[:, ch * 16:(ch + 1) * 16, :], fo[0:64, 0:4096].rearrange("p (k c) -> p k c", c=256), fo_res,
               reads=[fo_res], writes=self.rFS[0:8])
    for sq in range(2):
        base = TS + sq * 256
        vt = self.hT[sq % 2]
        vv = vt[:].rearrange("p a b -> p (a b)")[:, 0:1024].rearrange("p (k c) -> p k c", c=512)
        kb.dma('sp', vv, self.VS[base:base + 256, :].rearrange("(k p) c -> p k c", p=128), vt, reads=[self.rVS[8]], writes=[vt])
        for mt in range(2):
            acc = self.next_bank()
            n = 0
            for cs in range(2):
                for kt in range(2):
                    kb.op('pe', lambda cs=cs, kt=kt, n=n, mt=mt: nc.tensor.matmul(
                        acc[:, 0:256], lhsT=self.cs256_bf[:, cs, kt, mt * 128:(mt + 1) * 128], rhs=vv[:, kt, cs * 256:(cs + 1) * 256],
                        start=(n == 0), stop=(n == 3)), reads=[self.cs256_bf, vt], writes=[acc])
                    n += 1
            fb = self.tmstage[mt % 2]
            kb.op('dve', lambda: nc.vector.tensor_copy(out=fb[:, 0:256], in_=acc[:, 0:256]), reads=[acc], writes=[fb])
            kb.dma('pool', self.FS[base + mt * 128:base + (mt + 1) * 128, :], fb[:, 0:256], fb, reads=[fb], writes=[self.rFS[8]])
    for g in range(NG):
        tsl = slice(g * 512, (g + 1) * 512)
        ftok = self.wkb[2]
        fT = [self.wkb[0], self.wkb[1]]
        for j in range(4):
            ft = self.wkb[2 + j % 2]
            tt = g * 4 + j
            kb.dma('sp', ft[:, 0:256], self.FS[tt * 128:(tt + 1) * 128, :], ft, reads=[self.rFS[g]], writes=[ft])
            pT = self.bank[j % 2]
            pTv = pT[:].bitcast(BF16)
            for ct in range(2):
                kb.op('pe', lambda ct=ct: nc.tensor.transpose(pTv[:, ct * 128:(ct + 1) * 128], ft[:, ct * 128:(ct + 1) * 128], self.ident[:]),
                      reads=[ft, self.ident], writes=[pT])
            for ct in range(2):
                kb.op('act', lambda ct=ct: nc.scalar.copy(out=fT[ct][:, j * 128:(j + 1) * 128], in_=pTv[:, ct * 128:(ct + 1) * 128]),
                      reads=[pT], writes=[fT[ct]])
        for dtile in range(2):
            zT, sz = self.wk[2 + dtile], self.wk[4 + dtile]
            kb.dma('sp', zT[:], self.PT[C_FTZ + dtile * 128:C_FTZ + (dtile + 1) * 128, tsl], zT, reads=[self.rPT[g]], writes=[zT])
            kb.op('act', lambda: nc.scalar.activation(out=sz[:], in_=zT[:], func=AF.Silu), reads=[zT], writes=[sz])
            acc = self.next_bank()
            for ct in range(2):
                kb.op('pe', lambda ct=ct: nc.tensor.matmul(acc[:], lhsT=self.ftw_bf[:, ct, dtile * 128:(dtile + 1) * 128], rhs=fT[ct][:],
                                                           start=(ct == 0), stop=(ct == 1)),
                      reads=[self.ftw_bf, fT[ct]], writes=[acc])
            yb = self.ybuf[self.ybuf_i % 2]
            self.ybuf_i += 1
            kb.op('dve', lambda: nc.vector.tensor_tensor(out=yb[:], in0=acc[:], in1=sz[:], op=ALU.mult), reads=[acc, sz], writes=[yb])
            self.store_y(yb, 768 + dtile * 128, g)


Model.ft_init = _ft_init
Model.ft_setup = _ft_setup
Model.ft_phase = ft_phase


S5_L = 256
MAGIC = 12582912.0
P_ARE, P_AIM, P_LDT, P_DT, P_MAG, P_TH, P_K, P_R, P_SN, P_CS, P_ABRE, P_ABIM, P_DEN, P_NR, P_CRE, P_CIM, P_T1, P_T2, P_NCIM = range(19)


def _s5_init(self):
    kb, nc = self.kb, self.nc
    L = DEPTH
    dt_in = lambda name, shape, dt=F32: nc.dram_tensor(name, list(shape), dt, kind="ExternalInput").ap()
    self.s5_a_reT = dt_in("s5_a_reT", [L, 128, 16])
    self.s5_a_imT = dt_in("s5_a_imT", [L, 128, 16])
    self.s5_logdtT = dt_in("s5_logdtT", [L, 128, 16])
    self.s5_b_re = dt_in("s5_b_re", [L, 2, 16, 64, 16])
    self.s5_b_im = dt_in("s5_b_im", [L, 2, 16, 64, 16])
    self.s5_c_re = dt_in("s5_c_re", [L, 2, 16, 16, 64])
    self.s5_c_im = dt_in("s5_c_im", [L, 2, 16, 16, 64])
    self.s5_dT = dt_in("s5_dT", [L, 128, 2])
    self.s5_glu_bT = dt_in("s5_glu_bT", [L, 128, 2])
    self.s5_glu_w = dt_in("s5_glu_w", [L, 256, 256])
    self.s5_s0T = dt_in("s5_s0T", [L, 128, 32])
    self.new_s5T = nc.dram_tensor("new_s5T", [L, 128, 64], F32, kind="ExternalOutput").ap()
    self.YS = nc.dram_tensor("YS_scr", [2, 256, NT], F32, kind="Internal").ap()
    self.YG = nc.dram_tensor("YG_scr", [256, NT], BF16, kind="Internal").ap()
    self.rYS = [[[Res(f"YS{d}_{c}_{k}") for k in range(NT // S5_L)] for c in range(2)] for d in range(2)]
    self.rYG = [[Res(f"YG{c}_{g}") for g in range(NG)] for c in range(2)]
    self.s5par = kb.sb("s5par", [128, 19, 16], F32)
    self.s5b = kb.sb("s5b", [128, 4, 16], F32)
    self.s5s0 = kb.sb("s5s0", [128, 32], F32)
    self.s5carry = kb.sb("s5carry", [128, 4, 2], F32)
    self.s5carry2 = kb.sb("s5carry2", [128, 2, 4], F32)
    self.s5ctmp = kb.sb("s5ctmp", [128, 2, 4], F32)
    self.s5carry2b = kb.sb("s5carry2b", [128, 2, 4], F32)
    self.s5ctmpb = kb.sb("s5ctmpb", [128, 2, 4], F32)
    self.s5fin = kb.sb("s5fin", [128, 64], F32)
    self.s5d = kb.sb("s5d", [128, 2], F32)
    self.s5gb = kb.sb("s5gb", [128, 2], F32)
    self.gluw_bf = kb.sb("gluw_bf", [128, 2, 256], BF16)
    self.hp_col = kb.sb("hp_col", [128, 1], F32)
    self.ident32 = kb.sb("ident32", [128, 128], F32)


def _s5_setup(self):
    kb, nc = self.kb, self.nc
    kb.op('dve', lambda: nc.vector.memset(self.hp_col[:], float(np.pi / 2)), writes=[self.hp_col])
    kb.op('dve', lambda: nc.vector.tensor_copy(out=self.ident32[:], in_=self.ident[:]), reads=[self.ident], writes=[self.ident32])


def s5_phase(self, l):
    kb, nc = self.kb, self.nc
    P = self.s5par
    V, A, G_ = 'dve', 'act', 'pool'

    def pc(i):
        return P[:, i, :]

    def dve_tt(o, a, b, op):
        kb.op(V, lambda: nc.vector.tensor_tensor(out=pc(o), in0=pc(a), in1=pc(b), op=op), reads=[P], writes=[P])

    def dve_ts(o, a, s1, s2, op0, op1=None):
        if op1 is None:
            kb.op(V, lambda: nc.vector.tensor_scalar(out=pc(o), in0=pc(a), scalar1=s1, scalar2=None, op0=op0), reads=[P], writes=[P])
        else:
            kb.op(V, lambda: nc.vector.tensor_scalar(out=pc(o), in0=pc(a), scalar1=s1, scalar2=s2, op0=op0, op1=op1), reads=[P], writes=[P])

    def act_f(o, a, func, **kw):
        rd = [P] + ([self.hp_col] if 'bias' in kw else [])
        kb.op(A, lambda: nc.scalar.activation(out=pc(o), in_=pc(a), func=func, **kw), reads=rd, writes=[P])

    kb.dma('sp', pc(P_ARE), self.s5_a_reT[l], P, writes=[P])
    kb.dma('sp', pc(P_AIM), self.s5_a_imT[l], P, writes=[P])
    kb.dma('sp', pc(P_LDT), self.s5_logdtT[l], P, writes=[P])
    kb.dma('sp', self.s5s0[:], self.s5_s0T[l], self.s5s0, writes=[self.s5s0])
    kb.dma('sp', self.s5d[:], self.s5_dT[l], self.s5d, writes=[self.s5d])
    kb.dma('sp', self.s5gb[:], self.s5_glu_bT[l], self.s5gb, writes=[self.s5gb])
    stg = self.wstage[self.wstage_i % 2]
    self.wstage_i += 1
    kb.dma('sp', stg[:, 0:512].rearrange("p (k d) -> p k d", k=2), self.s5_glu_w[l].rearrange("(k p) d -> p k d", p=128), stg, writes=[stg])
    kb.op(V, lambda: nc.vector.tensor_copy(out=self.gluw_bf[:].rearrange("p k d -> p (k d)"), in_=stg[:, 0:512]), reads=[stg], writes=[self.gluw_bf])
    act_f(P_DT, P_LDT, AF.Exp)
    dve_tt(P_T1, P_ARE, P_DT, ALU.mult)
    act_f(P_MAG, P_T1, AF.Exp)
    dve_tt(P_TH, P_AIM, P_DT, ALU.mult)
    dve_ts(P_K, P_TH, float(1 / (2 * np.pi)), MAGIC, ALU.mult, ALU.add)
    dve_ts(P_K, P_K, MAGIC, None, ALU.subtract)
    kb.op(V, lambda: nc.vector.scalar_tensor_tensor(out=pc(P_R), in0=pc(P_K), scalar=float(-2 * np.pi), in1=pc(P_TH), op0=ALU.mult, op1=ALU.add), reads=[P], writes=[P])
    act_f(P_SN, P_R, AF.Sin)
    act_f(P_T2, P_R, AF.Abs)
    act_f(P_CS, P_T2, AF.Sin, scale=-1.0, bias=self.hp_col[:])
    dve_tt(P_ABRE, P_MAG, P_CS, ALU.mult)
    dve_tt(P_ABIM, P_MAG, P_SN, ALU.mult)
    dve_tt(P_DEN, P_ARE, P_ARE, ALU.mult)
    dve_tt(P_T1, P_AIM, P_AIM, ALU.mult)
    dve_tt(P_DEN, P_DEN, P_T1, ALU.add)
    kb.op(V, lambda: nc.vector.reciprocal(out=pc(P_DEN), in_=pc(P_DEN)), reads=[P], writes=[P])
    dve_ts(P_NR, P_ABRE, -1.0, None, ALU.add)
    dve_tt(P_T1, P_NR, P_ARE, ALU.mult)
    dve_tt(P_T2, P_ABIM, P_AIM, ALU.mult)
    dve_tt(P_T1, P_T1, P_T2, ALU.add)
    dve_tt(P_CRE, P_T1, P_DEN, ALU.mult)
    dve_tt(P_T1, P_ABIM, P_ARE, ALU.mult)
    dve_tt(P_T2, P_NR, P_AIM, ALU.mult)
    dve_tt(P_T1, P_T1, P_T2, ALU.subtract)
    dve_tt(P_CIM, P_T1, P_DEN, ALU.mult)
    dve_ts(P_NCIM, P_CIM, -1.0, None, ALU.mult)

    if not hasattr(self, '_s5_bufs'):
        self._s5_bufs = self.s5_make_bufs()
    C0, C1 = self._s5_bufs
    for src_, al in self.s5_alias:
        kb.alias_begin(src_, al)
    kb.op(V, lambda: nc.vector.memset(self.otmp[0][:], 0.0), writes=[self.otmp[0]])
    kb.op(V, lambda: nc.vector.memset(self.otmp[1][:], 0.0), writes=[self.otmp[1]])
    kb.op(V, lambda: nc.vector.memset(self.s5fin[:], 0.0), writes=[self.s5fin])
    run_chains([self.s5_chain(l, 0, C0), self.s5_chain(l, 1, C1)], head_start=(self.cfg.get('s5_hs', 2), 0))
    for src_, al in self.s5_alias:
        kb.alias_end(src_, al)
    kb.dma('pool', self.new_s5T[l], self.s5fin[:], self.s5fin, reads=[self.s5fin])
    Lc = S5_L
    for g in range(NG):
        tsl = slice(g * 512, (g + 1) * 512)
        yg = [self.wkb[2], self.wkb[3]]
        for ct in range(2):
            y0, y1, uf = self.wk[0], self.wk[1], self.wk[2]
            rows = slice(ct * 128, (ct + 1) * 128)
            kb.dma('sp', y0[:], self.YS[0, rows, tsl], y0, reads=[self.rYS[0][ct][2 * g], self.rYS[0][ct][2 * g + 1]], writes=[y0])
            kb.dma('sp', y1[:], self.YS[1, rows, tsl], y1, reads=[self.rYS[1][ct][2 * g], self.rYS[1][ct][2 * g + 1]], writes=[y1])
            kb.dma('sp', uf[:], self.PT[C_S5U + ct * 128:C_S5U + (ct + 1) * 128, tsl], uf, reads=[self.rPT[g]], writes=[uf])
            kb.op(V, lambda: nc.vector.tensor_tensor(out=y0[:], in0=y0[:], in1=y1[:], op=ALU.add), reads=[y0, y1], writes=[y0])
            kb.op(V, lambda: nc.vector.scalar_tensor_tensor(out=y0[:], in0=uf[:], scalar=self.s5d[:, ct:ct + 1], in1=y0[:], op0=ALU.mult, op1=ALU.add),
                  reads=[uf, self.s5d, y0], writes=[y0])
            g2 = self.wk[3]
            kb.op(V, lambda: nc.vector.tensor_tensor(out=g2[:], in0=y0[:], in1=y0[:], op=ALU.mult), reads=[y0], writes=[g2])
            kb.op(V, lambda: nc.vector.tensor_scalar(out=g2[:], in0=g2[:], scalar1=0.044715, scalar2=1.0, op0=ALU.mult, op1=ALU.add), reads=[g2], writes=[g2])
            kb.op(V, lambda: nc.vector.tensor_tensor(out=g2[:], in0=g2[:], in1=y0[:], op=ALU.mult), reads=[g2, y0], writes=[g2])
            kb.op(A, lambda: nc.scalar.activation(out=g2[:], in_=g2[:], func=AF.Sigmoid, scale=1.5957691216057308), reads=[g2], writes=[g2])
            kb.op(V, lambda ct=ct: nc.vector.tensor_tensor(out=yg[ct][:], in0=g2[:], in1=y0[:], op=ALU.mult), reads=[g2, y0], writes=[yg[ct]])
        for dtile in range(2):
            acc = self.next_bank()
            for ct in range(2):
                kb.op('pe', lambda ct=ct: nc.tensor.matmul(acc[:], lhsT=self.gluw_bf[:, ct, dtile * 128:(dtile + 1) * 128], rhs=yg[ct][:],
                                                           start=(ct == 0), stop=(ct == 1)), reads=[self.gluw_bf, yg[ct]], writes=[acc])
            sig, zT = self.wk[4], self.wk[5]
            kb.op(A, lambda: nc.scalar.activation(out=sig[:], in_=acc[:], func=AF.Sigmoid, bias=self.s5gb[:, dtile:dtile + 1]),
                  reads=[acc, self.s5gb], writes=[sig])
            kb.dma('sp', zT[:], self.PT[C_S5Z + dtile * 128:C_S5Z + (dtile + 1) * 128, tsl], zT, reads=[self.rPT[g]], writes=[zT])
            kb.op(A, lambda: nc.scalar.activation(out=zT[:], in_=zT[:], func=AF.Silu), reads=[zT], writes=[zT])
            kb.op(V, lambda: nc.vector.tensor_tensor(out=sig[:], in0=sig[:], in1=zT[:], op=ALU.mult), reads=[sig, zT], writes=[sig])
            yb = self.ybuf[self.ybuf_i % 2]
            self.ybuf_i += 1
            kb.op(V, lambda: nc.vector.tensor_tensor(out=yb[:], in0=sig[:], in1=yg[dtile][:], op=ALU.mult), reads=[sig, yg[dtile]], writes=[yb])
            self.store_y(yb, 512 + dtile * 128, g)


class S5Bufs:
    pass


def s5_make_bufs(self):
    hv = lambda r: r[:].rearrange("p a b -> p (a b)")
    wf = hv(self.w_in_bf)
    wo = hv(self.w_out_bf)
    C0, C1 = S5Bufs(), S5Bufs()
    slot = lambda n_, nm: Res(nm, wf[:, n_ * 4096:(n_ + 1) * 4096].bitcast(F32))
    C0.TCr, C0.TSr = self.xt[0], self.xt[1]
    C0.BTr, C0.CTr = self.xn[0], self.xn[1]
    C0.buR, C0.tabR, C0.zR = slot(0, "s5c0bu"), slot(1, "s5c0tab"), slot(2, "s5c0z")
    C0.sbR = self.hT[0]
    C0.carry, C0.ctmp = self.s5carry2, self.s5ctmp
    C0.uf, C0.ubr, C0.yst = self.wk[0], self.wkb[0], self.pstage[0]
    C0.t1r, C0.t2r = self.wk[2], self.wk[3]
    C0.bank_lo = 0
    C0.CTnR = self.junk
    C1.TCr = Res("s5c1TC", wo[:, 0:2048].bitcast(F32))
    C1.TSr = Res("s5c1TS", wo[:, 2048:4096].bitcast(F32))
    C1.BTr = Res("s5c1BT", wo[:, 4096:5120])
    C1.CTr = Res("s5c1CT", wo[:, 5120:6144])
    C1.buR, C1.tabR = slot(3, "s5c1bu"), slot(4, "s5c1tab")
    C1.zR = Res("s5c1z", self.wstage[0][:, 0:2048])
    C1.sbR = self.hT[1]
    C1.carry, C1.ctmp = self.s5carry2b, self.s5ctmpb
    C1.uf, C1.ubr, C1.yst = self.wk[1], self.wkb[1], self.pstage[1]
    C1.t1r, C1.t2r = self.wk[4], self.wk[5]
    C1.bank_lo = 4
    C1.CTnR = Res("s5c1CTn", wo[:, 6144:7168])
    self.s5_alias = [(self.w_in_bf, [C0.buR, C0.tabR, C0.zR, C1.buR, C1.tabR]),
                     (self.w_out_bf, [C1.TCr, C1.TSr, C1.BTr, C1.CTr, C1.CTnR]),
                     (self.wstage[0], [C1.zR])]
    return C0, C1


def s5_chain(self, l, d, C):
    kb, nc = self.kb, self.nc
    P = self.s5par
    V, A = 'dve', 'act'
    Lc = S5_L
    bst = [0]

    def nb():
        b = self.bank[C.bank_lo + bst[0] % 4]
        bst[0] += 1
        return b
    TCr, TSr, BTr, CTr = C.TCr, C.TSr, C.BTr, C.CTr
    TCv = TCr[:, :].rearrange("p (s t) -> p s t", s=4)
    TSv = TSr[:, :].rearrange("p (s t) -> p s t", s=4)
    BBr, XXr = self.otmp[0], self.otmp[1]
    BB = BBr[:].rearrange("p (v r c) -> p v r c", v=4, r=2)
    XX = XXr[:].rearrange("p (v r c) -> p v r c", v=4, r=2)
    BT = BTr[:, :].rearrange("p (s r c) -> p s r c", s=4, r=2)
    CT = CTr[:, :].rearrange("p (s r c) -> p s r c", s=4, r=2)
    CTn = C.CTnR[:, 0:512].rearrange("p (s c) -> p s c", s=4)
    bur, tabr, zR, sbr = C.buR, C.tabR, C.zR, C.sbR
    bu = bur[:, :].rearrange("p (s r t) -> p s r t", s=4, r=2)
    z = zR[:, :].rearrange("p (s r t) -> p s r t", s=4, r=2)
    tab = tabr[:, :].rearrange("p (r s t) -> p r s t", r=2, s=4)
    sbv = sbr[:].rearrange("p a b -> p (a b)").rearrange("p (r s t) -> p r s t", r=4, s=4)
    carry, ctmp = C.carry, C.ctmp
    uf, ubr = C.uf, C.ubr
    seqs = [(0, 0, TS // Lc)] + [(1 + q, (TS + q * 256) // Lc, 1) for q in range(2)]
    for ct in range(2):
        for s4 in range(4):
            s = ct * 4 + s4
            col = d * 8 + s
            b4 = self.s5b
            kb.dma('sp', b4[:, 0, :], self.s5_b_re[l, d, 2 * s:2 * s + 2].rearrange("g n p -> (g n) p"), b4, writes=[b4])
            kb.dma('sp', b4[:, 1, :], self.s5_b_im[l, d, 2 * s:2 * s + 2].rearrange("g n p -> (g n) p"), b4, writes=[b4])
            kb.op(V, lambda col=col: nc.vector.tensor_scalar(out=b4[:, 2, :], in0=b4[:, 1, :], scalar1=P[:, P_NCIM, col:col + 1], scalar2=None, op0=ALU.mult),
                  reads=[b4, P], writes=[b4])
            kb.op(V, lambda col=col: nc.vector.tensor_scalar(out=b4[:, 3, :], in0=b4[:, 0, :], scalar1=P[:, P_CIM, col:col + 1], scalar2=None, op0=ALU.mult),
                  reads=[b4, P], writes=[b4])
            for gl in range(2):
                pl = slice(gl * 64, (gl + 1) * 64)
                c0 = 32 * s4 + 16 * gl
                kb.op(V, lambda pl=pl, c0=c0, col=col, s4=s4: nc.vector.scalar_tensor_tensor(
                    out=BB[pl, s4, 0, c0:c0 + 16], in0=b4[pl, 0, :], scalar=P[pl, P_CRE, col:col + 1], in1=b4[pl, 2, :],
                    op0=ALU.mult, op1=ALU.add), reads=[b4, P], writes=[BBr])
                kb.op(V, lambda pl=pl, c0=c0, col=col, s4=s4: nc.vector.scalar_tensor_tensor(
                    out=BB[pl, s4, 1, c0:c0 + 16], in0=b4[pl, 1, :], scalar=P[pl, P_CRE, col:col + 1], in1=b4[pl, 3, :],
                    op0=ALU.mult, op1=ALU.add), reads=[b4, P], writes=[BBr])
                for ri, csrc in enumerate((self.s5_c_re, self.s5_c_im)):
                    kb.dma('sp', XX[c0:c0 + 16, s4, ri, 64 * gl:64 * gl + 64], csrc[l, d, 2 * s + gl], XXr, writes=[XXr])
            for ri in range(2):
                pt = nb()
                kb.op('pe', lambda ri=ri, s4=s4, pt=pt: nc.tensor.transpose(pt[:, 0:128], BB[:, s4, ri, :], self.ident32[:]),
                      reads=[BBr, self.ident32], writes=[pt])
                kb.op(A, lambda ri=ri, s4=s4, pt=pt: nc.scalar.copy(out=BT[:, s4, ri, :], in_=pt[:, 0:128]), reads=[pt], writes=[BTr])
                pt2 = nb()
                kb.op('pe', lambda ri=ri, s4=s4, pt2=pt2: nc.tensor.transpose(pt2[:, 0:128], XX[:, s4, ri, :], self.ident32[:]),
                      reads=[XXr, self.ident32], writes=[pt2])
                kb.op(A, lambda ri=ri, s4=s4, pt2=pt2: nc.scalar.activation(out=CT[:, s4, ri, :], in_=pt2[:, 0:128], func=AF.Identity,
                                                                        scale=(1.0 if ri == 0 else -1.0)), reads=[pt2], writes=[CTr])
                if ri == 0:
                    kb.op(A, lambda s4=s4, pt2=pt2: nc.scalar.activation(out=CTn[:, s4, :], in_=pt2[:, 0:128], func=AF.Identity, scale=-1.0),
                          reads=[pt2], writes=[C.CTnR])
            yield
        c0_ = d * 8 + ct * 4
        kb.op(V, lambda: nc.vector.tensor_copy(out=TCv[:, :, 0:1], in_=P[:, P_CS, c0_:c0_ + 4].unsqueeze(2)), reads=[P], writes=[TCr])
        kb.op(V, lambda: nc.vector.tensor_copy(out=TSv[:, :, 0:1], in_=P[:, P_SN, c0_:c0_ + 4].unsqueeze(2)), reads=[P], writes=[TSr])
        t1r, t2r = C.t1r, C.t2r
        m = 1
        while m < Lc:
            cb = TCv[:, :, m - 1:m].to_broadcast([128, 4, m])
            sbc = TSv[:, :, m - 1:m].to_broadcast([128, 4, m])
            t1 = t1r[:, 0:4 * m].rearrange("p (s t) -> p s t", s=4)
            t2 = t2r[:, 0:4 * m].rearrange("p (s t) -> p s t", s=4)
            kb.op(V, lambda m=m, sbc=sbc, t1=t1: nc.vector.tensor_tensor(out=t1, in0=TSv[:, :, 0:m], in1=sbc, op=ALU.mult), reads=[TSr], writes=[t1r])
            kb.op(V, lambda m=m, sbc=sbc, t2=t2: nc.vector.tensor_tensor(out=t2, in0=TCv[:, :, 0:m], in1=sbc, op=ALU.mult), reads=[TCr, TSr], writes=[t2r])
            kb.op(V, lambda m=m, cb=cb: nc.vector.tensor_tensor(out=TCv[:, :, m:2 * m], in0=TCv[:, :, 0:m], in1=cb, op=ALU.mult), reads=[TCr], writes=[TCr])
            kb.op(V, lambda m=m, cb=cb: nc.vector.tensor_tensor(out=TSv[:, :, m:2 * m], in0=TSv[:, :, 0:m], in1=cb, op=ALU.mult), reads=[TSr, TCr], writes=[TSr])
            kb.op(V, lambda m=m, t1=t1: nc.vector.tensor_tensor(out=TCv[:, :, m:2 * m], in0=TCv[:, :, m:2 * m], in1=t1, op=ALU.subtract), reads=[TCr, t1r], writes=[TCr])
            kb.op(V, lambda m=m, t2=t2: nc.vector.tensor_tensor(out=TSv[:, :, m:2 * m], in0=TSv[:, :, m:2 * m], in1=t2, op=ALU.add), reads=[TSr, t2r], writes=[TSr])
            m *= 2
            yield
        yield
        magb = P[:, P_MAG, c0_:c0_ + 4]
        for (sid, c_first, c_n) in seqs:
            order = list(range(c_first, c_first + c_n))
            if d == 1:
                order = order[::-1]
            for oi, ck in enumerate(order):
                t0 = ck * Lc
                tsl = slice(t0, t0 + Lc)
                g512 = t0 // 512
                kb.dma('sp', uf[:, 0:Lc], self.PT[C_S5U + ct * 128:C_S5U + (ct + 1) * 128, tsl], uf, reads=[self.rPT[g512]], writes=[uf])
                kb.op(A, lambda: nc.scalar.copy(out=ubr[:, 0:Lc], in_=uf[:, 0:Lc]), reads=[uf], writes=[ubr])
                rhs_u = ubr[:, 0:Lc] if d == 0 else ubr[:, Lc - 1::-1]
                if oi == 0:
                    if sid == 0:
                        s0v = self.s5s0[:, 2 * c0_:2 * c0_ + 8].rearrange("p (s r) -> p r s", r=2)
                        kb.op(V, lambda s0v=s0v: nc.vector.tensor_copy(out=carry[:], in_=s0v), reads=[self.s5s0], writes=[carry])
                    else:
                        kb.op(V, lambda: nc.vector.memset(carry[:], 0.0), writes=[carry])
                for s4 in range(4):
                    pbu = nb()
                    for ri in range(2):
                        kb.op('pe', lambda ri=ri, s4=s4, pbu=pbu: nc.tensor.matmul(pbu[:, ri * Lc:(ri + 1) * Lc], lhsT=BT[:, s4, ri, :], rhs=rhs_u, start=True, stop=True),
                              reads=[BTr, ubr], writes=[pbu])
                    kb.op(A, lambda s4=s4, pbu=pbu: nc.scalar.copy(out=bu[:, s4, :, :].rearrange("p r t -> p (r t)"), in_=pbu[:, 0:2 * Lc]), reads=[pbu], writes=[bur])
                yield
                bre, bim = bu[:, :, 0, :], bu[:, :, 1, :]
                tA, tB = tab[:, 0, :, :], tab[:, 1, :, :]
                zt0, zt1 = z[:, :, 0, :], z[:, :, 1, :]
                kb.op(V, lambda: nc.vector.tensor_tensor(out=tA, in0=bre, in1=TCv[:, :, :], op=ALU.mult), reads=[bur, TCr], writes=[tabr])
                kb.op(V, lambda: nc.vector.tensor_tensor(out=zt0, in0=bim, in1=TSv[:, :, :], op=ALU.mult), reads=[bur, TSr], writes=[zR])
                kb.op(V, lambda: nc.vector.tensor_tensor(out=tB, in0=bim, in1=TCv[:, :, :], op=ALU.mult), reads=[bur, TCr], writes=[tabr])
                kb.op(V, lambda: nc.vector.tensor_tensor(out=zt1, in0=bre, in1=TSv[:, :, :], op=ALU.mult), reads=[bur, TSr], writes=[zR])
                kb.op(V, lambda: nc.vector.tensor_tensor(out=tA, in0=tA, in1=zt0, op=ALU.add), reads=[tabr, zR], writes=[tabr])
                kb.op(V, lambda: nc.vector.tensor_tensor(out=tB, in0=tB, in1=zt1, op=ALU.subtract), reads=[tabr, zR], writes=[tabr])
                yield
                for s4 in range(4):
                    for ri in range(2):
                        kb.op(V, lambda ri=ri, s4=s4: nc.vector.tensor_tensor_scan(
                            out=z[:, s4, ri, :], data0=magb[:, s4:s4 + 1].to_broadcast([128, Lc]), data1=tab[:, ri, s4, :],
                            initial=carry[:, ri, s4:s4 + 1], op0=ALU.mult, op1=ALU.add), reads=[P, tabr, carry], writes=[zR])
                yield
                zre, zim = z[:, :, 0, :], z[:, :, 1, :]
                zl_re, zl_im = z[:, :, 0, Lc - 1:Lc], z[:, :, 1, Lc - 1:Lc]
                cl, sl_ = TCv[:, :, Lc - 1:Lc], TSv[:, :, Lc - 1:Lc]
                c_re, c_im = carry[:, 0, :].unsqueeze(2), carry[:, 1, :].unsqueeze(2)
                kb.op(V, lambda: nc.vector.tensor_tensor(out=ctmp[:, 0, :].unsqueeze(2), in0=zl_im, in1=sl_, op=ALU.mult), reads=[zR, TSr], writes=[ctmp])
                kb.op(V, lambda: nc.vector.tensor_tensor(out=ctmp[:, 1, :].unsqueeze(2), in0=zl_re, in1=sl_, op=ALU.mult), reads=[zR, TSr], writes=[ctmp])
                kb.op(V, lambda: nc.vector.tensor_tensor(out=c_re, in0=zl_re, in1=cl, op=ALU.mult), reads=[zR, TCr], writes=[carry])
                kb.op(V, lambda: nc.vector.tensor_tensor(out=c_im, in0=zl_im, in1=cl, op=ALU.mult), reads=[zR, TCr], writes=[carry])
                kb.op(V, lambda: nc.vector.tensor_tensor(out=carry[:, 0, :], in0=carry[:, 0, :], in1=ctmp[:, 0, :], op=ALU.subtract), reads=[carry, ctmp], writes=[carry])
                kb.op(V, lambda: nc.vector.tensor_tensor(out=carry[:, 1, :], in0=carry[:, 1, :], in1=ctmp[:, 1, :], op=ALU.add), reads=[carry, ctmp], writes=[carry])
                if sid > 0 and oi == len(order) - 1:
                    for s4 in range(4):
                        s = ct * 4 + s4
                        fi = (((sid - 1) * 2 + d) * 8 + s) * 2
                        kb.op(V, lambda s4=s4, fi=fi: nc.vector.tensor_copy(out=self.s5fin[:, fi:fi + 2], in_=carry[:, :, s4]),
                              reads=[carry], writes=[self.s5fin])
                kb.op(V, lambda: nc.vector.tensor_tensor(out=sbv[:, 0, :, :], in0=zre, in1=TCv[:, :, :], op=ALU.mult), reads=[zR, TCr], writes=[sbr])
                kb.op(V, lambda: nc.vector.tensor_tensor(out=sbv[:, 1, :, :], in0=zim, in1=TSv[:, :, :], op=ALU.mult), reads=[zR, TSr], writes=[sbr])
                kb.op(V, lambda: nc.vector.tensor_tensor(out=sbv[:, 2, :, :], in0=zre, in1=TSv[:, :, :], op=ALU.mult), reads=[zR, TSr], writes=[sbr])
                kb.op(V, lambda: nc.vector.tensor_tensor(out=sbv[:, 3, :, :], in0=zim, in1=TCv[:, :, :], op=ALU.mult), reads=[zR, TCr], writes=[sbr])
                yield
                acc_y = nb()
                for s4 in range(4):
                    lhs = [CT[:, s4, 0, :], CTn[:, s4, :], CT[:, s4, 1, :], CT[:, s4, 1, :]]
                    for pi in range(4):
                        kb.op('pe', lambda pi=pi, s4=s4, lhs=lhs: nc.tensor.matmul(acc_y[:, 0:Lc], lhsT=lhs[pi], rhs=sbv[:, pi, s4, :],
                                                                                   start=(s4 == 0 and pi == 0), stop=(s4 == 3 and pi == 3)),
                              reads=[CTr, C.CTnR, sbr], writes=[acc_y])
                yst = C.yst
                if d == 0:
                    kb.op(A, lambda: nc.scalar.copy(out=yst[:, 0:Lc], in_=acc_y[:, 0:Lc]), reads=[acc_y], writes=[yst])
                else:
                    kb.op(V, lambda: nc.vector.tensor_copy(out=yst[:, 0:Lc], in_=acc_y[:, Lc - 1::-1]), reads=[acc_y], writes=[yst])
                kb.dma('sp', self.YS[d, ct * 128:(ct + 1) * 128, tsl], yst[:, 0:Lc], yst, reads=[yst], writes=[self.rYS[d][ct][ck]])
                yield


Model.s5_make_bufs = s5_make_bufs
Model.s5_chain = s5_chain


def make_s5_inputs(inputs, k):
    f32 = lambda a: np.ascontiguousarray(np.asarray(a, dtype=np.float32))
    L = DEPTH

    def par_T(a):
        a = f32(a).reshape(L, 2, 8, 2, 64)
        return np.ascontiguousarray(a.transpose(0, 3, 4, 1, 2).reshape(L, 128, 16))
    ldt = np.broadcast_to(f32(inputs['s5_log_dt'])[:, :, :, None], (L, 2, 16, 64))
    sb = k // 4
    s0 = f32(inputs['state_s5'])[sb]
    s0 = s0.reshape(L, 2, 2, 8, 2, 64)
    s0T = np.ascontiguousarray(s0.transpose(0, 4, 5, 1, 3, 2).reshape(L, 128, 32))
    return dict(
        s5_a_reT=par_T(inputs['s5_a_re']), s5_a_imT=par_T(inputs['s5_a_im']), s5_logdtT=par_T(ldt),
        s5_b_re=f32(inputs['s5_b_re']), s5_b_im=f32(inputs['s5_b_im']),
        s5_c_re=f32(inputs['s5_c_re']), s5_c_im=f32(inputs['s5_c_im']),
        s5_dT=np.ascontiguousarray(f32(inputs['s5_d']).reshape(L, 2, 128).transpose(0, 2, 1)),
        s5_glu_bT=np.ascontiguousarray(f32(inputs['s5_glu_b']).reshape(L, 2, 128).transpose(0, 2, 1)),
        s5_glu_w=f32(inputs['s5_glu_w']), s5_s0T=s0T)


def unpack_new_s5(t):
    L = t.shape[0]
    a = t.reshape(L, 2, 64, 2, 2, 8, 2)
    a = a.transpose(3, 0, 4, 6, 5, 1, 2)
    return np.ascontiguousarray(a.reshape(2, L, 2, 2, 16, 64))


Model.s5_init = _s5_init
Model.s5_setup = _s5_setup
Model.s5_phase = s5_phase


def make_dn_consts():
    i = np.arange(64)
    U = [(i[:, None] <= i[None, :]), (i[:, None] >= i[None, :])]
    c = {}
    ud_blk = np.zeros((2, 128, 128), np.float32)
    ud2 = np.zeros((2, 128, 64), np.float32)
    mincl = np.zeros((2, 128, 64), np.float32)
    mstrict = np.zeros((2, 128, 64), np.float32)
    for d in range(2):
        for cc in range(2):
            ud_blk[d, cc * 64:(cc + 1) * 64, cc * 64:(cc + 1) * 64] = U[d]
            ud2[d, cc * 64:(cc + 1) * 64] = U[d]
            mincl[d, cc * 64:(cc + 1) * 64] = U[d].T
            mstrict[d, cc * 64:(cc + 1) * 64] = U[d].T & (i[:, None] != i[None, :])
    blk = np.zeros((128, 128), np.float32)
    blk[0:64, 0:64] = 1
    blk[64:128, 64:128] = 1
    eye2 = np.concatenate([np.eye(64), np.eye(64)], 0).astype(np.float32)
    packed = np.concatenate([ud_blk[0], ud_blk[1], ud2[0], ud2[1], blk, mincl[0], mincl[1], mstrict[0], mstrict[1], eye2], axis=1)
    return dict(dn_consts=np.ascontiguousarray(packed.astype(np.float32)))


DNC_UDBLK, DNC_UD2, DNC_BLK, DNC_MINCL, DNC_MSTR, DNC_EYE2 = 0, 256, 384, 512, 640, 768


def _dn_init(self):
    kb, nc = self.kb, self.nc
    L = DEPTH
    dt_in = lambda name, shape, dt=F32: nc.dram_tensor(name, list(shape), dt, kind="ExternalInput").ap()
    self.dn_consts_in = dt_in("dn_consts", [128, 832])
    self.dn_convT = dt_in("dn_convT", [L, 128, 6, 5])
    self.dn_cols = dt_in("dn_cols", [L, 16, 2])
    self.dn_norm_g = dt_in("dn_norm_g", [L, 64])
    self.dn_s0 = dt_in("dn_s0", [L, 2, 4, 64, 64])
    self.new_dn = nc.dram_tensor("new_dn", [2, L, 2, 4, 64, 64], F32, kind="ExternalOutput").ap()
    mk = lambda name, shape, dt: nc.dram_tensor(name, list(shape), dt, kind="Internal").ap()
    self.QT = mk("QT_scr", [256, NT], BF16)
    self.KT = mk("KT_scr", [256, NT], BF16)
    self.QTOK = mk("QTOK_scr", [NT, 256], BF16)
    self.KTOK = mk("KTOK_scr", [NT, 256], BF16)
    self.VTOK = mk("VTOK_scr", [NT, 256], BF16)
    self.BG = mk("BG_scr", [NT, 16], F32)
    self.OF = mk("OF_scr", [2, NT, 256], F32)
    self.ON = mk("ON_scr", [NT, 256], BF16)
    self.rDN0 = [Res(f"DN0_{t}") for t in range(NTT)]
    self.rOF = [[Res(f"OF{d}_{t}") for t in range(NTT)] for d in range(2)]
    self.rON = [Res(f"ON{t}") for t in range(NTT)]
    self.dnc = kb.sb("dnc", [128, 832], F32)
    self.dn_convw = kb.sb("dn_convw", [128, 6, 5], F32)
    self.dn_colsb_full = kb.sb("dn_colsb", [128, 4], F32)
    self.dn_ngbc = kb.sb("dn_ngbc", [128, 64], F32)
    self.dn_S = kb.sb("dn_S", [128, 4, 64], F32)
    self.dn_sm = kb.sb("dn_sm", [128, 80], F32)
    self.dn_bg2 = kb.sb("dn_bg2", [128, 2, 16], F32)
    self.dn_bg2b = kb.sb("dn_bg2b", [128, 2, 16], F32)
    self.dn_bg2c = kb.sb("dn_bg2c", [128, 2, 16], F32)
    self.dn_bg2d = kb.sb("dn_bg2d", [128, 2, 16], F32)
    self.dn_smb = kb.sb("dn_smb", [128, 80], F32)
    self.dn_Sb = kb.sb("dn_Sb", [128, 4, 64], F32)
    self.dn_xb = kb.sb("dn_xb", [128, 520], BF16)
    self.dn_diag = kb.sb("dn_diag", [128, 5, 128], BF16)
    self.dn_bg = kb.sb("dn_bg", [128, 16], F32)
    self.dn_qkb = kb.sb("dn_qkb", [128, 2, 2], F32)


def _dn_setup(self):
    kb, nc = self.kb, self.nc
    kb.dma('sp', self.dnc[:], self.dn_consts_in, self.dnc, writes=[self.dnc])
    kb.op('dve', lambda: nc.vector.memset(self.dn_qkb[:, 0, :], 64.0 * EPS), writes=[self.dn_qkb])
    kb.op('dve', lambda: nc.vector.memset(self.dn_qkb[:, 1, :], EPS), writes=[self.dn_qkb])


def dn_segments():
    segs = [(g * 512, 512, g > 0, g < 7) for g in range(8)]
    segs += [(TS, 256, False, False), (TS + 256, 256, False, False)]
    return segs


def dn_pre(self, l):
    self.dn_colsb = Res('dn_colsb_v', None)
    self.dn_colsb = self.dn_colsb_full
    kb, nc = self.kb, self.nc
    V, A, G_ = 'dve', 'act', 'dve'
    kb.dma('sp', self.dn_convw[:], self.dn_convT[l], self.dn_convw, writes=[self.dn_convw])
    kb.dma('sp', self.dn_colsb[0:16, 0:2], self.dn_cols[l], self.dn_colsb, writes=[self.dn_colsb])
    kb.dma('sp', self.dn_ngbc[:], self.dn_norm_g[l:l + 1, :].partition_broadcast(128), self.dn_ngbc, writes=[self.dn_ngbc])
    kb.op(A, lambda: nc.scalar.activation(out=self.dn_colsb[0:16, 2:3], in_=self.dn_colsb[0:16, 1:2], func=AF.Exp),
          reads=[self.dn_colsb], writes=[self.dn_colsb])
    kb.op(V, lambda: nc.vector.tensor_scalar(out=self.dn_colsb[0:16, 3:4], in0=self.dn_colsb[0:16, 2:3], scalar1=-1.0, scalar2=None, op0=ALU.mult),
          reads=[self.dn_colsb], writes=[self.dn_colsb])
    blkones = self.dnc[:, DNC_BLK:DNC_BLK + 128]
    for (t0, ln, hl, hr) in dn_segments():
        ntile = ln // 128
        for ct6 in range(self.cfg.get('dn_nct', 6)):
            xin = self.hT[ct6 % 2]
            xinf = xin[:].rearrange("p a b -> p (a b)").bitcast(F32)
            if not hl:
                kb.op(V, lambda: nc.vector.memset(xinf[:, 0:2], 0.0), writes=[xin])
            if not hr:
                kb.op(V, lambda: nc.vector.memset(xinf[:, ln + 2:ln + 4], 0.0), writes=[xin])
            a0 = t0 - (2 if hl else 0)
            a1 = t0 + ln + (2 if hr else 0)
            o0 = 0 if hl else 2
            rows = slice(C_QKV + ct6 * 128, C_QKV + (ct6 + 1) * 128)
            rds = [self.rPT[min(NG - 1, max(0, t // 512))] for t in (a0, a1 - 1)]
            kb.dma('sp', xinf[:, o0:o0 + (a1 - a0)], self.PT[rows, a0:a1], xin, reads=rds, writes=[xin])
            xbf = self.wkb[2 + ct6 % 2]
            xbf2 = self.dn_xb
            kb.op(A, lambda: nc.scalar.copy(out=xbf2[:, 0:ln + 4], in_=xinf[:, 0:ln + 4]), reads=[xin], writes=[xbf2])
            dg = self.dn_diag
            for k in range(5):
                kb.op(V, lambda k=k: nc.vector.tensor_scalar(out=dg[:, k, :], in0=self.ident[:], scalar1=self.dn_convw[:, ct6, k:k + 1], scalar2=None, op0=ALU.mult),
                      reads=[self.ident, self.dn_convw], writes=[dg])
            cps = self.next_bank()
            for k in range(5):
                kb.op('pe', lambda k=k: nc.tensor.matmul(cps[:, 0:ln], lhsT=dg[:, k, :], rhs=xbf2[:, k:k + ln], start=(k == 0), stop=(k == 4)),
                      reads=[dg, xbf2], writes=[cps])
            acc = cps[:, 0:ln]
            if ct6 < 4:
                xs = self.wk[0]
                kb.op(A, lambda: nc.scalar.activation(out=xs[:, 0:ln], in_=acc, func=AF.Silu), reads=[cps], writes=[xs])
                sq = self.wk[1]
                kb.op(V, lambda: nc.vector.tensor_tensor(out=sq[:, 0:ln], in0=xs[:, 0:ln], in1=xs[:, 0:ln], op=ALU.mult), reads=[xs], writes=[sq])
                ps = self.next_bank()
                kb.op('pe', lambda: nc.tensor.matmul(ps[:, 0:ln], lhsT=blkones, rhs=sq[:, 0:ln], start=True, stop=True),
                      reads=[self.dnc, sq], writes=[ps])
                isq = (ct6 < 2)
                rn = self.wk[2]
                kb.op(A, lambda: nc.scalar.activation(out=rn[:, 0:ln], in_=ps[:, 0:ln], func=AF.Sqrt, scale=(64.0 if isq else 1.0),
                                                      bias=self.dn_qkb[:, (0 if isq else 1), 0:1]), reads=[ps, self.dn_qkb], writes=[rn])
                kb.op(V, lambda: nc.vector.reciprocal(out=rn[:, 0:ln], in_=rn[:, 0:ln]), reads=[rn], writes=[rn])
                xb = self.wkb[ct6 % 2]
                kb.op(G_, lambda: nc.vector.tensor_tensor(out=xb[:, 0:ln], in0=xs[:, 0:ln], in1=rn[:, 0:ln], op=ALU.mult), reads=[xs, rn], writes=[xb])
                dstT = self.QT if isq else self.KT
                r0 = (ct6 % 2) * 128
                kb.dma('pool', dstT[r0:r0 + 128, t0:t0 + ln], xb[:, 0:ln], xb, reads=[xb],
                       writes=[self.rDN0[t0 // 128 + j] for j in range(ntile)])
                dst_tok = self.QTOK if isq else self.KTOK
            else:
                xb = self.wkb[ct6 % 2]
                kb.op(A, lambda: nc.scalar.activation(out=xb[:, 0:ln], in_=acc, func=AF.Silu), reads=[cps], writes=[xb])
                dst_tok = self.VTOK
                r0 = (ct6 % 2) * 128
            for j in range(ntile):
                pT = self.bank[j % 2]
                pTv = pT[:].bitcast(BF16)
                kb.op('pe', lambda j=j: nc.tensor.transpose(pTv[:, 0:128], xb[:, j * 128:(j + 1) * 128], self.ident[:]),
                      reads=[xb, self.ident], writes=[pT])
                ts_ = self.tmstage[j % 2]
                kb.op(A if j % 2 == 0 else V, (lambda: nc.scalar.copy(out=ts_[:, 0:128], in_=pTv[:, 0:128])) if j % 2 == 0 else
                      (lambda: nc.vector.tensor_copy(out=ts_[:, 0:128], in_=pTv[:, 0:128])), reads=[pT], writes=[ts_])
                tt = t0 // 128 + j
                kb.dma('pool', dst_tok[tt * 128:(tt + 1) * 128, r0:r0 + 128], ts_[:, 0:128], ts_, reads=[ts_], writes=[self.rDN0[tt]])
        if self.cfg.get('dn_nobg', 0):
            continue
        bgin = self.wk[3]
        rds = [self.rPT[min(NG - 1, t // 512)] for t in (t0, t0 + ln - 1)]
        kb.dma('sp', bgin[0:16, 0:ln], self.PT[C_BETA:C_BETA + 16, t0:t0 + ln], bgin, reads=rds, writes=[bgin])
        sg, gg = self.wk[4], self.wk[5]
        kb.op(A, lambda: nc.scalar.activation(out=sg[0:16, 0:ln], in_=bgin[0:16, 0:ln], func=AF.Sigmoid), reads=[bgin], writes=[sg])
        kb.op(A, lambda: nc.scalar.activation(out=gg[0:16, 0:ln], in_=bgin[0:16, 0:ln], func=AF.Exp, bias=self.dn_colsb[0:16, 0:1]),
              reads=[bgin, self.dn_colsb], writes=[gg])
        kb.op(V, lambda: nc.vector.tensor_scalar(out=gg[0:16, 0:ln], in0=gg[0:16, 0:ln], scalar1=1.0, scalar2=None, op0=ALU.add), reads=[gg], writes=[gg])
        kb.op(A, lambda: nc.scalar.activation(out=gg[0:16, 0:ln], in_=gg[0:16, 0:ln], func=AF.Ln), reads=[gg], writes=[gg])
        kb.op(V, lambda: nc.vector.tensor_scalar(out=gg[0:16, 0:ln], in0=gg[0:16, 0:ln], scalar1=self.dn_colsb[0:16, 3:4], scalar2=None, op0=ALU.mult),
              reads=[gg, self.dn_colsb], writes=[gg])
        for j in range(ntile):
            ps = self.next_bank()
            kb.op('pe', lambda j=j: nc.tensor.matmul(ps[:, 0:16], lhsT=sg[0:16, j * 128:(j + 1) * 128], rhs=self.ident32[0:16, 0:16], start=True, stop=True),
                  reads=[sg, self.ident32], writes=[ps])
            kb.op('pe', lambda j=j: nc.tensor.matmul(ps[:, 16:32], lhsT=gg[0:16, j * 128:(j + 1) * 128], rhs=self.ident32[0:16, 0:16], start=True, stop=True),
                  reads=[gg, self.ident32], writes=[ps])
            bgt = self.dn_bg
            kb.op(V, lambda: nc.vector.tensor_copy(out=bgt[:, 0:8], in_=ps[:, 0:8]), reads=[ps], writes=[bgt])
            kb.op(V, lambda: nc.vector.tensor_copy(out=bgt[:, 8:16], in_=ps[:, 24:32]), reads=[ps], writes=[bgt])
            tt = t0 // 128 + j
            kb.dma('pool', self.BG[tt * 128:(tt + 1) * 128, :], bgt[:], bgt, reads=[bgt], writes=[self.rDN0[tt]])


def dn_chain(self, l, d, B):
    kb, nc = self.kb, self.nc
    bstate = [0]

    def nb():
        b = self.bank[B.bank_lo + bstate[0] % 4]
        bstate[0] += 1
        return b
    V, A, G_ = 'dve', 'act', 'dve'
    dnc = self.dnc
    Ud = dnc[0:64, DNC_UDBLK + d * 128:DNC_UDBLK + d * 128 + 64]
    ud2 = dnc[0:64, DNC_UD2 + d * 64:DNC_UD2 + (d + 1) * 64]
    ones = dnc[0:64, DNC_BLK:DNC_BLK + 64]
    mincl = dnc[0:64, DNC_MINCL + d * 64:DNC_MINCL + (d + 1) * 64]
    mstr = dnc[0:64, DNC_MSTR + d * 64:DNC_MSTR + (d + 1) * 64]
    eye = dnc[0:64, DNC_EYE2:DNC_EYE2 + 64]
    id64 = self.ident32[0:64, 0:64]
    bc8 = lambda ap: ap.unsqueeze(1).to_broadcast([64, 8, 64])
    bcj = lambda ap: ap.unsqueeze(2).to_broadcast([64, 8, 64])
    v8 = lambda ap: ap.rearrange("p (b j) -> p b j", b=8)
    S = B.S
    sm = B.sm
    KT4 = self.KT.rearrange("(h k) t -> k h t", k=64)
    QT4 = self.QT.rearrange("(h k) t -> k h t", k=64)
    w = B.w
    F = lambda r: r[0:64, :]
    xt0, xt1, ot0, ot1 = self.xt[0], self.xt[1], self.otmp[0], self.otmp[1]
    seqs = [(0, 0, 32), (1, 32, 2), (2, 34, 2)]
    items = []
    for (sid, tt0, ntl) in seqs:
        order = list(range(tt0, tt0 + ntl))
        if d == 1:
            order = order[::-1]
        for n_, tt in enumerate(order):
            items.append((sid, tt, n_ == 0, n_ == len(order) - 1))
    tokv = lambda dr, tsl: dr[tsl, :].rearrange("(c p) x -> p c x", p=64)
    c3 = lambda r: r[0:64, :].rearrange("p (c x) -> p c x", c=2)

    def issue_loads(n_):
        tt = items[n_][1]
        I = B.inp[n_ % 2]
        tsl = slice(tt * 128, (tt + 1) * 128)
        kb.dma('sp', I.kT, KT4[:, :, tsl], I.kTr, reads=[self.rDN0[tt]], writes=[I.kTr])
        kb.dma('sp', I.qT, QT4[:, :, tsl], I.qTr, reads=[self.rDN0[tt]], writes=[I.qTr])
        kb.dma('sp', c3(I.tkr), tokv(self.KTOK, tsl), I.tkr, reads=[self.rDN0[tt]], writes=[I.tkr])
        kb.dma('sp', c3(I.tvr), tokv(self.VTOK, tsl), I.tvr, reads=[self.rDN0[tt]], writes=[I.tvr])
        kb.dma('sp', c3(I.tqr), tokv(self.QTOK, tsl), I.tqr, reads=[self.rDN0[tt]], writes=[I.tqr])
        kb.dma('sp', I.bg2[0:64, :, :], self.BG[tsl, :].rearrange("(c p) x -> p c x", p=64), I.bg2, reads=[self.rDN0[tt]], writes=[I.bg2])

    issue_loads(0)
    for n_, (sid, tt, first, last) in enumerate(items):
        if True:
            if first:
                if sid == 0:
                    kb.dma('sp', S[0:64], self.dn_s0[l, d].rearrange("h k v -> k h v"), S, writes=[S])
                else:
                    kb.op(V, lambda: nc.vector.memset(S[0:64], 0.0), writes=[S])
            if n_ + 1 < len(items):
                issue_loads(n_ + 1)
            I = B.inp[n_ % 2]
            tsl = slice(tt * 128, (tt + 1) * 128)
            kTr, qTr, kT, qT = I.kTr, I.qTr, I.kT, I.qT
            tkr, tvr, tqr, kpr = I.tkr, I.tvr, I.tqr, B.kpr
            bg2 = I.bg2
            g8 = bg2[0:64, :, 8 + 4 * d:12 + 4 * d]
            b8 = bg2[0:64, :, 4 * d:4 * d + 4]
            g8j = g8.unsqueeze(3).to_broadcast([64, 2, 4, 64])
            b8j = b8.unsqueeze(3).to_broadcast([64, 2, 4, 64])
            v24 = lambda ap: ap.rearrange("p (c h j) -> p c h j", c=2, h=4)
            ktok8, vtok8, qtok8 = v8(F(tkr)), v8(F(tvr)), v8(F(tqr))
            yield
            X1, X2 = nb(), nb()
            for c in range(2):
                cs = slice(c * 64, (c + 1) * 64)
                for h in range(4):
                    bs = slice((c * 4 + h) * 64, (c * 4 + h + 1) * 64)
                    kb.op('pe', lambda cs=cs, h=h, bs=bs: nc.tensor.matmul(X1[0:64, bs], lhsT=kT[:, h, cs], rhs=kT[:, h, cs], start=True, stop=True),
                          reads=[kTr], writes=[X1])
                    kb.op('pe', lambda cs=cs, h=h, bs=bs: nc.tensor.matmul(X2[0:64, bs], lhsT=qT[:, h, cs], rhs=kT[:, h, cs], start=True, stop=True),
                          reads=[kTr, qTr], writes=[X2])
            yield
            G4b, NGU = F(w[0]), F(w[1])
            kb.op(V, lambda: nc.vector.tensor_copy(out=v24(G4b), in_=g8j), reads=[bg2], writes=[w[0]])
            kb.op(V, lambda: nc.vector.scalar_tensor_tensor(out=v8(NGU), in0=bc8(ud2), scalar=-1.0, in1=v8(G4b), op0=ALU.mult, op1=ALU.mult),
                  reads=[dnc, w[0]], writes=[w[1]])
            Y = nb()
            kb.op('pe', lambda: nc.tensor.matmul(Y[0:64, :], lhsT=Ud, rhs=G4b, start=True, stop=False), reads=[dnc, w[0]], writes=[Y])
            kb.op('pe', lambda: nc.tensor.matmul(Y[0:64, :], lhsT=ones, rhs=NGU, start=False, stop=True), reads=[dnc, w[1]], writes=[Y])
            kb.op(V, lambda: nc.vector.tensor_copy(out=sm[0:64, 72:80].rearrange("p (c h) -> p c h", c=2), in_=g8), reads=[bg2], writes=[sm])
            Z = nb()
            kb.op('pe', lambda: nc.tensor.matmul(Z[0:64, 0:8], lhsT=Ud, rhs=sm[0:64, 72:80], start=True, stop=True), reads=[dnc, sm], writes=[Z])
            kb.op('pe', lambda: nc.tensor.matmul(Z[0:64, 8:16], lhsT=ones, rhs=sm[0:64, 72:80], start=True, stop=True), reads=[dnc, sm], writes=[Z])
            kb.op(V, lambda: nc.vector.tensor_copy(out=sm[0:64, 0:8], in_=Z[0:64, 0:8]), reads=[Z], writes=[sm])
            kb.op(V, lambda: nc.vector.tensor_tensor(out=sm[0:64, 8:16], in0=Z[0:64, 8:16], in1=sm[0:64, 0:8], op=ALU.subtract), reads=[Z, sm], writes=[sm])
            kb.op(V, lambda: nc.vector.tensor_copy(out=sm[0:64, 16:24], in_=Z[0:64, 8:16]), reads=[Z], writes=[sm])
            kb.op(A, lambda: nc.scalar.activation(out=sm[0:64, 24:48], in_=sm[0:64, 0:24], func=AF.Exp), reads=[sm], writes=[sm])
            egc, ekl, egl = sm[0:64, 24:32], sm[0:64, 32:40], sm[0:64, 40:48]
            kb.op(V, lambda: nc.vector.tensor_tensor(out=sm[0:64, 48:56].rearrange("p (c h) -> p c h", c=2), in0=b8, in1=egc.rearrange("p (c h) -> p c h", c=2), op=ALU.mult),
                  reads=[bg2, sm], writes=[sm])
            be = sm[0:64, 48:56]
            kb.op(V, lambda: nc.vector.tensor_copy(out=sm[0:64, 56:64].rearrange("p (c h) -> p c h", c=2), in_=b8), reads=[bg2], writes=[sm])
            bt = sm[0:64, 56:64]
            yield
            dec = F(w[2])
            kb.op(V, lambda: nc.vector.tensor_scalar(out=dec, in0=Y[0:64, :], scalar1=0.0, scalar2=None, op0=ALU.min), reads=[Y], writes=[w[2]])
            kb.op(A, lambda: nc.scalar.activation(out=dec, in_=dec, func=AF.Exp), reads=[w[2]], writes=[w[2]])
            kb.op(G_, lambda: nc.vector.tensor_tensor(out=v8(dec), in0=v8(dec), in1=bc8(mincl), op=ALU.mult), reads=[w[2], dnc], writes=[w[2]])
            yield
            aqk, Nm = F(w[3]), F(w[4])
            kb.op(V, lambda: nc.vector.tensor_tensor(out=aqk, in0=X2[0:64, :], in1=dec, op=ALU.mult), reads=[X2, w[2]], writes=[w[3]])
            kb.op(V, lambda: nc.vector.tensor_tensor(out=Nm, in0=X1[0:64, :], in1=dec, op=ALU.mult), reads=[X1, w[2]], writes=[w[4]])
            kb.op(V, lambda: nc.vector.scalar_tensor_tensor(out=v8(Nm), in0=v8(Nm), scalar=-1.0, in1=bcj(bt), op0=ALU.mult, op1=ALU.mult),
                  reads=[w[4], sm], writes=[w[4]])
            kb.op(G_, lambda: nc.vector.tensor_tensor(out=v8(Nm), in0=v8(Nm), in1=bc8(mstr), op=ALU.mult), reads=[w[4], dnc], writes=[w[4]])
            yield
            T1a, T1b = nb(), nb()
            for b in range(8):
                bs = slice(b * 64, (b + 1) * 64)
                kb.op('pe', lambda bs=bs: nc.tensor.matmul(T1a[0:64, bs], lhsT=Nm[:, bs], rhs=id64, start=True, stop=True),
                      reads=[w[4], self.ident32], writes=[T1a])
                kb.op('pe', lambda bs=bs: nc.tensor.matmul(T1b[0:64, bs], lhsT=aqk[:, bs], rhs=id64, start=True, stop=True),
                      reads=[w[3], self.ident32], writes=[T1b])
            Pr, PTr = B.Pr, B.PTr
            ubr, aqr, Rr_, WUr_ = B.ubR, B.aqR, B.RR, B.WUR
            Ub, aqkTb = ubr[0:64, :], aqr[0:64, :]
            kb.op(A, lambda: nc.scalar.copy(out=F(PTr[0]), in_=T1a[0:64, :]), reads=[T1a], writes=[PTr[0]])
            kb.op(A, lambda: nc.scalar.copy(out=F(Pr[0]), in_=Nm), reads=[w[4]], writes=[Pr[0]])
            kb.op(A, lambda: nc.scalar.copy(out=aqkTb, in_=T1b[0:64, :]), reads=[T1b], writes=[aqr])
            kb.op(V, lambda: nc.vector.tensor_tensor(out=v8(Ub), in0=v8(T1a[0:64, :]), in1=bc8(eye), op=ALU.add), reads=[T1a, dnc], writes=[ubr])
            yield
            cur = 0
            for kstep in range(0, 6):
                Pc, PTc = F(Pr[cur]), F(PTr[cur])
                nxt = 1 - cur
                if kstep >= 1:
                    ubk = nb()
                    for b in range(8):
                        bs = slice(b * 64, (b + 1) * 64)
                        kb.op('pe', lambda bs=bs, Pc=Pc, ubk=ubk: nc.tensor.matmul(ubk[0:64, bs], lhsT=Pc[:, bs], rhs=Ub[:, bs], start=True, stop=True),
                              reads=[Pr[cur], ubr], writes=[ubk])
                if kstep < 5:
                    sq1, sq2 = nb(), nb()
                    for b in range(8):
                        bs = slice(b * 64, (b + 1) * 64)
                        kb.op('pe', lambda bs=bs, Pc=Pc, PTc=PTc, sq1=sq1: nc.tensor.matmul(sq1[0:64, bs], lhsT=PTc[:, bs], rhs=Pc[:, bs], start=True, stop=True),
                              reads=[Pr[cur], PTr[cur]], writes=[sq1])
                        kb.op('pe', lambda bs=bs, Pc=Pc, PTc=PTc, sq2=sq2: nc.tensor.matmul(sq2[0:64, bs], lhsT=Pc[:, bs], rhs=PTc[:, bs], start=True, stop=True),
                              reads=[Pr[cur], PTr[cur]], writes=[sq2])
                if kstep >= 1:
                    kb.op(V, lambda ubk=ubk: nc.vector.tensor_tensor(out=Ub, in0=ubk[0:64, :], in1=Ub, op=ALU.add), reads=[ubk, ubr], writes=[ubr])
                if kstep < 5:
                    kb.op(A, lambda nxt=nxt, sq1=sq1: nc.scalar.copy(out=F(Pr[nxt]), in_=sq1[0:64, :]), reads=[sq1], writes=[Pr[nxt]])
                    kb.op(A if kstep % 2 == 0 else V, (lambda nxt=nxt, sq2=sq2: nc.scalar.copy(out=F(PTr[nxt]), in_=sq2[0:64, :])) if kstep % 2 == 0 else
                          (lambda nxt=nxt, sq2=sq2: nc.vector.tensor_copy(out=F(PTr[nxt]), in_=sq2[0:64, :])), reads=[sq2], writes=[PTr[nxt]])
                    cur = nxt
                yield
            Rf = Rr_[0:64, :]
            R8 = Rf.rearrange("p (b x) -> p b x", b=8)
            kb.op(G_, lambda: nc.vector.tensor_tensor(out=R8[:, :, 0:64], in0=ktok8, in1=bcj(be), op=ALU.mult), reads=[tkr, sm], writes=[Rr_])
            kb.op(G_, lambda: nc.vector.tensor_tensor(out=R8[:, :, 64:128], in0=vtok8, in1=bcj(bt), op=ALU.mult), reads=[tvr, sm], writes=[Rr_])
            W1 = [nb(), nb()]
            for b in range(8):
                kb.op('pe', lambda b=b: nc.tensor.matmul(W1[b // 4][0:64, (b % 4) * 128:(b % 4 + 1) * 128], lhsT=Ub[:, b * 64:(b + 1) * 64], rhs=R8[:, b, :], start=True, stop=True),
                      reads=[ubr, Rr_], writes=[W1[b // 4]])
            WUf = WUr_[0:64, :]
            WU8 = WUf.rearrange("p (b x) -> p b x", b=8)
            WU32r = B.WU32R
            WU32 = WU32r[0:64, :]
            WU32_8 = WU32.rearrange("p (b x) -> p b x", b=8)
            for c in range(2):
                kb.op(A, lambda c=c: nc.scalar.copy(out=WUf[:, c * 512:(c + 1) * 512], in_=W1[c][0:64, :]), reads=[W1[c]], writes=[WUr_])
                kb.op(V, lambda c=c: nc.vector.tensor_copy(out=WU32[:, c * 512:(c + 1) * 512], in_=W1[c][0:64, :]), reads=[W1[c]], writes=[WU32r])
            yield
            W2 = [nb(), nb()]
            for b in range(8):
                kb.op('pe', lambda b=b: nc.tensor.matmul(W2[b // 4][0:64, (b % 4) * 128:(b % 4 + 1) * 128], lhsT=aqkTb[:, b * 64:(b + 1) * 64], rhs=WU8[:, b, :], start=True, stop=True),
                      reads=[aqr, WUr_], writes=[W2[b // 4]])
            Mm, ccs = F(w[5]), F(w[0])
            kb.op(G_, lambda: nc.vector.tensor_tensor(out=v8(Mm), in0=qtok8, in1=bcj(egc), op=ALU.mult), reads=[tqr, sm], writes=[w[5]])
            for c in range(2):
                W2v = W2[c][0:64, :].rearrange("p (h x) -> p h x", h=4)
                Mc = Mm[:, c * 256:(c + 1) * 256].rearrange("p (h j) -> p h j", h=4)
                Cc = ccs[:, c * 256:(c + 1) * 256].rearrange("p (h j) -> p h j", h=4)
                kb.op(V, lambda Mc=Mc, W2v=W2v: nc.vector.tensor_tensor(out=Mc, in0=Mc, in1=W2v[:, :, 0:64], op=ALU.subtract), reads=[w[5], W2[c]], writes=[w[5]])
                kb.op(A, lambda Cc=Cc, W2v=W2v: nc.scalar.copy(out=Cc, in_=W2v[:, :, 64:128]), reads=[W2[c]], writes=[w[0]])
            yield
            T2 = nb()
            for b in range(8):
                bs = slice(b * 64, (b + 1) * 64)
                kb.op('pe', lambda bs=bs: nc.tensor.matmul(T2[0:64, bs], lhsT=Mm[:, bs], rhs=id64, start=True, stop=True),
                      reads=[w[5], self.ident32], writes=[T2])
            MTs = F(w[1])
            kb.op(A, lambda: nc.scalar.copy(out=MTs, in_=T2[0:64, :]), reads=[T2], writes=[w[1]])
            yield
            kpr = B.Kp32R
            Kp = kpr[0:64, :]
            Kp8 = v8(Kp)
            kb.op(G_, lambda: nc.vector.tensor_tensor(out=Kp8, in0=ktok8, in1=bcj(ekl), op=ALU.mult), reads=[tkr, sm], writes=[kpr])
            G1 = nb()
            for b in range(8):
                bs = slice(b * 64, (b + 1) * 64)
                kb.op('pe', lambda b=b, bs=bs: nc.tensor.matmul(G1[0:64, bs], lhsT=WU32_8[:, b, 0:64], rhs=Kp8[:, b, :], start=True, stop=True),
                      reads=[WU32r, kpr], writes=[G1])
            Gneg = F(w[2])
            kb.op(V, lambda: nc.vector.tensor_scalar(out=Gneg, in0=G1[0:64, :], scalar1=-1.0, scalar2=None, op0=ALU.mult), reads=[G1], writes=[w[2]])
            yield
            O1 = nb()
            corder = (0, 1) if d == 0 else (1, 0)
            for c in corder:
                for h in range(4):
                    b = c * 4 + h
                    bs = slice(b * 64, (b + 1) * 64)
                    kb.op('pe', lambda h=h, bs=bs: nc.tensor.matmul(O1[0:64, bs], lhsT=MTs[:, bs], rhs=S[0:64, h, :], start=True, stop=True),
                          reads=[w[1], S], writes=[O1])
                SS = nb()
                for h in range(4):
                    b = c * 4 + h
                    bs = slice(b * 64, (b + 1) * 64)
                    kb.op('pe', lambda h=h, b=b: nc.tensor.matmul(SS[0:64, h * 64:(h + 1) * 64], lhsT=Kp8[:, b, :], rhs=WU32_8[:, b, 64:128], start=True, stop=False),
                          reads=[kpr, WU32r], writes=[SS])
                    kb.op('pe', lambda h=h, bs=bs: nc.tensor.matmul(SS[0:64, h * 64:(h + 1) * 64], lhsT=Gneg[:, bs], rhs=S[0:64, h, :], start=False, stop=True),
                          reads=[w[2], S], writes=[SS])
                eglc = egl[:, c * 4:(c + 1) * 4].unsqueeze(2).to_broadcast([64, 4, 64])
                kb.op(V, lambda eglc=eglc: nc.vector.tensor_tensor(out=S[0:64], in0=S[0:64], in1=eglc, op=ALU.mult), reads=[S, sm], writes=[S])
                kb.op(V, lambda: nc.vector.tensor_tensor(out=S[0:64].rearrange("p h v -> p (h v)"), in0=S[0:64].rearrange("p h v -> p (h v)"), in1=SS[0:64, 0:256], op=ALU.add),
                      reads=[S, SS], writes=[S])
            od = F(w[3])
            kb.op(V, lambda: nc.vector.tensor_tensor(out=od, in0=O1[0:64, :], in1=ccs, op=ALU.add), reads=[O1, w[0]], writes=[w[3]])
            OFv = self.OF[d, tsl, :].rearrange("(c p) x -> p c x", p=64)
            od3 = od.rearrange("p (c x) -> p c x", c=2)
            kb.dma('sp', OFv, od3, w[3], reads=[w[3]], writes=[self.rOF[d][tt]])
            yield
            if last and sid > 0:
                kb.dma('sp', self.new_dn[sid - 1, l, d].rearrange("h k v -> k h v"), S[0:64], S, reads=[S])
                yield


def dn_final(self, l):
    kb, nc = self.kb, self.nc
    V, A = 'dve', 'act'
    v4 = lambda ap: ap.rearrange("p (h j) -> p h j", h=4)
    sm = self.dn_sm
    for g in range(NG):
        tsl = slice(g * 512, (g + 1) * 512)
        oT = [self.wkb[0], self.wkb[1]]
        for j in range(4):
            tt = g * 4 + j
            o0, o1 = self.pstage[0], self.pstage[1]
            kb.dma('sp', o0[:, 0:256], self.OF[0, tt * 128:(tt + 1) * 128, :], o0, reads=[self.rOF[0][tt]], writes=[o0])
            kb.dma('sp', o1[:, 0:256], self.OF[1, tt * 128:(tt + 1) * 128, :], o1, reads=[self.rOF[1][tt]], writes=[o1])
            kb.op(V, lambda: nc.vector.tensor_tensor(out=o0[:, 0:256], in0=o0[:, 0:256], in1=o1[:, 0:256], op=ALU.add), reads=[o0, o1], writes=[o0])
            kb.op(V, lambda: nc.vector.tensor_tensor(out=o0[:, 256:512], in0=o0[:, 0:256], in1=o0[:, 0:256], op=ALU.mult), reads=[o0], writes=[o0])
            kb.op(V, lambda: nc.vector.tensor_reduce(out=sm[:, 64:68], in_=v4(o0[:, 256:512]), axis=AX.X, op=ALU.add), reads=[o0], writes=[sm])
            kb.op(A, lambda: nc.scalar.activation(out=sm[:, 64:68], in_=sm[:, 64:68], func=AF.Sqrt, scale=1.0 / 64, bias=self.eps_col[:]),
                  reads=[sm, self.eps_col], writes=[sm])
            kb.op(V, lambda: nc.vector.reciprocal(out=sm[:, 64:68], in_=sm[:, 64:68]), reads=[sm], writes=[sm])
            kb.op(V, lambda: nc.vector.tensor_tensor(out=v4(o0[:, 0:256]), in0=v4(o0[:, 0:256]), in1=sm[:, 64:68].unsqueeze(2).to_broadcast([128, 4, 64]), op=ALU.mult),
                  reads=[o0, sm], writes=[o0])
            ont = self.wkb[2 + j % 2]
            kb.op(V, lambda: nc.vector.tensor_tensor(out=v4(ont[:, 0:256]), in0=v4(o0[:, 0:256]), in1=self.dn_ngbc[:].unsqueeze(1).to_broadcast([128, 4, 64]), op=ALU.mult),
                  reads=[o0, self.dn_ngbc], writes=[ont])
            pT = self.bank[j % 2]
            pTv = pT[:].bitcast(BF16)
            for ft in range(2):
                kb.op('pe', lambda ft=ft: nc.tensor.transpose(pTv[:, ft * 128:(ft + 1) * 128], ont[:, ft * 128:(ft + 1) * 128], self.ident[:]),
                      reads=[ont, self.ident], writes=[pT])
            for ft in range(2):
                kb.op('act', lambda ft=ft: nc.scalar.copy(out=oT[ft][:, j * 128:(j + 1) * 128], in_=pTv[:, ft * 128:(ft + 1) * 128]),
                      reads=[pT], writes=[oT[ft]])
        for ft in range(2):
            zT = self.wk[ft]
            kb.dma('sp', zT[:], self.PT[C_DNZ + ft * 128:C_DNZ + (ft + 1) * 128, tsl], zT, reads=[self.rPT[g]], writes=[zT])
            kb.op('act', lambda: nc.scalar.activation(out=zT[:], in_=zT[:], func=AF.Silu), reads=[zT], writes=[zT])
            yb = self.ybuf[self.ybuf_i % 2]
            self.ybuf_i += 1
            kb.op('dve', lambda: nc.vector.tensor_tensor(out=yb[:], in0=zT[:], in1=oT[ft][:], op=ALU.mult), reads=[zT, oT[ft]], writes=[yb])
            self.store_y(yb, 256 + ft * 128, g)


class DNBufs:
    pass


def dn_make_bufs(self):
    kb = self.kb
    hv = lambda r: r[:].rearrange("p a b -> p (a b)")
    B0, B1 = DNBufs(), DNBufs()
    B0.bank_lo, B1.bank_lo = 0, 4
    B0.kTr, B0.qTr = self.yT[0], self.yT[1]
    B0.kT, B0.qT = B0.kTr[0:64, 0:4, :], B0.qTr[0:64, 0:4, :]
    B0.tkr, B0.tvr, B0.tqr, B0.kpr = self.wkb
    B0.bg2, B0.sm, B0.S = self.dn_bg2, self.dn_sm, self.dn_S
    B0.w = list(self.wk)
    B0.Pr, B0.PTr = [self.ybuf[0], self.ybuf[1]], [self.tmstage[0], self.tmstage[1]]
    h0 = hv(self.hT[0])
    B0.ubR, B0.aqR = Res("dn_ub0", h0[:, 0:512]), Res("dn_aq0", h0[:, 512:1024])
    B0.RR, B0.WUR = Res("dn_R0", h0[:, 1024:2048]), Res("dn_WU0", h0[:, 2048:3072])
    self.dn_alias0 = (self.hT[0], [B0.ubR, B0.aqR, B0.RR, B0.WUR])
    wf = self.w_in_bf[:].rearrange("p a b -> p (a b)")
    off = [0]

    def take(n_bf16, name, f32=False):
        ap = wf[:, off[0]:off[0] + n_bf16]
        off[0] += n_bf16
        return Res(name, ap.bitcast(F32) if f32 else ap)
    B1.kTr, B1.qTr = take(512, "dn1_kT"), take(512, "dn1_qT")
    B1.kT = B1.kTr[0:64, :].rearrange("p (h t) -> p h t", h=4)
    B1.qT = B1.qTr[0:64, :].rearrange("p (h t) -> p h t", h=4)
    B1.tkr, B1.tvr, B1.tqr, B1.kpr = [take(512, f"dn1_tk{i}") for i in range(4)]
    B1.Pr = [take(512, f"dn1_P{i}") for i in range(2)]
    B1.PTr = [take(512, f"dn1_PT{i}") for i in range(2)]
    B1.w = [take(1024, f"dn1_w{i}", f32=True) for i in range(6)]
    B1.bg2, B1.sm, B1.S = self.dn_bg2b, self.dn_smb, self.dn_Sb
    h1 = hv(self.hT[1])
    B1.ubR, B1.aqR = Res("dn_ub1", h1[:, 0:512]), Res("dn_aq1", h1[:, 512:1024])
    B1.RR, B1.WUR = Res("dn_R1", h1[:, 1024:2048]), Res("dn_WU1", h1[:, 2048:3072])
    def mkset(kTr, qTr, tkr, tvr, tqr, bg2, yT_like):
        I = DNBufs()
        I.kTr, I.qTr, I.tkr, I.tvr, I.tqr, I.bg2 = kTr, qTr, tkr, tvr, tqr, bg2
        if yT_like:
            I.kT, I.qT = kTr[0:64, 0:4, :], qTr[0:64, 0:4, :]
        else:
            I.kT = kTr[0:64, :].rearrange("p (h t) -> p h t", h=4)
            I.qT = qTr[0:64, :].rearrange("p (h t) -> p h t", h=4)
        return I
    ex0 = [take(512, f"dn0x{i}") for i in range(5)]
    ex1 = [take(512, f"dn1x{i}") for i in range(5)]
    B0.inp = [mkset(B0.kTr, B0.qTr, B0.tkr, B0.tvr, B0.tqr, self.dn_bg2, True),
              mkset(ex0[0], ex0[1], ex0[2], ex0[3], ex0[4], self.dn_bg2c, False)]
    B1.inp = [mkset(B1.kTr, B1.qTr, B1.tkr, B1.tvr, B1.tqr, self.dn_bg2b, False),
              mkset(ex1[0], ex1[1], ex1[2], ex1[3], ex1[4], self.dn_bg2d, False)]
    B1.WU32R, B1.Kp32R = take(2048, "dn1_WU32", f32=True), take(1024, "dn1_Kp32", f32=True)
    wo_ = self.w_out_bf[:].rearrange("p a b -> p (a b)")
    B0.WU32R = Res("dn0_WU32", wo_[:, 0:2048].bitcast(F32))
    B0.Kp32R = Res("dn0_Kp32", wo_[:, 2048:3072].bitcast(F32))
    scr1 = [B1.kTr, B1.qTr, B1.tkr, B1.tvr, B1.tqr, B1.kpr] + B1.Pr + B1.PTr + B1.w + ex0 + ex1 + [B1.WU32R, B1.Kp32R]
    self.dn_alias1 = [(self.w_in_bf, scr1), (self.hT[1], [B1.ubR, B1.aqR, B1.RR, B1.WUR]), (self.w_out_bf, [B0.WU32R, B0.Kp32R])]
    return B0, B1


def run_chains(gens, head_start=()):
    active = list(gens)
    for g, n in zip(list(active), head_start):
        for _ in range(n):
            try:
                next(g)
            except StopIteration:
                active.remove(g)
                break
    while active:
        for g in list(active):
            try:
                next(g)
            except StopIteration:
                active.remove(g)


def dn_phase(self, l):
    kb = self.kb
    self.dn_pre(l)
    if not hasattr(self, '_dn_bufs'):
        self._dn_bufs = self.dn_make_bufs()
    B0, B1 = self._dn_bufs
    kb.alias_begin(*self.dn_alias0)
    for src_, al in self.dn_alias1:
        kb.alias_begin(src_, al)
    if self.cfg.get("dn_seq", 0):
        run_chains([self.dn_chain(l, 0, B0)])
        run_chains([self.dn_chain(l, 1, B1)])
    else:
        run_chains([self.dn_chain(l, 0, B0), self.dn_chain(l, 1, B1)], head_start=(self.cfg.get('dn_hs', 11), 0))
    kb.alias_end(*self.dn_alias0)
    for src_, al in self.dn_alias1:
        kb.alias_end(src_, al)
    self.dn_final(l)


def make_dn_inputs(inputs, k):
    f32 = lambda a: np.ascontiguousarray(np.asarray(a, dtype=np.float32))
    L = DEPTH
    conv = f32(inputs['dn_conv'])
    convT = np.ascontiguousarray(conv.reshape(L, 5, 6, 128).transpose(0, 3, 2, 1))
    cols = np.zeros((L, 16, 2), np.float32)
    cols[:, 8:16, 0] = f32(inputs['dn_dt_bias']).reshape(L, 8)
    cols[:, 8:16, 1] = f32(inputs['dn_a_log']).reshape(L, 8)
    sb = k // 4
    return dict(dn_convT=convT, dn_cols=cols, dn_norm_g=f32(inputs['dn_norm_g']),
                dn_s0=f32(inputs['state_delta'])[sb])


Model.dn_init = _dn_init
Model.dn_setup = _dn_setup
Model.dn_pre = dn_pre
Model.dn_chain = dn_chain
Model.dn_make_bufs = dn_make_bufs
Model.dn_final = dn_final
Model.dn_phase = dn_phase


def kernel(**inputs):
    nc = build_nc({})
    maps = make_in_maps(inputs, 8)
    res = run_bass_kernel_spmd(nc, maps, core_ids=list(range(8)))
    rs = res.results
    L = DEPTH
    y_prompt = np.zeros((16, 256, D), np.float32)
    y_sample = np.zeros((2, TS, D), np.float32)
    new_dn = np.zeros((16, L, 2, 4, 64, 64), np.float32)
    new_s5 = np.zeros((16, L, 2, 2, 16, 64), np.float32)
    for k in range(8):
        r = rs[k]
        ya = np.asarray(r["y_all"], dtype=np.float32)
        y_prompt[2 * k] = ya[TS:TS + 256]
        y_prompt[2 * k + 1] = ya[TS + 256:TS + 512]
        if k % 4 == 0:
            y_sample[k // 4] = ya[0:TS]
        new_dn[2 * k:2 * k + 2] = np.asarray(r["new_dn"], dtype=np.float32)
        new_s5[2 * k:2 * k + 2] = unpack_new_s5(np.asarray(r["new_s5T"], dtype=np.float32))
    return (y_prompt, y_sample, new_dn, new_s5)
```

```python
from contextlib import ExitStack
import numpy as np
import ml_dtypes
import concourse.bass as bass
import concourse.mybir as mybir
from concourse.bass_utils import run_bass_kernel_spmd

F32 = mybir.dt.float32
BF16 = mybir.dt.bfloat16
AF = mybir.ActivationFunctionType
ALU = mybir.AluOpType
AX = mybir.AxisListType

SEM_LIMIT = 30000


class Sem:
    def __init__(self, handle, is_dma):
        self.h = handle
        self.count = 0
        self.is_dma = is_dma


class Res:
    def __init__(self, name, t=None):
        self.name = name
        self.t = t
        self.w = None
        self.r = {}
        self.dsem = None
        self.psum = False

    def __getitem__(self, key):
        return self.t[key]


class KB:
    def __init__(self, nc, stack):
        self.nc = nc
        self.stack = stack
        self.eng = {'pe': nc.tensor, 'act': nc.scalar, 'dve': nc.vector,
                    'pool': nc.gpsimd, 'sp': nc.sync}
        self.nsem = 0
        self.esem = {e: self.new_sem(e, False) for e in ['pe', 'act', 'dve', 'pool']}
        self.known = {e: {} for e in self.eng}
        self.ninstr = 0
        self.all_dma_sems = []

    def new_sem(self, name, is_dma):
        self.nsem += 1
        h = self.stack.enter_context(self.nc.semaphore(f"s{self.nsem}_{name}"))
        return Sem(h, is_dma)

    def sb(self, name, shape, dtype):
        t = self.stack.enter_context(self.nc.sbuf_tensor(name, list(shape), dtype))
        return Res(name, t)

    def ps(self, name, shape, dtype):
        t = self.stack.enter_context(self.nc.psum_tensor(name, list(shape), dtype))
        r = Res(name, t)
        r.psum = True
        return r

    def dram(self, name, shape, dtype, kind="Internal"):
        t = self.nc.dram_tensor(name, list(shape), dtype, kind=kind)
        return t

    def _waits(self, eng, reads, writes):
        waits = {}

        def need(t):
            if t is None:
                return
            sem, val = t
            if sem.is_dma:
                val = sem.count
            if waits.get(sem, 0) < val:
                waits[sem] = val

        for r in reads:
            need(r.w)
            if r.psum:
                for s, v in r.r.items():
                    need((s, v))
        for w in writes:
            need(w.w)
            for s, v in w.r.items():
                need((s, v))
        E = self.eng[eng]
        kn = self.known[eng]
        own = self.esem.get(eng)
        for sem, val in waits.items():
            if eng == 'pe' and sem is own:
                continue
            if kn.get(sem, 0) >= val:
                continue
            E.wait_ge(sem.h, val)
            kn[sem] = val

    def _commit(self, ticket, reads, writes):
        sem, val = ticket
        for r in reads:
            if r.r.get(sem, 0) < val:
                r.r[sem] = val
        for w in writes:
            w.w = ticket
            w.r = {}

    def op(self, eng, emit, reads=(), writes=()):
        self._waits(eng, reads, writes)
        ins = emit()
        sem = self.esem[eng]
        sem.count += 1
        ins.then_inc(sem.h, 1)
        self._commit((sem, sem.count), reads, writes)
        if sem.count >= SEM_LIMIT:
            self.esem[eng] = self.new_sem(eng, False)
        self.ninstr += 1

    def dma(self, q, out, in_, sbres, reads=(), writes=(), **kw):
        self._waits(q, reads, writes)
        if sbres.dsem is None:
            sbres.dsem = {}
        qk = 'sw' if q == 'pool' else 'hw'
        if qk not in sbres.dsem or sbres.dsem[qk].count >= SEM_LIMIT:
            sbres.dsem[qk] = self.new_sem("d" + qk + "_" + sbres.name, True)
            self.all_dma_sems.append(sbres.dsem[qk])
        sem = sbres.dsem[qk]
        ins = self.eng[q].dma_start(out=out, in_=in_, **kw)
        sem.count += 16
        ins.then_inc(sem.h, 16)
        self._commit((sem, sem.count), reads, writes)
        self.ninstr += 1

    def alias_begin(self, src, aliases):
        for a in aliases:
            a.w = src.w
            a.r = dict(src.r)

    def alias_end(self, src, aliases):
        for a in aliases:
            items = list(a.r.items()) + ([a.w] if a.w is not None else [])
            for s, v in items:
                if src.r.get(s, 0) < v:
                    src.r[s] = v

    def finish(self, q='sp'):
        E = self.eng[q]
        for sem in self.all_dma_sems:
            if sem.count > 0 and self.known[q].get(sem, 0) < sem.count:
                E.wait_ge(sem.h, sem.count)
        for e, sem in self.esem.items():
            if sem.count > 0:
                E.wait_ge(sem.h, sem.count)


D = 1024
DEPTH = 4
TS = 4096
TP = 512
NT = TS + TP
NTT = NT // 128
NG = NT // 512
DIN = 2576
C_POOL_U, C_POOL_Z, C_QKV, C_DNZ, C_BETA, C_ALPHA, C_S5U, C_S5Z, C_FTU, C_FTZ = \
    0, 256, 512, 1280, 1536, 1544, 1552, 1808, 2064, 2320
COL_TILES = [(c, 128) for c in range(0, 1536, 128)] + [(1536, 16)] + \
            [(c, 128) for c in range(1552, 2576, 128)]
EPS = 1e-6


def cond_of_tile(tt):
    return 0 if tt * 128 < TS else 1


class Model:
    def __init__(self, nc, st, cfg):
        self.nc = nc
        self.cfg = cfg
        self.depth = cfg.get('depth', DEPTH)
        kb = self.kb = KB(nc, st)
        dt_in = lambda name, shape, dt=F32: nc.dram_tensor(name, list(shape), dt, kind="ExternalInput").ap()
        L = DEPTH
        self.x_in = dt_in("x_all", [NT, D])
        self.condT = dt_in("condT", [128, 8, 2])
        self.w_ada = dt_in("w_ada", [L, D, 3 * D])
        self.b_adaT = dt_in("b_adaT", [L, 128, 24])
        self.b_ada = dt_in("b_ada", [L, 3 * D])
        self.norm_gT = dt_in("norm_gT", [L, 128, 8])
        self.w_in = dt_in("w_in", [L, D, DIN])
        self.w_out = dt_in("w_out", [L, D, D])
        self.final_g = dt_in("final_g", [1, D])
        self.ident_in = dt_in("ident", [128, 128], BF16)
        self.y_out = nc.dram_tensor("y_all", [NT, D], F32, kind="ExternalOutput").ap()
        self.X = nc.dram_tensor("X_scr", [NT, D], F32, kind="Internal").ap()
        self.PT = nc.dram_tensor("PT_scr", [DIN, NT], F32, kind="Internal").ap()
        self.PTOK = nc.dram_tensor("PTOK_scr", [NT, 256], BF16, kind="Internal").ap()
        self.YT = nc.dram_tensor("YT_scr", [D, NT], BF16, kind="Internal").ap()
        self.rX = [Res(f"X{t}") for t in range(NTT)]
        self.rPT = [Res(f"PT{g}") for g in range(NG)]
        self.rPTOK = [Res(f"PTOK{t}") for t in range(NTT)]
        self.rYT = [[Res(f"YT{k}_{g}") for g in range(NG)] for k in range(8)]
        self.ident = kb.sb("ident_sb", [128, 128], BF16)
        self.w_in_bf = kb.sb("w_in_bf", [128, 8, DIN], BF16)
        self.w_out_bf = kb.sb("w_out_bf", [128, 8, D], BF16)
        self.wstage = [kb.sb(f"wstage{i}", [128, DIN], F32) for i in range(2)]
        self.wstage_i = 0
        self.scT = kb.sb("scT", [128, 8, 2], F32)
        self.scbc = kb.sb("scbc", [128, 8, 2, 128], F32)
        self.b_col = kb.sb("b_col", [128, 24], F32)
        self.ada_acc = kb.sb("ada_acc", [128, 32], F32)
        self.ng_col = kb.sb("ng_col", [128, 8], F32)
        self.sh_col = kb.sb("sh_col", [128, 8, 2], F32)
        self.gs_col = kb.sb("gs_col", [128, 8, 2], F32)
        self.gate_bc = [kb.sb(f"gate_bc{c}", [128, D], F32) for c in range(2)]
        self.fg_bc = kb.sb("fg_bc", [128, D], F32)
        self.eps_col = kb.sb("eps_col", [128, 1], F32)
        self.xt = [kb.sb(f"xt{i}", [128, D], F32) for i in range(2)]
        self.xn = [kb.sb(f"xn{i}", [128, D], BF16) for i in range(2)]
        self.ss = [kb.sb(f"ss{i}", [128, 1], F32) for i in range(2)]
        self.rstd = [kb.sb(f"rstd{i}", [128, 1], F32) for i in range(2)]
        self.junk = kb.sb("junk", [128, D], BF16)
        self.hT = [kb.sb(f"hT{i}", [128, 8, 512], BF16) for i in range(2)]
        self.ss4 = [kb.sb(f"ss4_{i}", [128, 1], F32) for i in range(4)]
        self.rstd4 = [kb.sb(f"rstd4_{i}", [128, 1], F32) for i in range(4)]
        self.pstage = [kb.sb(f"pstage{i}", [128, 512], F32) for i in range(3)]
        self.pstage_i = 0
        self.tmstage = [kb.sb(f"tmstage{i}", [128, 512], BF16) for i in range(2)]
        self.yT = [kb.sb(f"yT{i}", [128, 8, 128], BF16) for i in range(2)]
        self.otmp = [kb.sb(f"otmp{i}", [128, D], F32) for i in range(2)]
        self.xt4 = [self.xt[0], self.xt[1], self.otmp[0], self.otmp[1]]
        self.bank = [kb.ps(f"bank{i}", [128, 512], F32) for i in range(8)]
        self.bank_i = 0
        wo_ = self.w_out_bf[:].rearrange("p a b -> p (a b)")
        self.xn4 = [Res(f"xn4_{i}", wo_[:, i * 1024:(i + 1) * 1024]) for i in range(8)]
        self.mixer_init()
        self.ft_init()
        self.s5_init()
        self.dn_init()

    def next_bank(self, lo=2, hi=8):
        b = self.bank[lo + self.bank_i % (hi - lo)]
        self.bank_i += 1
        return b

    def setup(self):
        kb, nc = self.kb, self.nc
        kb.dma('sp', self.ident[:], self.ident_in, self.ident, writes=[self.ident])
        kb.dma('sp', self.scT[:], self.condT, self.scT, writes=[self.scT])
        kb.dma('sp', self.fg_bc[:], self.final_g.partition_broadcast(128), self.fg_bc, writes=[self.fg_bc])
        kb.op('dve', lambda: nc.vector.memset(self.eps_col[:], EPS), writes=[self.eps_col])
        kb.op('act', lambda: nc.scalar.activation(out=self.scT[:], in_=self.scT[:], func=AF.Silu),
              reads=[self.scT], writes=[self.scT])
        for kt in range(8):
            for ci in range(2):
                kb.op('dve', lambda kt=kt, ci=ci: nc.vector.tensor_copy(
                    out=self.scbc[:, kt, ci, :],
                    in_=self.scT[:, kt, ci:ci + 1].to_broadcast([128, 128])),
                    reads=[self.scT], writes=[self.scbc])

    def load_weight_rows(self, dst_views, src_ap, ncols, cast_eng='pool'):
        kb, nc = self.kb, self.nc
        stg = self.wstage[self.wstage_i % 2]
        self.wstage_i += 1
        kb.dma('sp', stg[:, 0:ncols], src_ap, stg, writes=[stg])
        return stg

    def adaln(self, l):
        kb, nc = self.kb, self.nc
        kb.dma('sp', self.b_col[:], self.b_adaT[l], self.b_col, writes=[self.b_col])
        kb.dma('sp', self.ng_col[:], self.norm_gT[l], self.ng_col, writes=[self.ng_col])
        for ci in range(2):
            kb.dma('sp', self.gate_bc[ci][:], self.b_ada[l:l + 1, 2 * D:3 * D].partition_broadcast(128),
                   self.gate_bc[ci], writes=[self.gate_bc[ci]])
        pcol = self.bank[2]
        pg = [self.bank[3], self.bank[4], self.bank[5], self.bank[6]]
        for kt in range(8):
            stg = self.wstage[self.wstage_i % 2]
            self.wstage_i += 1
            kb.dma('sp', stg[:, 0:2 * D], self.w_ada[l, kt * 128:(kt + 1) * 128, 0:2 * D], stg, writes=[stg])
            stg2 = self.wstage[self.wstage_i % 2]
            self.wstage_i += 1
            kb.dma('sp', stg2[:, 0:D], self.w_ada[l, kt * 128:(kt + 1) * 128, 2 * D:3 * D], stg2, writes=[stg2])
            for mt in range(16):
                kb.op('pe', lambda mt=mt, kt=kt, stg=stg: nc.tensor.matmul(
                    pcol[:, mt * 2:mt * 2 + 2], lhsT=stg[:, mt * 128:(mt + 1) * 128], rhs=self.scT[:, kt, :],
                    start=True, stop=True), reads=[stg, self.scT], writes=[pcol])
            if kt == 0:
                kb.op('dve', lambda: nc.vector.tensor_copy(out=self.ada_acc[:], in_=pcol[:, 0:32]),
                      reads=[pcol], writes=[self.ada_acc])
            else:
                kb.op('dve', lambda: nc.vector.tensor_tensor(out=self.ada_acc[:], in0=pcol[:, 0:32], in1=self.ada_acc[:], op=ALU.add),
                      reads=[pcol, self.ada_acc], writes=[self.ada_acc])
            for ci in range(2):
                for hf in range(2):
                    kb.op('pe', lambda ci=ci, hf=hf, kt=kt, stg2=stg2: nc.tensor.matmul(
                        pg[ci * 2 + hf][:], lhsT=self.scbc[:, kt, ci, :],
                        rhs=stg2[:, hf * 512:(hf + 1) * 512],
                        start=(kt == 0), stop=(kt == 7)), reads=[stg2, self.scbc], writes=[pg[ci * 2 + hf]])
        pc3 = self.ada_acc[:].rearrange("p (m c) -> p m c", c=2)
        for ci in range(2):
            kb.op('dve', lambda ci=ci: nc.vector.tensor_tensor(
                out=self.sh_col[:, :, ci], in0=pc3[:, 0:8, ci], in1=self.b_col[:, 0:8], op=ALU.add),
                reads=[self.ada_acc, self.b_col], writes=[self.sh_col])
            kb.op('dve', lambda ci=ci: nc.vector.tensor_tensor(
                out=self.gs_col[:, :, ci], in0=pc3[:, 8:16, ci], in1=self.b_col[:, 8:16], op=ALU.add),
                reads=[self.ada_acc, self.b_col], writes=[self.gs_col])
            kb.op('dve', lambda ci=ci: nc.vector.scalar_tensor_tensor(
                out=self.gs_col[:, :, ci], in0=self.gs_col[:, :, ci], scalar=1.0, in1=self.ng_col[:],
                op0=ALU.add, op1=ALU.mult), reads=[self.gs_col, self.ng_col], writes=[self.gs_col])
            for hf in range(2):
                kb.op('dve', lambda ci=ci, hf=hf: nc.vector.tensor_tensor(
                    out=self.gate_bc[ci][:, hf * 512:(hf + 1) * 512], in0=pg[ci * 2 + hf][:],
                    in1=self.gate_bc[ci][:, hf * 512:(hf + 1) * 512], op=ALU.add),
                    reads=[pg[ci * 2 + hf], self.gate_bc[ci]], writes=[self.gate_bc[ci]])

    def load_layer_weights(self, l):
        kb, nc = self.kb, self.nc
        for kt in range(8):
            stg = self.wstage[self.wstage_i % 2]
            self.wstage_i += 1
            kb.dma('sp', stg[:, 0:DIN], self.w_in[l, kt * 128:(kt + 1) * 128, :], stg, writes=[stg])
            kb.op('act' if kt % 2 == 0 else 'dve', (lambda kt=kt, stg=stg: nc.scalar.copy(out=self.w_in_bf[:, kt, :], in_=stg[:, 0:DIN])) if kt % 2 == 0 else
                  (lambda kt=kt, stg=stg: nc.vector.tensor_copy(out=self.w_in_bf[:, kt, :], in_=stg[:, 0:DIN])),
                  reads=[stg], writes=[self.w_in_bf])

    def load_w_out(self, l):
        kb, nc = self.kb, self.nc
        for kt in range(8):
            stg = self.wstage[self.wstage_i % 2]
            self.wstage_i += 1
            kb.dma('sp', stg[:, 0:D], self.w_out[l, kt * 128:(kt + 1) * 128, :], stg, writes=[stg])
            kb.op('act' if kt % 2 == 0 else 'dve', (lambda kt=kt, stg=stg: nc.scalar.copy(out=self.w_out_bf[:, kt, :], in_=stg[:, 0:D])) if kt % 2 == 0 else
                  (lambda kt=kt, stg=stg: nc.vector.tensor_copy(out=self.w_out_bf[:, kt, :], in_=stg[:, 0:D])),
                  reads=[stg], writes=[self.w_out_bf])

    def phase_a(self, l):
        kb, nc = self.kb, self.nc
        src = self.x_in if l == 0 else self.X

        def stage1(g):
            for j in range(4):
                tt = g * 4 + j
                xt, xn, ss, rstd = self.xt4[j], self.xn4[(g % 2) * 4 + j], self.ss4[j], self.rstd4[j]
                kb.dma('sp', xt[:], src[tt * 128:(tt + 1) * 128, :], xt,
                       reads=([self.rX[tt]] if l > 0 else []), writes=[xt])
                kb.op('act', lambda xt=xt, ss=ss: nc.scalar.activation(out=self.junk[:], in_=xt[:], func=AF.Square, accum_out=ss[:]),
                      reads=[xt], writes=[self.junk, ss])
                kb.op('act', lambda ss=ss, rstd=rstd: nc.scalar.activation(out=rstd[:], in_=ss[:], func=AF.Sqrt, scale=1.0 / D, bias=self.eps_col[:]),
                      reads=[ss, self.eps_col], writes=[rstd])
                kb.op('dve', lambda rstd=rstd: nc.vector.reciprocal(out=rstd[:], in_=rstd[:]), reads=[rstd], writes=[rstd])
                kb.op('dve', lambda xt=xt, xn=xn, rstd=rstd: nc.vector.tensor_scalar(out=xn[:], in0=xt[:], scalar1=rstd[:], scalar2=None, op0=ALU.mult),
                      reads=[xt, rstd], writes=[xn])

        def stage2(g):
            hT = self.hT[g % 2]
            for j in range(4):
                tt = g * 4 + j
                ci = cond_of_tile(tt)
                xn = self.xn4[(g % 2) * 4 + j]
                pA, pB = self.bank[0], self.bank[1]
                pAv, pBv = pA[:].bitcast(BF16), pB[:].bitcast(BF16)
                for kt in range(8):
                    pv, pr = (pAv, pA) if kt < 4 else (pBv, pB)
                    kk = kt % 4
                    kb.op('pe', lambda kt=kt, kk=kk, pv=pv, xn=xn: nc.tensor.transpose(pv[:, kk * 128:(kk + 1) * 128], xn[:, kt * 128:(kt + 1) * 128], self.ident[:]),
                          reads=[xn, self.ident], writes=[pr])
                for kt in range(4):
                    kb.op('act', lambda kt=kt, j=j, ci=ci: nc.scalar.activation(
                        out=hT[:, kt, j * 128:(j + 1) * 128], in_=pAv[:, kt * 128:(kt + 1) * 128], func=AF.Identity,
                        scale=self.gs_col[:, kt, ci:ci + 1], bias=self.sh_col[:, kt, ci:ci + 1]),
                        reads=[pA, self.gs_col, self.sh_col], writes=[hT])
                pB3 = pBv[:, 0:512].rearrange("p (k t) -> p k t", k=4)
                ho = hT[:, 4:8, j * 128:(j + 1) * 128]
                kb.op('dve', lambda pB3=pB3, ho=ho, ci=ci: nc.vector.tensor_tensor(out=ho, in0=pB3, in1=self.gs_col[:, 4:8, ci:ci + 1].to_broadcast([128, 4, 128]), op=ALU.mult),
                      reads=[pB, self.gs_col], writes=[hT])
                kb.op('dve', lambda ho=ho, ci=ci: nc.vector.tensor_tensor(out=ho, in0=ho, in1=self.sh_col[:, 4:8, ci:ci + 1].to_broadcast([128, 4, 128]), op=ALU.add),
                      reads=[hT, self.sh_col], writes=[hT])

        def proj(g):
            hT = self.hT[g % 2]
            for i, (c0, cw) in enumerate(COL_TILES):
                acc = self.next_bank()
                for kt in range(8):
                    kb.op('pe', lambda kt=kt, acc=acc, c0=c0, cw=cw: nc.tensor.matmul(acc[0:cw, :], lhsT=self.w_in_bf[:, kt, c0:c0 + cw], rhs=hT[:, kt, :],
                                                                                      start=(kt == 0), stop=(kt == 7)),
                          reads=[self.w_in_bf, hT], writes=[acc])
                slots = self.pstage + self.wk
                stg = slots[self.pstage_i % len(slots)]
                self.pstage_i += 1
                if i % 2 == 0:
                    kb.op('act', lambda stg=stg, acc=acc, cw=cw: nc.scalar.copy(out=stg[0:cw, :], in_=acc[0:cw, :]), reads=[acc], writes=[stg])
                else:
                    kb.op('dve', lambda stg=stg, acc=acc, cw=cw: nc.vector.tensor_copy(out=stg[0:cw, :], in_=acc[0:cw, :]), reads=[acc], writes=[stg])
                kb.dma('pool', self.PT[c0:c0 + cw, g * 512:(g + 1) * 512], stg[0:cw, :], stg,
                       reads=[stg], writes=[self.rPT[g]])
            for j in range(4):
                tt = g * 4 + j
                acc = self.next_bank()
                for kt in range(8):
                    kb.op('pe', lambda kt=kt, acc=acc, j=j: nc.tensor.matmul(acc[:, 0:256], lhsT=hT[:, kt, j * 128:(j + 1) * 128], rhs=self.w_in_bf[:, kt, C_POOL_U:C_POOL_U + 256],
                                                                        start=(kt == 0), stop=(kt == 7)),
                          reads=[self.w_in_bf, hT], writes=[acc])
                stg = self.tmstage[tt % 2]
                kb.op('dve', lambda stg=stg, acc=acc: nc.vector.tensor_copy(out=stg[:, 0:256], in_=acc[:, 0:256]), reads=[acc], writes=[stg])
                kb.dma('pool', self.PTOK[tt * 128:(tt + 1) * 128, :], stg[:, 0:256], stg, reads=[stg], writes=[self.rPTOK[tt]])

        stage1(0)
        stage2(0)
        for g in range(NG):
            if g + 1 < NG:
                stage1(g + 1)
            proj(g)
            if g + 1 < NG:
                stage2(g + 1)

    def phase_b_stub(self, l):
        kb, nc = self.kb, self.nc
        for g in range(NG):
            for kt in range(8):
                stg = self.pstage[self.pstage_i % 3]
                self.pstage_i += 1
                kb.dma('sp', stg[:], self.PT[kt * 128:(kt + 1) * 128, g * 512:(g + 1) * 512], stg,
                       reads=[self.rPT[g]], writes=[stg])
                o = self.tmstage[kt % 2]
                kb.op('act', lambda: nc.scalar.copy(out=o[:], in_=stg[:]), reads=[stg], writes=[o])
                kb.dma('pool', self.YT[kt * 128:(kt + 1) * 128, g * 512:(g + 1) * 512], o[:], o,
                       reads=[o], writes=[self.rYT[kt][g]])

    def phase_c(self, l):
        kb, nc = self.kb, self.nc
        last = (l == self.depth - 1)
        src = self.x_in if l == 0 else self.X
        YTv = self.YT.rearrange("(k p) t -> p k t", p=128)
        for tt in range(NTT):
            ci = cond_of_tile(tt)
            s = tt % 2
            xt, yT, ot = self.xt[s], self.yT[s], self.otmp[s]
            kb.dma('sp', yT[:], YTv[:, :, tt * 128:(tt + 1) * 128], yT, reads=[self.rYT[k][tt // 4] for k in range(8)], writes=[yT])
            kb.dma('sp', xt[:], src[tt * 128:(tt + 1) * 128, :], xt,
                   reads=([self.rX[tt]] if l > 0 else []), writes=[xt])
            for hf in range(2):
                acc = self.next_bank()
                for kt in range(8):
                    kb.op('pe', lambda kt=kt: nc.tensor.matmul(acc[:], lhsT=yT[:, kt, :], rhs=self.w_out_bf[:, kt, hf * 512:(hf + 1) * 512],
                                                               start=(kt == 0), stop=(kt == 7)),
                          reads=[yT, self.w_out_bf], writes=[acc])
                sl = slice(hf * 512, (hf + 1) * 512)
                kb.op('dve', lambda: nc.vector.tensor_tensor(out=ot[:, sl], in0=acc[:], in1=self.gate_bc[ci][:, sl], op=ALU.mult),
                      reads=[acc, self.gate_bc[ci]], writes=[ot])
                kb.op('dve', lambda: nc.vector.tensor_tensor(out=xt[:, sl], in0=ot[:, sl], in1=xt[:, sl], op=ALU.add),
                      reads=[ot, xt], writes=[xt])
            if not last:
                kb.dma('pool', self.X[tt * 128:(tt + 1) * 128, :], xt[:], xt, reads=[xt], writes=[self.rX[tt]])
            else:
                ss, rstd = self.ss[s], self.rstd[s]
                kb.op('act', lambda: nc.scalar.activation(out=self.junk[:], in_=xt[:], func=AF.Square, accum_out=ss[:]),
                      reads=[xt], writes=[self.junk, ss])
                kb.op('act', lambda: nc.scalar.activation(out=rstd[:], in_=ss[:], func=AF.Sqrt, scale=1.0 / D, bias=self.eps_col[:]),
                      reads=[ss, self.eps_col], writes=[rstd])
                kb.op('dve', lambda: nc.vector.reciprocal(out=rstd[:], in_=rstd[:]), reads=[rstd], writes=[rstd])
                kb.op('dve', lambda: nc.vector.scalar_tensor_tensor(out=ot[:], in0=xt[:], scalar=rstd[:], in1=self.fg_bc[:],
                                                                    op0=ALU.mult, op1=ALU.mult),
                      reads=[xt, rstd, self.fg_bc], writes=[ot])
                kb.dma('pool', self.y_out[tt * 128:(tt + 1) * 128, :], ot[:], ot, reads=[ot])

    def build(self):
        self.setup()
        self.mixer_setup()
        self.ft_setup()
        self.s5_setup()
        self.dn_setup()
        for l in range(self.depth):
            self.adaln(l)
            self.load_layer_weights(l)
            self.kb.alias_begin(self.w_out_bf, self.xn4)
            self.phase_a(l)
            self.kb.alias_end(self.w_out_bf, self.xn4)
            if self.cfg.get('stub', False):
                self.phase_b_stub(l)
            else:
                self.phase_b(l)
            self.load_w_out(l)
            self.phase_c(l)
        self.kb.finish('sp')


def build_nc(cfg):
    nc = bass.Bass("TRN2", target_bir_lowering=False)
    with ExitStack() as st:
        m = Model(nc, st, cfg)
        m.build()
        print("instructions:", m.kb.ninstr, "sems:", m.kb.nsem)
    return nc


def make_in_maps(inputs, ncores=8):
    f32 = lambda a: np.ascontiguousarray(np.asarray(a, dtype=np.float32))
    x_prompt = f32(inputs['x_prompt']); x_sample = f32(inputs['x_sample'])
    c = f32(inputs['c']); c_ctx = f32(inputs['c_ctx'])
    L = DEPTH
    b_ada = f32(inputs['b_ada'])
    shared = dict(
        w_ada=f32(inputs['w_ada']), b_ada=b_ada,
        b_adaT=np.ascontiguousarray(b_ada.reshape(L, 24, 128).transpose(0, 2, 1)),
        norm_gT=np.ascontiguousarray(f32(inputs['norm_g']).reshape(L, 8, 128).transpose(0, 2, 1)),
        w_in=f32(inputs['w_in']), w_out=f32(inputs['w_out']),
        final_g=f32(inputs['final_g']).reshape(1, D),
        ident=np.eye(128).astype(ml_dtypes.bfloat16),
        pool_w=f32(inputs['pool_w']),
        pool_scaleT=np.ascontiguousarray(f32(inputs['pool_scale']).reshape(L, 2, 128).transpose(0, 2, 1)),
    )
    bm, inv = make_pool_consts()
    shared.update(bmats=bm, inv_cnt=inv)
    shared.update(make_ft_consts())
    shared.update(make_dn_consts())
    shared.update(ft_w=f32(inputs['ft_w']))
    maps = []
    for k in range(ncores):
        sb = k // 4
        x_all = np.concatenate([x_sample[sb], x_prompt[2 * k], x_prompt[2 * k + 1]], axis=0)
        cond = np.stack([c[sb], c_ctx], axis=0)
        condT = np.ascontiguousarray(cond.reshape(2, 8, 128).transpose(2, 1, 0))
        m = dict(shared)
        m.update(x_all=np.ascontiguousarray(x_all), condT=condT)
        m.update(make_s5_inputs(inputs, k))
        m.update(make_dn_inputs(inputs, k))
        maps.append(m)
    return maps


POOL_WINDOWS = (2, 4, 8, 16)
POOL_D2 = {2: (-1, 0), 4: (-1, 1), 8: (-2, 2), 16: (-4, 4)}
POOL_D1 = (-1, 1)


def pool_block_index():
    idx = {}
    n = 0
    for w in POOL_WINDOWS:
        lo, hi = POOL_D2[w]
        for d in range(lo, hi + 1):
            idx[('s', w, d)] = n
            n += 1
    for w in POOL_WINDOWS:
        for d in range(POOL_D1[0], POOL_D1[1] + 1):
            idx[('p', w, d)] = n
            n += 1
    return idx, n


def make_pool_consts():
    idx, n = pool_block_index()
    B = np.zeros((n, 128, 128), np.float32)
    i = np.arange(128)
    ril, cin = i // 64, i % 64
    for w in POOL_WINDOWS:
        lo, hi = POOL_D2[w]
        for d in range(lo, hi + 1):
            dr = 2 * d + ril[:, None] - ril[None, :]
            dc = cin[:, None] - cin[None, :]
            B[idx[('s', w, d)]] = ((dr >= -(w // 2)) & (dr < w - w // 2) & (dc >= -(w // 2)) & (dc < w - w // 2))
        for d in range(-1, 2):
            dt_ = 128 * d + i[:, None] - i[None, :]
            B[idx[('p', w, d)]] = ((dt_ >= -(w // 2)) & (dt_ < w - w // 2))
    inv = np.zeros((4, NT), np.float32)

    def cnt(Ln, w):
        pos = np.arange(Ln)
        lo = np.clip(pos - w // 2, 0, Ln)
        hi = np.clip(pos - w // 2 + w, 0, Ln)
        return (hi - lo).astype(np.float64)
    for gi, w in enumerate(POOL_WINDOWS):
        c64 = cnt(64, w)
        inv[gi, :TS] = (1.0 / (c64[:, None] * c64[None, :])).reshape(-1)
        c256 = 1.0 / cnt(256, w)
        inv[gi, TS:] = np.concatenate([c256, c256])
    return B.astype(ml_dtypes.bfloat16), inv


def seq_tile_range(tt):
    if tt < TS // 128:
        return 0, TS // 128
    lo = tt - (tt - TS // 128) % 2
    return lo, lo + 2


def _mixer_init(self):
    kb, nc = self.kb, self.nc
    L = DEPTH
    dt_in = lambda name, shape, dt=F32: nc.dram_tensor(name, list(shape), dt, kind="ExternalInput").ap()
    _, nblk = pool_block_index()
    self.bmats_in = dt_in("bmats", [nblk, 128, 128], BF16)
    self.inv_cnt = dt_in("inv_cnt", [4, NT])
    self.pool_w = dt_in("pool_w", [L, 4, 64, 64])
    self.pool_scaleT = dt_in("pool_scaleT", [L, 128, 2])
    self.bmats = kb.sb("bmats_sb", [128, nblk, 128], BF16)
    self.pwstage = kb.sb("pwstage", [128, 2, 64], F32)
    self.pwblk = kb.sb("pwblk", [128, 2, 128], BF16)
    self.pscale = kb.sb("pscale", [128, 2], F32)
    self.utok = kb.sb("utok", [128, 12, 256], BF16)
    self.wk = [kb.sb(f"wk{i}", [128, 512], F32) for i in range(6)]
    self.wkb = [kb.sb(f"wkb{i}", [128, 512], BF16) for i in range(4)]
    self.ybuf = [kb.sb(f"ybuf{i}", [128, 512], BF16) for i in range(2)]
    self.ybuf_i = 0
    self.zero_bf = kb.sb("zero_bf", [128, 512], BF16)


def _mixer_setup(self):
    kb, nc = self.kb, self.nc
    kb.dma('sp', self.bmats[:], self.bmats_in.rearrange("n p q -> p n q"), self.bmats, writes=[self.bmats])
    kb.op('dve', lambda: nc.vector.memset(self.zero_bf[:], 0.0), writes=[self.zero_bf])


def zero_rows(self, kts):
    kb = self.kb
    for kt in kts:
        for g in range(NG):
            kb.dma('pool', self.YT[kt * 128:(kt + 1) * 128, g * 512:(g + 1) * 512], self.zero_bf[:], self.zero_bf,
                   reads=[self.zero_bf], writes=[self.rYT[kt][g]])


def store_y(self, yb, row0, g):
    kb = self.kb
    kb.dma('pool', self.YT[row0:row0 + 128, g * 512:(g + 1) * 512], yb[:], yb,
           reads=[yb], writes=[self.rYT[row0 // 128][g]])


def pool_phase(self, l):
    kb, nc = self.kb, self.nc
    bidx, _ = pool_block_index()
    kb.dma('sp', self.pwstage[:], self.pool_w[l].rearrange("(a gl) c d -> (gl c) a d", gl=2), self.pwstage,
           writes=[self.pwstage])
    kb.dma('sp', self.pscale[:], self.pool_scaleT[l], self.pscale, writes=[self.pscale])
    kb.op('dve', lambda: nc.vector.memset(self.pwblk[:], 0.0), writes=[self.pwblk])
    for a in range(2):
        kb.op('dve', lambda a=a: nc.vector.tensor_copy(out=self.pwblk[0:64, a, 0:64], in_=self.pwstage[0:64, a, :]),
              reads=[self.pwstage], writes=[self.pwblk])
        kb.op('dve', lambda a=a: nc.vector.tensor_copy(out=self.pwblk[64:128, a, 64:128], in_=self.pwstage[64:128, a, :]),
              reads=[self.pwstage], writes=[self.pwblk])
    for g in range(NG):
        sample = (g * 512 < TS)
        t0 = 4 * g
        lo_l = max(seq_tile_range(t0)[0], t0 - 4) if sample else t0
        hi_l = min(seq_tile_range(t0)[1], t0 + 8) if sample else t0 + 4
        for m in range(lo_l, hi_l):
            kb.dma('sp', self.utok[:, m - (t0 - 4), :], self.PTOK[m * 128:(m + 1) * 128, 0:256], self.utok,
                   reads=[self.rPTOK[m]], writes=[self.utok])
        for a in range(2):
            uT, zT, inv, tmp, sz = self.wk[0], self.wk[1], self.wk[2], self.wk[3], self.wk[4]
            diffb = self.wkb[0]
            tsl = slice(g * 512, (g + 1) * 512)
            kb.dma('sp', uT[:], self.PT[C_POOL_U + a * 128:C_POOL_U + (a + 1) * 128, tsl], uT, reads=[self.rPT[g]], writes=[uT])
            kb.dma('sp', zT[:], self.PT[C_POOL_Z + a * 128:C_POOL_Z + (a + 1) * 128, tsl], zT, reads=[self.rPT[g]], writes=[zT])
            for gl in range(2):
                kb.dma('sp', inv[gl * 64:(gl + 1) * 64, :], self.inv_cnt[2 * a + gl:2 * a + gl + 1, tsl].partition_broadcast(64),
                       inv, writes=[inv])
            acc = self.next_bank()
            for gl in range(2):
                gi = 2 * a + gl
                w = POOL_WINDOWS[gi]
                for j in range(4):
                    m = t0 + j
                    slo, shi = seq_tile_range(m)
                    dlo, dhi = POOL_D2[w] if sample else POOL_D1
                    ds = [d for d in range(dlo, dhi + 1) if slo <= m + d < shi]
                    for ii, d in enumerate(ds):
                        bi = bidx[('s' if sample else 'p', w, d)]
                        kb.op('pe', lambda ii=ii, d=d, bi=bi, m=m, j=j, gl=gl, gi=gi, ds=ds: nc.tensor.matmul(
                            acc[gl * 64:(gl + 1) * 64, j * 128:(j + 1) * 128],
                            lhsT=self.utok[:, m + d - (t0 - 4), gi * 64:(gi + 1) * 64], rhs=self.bmats[:, bi, :],
                            start=(ii == 0), stop=(ii == len(ds) - 1)),
                            reads=[self.utok, self.bmats], writes=[acc])
            kb.op('dve', lambda: nc.vector.tensor_tensor(out=tmp[:], in0=acc[:], in1=inv[:], op=ALU.mult),
                  reads=[acc, inv], writes=[tmp])
            kb.op('dve', lambda: nc.vector.tensor_tensor(out=diffb[:], in0=tmp[:], in1=uT[:], op=ALU.subtract),
                  reads=[tmp, uT], writes=[diffb])
            acc2 = self.next_bank()
            kb.op('pe', lambda: nc.tensor.matmul(acc2[:], lhsT=self.pwblk[:, a, :], rhs=diffb[:], start=True, stop=True),
                  reads=[self.pwblk, diffb], writes=[acc2])
            kb.op('act', lambda: nc.scalar.activation(out=sz[:], in_=zT[:], func=AF.Silu), reads=[zT], writes=[sz])
            yb = self.ybuf[self.ybuf_i % 2]
            self.ybuf_i += 1
            kb.op('dve', lambda: nc.vector.scalar_tensor_tensor(out=yb[:], in0=acc2[:], scalar=self.pscale[:, a:a + 1], in1=sz[:],
                                                                op0=ALU.mult, op1=ALU.mult),
                  reads=[acc2, self.pscale, sz], writes=[yb])
            self.store_y(yb, a * 128, g)


def phase_b(self, l):
    br = self.cfg.get('branches', ('pool', 'dn', 's5', 'ft'))
    if 'pool' in br:
        self.pool_phase(l)
    else:
        self.zero_rows([0, 1])
    if 'dn' in br:
        self.dn_phase(l)
    else:
        self.zero_rows([2, 3])
    if 's5' in br:
        self.s5_phase(l)
    else:
        self.zero_rows([4, 5])
    if 'ft' in br:
        self.ft_phase(l)
    else:
        self.zero_rows([6, 7])
    if self.cfg.get('dump_yt', False) and l == 0:
        dbg = self.nc.dram_tensor("yt_dbg", [D, NT], BF16, kind="ExternalOutput").ap()
        r = Res("ytdbg")
        self.kb.dma('sp', dbg, self.YT, r, reads=[self.rYT[k][g] for k in range(8) for g in range(NG)])


Model.mixer_init = _mixer_init
Model.mixer_setup = _mixer_setup
Model.zero_rows = zero_rows
Model.store_y = store_y
Model.pool_phase = pool_phase
Model.phase_b = phase_b


def make_ft_consts():
    c = np.arange(64, dtype=np.float64)
    ang = 2 * np.pi * np.outer(c, c) / 64
    wdft = np.zeros((256, 512), np.float64)
    for h in range(4):
        wdft[h * 64:(h + 1) * 64, h * 64:(h + 1) * 64] = np.cos(ang) / 8.0
        wdft[h * 64:(h + 1) * 64, 256 + h * 64:256 + (h + 1) * 64] = -np.sin(ang) / 8.0
    C, S = np.cos(ang) / 64.0, np.sin(ang) / 64.0
    ma = np.zeros((128, 128), np.float64)
    ma[0:64, 0:64] = C
    ma[64:128, 0:64] = S
    ma[0:64, 64:128] = -S
    ma[64:128, 64:128] = C
    t2 = np.arange(64, dtype=np.float64)
    tp = np.arange(4096, dtype=np.float64)
    th = 2 * np.pi * (np.outer(t2, tp) % 4096) / 4096
    g = np.zeros((128, 64, 64), np.float64)
    g[0:64] = np.cos(th).reshape(64, 64, 64).transpose(0, 2, 1)
    g[64:128] = np.sin(th).reshape(64, 64, 64).transpose(0, 2, 1)
    t = np.arange(256, dtype=np.float64)
    a256 = 2 * np.pi * (np.outer(t, t) % 256) / 256
    cs256 = np.stack([np.cos(a256) / 16.0, np.sin(a256) / 16.0], axis=0)
    cs256 = cs256.reshape(2, 2, 128, 256).transpose(2, 0, 1, 3)
    f = lambda a: np.ascontiguousarray(a.astype(np.float32))
    return dict(ft_wdft=f(wdft.reshape(2, 128, 512).transpose(1, 0, 2)), ft_ma=f(ma),
                ft_g=f(g.reshape(128, 4096)), ft_cs256=f(cs256.reshape(128, 1024)))


def _ft_init(self):
    kb, nc = self.kb, self.nc
    L = DEPTH
    dt_in = lambda name, shape, dt=F32: nc.dram_tensor(name, list(shape), dt, kind="ExternalInput").ap()
    self.ft_wdft_in = dt_in("ft_wdft", [128, 2, 512])
    self.ft_ma_in = dt_in("ft_ma", [128, 128])
    self.ft_g_in = dt_in("ft_g", [128, 4096])
    self.ft_cs256_in = dt_in("ft_cs256", [128, 1024])
    self.ft_w_in = dt_in("ft_w", [L, 256, 256])
    self.wdft_bf = kb.sb("wdft_bf", [128, 2, 512], BF16)
    self.ma_bf = kb.sb("ma_bf", [128, 128], BF16)
    self.g_bf = kb.sb("g_bf", [128, 64, 64], BF16)
    self.cs256_bf = kb.sb("cs256_bf", [128, 2, 2, 256], BF16)
    self.ftw_bf = kb.sb("ftw_bf", [128, 2, 256], BF16)
    self.VS = nc.dram_tensor("VS_scr", [NT, 512], BF16, kind="Internal").ap()
    self.ZS = nc.dram_tensor("ZS_scr", [2, 64, 64, 256], BF16, kind="Internal").ap()
    self.FS = nc.dram_tensor("FS_scr", [NT, 256], BF16, kind="Internal").ap()
    self.rVS = [Res(f"VS{g}") for g in range(NG)]
    self.rZS = Res("ZS")
    self.rFS = [Res(f"FS{g}") for g in range(NG)]


def _ft_setup(self):
    kb, nc = self.kb, self.nc

    def load_cast(dst_ap, dst_res, src_ap, ncols):
        stg = self.wstage[self.wstage_i % 2]
        self.wstage_i += 1
        kb.dma('sp', stg[:, 0:ncols], src_ap, stg, writes=[stg])
        kb.op('dve', lambda: nc.vector.tensor_copy(out=dst_ap, in_=stg[:, 0:ncols]), reads=[stg], writes=[dst_res])
    load_cast(self.wdft_bf[:].rearrange("p a b -> p (a b)"), self.wdft_bf, self.ft_wdft_in.rearrange("p a b -> p (a b)"), 1024)
    load_cast(self.ma_bf[:], self.ma_bf, self.ft_ma_in, 128)
    gv = self.g_bf[:].rearrange("p a b -> p (a b)")
    load_cast(gv[:, 0:2048], self.g_bf, self.ft_g_in[:, 0:2048], 2048)
    load_cast(gv[:, 2048:4096], self.g_bf, self.ft_g_in[:, 2048:4096], 2048)
    load_cast(self.cs256_bf[:].rearrange("p a b c -> p (a b c)"), self.cs256_bf, self.ft_cs256_in, 1024)


def ft_phase(self, l):
    kb, nc = self.kb, self.nc
    stg = self.wstage[self.wstage_i % 2]
    self.wstage_i += 1
    kb.dma('sp', stg[:, 0:512].rearrange("p (k d) -> p k d", k=2), self.ft_w_in[l].rearrange("(k p) d -> p k d", p=128), stg, writes=[stg])
    kb.op('dve', lambda: nc.vector.tensor_copy(out=self.ftw_bf[:].rearrange("p k d -> p (k d)"), in_=stg[:, 0:512]),
          reads=[stg], writes=[self.ftw_bf])
    for g in range(NG):
        tsl = slice(g * 512, (g + 1) * 512)
        uTb = self.wkb[0:2]
        for ct in range(2):
            uf = self.wk[ct]
            kb.dma('sp', uf[:], self.PT[C_FTU + ct * 128:C_FTU + (ct + 1) * 128, tsl], uf, reads=[self.rPT[g]], writes=[uf])
            kb.op('act', lambda ct=ct, uf=uf: nc.scalar.copy(out=uTb[ct][:], in_=uf[:]), reads=[uf], writes=[uTb[ct]])
        for j in range(4):
            acc = self.next_bank()
            for ct in range(2):
                kb.op('pe', lambda ct=ct: nc.tensor.matmul(acc[:], lhsT=uTb[ct][:, j * 128:(j + 1) * 128], rhs=self.wdft_bf[:, ct, :],
                                                           start=(ct == 0), stop=(ct == 1)),
                      reads=[uTb[ct], self.wdft_bf], writes=[acc])
            vb = self.tmstage[j % 2]
            kb.op('dve', lambda: nc.vector.tensor_copy(out=vb[:], in_=acc[:]), reads=[acc], writes=[vb])
            tt = g * 4 + j
            kb.dma('pool', self.VS[tt * 128:(tt + 1) * 128, :], vb[:], vb, reads=[vb], writes=[self.rVS[g]])
    VSv = self.VS[0:TS, :].rearrange("(a b) (r c) -> r a b c", b=64, r=2)
    for ch in range(8):
        va = self.hT[ch % 2]
        vav = va[:].rearrange("p a b -> p (a b)")[:, 0:2048].rearrange("p (t c) -> p t c", c=256)
        for ri in range(2):
            kb.dma('sp', vav[ri * 64:(ri + 1) * 64, :, :], VSv[ri, :, ch * 8:(ch + 1) * 8, :], va,
                   reads=self.rVS[0:8], writes=[va])
        zo = va[:].rearrange("p a b -> p (a b)")[:, 2048:4096]
        vflat = va[:].rearrange("p a b -> p (a b)")
        for q in range(4):
            acc = self.next_bank()
            kb.op('pe', lambda q=q: nc.tensor.matmul(acc[:], lhsT=self.ma_bf[:], rhs=vflat[:, q * 512:(q + 1) * 512], start=True, stop=True),
                  reads=[self.ma_bf, va], writes=[acc])
            if q % 2 == 0:
                kb.op('act', lambda q=q: nc.scalar.copy(out=zo[:, q * 512:(q + 1) * 512], in_=acc[:]), reads=[acc], writes=[va])
            else:
                kb.op('dve', lambda q=q: nc.vector.tensor_copy(out=zo[:, q * 512:(q + 1) * 512], in_=acc[:]), reads=[acc], writes=[va])
        for ri in range(2):
            kb.dma('pool', self.ZS[ri, :, ch * 8:(ch + 1) * 8, :], zo[ri * 64:(ri + 1) * 64, :].rearrange("p (t c) -> p t c", c=256), va,
                   reads=[va], writes=[self.rZS])
    ZSv = self.ZS.rearrange("r k t c -> r t k c")
    FSv = self.FS[0:TS, :].rearrange("(k2 k1) c -> k2 k1 c", k1=64)
    for ch in range(4):
        zb = self.hT[ch % 2]
        zflat = zb[:].rearrange("p a b -> p (a b)")
        z2 = zflat.rearrange("p (k c) -> p k c", c=256)
        for ri in range(2):
            kb.dma('sp', z2[ri * 64:(ri + 1) * 64, :, :], ZSv[ri, :, ch * 16:(ch + 1) * 16, :], zb,
                   reads=[self.rZS], writes=[zb])
        fo_res = self.wstage[1]
        fo = fo_res[:].bitcast(BF16)
        for kp in range(8):
            acc = self.next_bank()
            for e in range(2):
                k1l = kp * 2 + e
                k1 = ch * 16 + k1l
                kb.op('pe', lambda e=e, k1=k1, k1l=k1l: nc.tensor.matmul(acc[0:64, e * 256:(e + 1) * 256], lhsT=self.g_bf[:, k1, :], rhs=z2[:, k1l, :],
                                                                         start=True, stop=True),
                      reads=[self.g_bf, zb], writes=[acc])
            if kp % 2 == 0:
                kb.op('act', lambda kp=kp: nc.scalar.copy(out=fo[0:64, kp * 512:(kp + 1) * 512], in_=acc[0:64, :]), reads=[acc], writes=[fo_res])
            else:
                kb.op('dve', lambda kp=kp: nc.vector.tensor_copy(out=fo[0:64, kp * 512:(kp + 1) * 512], in_=acc[0:64, :]), reads=[acc], writes=[fo_res])
        kb.dma('pool', FSv[:, ch * 16:(ch + 1) * 16, :], fo[0:64, 0:4096].rearrange("p (k c) -> p k c", c=256), fo_res,
               reads=[fo_res], writes=self.rFS[0:8])
    for sq in range(2):
        base = TS + sq * 256
        vt = self.hT[sq % 2]
        vv = vt[:].rearrange("p a b -> p (a b)")[:, 0:1024].rearrange("p (k c) -> p k c", c=512)
        kb.dma('sp', vv, self.VS[base:base + 256, :].rearrange("(k p) c -> p k c", p=128), vt, reads=[self.rVS[8]], writes=[vt])
        for mt in range(2):
            acc = self.next_bank()
            n = 0
            for cs in range(2):
                for kt in range(2):
                    kb.op('pe', lambda cs=cs, kt=kt, n=n, mt=mt: nc.tensor.matmul(
                        acc[:, 0:256], lhsT=self.cs256_bf[:, cs, kt, mt * 128:(mt + 1) * 128], rhs=vv[:, kt, cs * 256:(cs + 1) * 256],
                        start=(n == 0), stop=(n == 3)), reads=[self.cs256_bf, vt], writes=[acc])
                    n += 1
            fb = self.tmstage[mt % 2]
            kb.op('dve', lambda: nc.vector.tensor_copy(out=fb[:, 0:256], in_=acc[:, 0:256]), reads=[acc], writes=[fb])
            kb.dma('pool', self.FS[base + mt * 128:base + (mt + 1) * 128, :], fb[:, 0:256], fb, reads=[fb], writes=[self.rFS[8]])
    for g in range(NG):
        tsl = slice(g * 512, (g + 1) * 512)
        ftok = self.wkb[2]
        fT = [self.wkb[0], self.wkb[1]]
        for j in range(4):
            ft = self.wkb[2 + j % 2]
            tt = g * 4 + j
            kb.dma('sp', ft[:, 0:256], self.FS[tt * 128:(tt + 1) * 128, :], ft, reads=[self.rFS[g]], writes=[ft])
            pT = self.bank[j % 2]
            pTv = pT[:].bitcast(BF16)
            for ct in range(2):
                kb.op('pe', lambda ct=ct: nc.tensor.transpose(pTv[:, ct * 128:(ct + 1) * 128], ft[:, ct * 128:(ct + 1) * 128], self.ident[:]),
                      reads=[ft, self.ident], writes=[pT])
            for ct in range(2):
                kb.op('act', lambda ct=ct: nc.scalar.copy(out=fT[ct][:, j * 128:(j + 1) * 128], in_=pTv[:, ct * 128:(ct + 1) * 128]),
                      reads=[pT], writes=[fT[ct]])
        for dtile in range(2):
            zT, sz = self.wk[2 + dtile], self.wk[4 + dtile]
            kb.dma('sp', zT[:], self.PT[C_FTZ + dtile * 128:C_FTZ + (dtile + 1) * 128, tsl], zT, reads=[self.rPT[g]], writes=[zT])
            kb.op('act', lambda: nc.scalar.activation(out=sz[:], in_=zT[:], func=AF.Silu), reads=[zT], writes=[sz])
            acc = self.next_bank()
            for ct in range(2):
                kb.op('pe', lambda ct=ct: nc.tensor.matmul(acc[:], lhsT=self.ftw_bf[:, ct, dtile * 128:(dtile + 1) * 128], rhs=fT[ct][:],
                                                           start=(ct == 0), stop=(ct == 1)),
                      reads=[self.ftw_bf, fT[ct]], writes=[acc])
            yb = self.ybuf[self.ybuf_i % 2]
            self.ybuf_i += 1
            kb.op('dve', lambda: nc.vector.tensor_tensor(out=yb[:], in0=acc[:], in1=sz[:], op=ALU.mult), reads=[acc, sz], writes=[yb])
            self.store_y(yb, 768 + dtile * 128, g)


Model.ft_init = _ft_init
Model.ft_setup = _ft_setup
Model.ft_phase = ft_phase


S5_L = 256
MAGIC = 12582912.0
P_ARE, P_AIM, P_LDT, P_DT, P_MAG, P_TH, P_K, P_R, P_SN, P_CS, P_ABRE, P_ABIM, P_DEN, P_NR, P_CRE, P_CIM, P_T1, P_T2, P_NCIM = range(19)


def _s5_init(self):
    kb, nc = self.kb, self.nc
    L = DEPTH
    dt_in = lambda name, shape, dt=F32: nc.dram_tensor(name, list(shape), dt, kind="ExternalInput").ap()
    self.s5_a_reT = dt_in("s5_a_reT", [L, 128, 16])
    self.s5_a_imT = dt_in("s5_a_imT", [L, 128, 16])
    self.s5_logdtT = dt_in("s5_logdtT", [L, 128, 16])
    self.s5_b_re = dt_in("s5_b_re", [L, 2, 16, 64, 16])
    self.s5_b_im = dt_in("s5_b_im", [L, 2, 16, 64, 16])
    self.s5_c_re = dt_in("s5_c_re", [L, 2, 16, 16, 64])
    self.s5_c_im = dt_in("s5_c_im", [L, 2, 16, 16, 64])
    self.s5_dT = dt_in("s5_dT", [L, 128, 2])
    self.s5_glu_bT = dt_in("s5_glu_bT", [L, 128, 2])
    self.s5_glu_w = dt_in("s5_glu_w", [L, 256, 256])
    self.s5_s0T = dt_in("s5_s0T", [L, 128, 32])
    self.new_s5T = nc.dram_tensor("new_s5T", [L, 128, 64], F32, kind="ExternalOutput").ap()
    self.YS = nc.dram_tensor("YS_scr", [2, 256, NT], F32, kind="Internal").ap()
    self.YG = nc.dram_tensor("YG_scr", [256, NT], BF16, kind="Internal").ap()
    self.rYS = [[[Res(f"YS{d}_{c}_{k}") for k in range(NT // S5_L)] for c in range(2)] for d in range(2)]
    self.rYG = [[Res(f"YG{c}_{g}") for g in range(NG)] for c in range(2)]
    self.s5par = kb.sb("s5par", [128, 19, 16], F32)
    self.s5b = kb.sb("s5b", [128, 4, 16], F32)
    self.s5s0 = kb.sb("s5s0", [128, 32], F32)
    self.s5carry = kb.sb("s5carry", [128, 4, 2], F32)
    self.s5carry2 = kb.sb("s5carry2", [128, 2, 4], F32)
    self.s5ctmp = kb.sb("s5ctmp", [128, 2, 4], F32)
    self.s5carry2b = kb.sb("s5carry2b", [128, 2, 4], F32)
    self.s5ctmpb = kb.sb("s5ctmpb", [128, 2, 4], F32)
    self.s5fin = kb.sb("s5fin", [128, 64], F32)
    self.s5d = kb.sb("s5d", [128, 2], F32)
    self.s5gb = kb.sb("s5gb", [128, 2], F32)
    self.gluw_bf = kb.sb("gluw_bf", [128, 2, 256], BF16)
    self.hp_col = kb.sb("hp_col", [128, 1], F32)
    self.ident32 = kb.sb("ident32", [128, 128], F32)


def _s5_setup(self):
    kb, nc = self.kb, self.nc
    kb.op('dve', lambda: nc.vector.memset(self.hp_col[:], float(np.pi / 2)), writes=[self.hp_col])
    kb.op('dve', lambda: nc.vector.tensor_copy(out=self.ident32[:], in_=self.ident[:]), reads=[self.ident], writes=[self.ident32])


def s5_phase(self, l):
    kb, nc = self.kb, self.nc
    P = self.s5par
    V, A, G_ = 'dve', 'act', 'pool'

    def pc(i):
        return P[:, i, :]

    def dve_tt(o, a, b, op):
        kb.op(V, lambda: nc.vector.tensor_tensor(out=pc(o), in0=pc(a), in1=pc(b), op=op), reads=[P], writes=[P])

    def dve_ts(o, a, s1, s2, op0, op1=None):
        if op1 is None:
            kb.op(V, lambda: nc.vector.tensor_scalar(out=pc(o), in0=pc(a), scalar1=s1, scalar2=None, op0=op0), reads=[P], writes=[P])
        else:
            kb.op(V, lambda: nc.vector.tensor_scalar(out=pc(o), in0=pc(a), scalar1=s1, scalar2=s2, op0=op0, op1=op1), reads=[P], writes=[P])

    def act_f(o, a, func, **kw):
        rd = [P] + ([self.hp_col] if 'bias' in kw else [])
        kb.op(A, lambda: nc.scalar.activation(out=pc(o), in_=pc(a), func=func, **kw), reads=rd, writes=[P])

    kb.dma('sp', pc(P_ARE), self.s5_a_reT[l], P, writes=[P])
    kb.dma('sp', pc(P_AIM), self.s5_a_imT[l], P, writes=[P])
    kb.dma('sp', pc(P_LDT), self.s5_logdtT[l], P, writes=[P])
    kb.dma('sp', self.s5s0[:], self.s5_s0T[l], self.s5s0, writes=[self.s5s0])
    kb.dma('sp', self.s5d[:], self.s5_dT[l], self.s5d, writes=[self.s5d])
    kb.dma('sp', self.s5gb[:], self.s5_glu_bT[l], self.s5gb, writes=[self.s5gb])
    stg = self.wstage[self.wstage_i % 2]
    self.wstage_i += 1
    kb.dma('sp', stg[:, 0:512].rearrange("p (k d) -> p k d", k=2), self.s5_glu_w[l].rearrange("(k p) d -> p k d", p=128), stg, writes=[stg])
    kb.op(V, lambda: nc.vector.tensor_copy(out=self.gluw_bf[:].rearrange("p k d -> p (k d)"), in_=stg[:, 0:512]), reads=[stg], writes=[self.gluw_bf])
    act_f(P_DT, P_LDT, AF.Exp)
    dve_tt(P_T1, P_ARE, P_DT, ALU.mult)
    act_f(P_MAG, P_T1, AF.Exp)
    dve_tt(P_TH, P_AIM, P_DT, ALU.mult)
    dve_ts(P_K, P_TH, float(1 / (2 * np.pi)), MAGIC, ALU.mult, ALU.add)
    dve_ts(P_K, P_K, MAGIC, None, ALU.subtract)
    kb.op(V, lambda: nc.vector.scalar_tensor_tensor(out=pc(P_R), in0=pc(P_K), scalar=float(-2 * np.pi), in1=pc(P_TH), op0=ALU.mult, op1=ALU.add), reads=[P], writes=[P])
    act_f(P_SN, P_R, AF.Sin)
    act_f(P_T2, P_R, AF.Abs)
    act_f(P_CS, P_T2, AF.Sin, scale=-1.0, bias=self.hp_col[:])
    dve_tt(P_ABRE, P_MAG, P_CS, ALU.mult)
    dve_tt(P_ABIM, P_MAG, P_SN, ALU.mult)
    dve_tt(P_DEN, P_ARE, P_ARE, ALU.mult)
    dve_tt(P_T1, P_AIM, P_AIM, ALU.mult)
    dve_tt(P_DEN, P_DEN, P_T1, ALU.add)
    kb.op(V, lambda: nc.vector.reciprocal(out=pc(P_DEN), in_=pc(P_DEN)), reads=[P], writes=[P])
    dve_ts(P_NR, P_ABRE, -1.0, None, ALU.add)
    dve_tt(P_T1, P_NR, P_ARE, ALU.mult)
    dve_tt(P_T2, P_ABIM, P_AIM, ALU.mult)
    dve_tt(P_T1, P_T1, P_T2, ALU.add)
    dve_tt(P_CRE, P_T1, P_DEN, ALU.mult)
    dve_tt(P_T1, P_ABIM, P_ARE, ALU.mult)
    dve_tt(P_T2, P_NR, P_AIM, ALU.mult)
    dve_tt(P_T1, P_T1, P_T2, ALU.subtract)
    dve_tt(P_CIM, P_T1, P_DEN, ALU.mult)
    dve_ts(P_NCIM, P_CIM, -1.0, None, ALU.mult)

    if not hasattr(self, '_s5_bufs'):
        self._s5_bufs = self.s5_make_bufs()
    C0, C1 = self._s5_bufs
    for src_, al in self.s5_alias:
        kb.alias_begin(src_, al)
    kb.op(V, lambda: nc.vector.memset(self.otmp[0][:], 0.0), writes=[self.otmp[0]])
    kb.op(V, lambda: nc.vector.memset(self.otmp[1][:], 0.0), writes=[self.otmp[1]])
    kb.op(V, lambda: nc.vector.memset(self.s5fin[:], 0.0), writes=[self.s5fin])
    run_chains([self.s5_chain(l, 0, C0), self.s5_chain(l, 1, C1)], head_start=(self.cfg.get('s5_hs', 2), 0))
    for src_, al in self.s5_alias:
        kb.alias_end(src_, al)
    kb.dma('pool', self.new_s5T[l], self.s5fin[:], self.s5fin, reads=[self.s5fin])
    Lc = S5_L
    for g in range(NG):
        tsl = slice(g * 512, (g + 1) * 512)
        yg = [self.wkb[2], self.wkb[3]]
        for ct in range(2):
            y0, y1, uf = self.wk[0], self.wk[1], self.wk[2]
            rows = slice(ct * 128, (ct + 1) * 128)
            kb.dma('sp', y0[:], self.YS[0, rows, tsl], y0, reads=[self.rYS[0][ct][2 * g], self.rYS[0][ct][2 * g + 1]], writes=[y0])
            kb.dma('sp', y1[:], self.YS[1, rows, tsl], y1, reads=[self.rYS[1][ct][2 * g], self.rYS[1][ct][2 * g + 1]], writes=[y1])
            kb.dma('sp', uf[:], self.PT[C_S5U + ct * 128:C_S5U + (ct + 1) * 128, tsl], uf, reads=[self.rPT[g]], writes=[uf])
            kb.op(V, lambda: nc.vector.tensor_tensor(out=y0[:], in0=y0[:], in1=y1[:], op=ALU.add), reads=[y0, y1], writes=[y0])
            kb.op(V, lambda: nc.vector.scalar_tensor_tensor(out=y0[:], in0=uf[:], scalar=self.s5d[:, ct:ct + 1], in1=y0[:], op0=ALU.mult, op1=ALU.add),
                  reads=[uf, self.s5d, y0], writes=[y0])
            g2 = self.wk[3]
            kb.op(V, lambda: nc.vector.tensor_tensor(out=g2[:], in0=y0[:], in1=y0[:], op=ALU.mult), reads=[y0], writes=[g2])
            kb.op(V, lambda: nc.vector.tensor_scalar(out=g2[:], in0=g2[:], scalar1=0.044715, scalar2=1.0, op0=ALU.mult, op1=ALU.add), reads=[g2], writes=[g2])
            kb.op(V, lambda: nc.vector.tensor_tensor(out=g2[:], in0=g2[:], in1=y0[:], op=ALU.mult), reads=[g2, y0], writes=[g2])
            kb.op(A, lambda: nc.scalar.activation(out=g2[:], in_=g2[:], func=AF.Sigmoid, scale=1.5957691216057308), reads=[g2], writes=[g2])
            kb.op(V, lambda ct=ct: nc.vector.tensor_tensor(out=yg[ct][:], in0=g2[:], in1=y0[:], op=ALU.mult), reads=[g2, y0], writes=[yg[ct]])
        for dtile in range(2):
            acc = self.next_bank()
            for ct in range(2):
                kb.op('pe', lambda ct=ct: nc.tensor.matmul(acc[:], lhsT=self.gluw_bf[:, ct, dtile * 128:(dtile + 1) * 128], rhs=yg[ct][:],
                                                           start=(ct == 0), stop=(ct == 1)), reads=[self.gluw_bf, yg[ct]], writes=[acc])
            sig, zT = self.wk[4], self.wk[5]
            kb.op(A, lambda: nc.scalar.activation(out=sig[:], in_=acc[:], func=AF.Sigmoid, bias=self.s5gb[:, dtile:dtile + 1]),
                  reads=[acc, self.s5gb], writes=[sig])
            kb.dma('sp', zT[:], self.PT[C_S5Z + dtile * 128:C_S5Z + (dtile + 1) * 128, tsl], zT, reads=[self.rPT[g]], writes=[zT])
            kb.op(A, lambda: nc.scalar.activation(out=zT[:], in_=zT[:], func=AF.Silu), reads=[zT], writes=[zT])
            kb.op(V, lambda: nc.vector.tensor_tensor(out=sig[:], in0=sig[:], in1=zT[:], op=ALU.mult), reads=[sig, zT], writes=[sig])
            yb = self.ybuf[self.ybuf_i % 2]
            self.ybuf_i += 1
            kb.op(V, lambda: nc.vector.tensor_tensor(out=yb[:], in0=sig[:], in1=yg[dtile][:], op=ALU.mult), reads=[sig, yg[dtile]], writes=[yb])
            self.store_y(yb, 512 + dtile * 128, g)


class S5Bufs:
    pass


def s5_make_bufs(self):
    hv = lambda r: r[:].rearrange("p a b -> p (a b)")
    wf = hv(self.w_in_bf)
    wo = hv(self.w_out_bf)
    C0, C1 = S5Bufs(), S5Bufs()
    slot = lambda n_, nm: Res(nm, wf[:, n_ * 4096:(n_ + 1) * 4096].bitcast(F32))
    C0.TCr, C0.TSr = self.xt[0], self.xt[1]
    C0.BTr, C0.CTr = self.xn[0], self.xn[1]
    C0.buR, C0.tabR, C0.zR = slot(0, "s5c0bu"), slot(1, "s5c0tab"), slot(2, "s5c0z")
    C0.sbR = self.hT[0]
    C0.carry, C0.ctmp = self.s5carry2, self.s5ctmp
    C0.uf, C0.ubr, C0.yst = self.wk[0], self.wkb[0], self.pstage[0]
    C0.t1r, C0.t2r = self.wk[2], self.wk[3]
    C0.bank_lo = 0
    C0.CTnR = self.junk
    C1.TCr = Res("s5c1TC", wo[:, 0:2048].bitcast(F32))
    C1.TSr = Res("s5c1TS", wo[:, 2048:4096].bitcast(F32))
    C1.BTr = Res("s5c1BT", wo[:, 4096:5120])
    C1.CTr = Res("s5c1CT", wo[:, 5120:6144])
    C1.buR, C1.tabR = slot(3, "s5c1bu"), slot(4, "s5c1tab")
    C1.zR = Res("s5c1z", self.wstage[0][:, 0:2048])
    C1.sbR = self.hT[1]
    C1.carry, C1.ctmp = self.s5carry2b, self.s5ctmpb
    C1.uf, C1.ubr, C1.yst = self.wk[1], self.wkb[1], self.pstage[1]
    C1.t1r, C1.t2r = self.wk[4], self.wk[5]
    C1.bank_lo = 4
    C1.CTnR = Res("s5c1CTn", wo[:, 6144:7168])
    self.s5_alias = [(self.w_in_bf, [C0.buR, C0.tabR, C0.zR, C1.buR, C1.tabR]),
                     (self.w_out_bf, [C1.TCr, C1.TSr, C1.BTr, C1.CTr, C1.CTnR]),
                     (self.wstage[0], [C1.zR])]
    return C0, C1


def s5_chain(self, l, d, C):
    kb, nc = self.kb, self.nc
    P = self.s5par
    V, A = 'dve', 'act'
    Lc = S5_L
    bst = [0]

    def nb():
        b = self.bank[C.bank_lo + bst[0] % 4]
        bst[0] += 1
        return b
    TCr, TSr, BTr, CTr = C.TCr, C.TSr, C.BTr, C.CTr
    TCv = TCr[:, :].rearrange("p (s t) -> p s t", s=4)
    TSv = TSr[:, :].rearrange("p (s t) -> p s t", s=4)
    BBr, XXr = self.otmp[0], self.otmp[1]
    BB = BBr[:].rearrange("p (v r c) -> p v r c", v=4, r=2)
    XX = XXr[:].rearrange("p (v r c) -> p v r c", v=4, r=2)
    BT = BTr[:, :].rearrange("p (s r c) -> p s r c", s=4, r=2)
    CT = CTr[:, :].rearrange("p (s r c) -> p s r c", s=4, r=2)
    CTn = C.CTnR[:, 0:512].rearrange("p (s c) -> p s c", s=4)
    bur, tabr, zR, sbr = C.buR, C.tabR, C.zR, C.sbR
    bu = bur[:, :].rearrange("p (s r t) -> p s r t", s=4, r=2)
    z = zR[:, :].rearrange("p (s r t) -> p s r t", s=4, r=2)
    tab = tabr[:, :].rearrange("p (r s t) -> p r s t", r=2, s=4)
    sbv = sbr[:].rearrange("p a b -> p (a b)").rearrange("p (r s t) -> p r s t", r=4, s=4)
    carry, ctmp = C.carry, C.ctmp
    uf, ubr = C.uf, C.ubr
    seqs = [(0, 0, TS // Lc)] + [(1 + q, (TS + q * 256) // Lc, 1) for q in range(2)]
    for ct in range(2):
        for s4 in range(4):
            s = ct * 4 + s4
            col = d * 8 + s
            b4 = self.s5b
            kb.dma('sp', b4[:, 0, :], self.s5_b_re[l, d, 2 * s:2 * s + 2].rearrange("g n p -> (g n) p"), b4, writes=[b4])
            kb.dma('sp', b4[:, 1, :], self.s5_b_im[l, d, 2 * s:2 * s + 2].rearrange("g n p -> (g n) p"), b4, writes=[b4])
            kb.op(V, lambda col=col: nc.vector.tensor_scalar(out=b4[:, 2, :], in0=b4[:, 1, :], scalar1=P[:, P_NCIM, col:col + 1], scalar2=None, op0=ALU.mult),
                  reads=[b4, P], writes=[b4])
            kb.op(V, lambda col=col: nc.vector.tensor_scalar(out=b4[:, 3, :], in0=b4[:, 0, :], scalar1=P[:, P_CIM, col:col + 1], scalar2=None, op0=ALU.mult),
                  reads=[b4, P], writes=[b4])
            for gl in range(2):
                pl = slice(gl * 64, (gl + 1) * 64)
                c0 = 32 * s4 + 16 * gl
                kb.op(V, lambda pl=pl, c0=c0, col=col, s4=s4: nc.vector.scalar_tensor_tensor(
                    out=BB[pl, s4, 0, c0:c0 + 16], in0=b4[pl, 0, :], scalar=P[pl, P_CRE, col:col + 1], in1=b4[pl, 2, :],
                    op0=ALU.mult, op1=ALU.add), reads=[b4, P], writes=[BBr])
                kb.op(V, lambda pl=pl, c0=c0, col=col, s4=s4: nc.vector.scalar_tensor_tensor(
                    out=BB[pl, s4, 1, c0:c0 + 16], in0=b4[pl, 1, :], scalar=P[pl, P_CRE, col:col + 1], in1=b4[pl, 3, :],
                    op0=ALU.mult, op1=ALU.add), reads=[b4, P], writes=[BBr])
                for ri, csrc in enumerate((self.s5_c_re, self.s5_c_im)):
                    kb.dma('sp', XX[c0:c0 + 16, s4, ri, 64 * gl:64 * gl + 64], csrc[l, d, 2 * s + gl], XXr, writes=[XXr])
            for ri in range(2):
                pt = nb()
                kb.op('pe', lambda ri=ri, s4=s4, pt=pt: nc.tensor.transpose(pt[:, 0:128], BB[:, s4, ri, :], self.ident32[:]),
                      reads=[BBr, self.ident32], writes=[pt])
                kb.op(A, lambda ri=ri, s4=s4, pt=pt: nc.scalar.copy(out=BT[:, s4, ri, :], in_=pt[:, 0:128]), reads=[pt], writes=[BTr])
                pt2 = nb()
                kb.op('pe', lambda ri=ri, s4=s4, pt2=pt2: nc.tensor.transpose(pt2[:, 0:128], XX[:, s4, ri, :], self.ident32[:]),
                      reads=[XXr, self.ident32], writes=[pt2])
                kb.op(A, lambda ri=ri, s4=s4, pt2=pt2: nc.scalar.activation(out=CT[:, s4, ri, :], in_=pt2[:, 0:128], func=AF.Identity,
                                                                        scale=(1.0 if ri == 0 else -1.0)), reads=[pt2], writes=[CTr])
                if ri == 0:
                    kb.op(A, lambda s4=s4, pt2=pt2: nc.scalar.activation(out=CTn[:, s4, :], in_=pt2[:, 0:128], func=AF.Identity, scale=-1.0),
                          reads=[pt2], writes=[C.CTnR])
            yield
        c0_ = d * 8 + ct * 4
        kb.op(V, lambda: nc.vector.tensor_copy(out=TCv[:, :, 0:1], in_=P[:, P_CS, c0_:c0_ + 4].unsqueeze(2)), reads=[P], writes=[TCr])
        kb.op(V, lambda: nc.vector.tensor_copy(out=TSv[:, :, 0:1], in_=P[:, P_SN, c0_:c0_ + 4].unsqueeze(2)), reads=[P], writes=[TSr])
        t1r, t2r = C.t1r, C.t2r
        m = 1
        while m < Lc:
            cb = TCv[:, :, m - 1:m].to_broadcast([128, 4, m])
            sbc = TSv[:, :, m - 1:m].to_broadcast([128, 4, m])
            t1 = t1r[:, 0:4 * m].rearrange("p (s t) -> p s t", s=4)
            t2 = t2r[:, 0:4 * m].rearrange("p (s t) -> p s t", s=4)
            kb.op(V, lambda m=m, sbc=sbc, t1=t1: nc.vector.tensor_tensor(out=t1, in0=TSv[:, :, 0:m], in1=sbc, op=ALU.mult), reads=[TSr], writes=[t1r])
            kb.op(V, lambda m=m, sbc=sbc, t2=t2: nc.vector.tensor_tensor(out=t2, in0=TCv[:, :, 0:m], in1=sbc, op=ALU.mult), reads=[TCr, TSr], writes=[t2r])
            kb.op(V, lambda m=m, cb=cb: nc.vector.tensor_tensor(out=TCv[:, :, m:2 * m], in0=TCv[:, :, 0:m], in1=cb, op=ALU.mult), reads=[TCr], writes=[TCr])
            kb.op(V, lambda m=m, cb=cb: nc.vector.tensor_tensor(out=TSv[:, :, m:2 * m], in0=TSv[:, :, 0:m], in1=cb, op=ALU.mult), reads=[TSr, TCr], writes=[TSr])
            kb.op(V, lambda m=m, t1=t1: nc.vector.tensor_tensor(out=TCv[:, :, m:2 * m], in0=TCv[:, :, m:2 * m], in1=t1, op=ALU.subtract), reads=[TCr, t1r], writes=[TCr])
            kb.op(V, lambda m=m, t2=t2: nc.vector.tensor_tensor(out=TSv[:, :, m:2 * m], in0=TSv[:, :, m:2 * m], in1=t2, op=ALU.add), reads=[TSr, t2r], writes=[TSr])
            m *= 2
            yield
        yield
        magb = P[:, P_MAG, c0_:c0_ + 4]
        for (sid, c_first, c_n) in seqs:
            order = list(range(c_first, c_first + c_n))
            if d == 1:
                order = order[::-1]
            for oi, ck in enumerate(order):
                t0 = ck * Lc
                tsl = slice(t0, t0 + Lc)
                g512 = t0 // 512
                kb.dma('sp', uf[:, 0:Lc], self.PT[C_S5U + ct * 128:C_S5U + (ct + 1) * 128, tsl], uf, reads=[self.rPT[g512]], writes=[uf])
                kb.op(A, lambda: nc.scalar.copy(out=ubr[:, 0:Lc], in_=uf[:, 0:Lc]), reads=[uf], writes=[ubr])
                rhs_u = ubr[:, 0:Lc] if d == 0 else ubr[:, Lc - 1::-1]
                if oi == 0:
                    if sid == 0:
                        s0v = self.s5s0[:, 2 * c0_:2 * c0_ + 8].rearrange("p (s r) -> p r s", r=2)
                        kb.op(V, lambda s0v=s0v: nc.vector.tensor_copy(out=carry[:], in_=s0v), reads=[self.s5s0], writes=[carry])
                    else:
                        kb.op(V, lambda: nc.vector.memset(carry[:], 0.0), writes=[carry])
                for s4 in range(4):
                    pbu = nb()
                    for ri in range(2):
                        kb.op('pe', lambda ri=ri, s4=s4, pbu=pbu: nc.tensor.matmul(pbu[:, ri * Lc:(ri + 1) * Lc], lhsT=BT[:, s4, ri, :], rhs=rhs_u, start=True, stop=True),
                              reads=[BTr, ubr], writes=[pbu])
                    kb.op(A, lambda s4=s4, pbu=pbu: nc.scalar.copy(out=bu[:, s4, :, :].rearrange("p r t -> p (r t)"), in_=pbu[:, 0:2 * Lc]), reads=[pbu], writes=[bur])
                yield
                bre, bim = bu[:, :, 0, :], bu[:, :, 1, :]
                tA, tB = tab[:, 0, :, :], tab[:, 1, :, :]
                zt0, zt1 = z[:, :, 0, :], z[:, :, 1, :]
                kb.op(V, lambda: nc.vector.tensor_tensor(out=tA, in0=bre, in1=TCv[:, :, :], op=ALU.mult), reads=[bur, TCr], writes=[tabr])
                kb.op(V, lambda: nc.vector.tensor_tensor(out=zt0, in0=bim, in1=TSv[:, :, :], op=ALU.mult), reads=[bur, TSr], writes=[zR])
                kb.op(V, lambda: nc.vector.tensor_tensor(out=tB, in0=bim, in1=TCv[:, :, :], op=ALU.mult), reads=[bur, TCr], writes=[tabr])
                kb.op(V, lambda: nc.vector.tensor_tensor(out=zt1, in0=bre, in1=TSv[:, :, :], op=ALU.mult), reads=[bur, TSr], writes=[zR])
                kb.op(V, lambda: nc.vector.tensor_tensor(out=tA, in0=tA, in1=zt0, op=ALU.add), reads=[tabr, zR], writes=[tabr])
                kb.op(V, lambda: nc.vector.tensor_tensor(out=tB, in0=tB, in1=zt1, op=ALU.subtract), reads=[tabr, zR], writes=[tabr])
                yield
                for s4 in range(4):
                    for ri in range(2):
                        kb.op(V, lambda ri=ri, s4=s4: nc.vector.tensor_tensor_scan(
                            out=z[:, s4, ri, :], data0=magb[:, s4:s4 + 1].to_broadcast([128, Lc]), data1=tab[:, ri, s4, :],
                            initial=carry[:, ri, s4:s4 + 1], op0=ALU.mult, op1=ALU.add), reads=[P, tabr, carry], writes=[zR])
                yield
                zre, zim = z[:, :, 0, :], z[:, :, 1, :]
                zl_re, zl_im = z[:, :, 0, Lc - 1:Lc], z[:, :, 1, Lc - 1:Lc]
                cl, sl_ = TCv[:, :, Lc - 1:Lc], TSv[:, :, Lc - 1:Lc]
                c_re, c_im = carry[:, 0, :].unsqueeze(2), carry[:, 1, :].unsqueeze(2)
                kb.op(V, lambda: nc.vector.tensor_tensor(out=ctmp[:, 0, :].unsqueeze(2), in0=zl_im, in1=sl_, op=ALU.mult), reads=[zR, TSr], writes=[ctmp])
                kb.op(V, lambda: nc.vector.tensor_tensor(out=ctmp[:, 1, :].unsqueeze(2), in0=zl_re, in1=sl_, op=ALU.mult), reads=[zR, TSr], writes=[ctmp])
                kb.op(V, lambda: nc.vector.tensor_tensor(out=c_re, in0=zl_re, in1=cl, op=ALU.mult), reads=[zR, TCr], writes=[carry])
                kb.op(V, lambda: nc.vector.tensor_tensor(out=c_im, in0=zl_im, in1=cl, op=ALU.mult), reads=[zR, TCr], writes=[carry])
                kb.op(V, lambda: nc.vector.tensor_tensor(out=carry[:, 0, :], in0=carry[:, 0, :], in1=ctmp[:, 0, :], op=ALU.subtract), reads=[carry, ctmp], writes=[carry])
                kb.op(V, lambda: nc.vector.tensor_tensor(out=carry[:, 1, :], in0=carry[:, 1, :], in1=ctmp[:, 1, :], op=ALU.add), reads=[carry, ctmp], writes=[carry])
                if sid > 0 and oi == len(order) - 1:
                    for s4 in range(4):
                        s = ct * 4 + s4
                        fi = (((sid - 1) * 2 + d) * 8 + s) * 2
                        kb.op(V, lambda s4=s4, fi=fi: nc.vector.tensor_copy(out=self.s5fin[:, fi:fi + 2], in_=carry[:, :, s4]),
                              reads=[carry], writes=[self.s5fin])
                kb.op(V, lambda: nc.vector.tensor_tensor(out=sbv[:, 0, :, :], in0=zre, in1=TCv[:, :, :], op=ALU.mult), reads=[zR, TCr], writes=[sbr])
                kb.op(V, lambda: nc.vector.tensor_tensor(out=sbv[:, 1, :, :], in0=zim, in1=TSv[:, :, :], op=ALU.mult), reads=[zR, TSr], writes=[sbr])
                kb.op(V, lambda: nc.vector.tensor_tensor(out=sbv[:, 2, :, :], in0=zre, in1=TSv[:, :, :], op=ALU.mult), reads=[zR, TSr], writes=[sbr])
                kb.op(V, lambda: nc.vector.tensor_tensor(out=sbv[:, 3, :, :], in0=zim, in1=TCv[:, :, :], op=ALU.mult), reads=[zR, TCr], writes=[sbr])
                yield
                acc_y = nb()
                for s4 in range(4):
                    lhs = [CT[:, s4, 0, :], CTn[:, s4, :], CT[:, s4, 1, :], CT[:, s4, 1, :]]
                    for pi in range(4):
                        kb.op('pe', lambda pi=pi, s4=s4, lhs=lhs: nc.tensor.matmul(acc_y[:, 0:Lc], lhsT=lhs[pi], rhs=sbv[:, pi, s4, :],
                                                                                   start=(s4 == 0 and pi == 0), stop=(s4 == 3 and pi == 3)),
                              reads=[CTr, C.CTnR, sbr], writes=[acc_y])
                yst = C.yst
                if d == 0:
                    kb.op(A, lambda: nc.scalar.copy(out=yst[:, 0:Lc], in_=acc_y[:, 0:Lc]), reads=[acc_y], writes=[yst])
                else:
                    kb.op(V, lambda: nc.vector.tensor_copy(out=yst[:, 0:Lc], in_=acc_y[:, Lc - 1::-1]), reads=[acc_y], writes=[yst])
                kb.dma('sp', self.YS[d, ct * 128:(ct + 1) * 128, tsl], yst[:, 0:Lc], yst, reads=[yst], writes=[self.rYS[d][ct][ck]])
                yield


Model.s5_make_bufs = s5_make_bufs
Model.s5_chain = s5_chain


def make_s5_inputs(inputs, k):
    f32 = lambda a: np.ascontiguousarray(np.asarray(a, dtype=np.float32))
    L = DEPTH

    def par_T(a):
        a = f32(a).reshape(L, 2, 8, 2, 64)
        return np.ascontiguousarray(a.transpose(0, 3, 4, 1, 2).reshape(L, 128, 16))
    ldt = np.broadcast_to(f32(inputs['s5_log_dt'])[:, :, :, None], (L, 2, 16, 64))
    sb = k // 4
    s0 = f32(inputs['state_s5'])[sb]
    s0 = s0.reshape(L, 2, 2, 8, 2, 64)
    s0T = np.ascontiguousarray(s0.transpose(0, 4, 5, 1, 3, 2).reshape(L, 128, 32))
    return dict(
        s5_a_reT=par_T(inputs['s5_a_re']), s5_a_imT=par_T(inputs['s5_a_im']), s5_logdtT=par_T(ldt),
        s5_b_re=f32(inputs['s5_b_re']), s5_b_im=f32(inputs['s5_b_im']),
        s5_c_re=f32(inputs['s5_c_re']), s5_c_im=f32(inputs['s5_c_im']),
        s5_dT=np.ascontiguousarray(f32(inputs['s5_d']).reshape(L, 2, 128).transpose(0, 2, 1)),
        s5_glu_bT=np.ascontiguousarray(f32(inputs['s5_glu_b']).reshape(L, 2, 128).transpose(0, 2, 1)),
        s5_glu_w=f32(inputs['s5_glu_w']), s5_s0T=s0T)


def unpack_new_s5(t):
    L = t.shape[0]
    a = t.reshape(L, 2, 64, 2, 2, 8, 2)
    a = a.transpose(3, 0, 4, 6, 5, 1, 2)
    return np.ascontiguousarray(a.reshape(2, L, 2, 2, 16, 64))


Model.s5_init = _s5_init
Model.s5_setup = _s5_setup
Model.s5_phase = s5_phase


def make_dn_consts():
    i = np.arange(64)
    U = [(i[:, None] <= i[None, :]), (i[:, None] >= i[None, :])]
    c = {}
    ud_blk = np.zeros((2, 128, 128), np.float32)
    ud2 = np.zeros((2, 128, 64), np.float32)
    mincl = np.zeros((2, 128, 64), np.float32)
    mstrict = np.zeros((2, 128, 64), np.float32)
    for d in range(2):
        for cc in range(2):
            ud_blk[d, cc * 64:(cc + 1) * 64, cc * 64:(cc + 1) * 64] = U[d]
            ud2[d, cc * 64:(cc + 1) * 64] = U[d]
            mincl[d, cc * 64:(cc + 1) * 64] = U[d].T
            mstrict[d, cc * 64:(cc + 1) * 64] = U[d].T & (i[:, None] != i[None, :])
    blk = np.zeros((128, 128), np.float32)
    blk[0:64, 0:64] = 1
    blk[64:128, 64:128] = 1
    eye2 = np.concatenate([np.eye(64), np.eye(64)], 0).astype(np.float32)
    packed = np.concatenate([ud_blk[0], ud_blk[1], ud2[0], ud2[1], blk, mincl[0], mincl[1], mstrict[0], mstrict[1], eye2], axis=1)
    return dict(dn_consts=np.ascontiguousarray(packed.astype(np.float32)))


DNC_UDBLK, DNC_UD2, DNC_BLK, DNC_MINCL, DNC_MSTR, DNC_EYE2 = 0, 256, 384, 512, 640, 768


def _dn_init(self):
    kb, nc = self.kb, self.nc
    L = DEPTH
    dt_in = lambda name, shape, dt=F32: nc.dram_tensor(name, list(shape), dt, kind="ExternalInput").ap()
    self.dn_consts_in = dt_in("dn_consts", [128, 832])
    self.dn_convT = dt_in("dn_convT", [L, 128, 6, 5])
    self.dn_cols = dt_in("dn_cols", [L, 16, 2])
    self.dn_norm_g = dt_in("dn_norm_g", [L, 64])
    self.dn_s0 = dt_in("dn_s0", [L, 2, 4, 64, 64])
    self.new_dn = nc.dram_tensor("new_dn", [2, L, 2, 4, 64, 64], F32, kind="ExternalOutput").ap()
    mk = lambda name, shape, dt: nc.dram_tensor(name, list(shape), dt, kind="Internal").ap()
    self.QT = mk("QT_scr", [256, NT], BF16)
    self.KT = mk("KT_scr", [256, NT], BF16)
    self.QTOK = mk("QTOK_scr", [NT, 256], BF16)
    self.KTOK = mk("KTOK_scr", [NT, 256], BF16)
    self.VTOK = mk("VTOK_scr", [NT, 256], BF16)
    self.BG = mk("BG_scr", [NT, 16], F32)
    self.OF = mk("OF_scr", [2, NT, 256], F32)
    self.ON = mk("ON_scr", [NT, 256], BF16)
    self.rDN0 = [Res(f"DN0_{t}") for t in range(NTT)]
    self.rOF = [[Res(f"OF{d}_{t}") for t in range(NTT)] for d in range(2)]
    self.rON = [Res(f"ON{t}") for t in range(NTT)]
    self.dnc = kb.sb("dnc", [128, 832], F32)
    self.dn_convw = kb.sb("dn_convw", [128, 6, 5], F32)
    self.dn_colsb_full = kb.sb("dn_colsb", [128, 4], F32)
    self.dn_ngbc = kb.sb("dn_ngbc", [128, 64], F32)
    self.dn_S = kb.sb("dn_S", [128, 4, 64], F32)
    self.dn_sm = kb.sb("dn_sm", [128, 80], F32)
    self.dn_bg2 = kb.sb("dn_bg2", [128, 2, 16], F32)
    self.dn_bg2b = kb.sb("dn_bg2b", [128, 2, 16], F32)
    self.dn_bg2c = kb.sb("dn_bg2c", [128, 2, 16], F32)
    self.dn_bg2d = kb.sb("dn_bg2d", [128, 2, 16], F32)
    self.dn_smb = kb.sb("dn_smb", [128, 80], F32)
    self.dn_Sb = kb.sb("dn_Sb", [128, 4, 64], F32)
    self.dn_xb = kb.sb("dn_xb", [128, 520], BF16)
    self.dn_diag = kb.sb("dn_diag", [128, 5, 128], BF16)
    self.dn_bg = kb.sb("dn_bg", [128, 16], F32)
    self.dn_qkb = kb.sb("dn_qkb", [128, 2, 2], F32)


def _dn_setup(self):
    kb, nc = self.kb, self.nc
    kb.dma('sp', self.dnc[:], self.dn_consts_in, self.dnc, writes=[self.dnc])
    kb.op('dve', lambda: nc.vector.memset(self.dn_qkb[:, 0, :], 64.0 * EPS), writes=[self.dn_qkb])
    kb.op('dve', lambda: nc.vector.memset(self.dn_qkb[:, 1, :], EPS), writes=[self.dn_qkb])


def dn_segments():
    segs = [(g * 512, 512, g > 0, g < 7) for g in range(8)]
    segs += [(TS, 256, False, False), (TS + 256, 256, False, False)]
    return segs


def dn_pre(self, l):
    self.dn_colsb = Res('dn_colsb_v', None)
    self.dn_colsb = self.dn_colsb_full
    kb, nc = self.kb, self.nc
    V, A, G_ = 'dve', 'act', 'dve'
    kb.dma('sp', self.dn_convw[:], self.dn_convT[l], self.dn_convw, writes=[self.dn_convw])
    kb.dma('sp', self.dn_colsb[0:16, 0:2], self.dn_cols[l], self.dn_colsb, writes=[self.dn_colsb])
    kb.dma('sp', self.dn_ngbc[:], self.dn_norm_g[l:l + 1, :].partition_broadcast(128), self.dn_ngbc, writes=[self.dn_ngbc])
    kb.op(A, lambda: nc.scalar.activation(out=self.dn_colsb[0:16, 2:3], in_=self.dn_colsb[0:16, 1:2], func=AF.Exp),
          reads=[self.dn_colsb], writes=[self.dn_colsb])
    kb.op(V, lambda: nc.vector.tensor_scalar(out=self.dn_colsb[0:16, 3:4], in0=self.dn_colsb[0:16, 2:3], scalar1=-1.0, scalar2=None, op0=ALU.mult),
          reads=[self.dn_colsb], writes=[self.dn_colsb])
    blkones = self.dnc[:, DNC_BLK:DNC_BLK + 128]
    for (t0, ln, hl, hr) in dn_segments():
        ntile = ln // 128
        for ct6 in range(self.cfg.get('dn_nct', 6)):
            xin = self.hT[ct6 % 2]
            xinf = xin[:].rearrange("p a b -> p (a b)").bitcast(F32)
            if not hl:
                kb.op(V, lambda: nc.vector.memset(xinf[:, 0:2], 0.0), writes=[xin])
            if not hr:
                kb.op(V, lambda: nc.vector.memset(xinf[:, ln + 2:ln + 4], 0.0), writes=[xin])
            a0 = t0 - (2 if hl else 0)
            a1 = t0 + ln + (2 if hr else 0)
            o0 = 0 if hl else 2
            rows = slice(C_QKV + ct6 * 128, C_QKV + (ct6 + 1) * 128)
            rds = [self.rPT[min(NG - 1, max(0, t // 512))] for t in (a0, a1 - 1)]
            kb.dma('sp', xinf[:, o0:o0 + (a1 - a0)], self.PT[rows, a0:a1], xin, reads=rds, writes=[xin])
            xbf = self.wkb[2 + ct6 % 2]
            xbf2 = self.dn_xb
            kb.op(A, lambda: nc.scalar.copy(out=xbf2[:, 0:ln + 4], in_=xinf[:, 0:ln + 4]), reads=[xin], writes=[xbf2])
            dg = self.dn_diag
            for k in range(5):
                kb.op(V, lambda k=k: nc.vector.tensor_scalar(out=dg[:, k, :], in0=self.ident[:], scalar1=self.dn_convw[:, ct6, k:k + 1], scalar2=None, op0=ALU.mult),
                      reads=[self.ident, self.dn_convw], writes=[dg])
            cps = self.next_bank()
            for k in range(5):
                kb.op('pe', lambda k=k: nc.tensor.matmul(cps[:, 0:ln], lhsT=dg[:, k, :], rhs=xbf2[:, k:k + ln], start=(k == 0), stop=(k == 4)),
                      reads=[dg, xbf2], writes=[cps])
            acc = cps[:, 0:ln]
            if ct6 < 4:
                xs = self.wk[0]
                kb.op(A, lambda: nc.scalar.activation(out=xs[:, 0:ln], in_=acc, func=AF.Silu), reads=[cps], writes=[xs])
                sq = self.wk[1]
                kb.op(V, lambda: nc.vector.tensor_tensor(out=sq[:, 0:ln], in0=xs[:, 0:ln], in1=xs[:, 0:ln], op=ALU.mult), reads=[xs], writes=[sq])
                ps = self.next_bank()
                kb.op('pe', lambda: nc.tensor.matmul(ps[:, 0:ln], lhsT=blkones, rhs=sq[:, 0:ln], start=True, stop=True),
                      reads=[self.dnc, sq], writes=[ps])
                isq = (ct6 < 2)
                rn = self.wk[2]
                kb.op(A, lambda: nc.scalar.activation(out=rn[:, 0:ln], in_=ps[:, 0:ln], func=AF.Sqrt, scale=(64.0 if isq else 1.0),
                                                      bias=self.dn_qkb[:, (0 if isq else 1), 0:1]), reads=[ps, self.dn_qkb], writes=[rn])
                kb.op(V, lambda: nc.vector.reciprocal(out=rn[:, 0:ln], in_=rn[:, 0:ln]), reads=[rn], writes=[rn])
                xb = self.wkb[ct6 % 2]
                kb.op(G_, lambda: nc.vector.tensor_tensor(out=xb[:, 0:ln], in0=xs[:, 0:ln], in1=rn[:, 0:ln], op=ALU.mult), reads=[xs, rn], writes=[xb])
                dstT = self.QT if isq else self.KT
                r0 = (ct6 % 2) * 128
                kb.dma('pool', dstT[r0:r0 + 128, t0:t0 + ln], xb[:, 0:ln], xb, reads=[xb],
                       writes=[self.rDN0[t0 // 128 + j] for j in range(ntile)])
                dst_tok = self.QTOK if isq else self.KTOK
            else:
                xb = self.wkb[ct6 % 2]
                kb.op(A, lambda: nc.scalar.activation(out=xb[:, 0:ln], in_=acc, func=AF.Silu), reads=[cps], writes=[xb])
                dst_tok = self.VTOK
                r0 = (ct6 % 2) * 128
            for j in range(ntile):
                pT = self.bank[j % 2]
                pTv = pT[:].bitcast(BF16)
                kb.op('pe', lambda j=j: nc.tensor.transpose(pTv[:, 0:128], xb[:, j * 128:(j + 1) * 128], self.ident[:]),
                      reads=[xb, self.ident], writes=[pT])
                ts_ = self.tmstage[j % 2]
                kb.op(A if j % 2 == 0 else V, (lambda: nc.scalar.copy(out=ts_[:, 0:128], in_=pTv[:, 0:128])) if j % 2 == 0 else
                      (lambda: nc.vector.tensor_copy(out=ts_[:, 0:128], in_=pTv[:, 0:128])), reads=[pT], writes=[ts_])
                tt = t0 // 128 + j
                kb.dma('pool', dst_tok[tt * 128:(tt + 1) * 128, r0:r0 + 128], ts_[:, 0:128], ts_, reads=[ts_], writes=[self.rDN0[tt]])
        if self.cfg.get('dn_nobg', 0):
            continue
        bgin = self.wk[3]
        rds = [self.rPT[min(NG - 1, t // 512)] for t in (t0, t0 + ln - 1)]
        kb.dma('sp', bgin[0:16, 0:ln], self.PT[C_BETA:C_BETA + 16, t0:t0 + ln], bgin, reads=rds, writes=[bgin])
        sg, gg = self.wk[4], self.wk[5]
        kb.op(A, lambda: nc.scalar.activation(out=sg[0:16, 0:ln], in_=bgin[0:16, 0:ln], func=AF.Sigmoid), reads=[bgin], writes=[sg])
        kb.op(A, lambda: nc.scalar.activation(out=gg[0:16, 0:ln], in_=bgin[0:16, 0:ln], func=AF.Exp, bias=self.dn_colsb[0:16, 0:1]),
              reads=[bgin, self.dn_colsb], writes=[gg])
        kb.op(V, lambda: nc.vector.tensor_scalar(out=gg[0:16, 0:ln], in0=gg[0:16, 0:ln], scalar1=1.0, scalar2=None, op0=ALU.add), reads=[gg], writes=[gg])
        kb.op(A, lambda: nc.scalar.activation(out=gg[0:16, 0:ln], in_=gg[0:16, 0:ln], func=AF.Ln), reads=[gg], writes=[gg])
        kb.op(V, lambda: nc.vector.tensor_scalar(out=gg[0:16, 0:ln], in0=gg[0:16, 0:ln], scalar1=self.dn_colsb[0:16, 3:4], scalar2=None, op0=ALU.mult),
              reads=[gg, self.dn_colsb], writes=[gg])
        for j in range(ntile):
            ps = self.next_bank()
            kb.op('pe', lambda j=j: nc.tensor.matmul(ps[:, 0:16], lhsT=sg[0:16, j * 128:(j + 1) * 128], rhs=self.ident32[0:16, 0:16], start=True, stop=True),
                  reads=[sg, self.ident32], writes=[ps])
            kb.op('pe', lambda j=j: nc.tensor.matmul(ps[:, 16:32], lhsT=gg[0:16, j * 128:(j + 1) * 128], rhs=self.ident32[0:16, 0:16], start=True, stop=True),
                  reads=[gg, self.ident32], writes=[ps])
            bgt = self.dn_bg
            kb.op(V, lambda: nc.vector.tensor_copy(out=bgt[:, 0:8], in_=ps[:, 0:8]), reads=[ps], writes=[bgt])
            kb.op(V, lambda: nc.vector.tensor_copy(out=bgt[:, 8:16], in_=ps[:, 24:32]), reads=[ps], writes=[bgt])
            tt = t0 // 128 + j
            kb.dma('pool', self.BG[tt * 128:(tt + 1) * 128, :], bgt[:], bgt, reads=[bgt], writes=[self.rDN0[tt]])


def dn_chain(self, l, d, B):
    kb, nc = self.kb, self.nc
    bstate = [0]

    def nb():
        b = self.bank[B.bank_lo + bstate[0] % 4]
        bstate[0] += 1
        return b
    V, A, G_ = 'dve', 'act', 'dve'
    dnc = self.dnc
    Ud = dnc[0:64, DNC_UDBLK + d * 128:DNC_UDBLK + d * 128 + 64]
    ud2 = dnc[0:64, DNC_UD2 + d * 64:DNC_UD2 + (d + 1) * 64]
    ones = dnc[0:64, DNC_BLK:DNC_BLK + 64]
    mincl = dnc[0:64, DNC_MINCL + d * 64:DNC_MINCL + (d + 1) * 64]
    mstr = dnc[0:64, DNC_MSTR + d * 64:DNC_MSTR + (d + 1) * 64]
    eye = dnc[0:64, DNC_EYE2:DNC_EYE2 + 64]
    id64 = self.ident32[0:64, 0:64]
    bc8 = lambda ap: ap.unsqueeze(1).to_broadcast([64, 8, 64])
    bcj = lambda ap: ap.unsqueeze(2).to_broadcast([64, 8, 64])
    v8 = lambda ap: ap.rearrange("p (b j) -> p b j", b=8)
    S = B.S
    sm = B.sm
    KT4 = self.KT.rearrange("(h k) t -> k h t", k=64)
    QT4 = self.QT.rearrange("(h k) t -> k h t", k=64)
    w = B.w
    F = lambda r: r[0:64, :]
    xt0, xt1, ot0, ot1 = self.xt[0], self.xt[1], self.otmp[0], self.otmp[1]
    seqs = [(0, 0, 32), (1, 32, 2), (2, 34, 2)]
    items = []
    for (sid, tt0, ntl) in seqs:
        order = list(range(tt0, tt0 + ntl))
        if d == 1:
            order = order[::-1]
        for n_, tt in enumerate(order):
            items.append((sid, tt, n_ == 0, n_ == len(order) - 1))
    tokv = lambda dr, tsl: dr[tsl, :].rearrange("(c p) x -> p c x", p=64)
    c3 = lambda r: r[0:64, :].rearrange("p (c x) -> p c x", c=2)

    def issue_loads(n_):
        tt = items[n_][1]
        I = B.inp[n_ % 2]
        tsl = slice(tt * 128, (tt + 1) * 128)
        kb.dma('sp', I.kT, KT4[:, :, tsl], I.kTr, reads=[self.rDN0[tt]], writes=[I.kTr])
        kb.dma('sp', I.qT, QT4[:, :, tsl], I.qTr, reads=[self.rDN0[tt]], writes=[I.qTr])
        kb.dma('sp', c3(I.tkr), tokv(self.KTOK, tsl), I.tkr, reads=[self.rDN0[tt]], writes=[I.tkr])
        kb.dma('sp', c3(I.tvr), tokv(self.VTOK, tsl), I.tvr, reads=[self.rDN0[tt]], writes=[I.tvr])
        kb.dma('sp', c3(I.tqr), tokv(self.QTOK, tsl), I.tqr, reads=[self.rDN0[tt]], writes=[I.tqr])
        kb.dma('sp', I.bg2[0:64, :, :], self.BG[tsl, :].rearrange("(c p) x -> p c x", p=64), I.bg2, reads=[self.rDN0[tt]], writes=[I.bg2])

    issue_loads(0)
    for n_, (sid, tt, first, last) in enumerate(items):
        if True:
            if first:
                if sid == 0:
                    kb.dma('sp', S[0:64], self.dn_s0[l, d].rearrange("h k v -> k h v"), S, writes=[S])
                else:
                    kb.op(V, lambda: nc.vector.memset(S[0:64], 0.0), writes=[S])
            if n_ + 1 < len(items):
                issue_loads(n_ + 1)
            I = B.inp[n_ % 2]
            tsl = slice(tt * 128, (tt + 1) * 128)
            kTr, qTr, kT, qT = I.kTr, I.qTr, I.kT, I.qT
            tkr, tvr, tqr, kpr = I.tkr, I.tvr, I.tqr, B.kpr
            bg2 = I.bg2
            g8 = bg2[0:64, :, 8 + 4 * d:12 + 4 * d]
            b8 = bg2[0:64, :, 4 * d:4 * d + 4]
            g8j = g8.unsqueeze(3).to_broadcast([64, 2, 4, 64])
            b8j = b8.unsqueeze(3).to_broadcast([64, 2, 4, 64])
            v24 = lambda ap: ap.rearrange("p (c h j) -> p c h j", c=2, h=4)
            ktok8, vtok8, qtok8 = v8(F(tkr)), v8(F(tvr)), v8(F(tqr))
            yield
            X1, X2 = nb(), nb()
            for c in range(2):
                cs = slice(c * 64, (c + 1) * 64)
                for h in range(4):
                    bs = slice((c * 4 + h) * 64, (c * 4 + h + 1) * 64)
                    kb.op('pe', lambda cs=cs, h=h, bs=bs: nc.tensor.matmul(X1[0:64, bs], lhsT=kT[:, h, cs], rhs=kT[:, h, cs], start=True, stop=True),
                          reads=[kTr], writes=[X1])
                    kb.op('pe', lambda cs=cs, h=h, bs=bs: nc.tensor.matmul(X2[0:64, bs], lhsT=qT[:, h, cs], rhs=kT[:, h, cs], start=True, stop=True),
                          reads=[kTr, qTr], writes=[X2])
            yield
            G4b, NGU = F(w[0]), F(w[1])
            kb.op(V, lambda: nc.vector.tensor_copy(out=v24(G4b), in_=g8j), reads=[bg2], writes=[w[0]])
            kb.op(V, lambda: nc.vector.scalar_tensor_tensor(out=v8(NGU), in0=bc8(ud2), scalar=-1.0, in1=v8(G4b), op0=ALU.mult, op1=ALU.mult),
                  reads=[dnc, w[0]], writes=[w[1]])
            Y = nb()
            kb.op('pe', lambda: nc.tensor.matmul(Y[0:64, :], lhsT=Ud, rhs=G4b, start=True, stop=False), reads=[dnc, w[0]], writes=[Y])
            kb.op('pe', lambda: nc.tensor.matmul(Y[0:64, :], lhsT=ones, rhs=NGU, start=False, stop=True), reads=[dnc, w[1]], writes=[Y])
            kb.op(V, lambda: nc.vector.tensor_copy(out=sm[0:64, 72:80].rearrange("p (c h) -> p c h", c=2), in_=g8), reads=[bg2], writes=[sm])
            Z = nb()
            kb.op('pe', lambda: nc.tensor.matmul(Z[0:64, 0:8], lhsT=Ud, rhs=sm[0:64, 72:80], start=True, stop=True), reads=[dnc, sm], writes=[Z])
            kb.op('pe', lambda: nc.tensor.matmul(Z[0:64, 8:16], lhsT=ones, rhs=sm[0:64, 72:80], start=True, stop=True), reads=[dnc, sm], writes=[Z])
            kb.op(V, lambda: nc.vector.tensor_copy(out=sm[0:64, 0:8], in_=Z[0:64, 0:8]), reads=[Z], writes=[sm])
            kb.op(V, lambda: nc.vector.tensor_tensor(out=sm[0:64, 8:16], in0=Z[0:64, 8:16], in1=sm[0:64, 0:8], op=ALU.subtract), reads=[Z, sm], writes=[sm])
            kb.op(V, lambda: nc.vector.tensor_copy(out=sm[0:64, 16:24], in_=Z[0:64, 8:16]), reads=[Z], writes=[sm])
            kb.op(A, lambda: nc.scalar.activation(out=sm[0:64, 24:48], in_=sm[0:64, 0:24], func=AF.Exp), reads=[sm], writes=[sm])
            egc, ekl, egl = sm[0:64, 24:32], sm[0:64, 32:40], sm[0:64, 40:48]
            kb.op(V, lambda: nc.vector.tensor_tensor(out=sm[0:64, 48:56].rearrange("p (c h) -> p c h", c=2), in0=b8, in1=egc.rearrange("p (c h) -> p c h", c=2), op=ALU.mult),
                  reads=[bg2, sm], writes=[sm])
            be = sm[0:64, 48:56]
            kb.op(V, lambda: nc.vector.tensor_copy(out=sm[0:64, 56:64].rearrange("p (c h) -> p c h", c=2), in_=b8), reads=[bg2], writes=[sm])
            bt = sm[0:64, 56:64]
            yield
            dec = F(w[2])
            kb.op(V, lambda: nc.vector.tensor_scalar(out=dec, in0=Y[0:64, :], scalar1=0.0, scalar2=None, op0=ALU.min), reads=[Y], writes=[w[2]])
            kb.op(A, lambda: nc.scalar.activation(out=dec, in_=dec, func=AF.Exp), reads=[w[2]], writes=[w[2]])
            kb.op(G_, lambda: nc.vector.tensor_tensor(out=v8(dec), in0=v8(dec), in1=bc8(mincl), op=ALU.mult), reads=[w[2], dnc], writes=[w[2]])
            yield
            aqk, Nm = F(w[3]), F(w[4])
            kb.op(V, lambda: nc.vector.tensor_tensor(out=aqk, in0=X2[0:64, :], in1=dec, op=ALU.mult), reads=[X2, w[2]], writes=[w[3]])
            kb.op(V, lambda: nc.vector.tensor_tensor(out=Nm, in0=X1[0:64, :], in1=dec, op=ALU.mult), reads=[X1, w[2]], writes=[w[4]])
            kb.op(V, lambda: nc.vector.scalar_tensor_tensor(out=v8(Nm), in0=v8(Nm), scalar=-1.0, in1=bcj(bt), op0=ALU.mult, op1=ALU.mult),
                  reads=[w[4], sm], writes=[w[4]])
            kb.op(G_, lambda: nc.vector.tensor_tensor(out=v8(Nm), in0=v8(Nm), in1=bc8(mstr), op=ALU.mult), reads=[w[4], dnc], writes=[w[4]])
            yield
            T1a, T1b = nb(), nb()
            for b in range(8):
                bs = slice(b * 64, (b + 1) * 64)
                kb.op('pe', lambda bs=bs: nc.tensor.matmul(T1a[0:64, bs], lhsT=Nm[:, bs], rhs=id64, start=True, stop=True),
                      reads=[w[4], self.ident32], writes=[T1a])
                kb.op('pe', lambda bs=bs: nc.tensor.matmul(T1b[0:64, bs], lhsT=aqk[:, bs], rhs=id64, start=True, stop=True),
                      reads=[w[3], self.ident32], writes=[T1b])
            Pr, PTr = B.Pr, B.PTr
            ubr, aqr, Rr_, WUr_ = B.ubR, B.aqR, B.RR, B.WUR
            Ub, aqkTb = ubr[0:64, :], aqr[0:64, :]
            kb.op(A, lambda: nc.scalar.copy(out=F(PTr[0]), in_=T1a[0:64, :]), reads=[T1a], writes=[PTr[0]])
            kb.op(A, lambda: nc.scalar.copy(out=F(Pr[0]), in_=Nm), reads=[w[4]], writes=[Pr[0]])
            kb.op(A, lambda: nc.scalar.copy(out=aqkTb, in_=T1b[0:64, :]), reads=[T1b], writes=[aqr])
            kb.op(V, lambda: nc.vector.tensor_tensor(out=v8(Ub), in0=v8(T1a[0:64, :]), in1=bc8(eye), op=ALU.add), reads=[T1a, dnc], writes=[ubr])
            yield
            cur = 0
            for kstep in range(0, 6):
                Pc, PTc = F(Pr[cur]), F(PTr[cur])
                nxt = 1 - cur
                if kstep >= 1:
                    ubk = nb()
                    for b in range(8):
                        bs = slice(b * 64, (b + 1) * 64)
                        kb.op('pe', lambda bs=bs, Pc=Pc, ubk=ubk: nc.tensor.matmul(ubk[0:64, bs], lhsT=Pc[:, bs], rhs=Ub[:, bs], start=True, stop=True),
                              reads=[Pr[cur], ubr], writes=[ubk])
                if kstep < 5:
                    sq1, sq2 = nb(), nb()
                    for b in range(8):
                        bs = slice(b * 64, (b + 1) * 64)
                        kb.op('pe', lambda bs=bs, Pc=Pc, PTc=PTc, sq1=sq1: nc.tensor.matmul(sq1[0:64, bs], lhsT=PTc[:, bs], rhs=Pc[:, bs], start=True, stop=True),
                              reads=[Pr[cur], PTr[cur]], writes=[sq1])
                        kb.op('pe', lambda bs=bs, Pc=Pc, PTc=PTc, sq2=sq2: nc.tensor.matmul(sq2[0:64, bs], lhsT=Pc[:, bs], rhs=PTc[:, bs], start=True, stop=True),
                              reads=[Pr[cur], PTr[cur]], writes=[sq2])
                if kstep >= 1:
                    kb.op(V, lambda ubk=ubk: nc.vector.tensor_tensor(out=Ub, in0=ubk[0:64, :], in1=Ub, op=ALU.add), reads=[ubk, ubr], writes=[ubr])
                if kstep < 5:
                    kb.op(A, lambda nxt=nxt, sq1=sq1: nc.scalar.copy(out=F(Pr[nxt]), in_=sq1[0:64, :]), reads=[sq1], writes=[Pr[nxt]])
                    kb.op(A if kstep % 2 == 0 else V, (lambda nxt=nxt, sq2=sq2: nc.scalar.copy(out=F(PTr[nxt]), in_=sq2[0:64, :])) if kstep % 2 == 0 else
                          (lambda nxt=nxt, sq2=sq2: nc.vector.tensor_copy(out=F(PTr[nxt]), in_=sq2[0:64, :])), reads=[sq2], writes=[PTr[nxt]])
                    cur = nxt
                yield
            Rf = Rr_[0:64, :]
            R8 = Rf.rearrange("p (b x) -> p b x", b=8)
            kb.op(G_, lambda: nc.vector.tensor_tensor(out=R8[:, :, 0:64], in0=ktok8, in1=bcj(be), op=ALU.mult), reads=[tkr, sm], writes=[Rr_])
            kb.op(G_, lambda: nc.vector.tensor_tensor(out=R8[:, :, 64:128], in0=vtok8, in1=bcj(bt), op=ALU.mult), reads=[tvr, sm], writes=[Rr_])
            W1 = [nb(), nb()]
            for b in range(8):
                kb.op('pe', lambda b=b: nc.tensor.matmul(W1[b // 4][0:64, (b % 4) * 128:(b % 4 + 1) * 128], lhsT=Ub[:, b * 64:(b + 1) * 64], rhs=R8[:, b, :], start=True, stop=True),
                      reads=[ubr, Rr_], writes=[W1[b // 4]])
            WUf = WUr_[0:64, :]
            WU8 = WUf.rearrange("p (b x) -> p b x", b=8)
            for c in range(2):
                kb.op(A, lambda c=c: nc.scalar.copy(out=WUf[:, c * 512:(c + 1) * 512], in_=W1[c][0:64, :]), reads=[W1[c]], writes=[WUr_])
            yield
            W2 = [nb(), nb()]
            for b in range(8):
                kb.op('pe', lambda b=b: nc.tensor.matmul(W2[b // 4][0:64, (b % 4) * 128:(b % 4 + 1) * 128], lhsT=aqkTb[:, b * 64:(b + 1) * 64], rhs=WU8[:, b, :], start=True, stop=True),
                      reads=[aqr, WUr_], writes=[W2[b // 4]])
            Mm, ccs = F(w[5]), F(w[0])
            kb.op(G_, lambda: nc.vector.tensor_tensor(out=v8(Mm), in0=qtok8, in1=bcj(egc), op=ALU.mult), reads=[tqr, sm], writes=[w[5]])
            for c in range(2):
                W2v = W2[c][0:64, :].rearrange("p (h x) -> p h x", h=4)
                Mc = Mm[:, c * 256:(c + 1) * 256].rearrange("p (h j) -> p h j", h=4)
                Cc = ccs[:, c * 256:(c + 1) * 256].rearrange("p (h j) -> p h j", h=4)
                kb.op(V, lambda Mc=Mc, W2v=W2v: nc.vector.tensor_tensor(out=Mc, in0=Mc, in1=W2v[:, :, 0:64], op=ALU.subtract), reads=[w[5], W2[c]], writes=[w[5]])
                kb.op(A, lambda Cc=Cc, W2v=W2v: nc.scalar.copy(out=Cc, in_=W2v[:, :, 64:128]), reads=[W2[c]], writes=[w[0]])
            yield
            T2 = nb()
            for b in range(8):
                bs = slice(b * 64, (b + 1) * 64)
                kb.op('pe', lambda bs=bs: nc.tensor.matmul(T2[0:64, bs], lhsT=Mm[:, bs], rhs=id64, start=True, stop=True),
                      reads=[w[5], self.ident32], writes=[T2])
            MTs = F(w[1])
            kb.op(A, lambda: nc.scalar.copy(out=MTs, in_=T2[0:64, :]), reads=[T2], writes=[w[1]])
            yield
            Kp = F(kpr)
            Kp8 = v8(Kp)
            kb.op(G_, lambda: nc.vector.tensor_tensor(out=Kp8, in0=ktok8, in1=bcj(ekl), op=ALU.mult), reads=[tkr, sm], writes=[kpr])
            G1 = nb()
            for b in range(8):
                bs = slice(b * 64, (b + 1) * 64)
                kb.op('pe', lambda b=b, bs=bs: nc.tensor.matmul(G1[0:64, bs], lhsT=WU8[:, b, 0:64], rhs=Kp8[:, b, :], start=True, stop=True),
                      reads=[WUr_, kpr], writes=[G1])
            Gneg = F(w[2])
            kb.op(A, lambda: nc.scalar.activation(out=Gneg, in_=G1[0:64, :], func=AF.Identity, scale=-1.0), reads=[G1], writes=[w[2]])
            yield
            O1 = nb()
            corder = (0, 1) if d == 0 else (1, 0)
            for c in corder:
                for h in range(4):
                    b = c * 4 + h
                    bs = slice(b * 64, (b + 1) * 64)
                    kb.op('pe', lambda h=h, bs=bs: nc.tensor.matmul(O1[0:64, bs], lhsT=MTs[:, bs], rhs=S[0:64, h, :], start=True, stop=True),
                          reads=[w[1], S], writes=[O1])
                SS = nb()
                for h in range(4):
                    b = c * 4 + h
                    bs = slice(b * 64, (b + 1) * 64)
                    kb.op('pe', lambda h=h, b=b: nc.tensor.matmul(SS[0:64, h * 64:(h + 1) * 64], lhsT=Kp8[:, b, :], rhs=WU8[:, b, 64:128], start=True, stop=False),
                          reads=[kpr, WUr_], writes=[SS])
                    kb.op('pe', lambda h=h, bs=bs: nc.tensor.matmul(SS[0:64, h * 64:(h + 1) * 64], lhsT=Gneg[:, bs], rhs=S[0:64, h, :], start=False, stop=True),
                          reads=[w[2], S], writes=[SS])
                eglc = egl[:, c * 4:(c + 1) * 4].unsqueeze(2).to_broadcast([64, 4, 64])
                kb.op(V, lambda eglc=eglc: nc.vector.tensor_tensor(out=S[0:64], in0=S[0:64], in1=eglc, op=ALU.mult), reads=[S, sm], writes=[S])
                kb.op(V, lambda: nc.vector.tensor_tensor(out=S[0:64].rearrange("p h v -> p (h v)"), in0=S[0:64].rearrange("p h v -> p (h v)"), in1=SS[0:64, 0:256], op=ALU.add),
                      reads=[S, SS], writes=[S])
            od = F(w[3])
            kb.op(V, lambda: nc.vector.tensor_tensor(out=od, in0=O1[0:64, :], in1=ccs, op=ALU.add), reads=[O1, w[0]], writes=[w[3]])
            OFv = self.OF[d, tsl, :].rearrange("(c p) x -> p c x", p=64)
            od3 = od.rearrange("p (c x) -> p c x", c=2)
            kb.dma('sp', OFv, od3, w[3], reads=[w[3]], writes=[self.rOF[d][tt]])
            yield
            if last and sid > 0:
                kb.dma('sp', self.new_dn[sid - 1, l, d].rearrange("h k v -> k h v"), S[0:64], S, reads=[S])
                yield


def dn_final(self, l):
    kb, nc = self.kb, self.nc
    V, A = 'dve', 'act'
    v4 = lambda ap: ap.rearrange("p (h j) -> p h j", h=4)
    sm = self.dn_sm
    for g in range(NG):
        tsl = slice(g * 512, (g + 1) * 512)
        oT = [self.wkb[0], self.wkb[1]]
        for j in range(4):
            tt = g * 4 + j
            o0, o1 = self.pstage[0], self.pstage[1]
            kb.dma('sp', o0[:, 0:256], self.OF[0, tt * 128:(tt + 1) * 128, :], o0, reads=[self.rOF[0][tt]], writes=[o0])
            kb.dma('sp', o1[:, 0:256], self.OF[1, tt * 128:(tt + 1) * 128, :], o1, reads=[self.rOF[1][tt]], writes=[o1])
            kb.op(V, lambda: nc.vector.tensor_tensor(out=o0[:, 0:256], in0=o0[:, 0:256], in1=o1[:, 0:256], op=ALU.add), reads=[o0, o1], writes=[o0])
            kb.op(V, lambda: nc.vector.tensor_tensor(out=o0[:, 256:512], in0=o0[:, 0:256], in1=o0[:, 0:256], op=ALU.mult), reads=[o0], writes=[o0])
            kb.op(V, lambda: nc.vector.tensor_reduce(out=sm[:, 64:68], in_=v4(o0[:, 256:512]), axis=AX.X, op=ALU.add), reads=[o0], writes=[sm])
            kb.op(A, lambda: nc.scalar.activation(out=sm[:, 64:68], in_=sm[:, 64:68], func=AF.Sqrt, scale=1.0 / 64, bias=self.eps_col[:]),
                  reads=[sm, self.eps_col], writes=[sm])
            kb.op(V, lambda: nc.vector.reciprocal(out=sm[:, 64:68], in_=sm[:, 64:68]), reads=[sm], writes=[sm])
            kb.op(V, lambda: nc.vector.tensor_tensor(out=v4(o0[:, 0:256]), in0=v4(o0[:, 0:256]), in1=sm[:, 64:68].unsqueeze(2).to_broadcast([128, 4, 64]), op=ALU.mult),
                  reads=[o0, sm], writes=[o0])
            ont = self.wkb[2 + j % 2]
            kb.op(V, lambda: nc.vector.tensor_tensor(out=v4(ont[:, 0:256]), in0=v4(o0[:, 0:256]), in1=self.dn_ngbc[:].unsqueeze(1).to_broadcast([128, 4, 64]), op=ALU.mult),
                  reads=[o0, self.dn_ngbc], writes=[ont])
            pT = self.bank[j % 2]
            pTv = pT[:].bitcast(BF16)
            for ft in range(2):
                kb.op('pe', lambda ft=ft: nc.tensor.transpose(pTv[:, ft * 128:(ft + 1) * 128], ont[:, ft * 128:(ft + 1) * 128], self.ident[:]),
                      reads=[ont, self.ident], writes=[pT])
            for ft in range(2):
                kb.op('act', lambda ft=ft: nc.scalar.copy(out=oT[ft][:, j * 128:(j + 1) * 128], in_=pTv[:, ft * 128:(ft + 1) * 128]),
                      reads=[pT], writes=[oT[ft]])
        for ft in range(2):
            zT = self.wk[ft]
            kb.dma('sp', zT[:], self.PT[C_DNZ + ft * 128:C_DNZ + (ft + 1) * 128, tsl], zT, reads=[self.rPT[g]], writes=[zT])
            kb.op('act', lambda: nc.scalar.activation(out=zT[:], in_=zT[:], func=AF.Silu), reads=[zT], writes=[zT])
            yb = self.ybuf[self.ybuf_i % 2]
            self.ybuf_i += 1
            kb.op('dve', lambda: nc.vector.tensor_tensor(out=yb[:], in0=zT[:], in1=oT[ft][:], op=ALU.mult), reads=[zT, oT[ft]], writes=[yb])
            self.store_y(yb, 256 + ft * 128, g)


class DNBufs:
    pass


def dn_make_bufs(self):
    kb = self.kb
    hv = lambda r: r[:].rearrange("p a b -> p (a b)")
    B0, B1 = DNBufs(), DNBufs()
    B0.bank_lo, B1.bank_lo = 0, 4
    B0.kTr, B0.qTr = self.yT[0], self.yT[1]
    B0.kT, B0.qT = B0.kTr[0:64, 0:4, :], B0.qTr[0:64, 0:4, :]
    B0.tkr, B0.tvr, B0.tqr, B0.kpr = self.wkb
    B0.bg2, B0.sm, B0.S = self.dn_bg2, self.dn_sm, self.dn_S
    B0.w = list(self.wk)
    B0.Pr, B0.PTr = [self.ybuf[0], self.ybuf[1]], [self.tmstage[0], self.tmstage[1]]
    h0 = hv(self.hT[0])
    B0.ubR, B0.aqR = Res("dn_ub0", h0[:, 0:512]), Res("dn_aq0", h0[:, 512:1024])
    B0.RR, B0.WUR = Res("dn_R0", h0[:, 1024:2048]), Res("dn_WU0", h0[:, 2048:3072])
    self.dn_alias0 = (self.hT[0], [B0.ubR, B0.aqR, B0.RR, B0.WUR])
    wf = self.w_in_bf[:].rearrange("p a b -> p (a b)")
    off = [0]

    def take(n_bf16, name, f32=False):
        ap = wf[:, off[0]:off[0] + n_bf16]
        off[0] += n_bf16
        return Res(name, ap.bitcast(F32) if f32 else ap)
    B1.kTr, B1.qTr = take(512, "dn1_kT"), take(512, "dn1_qT")
    B1.kT = B1.kTr[0:64, :].rearrange("p (h t) -> p h t", h=4)
    B1.qT = B1.qTr[0:64, :].rearrange("p (h t) -> p h t", h=4)
    B1.tkr, B1.tvr, B1.tqr, B1.kpr = [take(512, f"dn1_tk{i}") for i in range(4)]
    B1.Pr = [take(512, f"dn1_P{i}") for i in range(2)]
    B1.PTr = [take(512, f"dn1_PT{i}") for i in range(2)]
    B1.w = [take(1024, f"dn1_w{i}", f32=True) for i in range(6)]
    B1.bg2, B1.sm, B1.S = self.dn_bg2b, self.dn_smb, self.dn_Sb
    h1 = hv(self.hT[1])
    B1.ubR, B1.aqR = Res("dn_ub1", h1[:, 0:512]), Res("dn_aq1", h1[:, 512:1024])
    B1.RR, B1.WUR = Res("dn_R1", h1[:, 1024:2048]), Res("dn_WU1", h1[:, 2048:3072])
    def mkset(kTr, qTr, tkr, tvr, tqr, bg2, yT_like):
        I = DNBufs()
        I.kTr, I.qTr, I.tkr, I.tvr, I.tqr, I.bg2 = kTr, qTr, tkr, tvr, tqr, bg2
        if yT_like:
            I.kT, I.qT = kTr[0:64, 0:4, :], qTr[0:64, 0:4, :]
        else:
            I.kT = kTr[0:64, :].rearrange("p (h t) -> p h t", h=4)
            I.qT = qTr[0:64, :].rearrange("p (h t) -> p h t", h=4)
        return I
    ex0 = [take(512, f"dn0x{i}") for i in range(5)]
    ex1 = [take(512, f"dn1x{i}") for i in range(5)]
    B0.inp = [mkset(B0.kTr, B0.qTr, B0.tkr, B0.tvr, B0.tqr, self.dn_bg2, True),
              mkset(ex0[0], ex0[1], ex0[2], ex0[3], ex0[4], self.dn_bg2c, False)]
    B1.inp = [mkset(B1.kTr, B1.qTr, B1.tkr, B1.tvr, B1.tqr, self.dn_bg2b, False),
              mkset(ex1[0], ex1[1], ex1[2], ex1[3], ex1[4], self.dn_bg2d, False)]
    scr1 = [B1.kTr, B1.qTr, B1.tkr, B1.tvr, B1.tqr, B1.kpr] + B1.Pr + B1.PTr + B1.w + ex0 + ex1
    self.dn_alias1 = [(self.w_in_bf, scr1), (self.hT[1], [B1.ubR, B1.aqR, B1.RR, B1.WUR])]
    return B0, B1


def run_chains(gens, head_start=()):
    active = list(gens)
    for g, n in zip(list(active), head_start):
        for _ in range(n):
            try:
                next(g)
            except StopIteration:
                active.remove(g)
                break
    while active:
        for g in list(active):
            try:
                next(g)
            except StopIteration:
                active.remove(g)


def dn_phase(self, l):
    kb = self.kb
    self.dn_pre(l)
    if not hasattr(self, '_dn_bufs'):
        self._dn_bufs = self.dn_make_bufs()
    B0, B1 = self._dn_bufs
    kb.alias_begin(*self.dn_alias0)
    for src_, al in self.dn_alias1:
        kb.alias_begin(src_, al)
    if self.cfg.get("dn_seq", 0):
        run_chains([self.dn_chain(l, 0, B0)])
        run_chains([self.dn_chain(l, 1, B1)])
    else:
        run_chains([self.dn_chain(l, 0, B0), self.dn_chain(l, 1, B1)], head_start=(self.cfg.get('dn_hs', 11), 0))
    kb.alias_end(*self.dn_alias0)
    for src_, al in self.dn_alias1:
        kb.alias_end(src_, al)
    self.dn_final(l)


def make_dn_inputs(inputs, k):
    f32 = lambda a: np.ascontiguousarray(np.asarray(a, dtype=np.float32))
    L = DEPTH
    conv = f32(inputs['dn_conv'])
    convT = np.ascontiguousarray(conv.reshape(L, 5, 6, 128).transpose(0, 3, 2, 1))
    cols = np.zeros((L, 16, 2), np.float32)
    cols[:, 8:16, 0] = f32(inputs['dn_dt_bias']).reshape(L, 8)
    cols[:, 8:16, 1] = f32(inputs['dn_a_log']).reshape(L, 8)
    sb = k // 4
    return dict(dn_convT=convT, dn_cols=cols, dn_norm_g=f32(inputs['dn_norm_g']),
                dn_s0=f32(inputs['state_delta'])[sb])


Model.dn_init = _dn_init
Model.dn_setup = _dn_setup
Model.dn_pre = dn_pre
Model.dn_chain = dn_chain
Model.dn_make_bufs = dn_make_bufs
Model.dn_final = dn_final
Model.dn_phase = dn_phase


def kernel(**inputs):
    nc = build_nc({})
    maps = make_in_maps(inputs, 8)
    res = run_bass_kernel_spmd(nc, maps, core_ids=list(range(8)))
    rs = res.results
    L = DEPTH
    y_prompt = np.zeros((16, 256, D), np.float32)
    y_sample = np.zeros((2, TS, D), np.float32)
    new_dn = np.zeros((16, L, 2, 4, 64, 64), np.float32)
    new_s5 = np.zeros((16, L, 2, 2, 16, 64), np.float32)
    for k in range(8):
        r = rs[k]
        ya = np.asarray(r["y_all"], dtype=np.float32)
        y_prompt[2 * k] = ya[TS:TS + 256]
        y_prompt[2 * k + 1] = ya[TS + 256:TS + 512]
        if k % 4 == 0:
            y_sample[k // 4] = ya[0:TS]
        new_dn[2 * k:2 * k + 2] = np.asarray(r["new_dn"], dtype=np.float32)
        new_s5[2 * k:2 * k + 2] = unpack_new_s5(np.asarray(r["new_s5T"], dtype=np.float32))
    return (y_prompt, y_sample, new_dn, new_s5)
```

```python
from contextlib import ExitStack
import numpy as np
import ml_dtypes
import concourse.bass as bass
import concourse.mybir as mybir
from concourse.bass_utils import run_bass_kernel_spmd

F32 = mybir.dt.float32
BF16 = mybir.dt.bfloat16
AF = mybir.ActivationFunctionType
ALU = mybir.AluOpType
AX = mybir.AxisListType

SEM_LIMIT = 30000


class Sem:
    def __init__(self, handle, is_dma):
        self.h = handle
        self.count = 0
        self.is_dma = is_dma


class Res:
    def __init__(self, name, t=None):
        self.name = name
        self.t = t
        self.w = None
        self.r = {}
        self.dsem = None
        self.psum = False

    def __getitem__(self, key):
        return self.t[key]


class KB:
    def __init__(self, nc, stack):
        self.nc = nc
        self.stack = stack
        self.eng = {'pe': nc.tensor, 'act': nc.scalar, 'dve': nc.vector,
                    'pool': nc.gpsimd, 'sp': nc.sync}
        self.nsem = 0
        self.esem = {e: self.new_sem(e, False) for e in ['pe', 'act', 'dve', 'pool']}
        self.known = {e: {} for e in self.eng}
        self.ninstr = 0
        self.all_dma_sems = []

    def new_sem(self, name, is_dma):
        self.nsem += 1
        h = self.stack.enter_context(self.nc.semaphore(f"s{self.nsem}_{name}"))
        return Sem(h, is_dma)

    def sb(self, name, shape, dtype):
        t = self.stack.enter_context(self.nc.sbuf_tensor(name, list(shape), dtype))
        return Res(name, t)

    def ps(self, name, shape, dtype):
        t = self.stack.enter_context(self.nc.psum_tensor(name, list(shape), dtype))
        r = Res(name, t)
        r.psum = True
        return r

    def dram(self, name, shape, dtype, kind="Internal"):
        t = self.nc.dram_tensor(name, list(shape), dtype, kind=kind)
        return t

    def _waits(self, eng, reads, writes):
        waits = {}

        def need(t):
            if t is None:
                return
            sem, val = t
            if sem.is_dma:
                val = sem.count
            if waits.get(sem, 0) < val:
                waits[sem] = val

        for r in reads:
            need(r.w)
            if r.psum:
                for s, v in r.r.items():
                    need((s, v))
        for w in writes:
            need(w.w)
            for s, v in w.r.items():
                need((s, v))
        E = self.eng[eng]
        kn = self.known[eng]
        own = self.esem.get(eng)
        for sem, val in waits.items():
            if eng == 'pe' and sem is own:
                continue
            if kn.get(sem, 0) >= val:
                continue
            E.wait_ge(sem.h, val)
            kn[sem] = val

    def _commit(self, ticket, reads, writes):
        sem, val = ticket
        for r in reads:
            if r.r.get(sem, 0) < val:
                r.r[sem] = val
        for w in writes:
            w.w = ticket
            w.r = {}

    def op(self, eng, emit, reads=(), writes=()):
        self._waits(eng, reads, writes)
        ins = emit()
        sem = self.esem[eng]
        sem.count += 1
        ins.then_inc(sem.h, 1)
        self._commit((sem, sem.count), reads, writes)
        if sem.count >= SEM_LIMIT:
            self.esem[eng] = self.new_sem(eng, False)
        self.ninstr += 1

    def dma(self, q, out, in_, sbres, reads=(), writes=(), **kw):
        self._waits(q, reads, writes)
        if sbres.dsem is None:
            sbres.dsem = {}
        qk = 'sw' if q == 'pool' else 'hw'
        if qk not in sbres.dsem or sbres.dsem[qk].count >= SEM_LIMIT:
            sbres.dsem[qk] = self.new_sem("d" + qk + "_" + sbres.name, True)
            self.all_dma_sems.append(sbres.dsem[qk])
        sem = sbres.dsem[qk]
        ins = self.eng[q].dma_start(out=out, in_=in_, **kw)
        sem.count += 16
        ins.then_inc(sem.h, 16)
        self._commit((sem, sem.count), reads, writes)
        self.ninstr += 1

    def alias_begin(self, src, aliases):
        for a in aliases:
            a.w = src.w
            a.r = dict(src.r)

    def alias_end(self, src, aliases):
        for a in aliases:
            items = list(a.r.items()) + ([a.w] if a.w is not None else [])
            for s, v in items:
                if src.r.get(s, 0) < v:
                    src.r[s] = v

    def finish(self, q='sp'):
        E = self.eng[q]
        for sem in self.all_dma_sems:
            if sem.count > 0 and self.known[q].get(sem, 0) < sem.count:
                E.wait_ge(sem.h, sem.count)
        for e, sem in self.esem.items():
            if sem.count > 0:
                E.wait_ge(sem.h, sem.count)


D = 1024
DEPTH = 4
TS = 4096
TP = 512
NT = TS + TP
NTT = NT // 128
NG = NT // 512
DIN = 2576
C_POOL_U, C_POOL_Z, C_QKV, C_DNZ, C_BETA, C_ALPHA, C_S5U, C_S5Z, C_FTU, C_FTZ = \
    0, 256, 512, 1280, 1536, 1544, 1552, 1808, 2064, 2320
COL_TILES = [(c, 128) for c in range(0, 1536, 128)] + [(1536, 16)] + \
            [(c, 128) for c in range(1552, 2576, 128)]
EPS = 1e-6


def cond_of_tile(tt):
    return 0 if tt * 128 < TS else 1


class Model:
    def __init__(self, nc, st, cfg):
        self.nc = nc
        self.cfg = cfg
        self.depth = cfg.get('depth', DEPTH)
        kb = self.kb = KB(nc, st)
        dt_in = lambda name, shape, dt=F32: nc.dram_tensor(name, list(shape), dt, kind="ExternalInput").ap()
        L = DEPTH
        self.x_in = dt_in("x_all", [NT, D])
        self.condT = dt_in("condT", [128, 8, 2])
        self.w_ada = dt_in("w_ada", [L, D, 3 * D])
        self.b_adaT = dt_in("b_adaT", [L, 128, 24])
        self.b_ada = dt_in("b_ada", [L, 3 * D])
        self.norm_gT = dt_in("norm_gT", [L, 128, 8])
        self.w_in = dt_in("w_in", [L, D, DIN])
        self.w_out = dt_in("w_out", [L, D, D])
        self.final_g = dt_in("final_g", [1, D])
        self.ident_in = dt_in("ident", [128, 128], BF16)
        self.y_out = nc.dram_tensor("y_all", [NT, D], F32, kind="ExternalOutput").ap()
        self.X = nc.dram_tensor("X_scr", [NT, D], F32, kind="Internal").ap()
        self.PT = nc.dram_tensor("PT_scr", [DIN, NT], F32, kind="Internal").ap()
        self.PTOK = nc.dram_tensor("PTOK_scr", [NT, 256], BF16, kind="Internal").ap()
        self.YT = nc.dram_tensor("YT_scr", [D, NT], BF16, kind="Internal").ap()
        self.rX = [Res(f"X{t}") for t in range(NTT)]
        self.rPT = [Res(f"PT{g}") for g in range(NG)]
        self.rPTOK = [Res(f"PTOK{t}") for t in range(NTT)]
        self.rYT = [[Res(f"YT{k}_{g}") for g in range(NG)] for k in range(8)]
        self.ident = kb.sb("ident_sb", [128, 128], BF16)
        self.w_in_bf = kb.sb("w_in_bf", [128, 8, DIN], BF16)
        self.w_out_bf = kb.sb("w_out_bf", [128, 8, D], BF16)
        self.wstage = [kb.sb(f"wstage{i}", [128, DIN], F32) for i in range(2)]
        self.wstage_i = 0
        self.scT = kb.sb("scT", [128, 8, 2], F32)
        self.scbc = kb.sb("scbc", [128, 8, 2, 128], F32)
        self.b_col = kb.sb("b_col", [128, 24], F32)
        self.ada_acc = kb.sb("ada_acc", [128, 32], F32)
        self.ng_col = kb.sb("ng_col", [128, 8], F32)
        self.sh_col = kb.sb("sh_col", [128, 8, 2], F32)
        self.gs_col = kb.sb("gs_col", [128, 8, 2], F32)
        self.gate_bc = [kb.sb(f"gate_bc{c}", [128, D], F32) for c in range(2)]
        self.fg_bc = kb.sb("fg_bc", [128, D], F32)
        self.eps_col = kb.sb("eps_col", [128, 1], F32)
        self.xt = [kb.sb(f"xt{i}", [128, D], F32) for i in range(2)]
        self.xn = [kb.sb(f"xn{i}", [128, D], BF16) for i in range(2)]
        self.ss = [kb.sb(f"ss{i}", [128, 1], F32) for i in range(2)]
        self.rstd = [kb.sb(f"rstd{i}", [128, 1], F32) for i in range(2)]
        self.junk = kb.sb("junk", [128, D], BF16)
        self.hT = [kb.sb(f"hT{i}", [128, 8, 512], BF16) for i in range(2)]
        self.ss4 = [kb.sb(f"ss4_{i}", [128, 1], F32) for i in range(4)]
        self.rstd4 = [kb.sb(f"rstd4_{i}", [128, 1], F32) for i in range(4)]
        self.pstage = [kb.sb(f"pstage{i}", [128, 512], F32) for i in range(3)]
        self.pstage_i = 0
        self.tmstage = [kb.sb(f"tmstage{i}", [128, 512], BF16) for i in range(2)]
        self.yT = [kb.sb(f"yT{i}", [128, 8, 128], BF16) for i in range(2)]
        self.otmp = [kb.sb(f"otmp{i}", [128, D], F32) for i in range(2)]
        self.xt4 = [self.xt[0], self.xt[1], self.otmp[0], self.otmp[1]]
        self.bank = [kb.ps(f"bank{i}", [128, 512], F32) for i in range(8)]
        self.bank_i = 0
        wo_ = self.w_out_bf[:].rearrange("p a b -> p (a b)")
        self.xn4 = [Res(f"xn4_{i}", wo_[:, i * 1024:(i + 1) * 1024]) for i in range(8)]
        self.mixer_init()
        self.ft_init()
        self.s5_init()
        self.dn_init()

    def next_bank(self, lo=2, hi=8):
        b = self.bank[lo + self.bank_i % (hi - lo)]
        self.bank_i += 1
        return b

    def setup(self):
        kb, nc = self.kb, self.nc
        kb.dma('sp', self.ident[:], self.ident_in, self.ident, writes=[self.ident])
        kb.dma('sp', self.scT[:], self.condT, self.scT, writes=[self.scT])
        kb.dma('sp', self.fg_bc[:], self.final_g.partition_broadcast(128), self.fg_bc, writes=[self.fg_bc])
        kb.op('dve', lambda: nc.vector.memset(self.eps_col[:], EPS), writes=[self.eps_col])
        kb.op('act', lambda: nc.scalar.activation(out=self.scT[:], in_=self.scT[:], func=AF.Silu),
              reads=[self.scT], writes=[self.scT])
        for kt in range(8):
            for ci in range(2):
                kb.op('dve', lambda kt=kt, ci=ci: nc.vector.tensor_copy(
                    out=self.scbc[:, kt, ci, :],
                    in_=self.scT[:, kt, ci:ci + 1].to_broadcast([128, 128])),
                    reads=[self.scT], writes=[self.scbc])

    def load_weight_rows(self, dst_views, src_ap, ncols, cast_eng='pool'):
        kb, nc = self.kb, self.nc
        stg = self.wstage[self.wstage_i % 2]
        self.wstage_i += 1
        kb.dma('sp', stg[:, 0:ncols], src_ap, stg, writes=[stg])
        return stg

    def adaln(self, l):
        kb, nc = self.kb, self.nc
        kb.dma('sp', self.b_col[:], self.b_adaT[l], self.b_col, writes=[self.b_col])
        kb.dma('sp', self.ng_col[:], self.norm_gT[l], self.ng_col, writes=[self.ng_col])
        for ci in range(2):
            kb.dma('sp', self.gate_bc[ci][:], self.b_ada[l:l + 1, 2 * D:3 * D].partition_broadcast(128),
                   self.gate_bc[ci], writes=[self.gate_bc[ci]])
        pcol = self.bank[2]
        pg = [self.bank[3], self.bank[4], self.bank[5], self.bank[6]]
        for kt in range(8):
            stg = self.wstage[self.wstage_i % 2]
            self.wstage_i += 1
            kb.dma('sp', stg[:, 0:2 * D], self.w_ada[l, kt * 128:(kt + 1) * 128, 0:2 * D], stg, writes=[stg])
            stg2 = self.wstage[self.wstage_i % 2]
            self.wstage_i += 1
            kb.dma('sp', stg2[:, 0:D], self.w_ada[l, kt * 128:(kt + 1) * 128, 2 * D:3 * D], stg2, writes=[stg2])
            for mt in range(16):
                kb.op('pe', lambda mt=mt, kt=kt, stg=stg: nc.tensor.matmul(
                    pcol[:, mt * 2:mt * 2 + 2], lhsT=stg[:, mt * 128:(mt + 1) * 128], rhs=self.scT[:, kt, :],
                    start=True, stop=True), reads=[stg, self.scT], writes=[pcol])
            if kt == 0:
                kb.op('dve', lambda: nc.vector.tensor_copy(out=self.ada_acc[:], in_=pcol[:, 0:32]),
                      reads=[pcol], writes=[self.ada_acc])
            else:
                kb.op('dve', lambda: nc.vector.tensor_tensor(out=self.ada_acc[:], in0=pcol[:, 0:32], in1=self.ada_acc[:], op=ALU.add),
                      reads=[pcol, self.ada_acc], writes=[self.ada_acc])
            for ci in range(2):
                for hf in range(2):
                    kb.op('pe', lambda ci=ci, hf=hf, kt=kt, stg2=stg2: nc.tensor.matmul(
                        pg[ci * 2 + hf][:], lhsT=self.scbc[:, kt, ci, :],
                        rhs=stg2[:, hf * 512:(hf + 1) * 512],
                        start=(kt == 0), stop=(kt == 7)), reads=[stg2, self.scbc], writes=[pg[ci * 2 + hf]])
        pc3 = self.ada_acc[:].rearrange("p (m c) -> p m c", c=2)
        for ci in range(2):
            kb.op('dve', lambda ci=ci: nc.vector.tensor_tensor(
                out=self.sh_col[:, :, ci], in0=pc3[:, 0:8, ci], in1=self.b_col[:, 0:8], op=ALU.add),
                reads=[self.ada_acc, self.b_col], writes=[self.sh_col])
            kb.op('dve', lambda ci=ci: nc.vector.tensor_tensor(
                out=self.gs_col[:, :, ci], in0=pc3[:, 8:16, ci], in1=self.b_col[:, 8:16], op=ALU.add),
                reads=[self.ada_acc, self.b_col], writes=[self.gs_col])
            kb.op('dve', lambda ci=ci: nc.vector.scalar_tensor_tensor(
                out=self.gs_col[:, :, ci], in0=self.gs_col[:, :, ci], scalar=1.0, in1=self.ng_col[:],
                op0=ALU.add, op1=ALU.mult), reads=[self.gs_col, self.ng_col], writes=[self.gs_col])
            for hf in range(2):
                kb.op('dve', lambda ci=ci, hf=hf: nc.vector.tensor_tensor(
                    out=self.gate_bc[ci][:, hf * 512:(hf + 1) * 512], in0=pg[ci * 2 + hf][:],
                    in1=self.gate_bc[ci][:, hf * 512:(hf + 1) * 512], op=ALU.add),
                    reads=[pg[ci * 2 + hf], self.gate_bc[ci]], writes=[self.gate_bc[ci]])

    def load_layer_weights(self, l):
        kb, nc = self.kb, self.nc
        for kt in range(8):
            stg = self.wstage[self.wstage_i % 2]
            self.wstage_i += 1
            kb.dma('sp', stg[:, 0:DIN], self.w_in[l, kt * 128:(kt + 1) * 128, :], stg, writes=[stg])
            kb.op('act' if kt % 2 == 0 else 'dve', (lambda kt=kt, stg=stg: nc.scalar.copy(out=self.w_in_bf[:, kt, :], in_=stg[:, 0:DIN])) if kt % 2 == 0 else
                  (lambda kt=kt, stg=stg: nc.vector.tensor_copy(out=self.w_in_bf[:, kt, :], in_=stg[:, 0:DIN])),
                  reads=[stg], writes=[self.w_in_bf])

    def load_w_out(self, l):
        kb, nc = self.kb, self.nc
        for kt in range(8):
            stg = self.wstage[self.wstage_i % 2]
            self.wstage_i += 1
            kb.dma('sp', stg[:, 0:D], self.w_out[l, kt * 128:(kt + 1) * 128, :], stg, writes=[stg])
            kb.op('act' if kt % 2 == 0 else 'dve', (lambda kt=kt, stg=stg: nc.scalar.copy(out=self.w_out_bf[:, kt, :], in_=stg[:, 0:D])) if kt % 2 == 0 else
                  (lambda kt=kt, stg=stg: nc.vector.tensor_copy(out=self.w_out_bf[:, kt, :], in_=stg[:, 0:D])),
                  reads=[stg], writes=[self.w_out_bf])

    def phase_a(self, l):
        kb, nc = self.kb, self.nc
        src = self.x_in if l == 0 else self.X

        def stage1(g):
            for j in range(4):
                tt = g * 4 + j
                xt, xn, ss, rstd = self.xt4[j], self.xn4[(g % 2) * 4 + j], self.ss4[j], self.rstd4[j]
                kb.dma('sp', xt[:], src[tt * 128:(tt + 1) * 128, :], xt,
                       reads=([self.rX[tt]] if l > 0 else []), writes=[xt])
                kb.op('act', lambda xt=xt, ss=ss: nc.scalar.activation(out=self.junk[:], in_=xt[:], func=AF.Square, accum_out=ss[:]),
                      reads=[xt], writes=[self.junk, ss])
                kb.op('act', lambda ss=ss, rstd=rstd: nc.scalar.activation(out=rstd[:], in_=ss[:], func=AF.Sqrt, scale=1.0 / D, bias=self.eps_col[:]),
                      reads=[ss, self.eps_col], writes=[rstd])
                kb.op('dve', lambda rstd=rstd: nc.vector.reciprocal(out=rstd[:], in_=rstd[:]), reads=[rstd], writes=[rstd])
                kb.op('dve', lambda xt=xt, xn=xn, rstd=rstd: nc.vector.tensor_scalar(out=xn[:], in0=xt[:], scalar1=rstd[:], scalar2=None, op0=ALU.mult),
                      reads=[xt, rstd], writes=[xn])

        def stage2(g):
            hT = self.hT[g % 2]
            for j in range(4):
                tt = g * 4 + j
                ci = cond_of_tile(tt)
                xn = self.xn4[(g % 2) * 4 + j]
                pA, pB = self.bank[0], self.bank[1]
                pAv, pBv = pA[:].bitcast(BF16), pB[:].bitcast(BF16)
                for kt in range(8):
                    pv, pr = (pAv, pA) if kt < 4 else (pBv, pB)
                    kk = kt % 4
                    kb.op('pe', lambda kt=kt, kk=kk, pv=pv, xn=xn: nc.tensor.transpose(pv[:, kk * 128:(kk + 1) * 128], xn[:, kt * 128:(kt + 1) * 128], self.ident[:]),
                          reads=[xn, self.ident], writes=[pr])
                for kt in range(4):
                    kb.op('act', lambda kt=kt, j=j, ci=ci: nc.scalar.activation(
                        out=hT[:, kt, j * 128:(j + 1) * 128], in_=pAv[:, kt * 128:(kt + 1) * 128], func=AF.Identity,
                        scale=self.gs_col[:, kt, ci:ci + 1], bias=self.sh_col[:, kt, ci:ci + 1]),
                        reads=[pA, self.gs_col, self.sh_col], writes=[hT])
                pB3 = pBv[:, 0:512].rearrange("p (k t) -> p k t", k=4)
                ho = hT[:, 4:8, j * 128:(j + 1) * 128]
                kb.op('dve', lambda pB3=pB3, ho=ho, ci=ci: nc.vector.tensor_tensor(out=ho, in0=pB3, in1=self.gs_col[:, 4:8, ci:ci + 1].to_broadcast([128, 4, 128]), op=ALU.mult),
                      reads=[pB, self.gs_col], writes=[hT])
                kb.op('dve', lambda ho=ho, ci=ci: nc.vector.tensor_tensor(out=ho, in0=ho, in1=self.sh_col[:, 4:8, ci:ci + 1].to_broadcast([128, 4, 128]), op=ALU.add),
                      reads=[hT, self.sh_col], writes=[hT])

        def proj(g):
            hT = self.hT[g % 2]
            for i, (c0, cw) in enumerate(COL_TILES):
                acc = self.next_bank()
                for kt in range(8):
                    kb.op('pe', lambda kt=kt, acc=acc, c0=c0, cw=cw: nc.tensor.matmul(acc[0:cw, :], lhsT=self.w_in_bf[:, kt, c0:c0 + cw], rhs=hT[:, kt, :],
                                                                                      start=(kt == 0), stop=(kt == 7)),
                          reads=[self.w_in_bf, hT], writes=[acc])
                slots = self.pstage + self.wk
                stg = slots[self.pstage_i % len(slots)]
                self.pstage_i += 1
                if i % 2 == 0:
                    kb.op('act', lambda stg=stg, acc=acc, cw=cw: nc.scalar.copy(out=stg[0:cw, :], in_=acc[0:cw, :]), reads=[acc], writes=[stg])
                else:
                    kb.op('dve', lambda stg=stg, acc=acc, cw=cw: nc.vector.tensor_copy(out=stg[0:cw, :], in_=acc[0:cw, :]), reads=[acc], writes=[stg])
                kb.dma('act', self.PT[c0:c0 + cw, g * 512:(g + 1) * 512], stg[0:cw, :], stg,
                       reads=[stg], writes=[self.rPT[g]])
            for j in range(4):
                tt = g * 4 + j
                acc = self.next_bank()
                for kt in range(8):
                    kb.op('pe', lambda kt=kt, acc=acc, j=j: nc.tensor.matmul(acc[:, 0:256], lhsT=hT[:, kt, j * 128:(j + 1) * 128], rhs=self.w_in_bf[:, kt, C_POOL_U:C_POOL_U + 256],
                                                                        start=(kt == 0), stop=(kt == 7)),
                          reads=[self.w_in_bf, hT], writes=[acc])
                stg = self.tmstage[tt % 2]
                kb.op('dve', lambda stg=stg, acc=acc: nc.vector.tensor_copy(out=stg[:, 0:256], in_=acc[:, 0:256]), reads=[acc], writes=[stg])
                kb.dma('act', self.PTOK[tt * 128:(tt + 1) * 128, :], stg[:, 0:256], stg, reads=[stg], writes=[self.rPTOK[tt]])

        stage1(0)
        stage2(0)
        for g in range(NG):
            if g + 1 < NG:
                stage1(g + 1)
            proj(g)
            if g + 1 < NG:
                stage2(g + 1)

    def phase_b_stub(self, l):
        kb, nc = self.kb, self.nc
        for g in range(NG):
            for kt in range(8):
                stg = self.pstage[self.pstage_i % 3]
                self.pstage_i += 1
                kb.dma('sp', stg[:], self.PT[kt * 128:(kt + 1) * 128, g * 512:(g + 1) * 512], stg,
                       reads=[self.rPT[g]], writes=[stg])
                o = self.tmstage[kt % 2]
                kb.op('act', lambda: nc.scalar.copy(out=o[:], in_=stg[:]), reads=[stg], writes=[o])
                kb.dma('act', self.YT[kt * 128:(kt + 1) * 128, g * 512:(g + 1) * 512], o[:], o,
                       reads=[o], writes=[self.rYT[kt][g]])

    def phase_c(self, l):
        kb, nc = self.kb, self.nc
        last = (l == self.depth - 1)
        src = self.x_in if l == 0 else self.X
        YTv = self.YT.rearrange("(k p) t -> p k t", p=128)
        for tt in range(NTT):
            ci = cond_of_tile(tt)
            s = tt % 2
            xt, yT, ot = self.xt[s], self.yT[s], self.otmp[s]
            kb.dma('sp', yT[:], YTv[:, :, tt * 128:(tt + 1) * 128], yT, reads=[self.rYT[k][tt // 4] for k in range(8)], writes=[yT])
            kb.dma('sp', xt[:], src[tt * 128:(tt + 1) * 128, :], xt,
                   reads=([self.rX[tt]] if l > 0 else []), writes=[xt])
            for hf in range(2):
                acc = self.next_bank()
                for kt in range(8):
                    kb.op('pe', lambda kt=kt: nc.tensor.matmul(acc[:], lhsT=yT[:, kt, :], rhs=self.w_out_bf[:, kt, hf * 512:(hf + 1) * 512],
                                                               start=(kt == 0), stop=(kt == 7)),
                          reads=[yT, self.w_out_bf], writes=[acc])
                sl = slice(hf * 512, (hf + 1) * 512)
                kb.op('dve', lambda: nc.vector.tensor_tensor(out=ot[:, sl], in0=acc[:], in1=self.gate_bc[ci][:, sl], op=ALU.mult),
                      reads=[acc, self.gate_bc[ci]], writes=[ot])
                kb.op('dve', lambda: nc.vector.tensor_tensor(out=xt[:, sl], in0=ot[:, sl], in1=xt[:, sl], op=ALU.add),
                      reads=[ot, xt], writes=[xt])
            if not last:
                kb.dma('act', self.X[tt * 128:(tt + 1) * 128, :], xt[:], xt, reads=[xt], writes=[self.rX[tt]])
            else:
                ss, rstd = self.ss[s], self.rstd[s]
                kb.op('act', lambda: nc.scalar.activation(out=self.junk[:], in_=xt[:], func=AF.Square, accum_out=ss[:]),
                      reads=[xt], writes=[self.junk, ss])
                kb.op('act', lambda: nc.scalar.activation(out=rstd[:], in_=ss[:], func=AF.Sqrt, scale=1.0 / D, bias=self.eps_col[:]),
                      reads=[ss, self.eps_col], writes=[rstd])
                kb.op('dve', lambda: nc.vector.reciprocal(out=rstd[:], in_=rstd[:]), reads=[rstd], writes=[rstd])
                kb.op('dve', lambda: nc.vector.scalar_tensor_tensor(out=ot[:], in0=xt[:], scalar=rstd[:], in1=self.fg_bc[:],
                                                                    op0=ALU.mult, op1=ALU.mult),
                      reads=[xt, rstd, self.fg_bc], writes=[ot])
                kb.dma('act', self.y_out[tt * 128:(tt + 1) * 128, :], ot[:], ot, reads=[ot])

    def build(self):
        self.setup()
        self.mixer_setup()
        self.ft_setup()
        self.s5_setup()
        self.dn_setup()
        for l in range(self.depth):
            self.adaln(l)
            self.load_layer_weights(l)
            self.kb.alias_begin(self.w_out_bf, self.xn4)
            self.phase_a(l)
            self.kb.alias_end(self.w_out_bf, self.xn4)
            if self.cfg.get('stub', False):
                self.phase_b_stub(l)
            else:
                self.phase_b(l)
            self.load_w_out(l)
            self.phase_c(l)
        self.kb.finish('sp')


def build_nc(cfg):
    nc = bass.Bass("TRN2", target_bir_lowering=False)
    with ExitStack() as st:
        m = Model(nc, st, cfg)
        m.build()
        print("instructions:", m.kb.ninstr, "sems:", m.kb.nsem)
    return nc


def make_in_maps(inputs, ncores=8):
    f32 = lambda a: np.ascontiguousarray(np.asarray(a, dtype=np.float32))
    x_prompt = f32(inputs['x_prompt']); x_sample = f32(inputs['x_sample'])
    c = f32(inputs['c']); c_ctx = f32(inputs['c_ctx'])
    L = DEPTH
    b_ada = f32(inputs['b_ada'])
    shared = dict(
        w_ada=f32(inputs['w_ada']), b_ada=b_ada,
        b_adaT=np.ascontiguousarray(b_ada.reshape(L, 24, 128).transpose(0, 2, 1)),
        norm_gT=np.ascontiguousarray(f32(inputs['norm_g']).reshape(L, 8, 128).transpose(0, 2, 1)),
        w_in=f32(inputs['w_in']), w_out=f32(inputs['w_out']),
        final_g=f32(inputs['final_g']).reshape(1, D),
        ident=np.eye(128).astype(ml_dtypes.bfloat16),
        pool_w=f32(inputs['pool_w']),
        pool_scaleT=np.ascontiguousarray(f32(inputs['pool_scale']).reshape(L, 2, 128).transpose(0, 2, 1)),
    )
    bm, inv = make_pool_consts()
    shared.update(bmats=bm, inv_cnt=inv)
    shared.update(make_ft_consts())
    shared.update(make_dn_consts())
    shared.update(ft_w=f32(inputs['ft_w']))
    maps = []
    for k in range(ncores):
        sb = k // 4
        x_all = np.concatenate([x_sample[sb], x_prompt[2 * k], x_prompt[2 * k + 1]], axis=0)
        cond = np.stack([c[sb], c_ctx], axis=0)
        condT = np.ascontiguousarray(cond.reshape(2, 8, 128).transpose(2, 1, 0))
        m = dict(shared)
        m.update(x_all=np.ascontiguousarray(x_all), condT=condT)
        m.update(make_s5_inputs(inputs, k))
        m.update(make_dn_inputs(inputs, k))
        maps.append(m)
    return maps


POOL_WINDOWS = (2, 4, 8, 16)
POOL_D2 = {2: (-1, 0), 4: (-1, 1), 8: (-2, 2), 16: (-4, 4)}
POOL_D1 = (-1, 1)


def pool_block_index():
    idx = {}
    n = 0
    for w in POOL_WINDOWS:
        lo, hi = POOL_D2[w]
        for d in range(lo, hi + 1):
            idx[('s', w, d)] = n
            n += 1
    for w in POOL_WINDOWS:
        for d in range(POOL_D1[0], POOL_D1[1] + 1):
            idx[('p', w, d)] = n
            n += 1
    return idx, n


def make_pool_consts():
    idx, n = pool_block_index()
    B = np.zeros((n, 128, 128), np.float32)
    i = np.arange(128)
    ril, cin = i // 64, i % 64
    for w in POOL_WINDOWS:
        lo, hi = POOL_D2[w]
        for d in range(lo, hi + 1):
            dr = 2 * d + ril[:, None] - ril[None, :]
            dc = cin[:, None] - cin[None, :]
            B[idx[('s', w, d)]] = ((dr >= -(w // 2)) & (dr < w - w // 2) & (dc >= -(w // 2)) & (dc < w - w // 2))
        for d in range(-1, 2):
            dt_ = 128 * d + i[:, None] - i[None, :]
            B[idx[('p', w, d)]] = ((dt_ >= -(w // 2)) & (dt_ < w - w // 2))
    inv = np.zeros((4, NT), np.float32)

    def cnt(Ln, w):
        pos = np.arange(Ln)
        lo = np.clip(pos - w // 2, 0, Ln)
        hi = np.clip(pos - w // 2 + w, 0, Ln)
        return (hi - lo).astype(np.float64)
    for gi, w in enumerate(POOL_WINDOWS):
        c64 = cnt(64, w)
        inv[gi, :TS] = (1.0 / (c64[:, None] * c64[None, :])).reshape(-1)
        c256 = 1.0 / cnt(256, w)
        inv[gi, TS:] = np.concatenate([c256, c256])
    return B.astype(ml_dtypes.bfloat16), inv


def seq_tile_range(tt):
    if tt < TS // 128:
        return 0, TS // 128
    lo = tt - (tt - TS // 128) % 2
    return lo, lo + 2


def _mixer_init(self):
    kb, nc = self.kb, self.nc
    L = DEPTH
    dt_in = lambda name, shape, dt=F32: nc.dram_tensor(name, list(shape), dt, kind="ExternalInput").ap()
    _, nblk = pool_block_index()
    self.bmats_in = dt_in("bmats", [nblk, 128, 128], BF16)
    self.inv_cnt = dt_in("inv_cnt", [4, NT])
    self.pool_w = dt_in("pool_w", [L, 4, 64, 64])
    self.pool_scaleT = dt_in("pool_scaleT", [L, 128, 2])
    self.bmats = kb.sb("bmats_sb", [128, nblk, 128], BF16)
    self.pwstage = kb.sb("pwstage", [128, 2, 64], F32)
    self.pwblk = kb.sb("pwblk", [128, 2, 128], BF16)
    self.pscale = kb.sb("pscale", [128, 2], F32)
    self.utok = kb.sb("utok", [128, 12, 256], BF16)
    self.wk = [kb.sb(f"wk{i}", [128, 512], F32) for i in range(6)]
    self.wkb = [kb.sb(f"wkb{i}", [128, 512], BF16) for i in range(4)]
    self.ybuf = [kb.sb(f"ybuf{i}", [128, 512], BF16) for i in range(2)]
    self.ybuf_i = 0
    self.zero_bf = kb.sb("zero_bf", [128, 512], BF16)


def _mixer_setup(self):
    kb, nc = self.kb, self.nc
    kb.dma('sp', self.bmats[:], self.bmats_in.rearrange("n p q -> p n q"), self.bmats, writes=[self.bmats])
    kb.op('dve', lambda: nc.vector.memset(self.zero_bf[:], 0.0), writes=[self.zero_bf])


def zero_rows(self, kts):
    kb = self.kb
    for kt in kts:
        for g in range(NG):
            kb.dma('act', self.YT[kt * 128:(kt + 1) * 128, g * 512:(g + 1) * 512], self.zero_bf[:], self.zero_bf,
                   reads=[self.zero_bf], writes=[self.rYT[kt][g]])


def store_y(self, yb, row0, g):
    kb = self.kb
    kb.dma('act', self.YT[row0:row0 + 128, g * 512:(g + 1) * 512], yb[:], yb,
           reads=[yb], writes=[self.rYT[row0 // 128][g]])


def pool_phase(self, l):
    kb, nc = self.kb, self.nc
    bidx, _ = pool_block_index()
    kb.dma('sp', self.pwstage[:], self.pool_w[l].rearrange("(a gl) c d -> (gl c) a d", gl=2), self.pwstage,
           writes=[self.pwstage])
    kb.dma('sp', self.pscale[:], self.pool_scaleT[l], self.pscale, writes=[self.pscale])
    kb.op('dve', lambda: nc.vector.memset(self.pwblk[:], 0.0), writes=[self.pwblk])
    for a in range(2):
        kb.op('dve', lambda a=a: nc.vector.tensor_copy(out=self.pwblk[0:64, a, 0:64], in_=self.pwstage[0:64, a, :]),
              reads=[self.pwstage], writes=[self.pwblk])
        kb.op('dve', lambda a=a: nc.vector.tensor_copy(out=self.pwblk[64:128, a, 64:128], in_=self.pwstage[64:128, a, :]),
              reads=[self.pwstage], writes=[self.pwblk])
    for g in range(NG):
        sample = (g * 512 < TS)
        t0 = 4 * g
        lo_l = max(seq_tile_range(t0)[0], t0 - 4) if sample else t0
        hi_l = min(seq_tile_range(t0)[1], t0 + 8) if sample else t0 + 4
        for m in range(lo_l, hi_l):
            kb.dma('sp', self.utok[:, m - (t0 - 4), :], self.PTOK[m * 128:(m + 1) * 128, 0:256], self.utok,
                   reads=[self.rPTOK[m]], writes=[self.utok])
        for a in range(2):
            uT, zT, inv, tmp, sz = self.wk[0], self.wk[1], self.wk[2], self.wk[3], self.wk[4]
            diffb = self.wkb[0]
            tsl = slice(g * 512, (g + 1) * 512)
            kb.dma('sp', uT[:], self.PT[C_POOL_U + a * 128:C_POOL_U + (a + 1) * 128, tsl], uT, reads=[self.rPT[g]], writes=[uT])
            kb.dma('sp', zT[:], self.PT[C_POOL_Z + a * 128:C_POOL_Z + (a + 1) * 128, tsl], zT, reads=[self.rPT[g]], writes=[zT])
            for gl in range(2):
                kb.dma('sp', inv[gl * 64:(gl + 1) * 64, :], self.inv_cnt[2 * a + gl:2 * a + gl + 1, tsl].partition_broadcast(64),
                       inv, writes=[inv])
            acc = self.next_bank()
            for gl in range(2):
                gi = 2 * a + gl
                w = POOL_WINDOWS[gi]
                for j in range(4):
                    m = t0 + j
                    slo, shi = seq_tile_range(m)
                    dlo, dhi = POOL_D2[w] if sample else POOL_D1
                    ds = [d for d in range(dlo, dhi + 1) if slo <= m + d < shi]
                    for ii, d in enumerate(ds):
                        bi = bidx[('s' if sample else 'p', w, d)]
                        kb.op('pe', lambda ii=ii, d=d, bi=bi, m=m, j=j, gl=gl, gi=gi, ds=ds: nc.tensor.matmul(
                            acc[gl * 64:(gl + 1) * 64, j * 128:(j + 1) * 128],
                            lhsT=self.utok[:, m + d - (t0 - 4), gi * 64:(gi + 1) * 64], rhs=self.bmats[:, bi, :],
                            start=(ii == 0), stop=(ii == len(ds) - 1)),
                            reads=[self.utok, self.bmats], writes=[acc])
            kb.op('dve', lambda: nc.vector.tensor_tensor(out=tmp[:], in0=acc[:], in1=inv[:], op=ALU.mult),
                  reads=[acc, inv], writes=[tmp])
            kb.op('dve', lambda: nc.vector.tensor_tensor(out=diffb[:], in0=tmp[:], in1=uT[:], op=ALU.subtract),
                  reads=[tmp, uT], writes=[diffb])
            acc2 = self.next_bank()
            kb.op('pe', lambda: nc.tensor.matmul(acc2[:], lhsT=self.pwblk[:, a, :], rhs=diffb[:], start=True, stop=True),
                  reads=[self.pwblk, diffb], writes=[acc2])
            kb.op('act', lambda: nc.scalar.activation(out=sz[:], in_=zT[:], func=AF.Silu), reads=[zT], writes=[sz])
            yb = self.ybuf[self.ybuf_i % 2]
            self.ybuf_i += 1
            kb.op('dve', lambda: nc.vector.scalar_tensor_tensor(out=yb[:], in0=acc2[:], scalar=self.pscale[:, a:a + 1], in1=sz[:],
                                                                op0=ALU.mult, op1=ALU.mult),
                  reads=[acc2, self.pscale, sz], writes=[yb])
            self.store_y(yb, a * 128, g)


def phase_b(self, l):
    br = self.cfg.get('branches', ('pool', 'dn', 's5', 'ft'))
    if 'pool' in br:
        self.pool_phase(l)
    else:
        self.zero_rows([0, 1])
    if 'dn' in br:
        self.dn_phase(l)
    else:
        self.zero_rows([2, 3])
    if 's5' in br:
        self.s5_phase(l)
    else:
        self.zero_rows([4, 5])
    if 'ft' in br:
        self.ft_phase(l)
    else:
        self.zero_rows([6, 7])
    if self.cfg.get('dump_yt', False) and l == 0:
        dbg = self.nc.dram_tensor("yt_dbg", [D, NT], BF16, kind="ExternalOutput").ap()
        r = Res("ytdbg")
        self.kb.dma('sp', dbg, self.YT, r, reads=[self.rYT[k][g] for k in range(8) for g in range(NG)])


Model.mixer_init = _mixer_init
Model.mixer_setup = _mixer_setup
Model.zero_rows = zero_rows
Model.store_y = store_y
Model.pool_phase = pool_phase
Model.phase_b = phase_b


def make_ft_consts():
    c = np.arange(64, dtype=np.float64)
    ang = 2 * np.pi * np.outer(c, c) / 64
    wdft = np.zeros((256, 512), np.float64)
    for h in range(4):
        wdft[h * 64:(h + 1) * 64, h * 64:(h + 1) * 64] = np.cos(ang) / 8.0
        wdft[h * 64:(h + 1) * 64, 256 + h * 64:256 + (h + 1) * 64] = -np.sin(ang) / 8.0
    C, S = np.cos(ang) / 64.0, np.sin(ang) / 64.0
    ma = np.zeros((128, 128), np.float64)
    ma[0:64, 0:64] = C
    ma[64:128, 0:64] = S
    ma[0:64, 64:128] = -S
    ma[64:128, 64:128] = C
    t2 = np.arange(64, dtype=np.float64)
    tp = np.arange(4096, dtype=np.float64)
    th = 2 * np.pi * (np.outer(t2, tp) % 4096) / 4096
    g = np.zeros((128, 64, 64), np.float64)
    g[0:64] = np.cos(th).reshape(64, 64, 64).transpose(0, 2, 1)
    g[64:128] = np.sin(th).reshape(64, 64, 64).transpose(0, 2, 1)
    t = np.arange(256, dtype=np.float64)
    a256 = 2 * np.pi * (np.outer(t, t) % 256) / 256
    cs256 = np.stack([np.cos(a256) / 16.0, np.sin(a256) / 16.0], axis=0)
    cs256 = cs256.reshape(2, 2, 128, 256).transpose(2, 0, 1, 3)
    f = lambda a: np.ascontiguousarray(a.astype(np.float32))
    return dict(ft_wdft=f(wdft.reshape(2, 128, 512).transpose(1, 0, 2)), ft_ma=f(ma),
                ft_g=f(g.reshape(128, 4096)), ft_cs256=f(cs256.reshape(128, 1024)))


def _ft_init(self):
    kb, nc = self.kb, self.nc
    L = DEPTH
    dt_in = lambda name, shape, dt=F32: nc.dram_tensor(name, list(shape), dt, kind="ExternalInput").ap()
    self.ft_wdft_in = dt_in("ft_wdft", [128, 2, 512])
    self.ft_ma_in = dt_in("ft_ma", [128, 128])
    self.ft_g_in = dt_in("ft_g", [128, 4096])
    self.ft_cs256_in = dt_in("ft_cs256", [128, 1024])
    self.ft_w_in = dt_in("ft_w", [L, 256, 256])
    self.wdft_bf = kb.sb("wdft_bf", [128, 2, 512], BF16)
    self.ma_bf = kb.sb("ma_bf", [128, 128], BF16)
    self.g_bf = kb.sb("g_bf", [128, 64, 64], BF16)
    self.cs256_bf = kb.sb("cs256_bf", [128, 2, 2, 256], BF16)
    self.ftw_bf = kb.sb("ftw_bf", [128, 2, 256], BF16)
    self.VS = nc.dram_tensor("VS_scr", [NT, 512], BF16, kind="Internal").ap()
    self.ZS = nc.dram_tensor("ZS_scr", [2, 64, 64, 256], BF16, kind="Internal").ap()
    self.FS = nc.dram_tensor("FS_scr", [NT, 256], BF16, kind="Internal").ap()
    self.rVS = [Res(f"VS{g}") for g in range(NG)]
    self.rZS = Res("ZS")
    self.rFS = [Res(f"FS{g}") for g in range(NG)]


def _ft_setup(self):
    kb, nc = self.kb, self.nc

    def load_cast(dst_ap, dst_res, src_ap, ncols):
        stg = self.wstage[self.wstage_i % 2]
        self.wstage_i += 1
        kb.dma('sp', stg[:, 0:ncols], src_ap, stg, writes=[stg])
        kb.op('dve', lambda: nc.vector.tensor_copy(out=dst_ap, in_=stg[:, 0:ncols]), reads=[stg], writes=[dst_res])
    load_cast(self.wdft_bf[:].rearrange("p a b -> p (a b)"), self.wdft_bf, self.ft_wdft_in.rearrange("p a b -> p (a b)"), 1024)
    load_cast(self.ma_bf[:], self.ma_bf, self.ft_ma_in, 128)
    gv = self.g_bf[:].rearrange("p a b -> p (a b)")
    load_cast(gv[:, 0:2048], self.g_bf, self.ft_g_in[:, 0:2048], 2048)
    load_cast(gv[:, 2048:4096], self.g_bf, self.ft_g_in[:, 2048:4096], 2048)
    load_cast(self.cs256_bf[:].rearrange("p a b c -> p (a b c)"), self.cs256_bf, self.ft_cs256_in, 1024)


def ft_phase(self, l):
    kb, nc = self.kb, self.nc
    stg = self.wstage[self.wstage_i % 2]
    self.wstage_i += 1
    kb.dma('sp', stg[:, 0:512].rearrange("p (k d) -> p k d", k=2), self.ft_w_in[l].rearrange("(k p) d -> p k d", p=128), stg, writes=[stg])
    kb.op('dve', lambda: nc.vector.tensor_copy(out=self.ftw_bf[:].rearrange("p k d -> p (k d)"), in_=stg[:, 0:512]),
          reads=[stg], writes=[self.ftw_bf])
    for g in range(NG):
        tsl = slice(g * 512, (g + 1) * 512)
        uTb = self.wkb[0:2]
        for ct in range(2):
            uf = self.wk[ct]
            kb.dma('sp', uf[:], self.PT[C_FTU + ct * 128:C_FTU + (ct + 1) * 128, tsl], uf, reads=[self.rPT[g]], writes=[uf])
            kb.op('act', lambda ct=ct, uf=uf: nc.scalar.copy(out=uTb[ct][:], in_=uf[:]), reads=[uf], writes=[uTb[ct]])
        for j in range(4):
            acc = self.next_bank()
            for ct in range(2):
                kb.op('pe', lambda ct=ct: nc.tensor.matmul(acc[:], lhsT=uTb[ct][:, j * 128:(j + 1) * 128], rhs=self.wdft_bf[:, ct, :],
                                                           start=(ct == 0), stop=(ct == 1)),
                      reads=[uTb[ct], self.wdft_bf], writes=[acc])
            vb = self.tmstage[j % 2]
            kb.op('dve', lambda: nc.vector.tensor_copy(out=vb[:], in_=acc[:]), reads=[acc], writes=[vb])
            tt = g * 4 + j
            kb.dma('act', self.VS[tt * 128:(tt + 1) * 128, :], vb[:], vb, reads=[vb], writes=[self.rVS[g]])
    VSv = self.VS[0:TS, :].rearrange("(a b) (r c) -> r a b c", b=64, r=2)
    for ch in range(8):
        va = self.hT[ch % 2]
        vav = va[:].rearrange("p a b -> p (a b)")[:, 0:2048].rearrange("p (t c) -> p t c", c=256)
        for ri in range(2):
            kb.dma('sp', vav[ri * 64:(ri + 1) * 64, :, :], VSv[ri, :, ch * 8:(ch + 1) * 8, :], va,
                   reads=self.rVS[0:8], writes=[va])
        zo = va[:].rearrange("p a b -> p (a b)")[:, 2048:4096]
        vflat = va[:].rearrange("p a b -> p (a b)")
        for q in range(4):
            acc = self.next_bank()
            kb.op('pe', lambda q=q: nc.tensor.matmul(acc[:], lhsT=self.ma_bf[:], rhs=vflat[:, q * 512:(q + 1) * 512], start=True, stop=True),
                  reads=[self.ma_bf, va], writes=[acc])
            if q % 2 == 0:
                kb.op('act', lambda q=q: nc.scalar.copy(out=zo[:, q * 512:(q + 1) * 512], in_=acc[:]), reads=[acc], writes=[va])
            else:
                kb.op('dve', lambda q=q: nc.vector.tensor_copy(out=zo[:, q * 512:(q + 1) * 512], in_=acc[:]), reads=[acc], writes=[va])
        for ri in range(2):
            kb.dma('act', self.ZS[ri, :, ch * 8:(ch + 1) * 8, :], zo[ri * 64:(ri + 1) * 64, :].rearrange("p (t c) -> p t c", c=256), va,
                   reads=[va], writes=[self.rZS])
    ZSv = self.ZS.rearrange("r k t c -> r t k c")
    FSv = self.FS[0:TS, :].rearrange("(k2 k1) c -> k2 k1 c", k1=64)
    for ch in range(4):
        zb = self.hT[ch % 2]
        zflat = zb[:].rearrange("p a b -> p (a b)")
        z2 = zflat.rearrange("p (k c) -> p k c", c=256)
        for ri in range(2):
            kb.dma('sp', z2[ri * 64:(ri + 1) * 64, :, :], ZSv[ri, :, ch * 16:(ch + 1) * 16, :], zb,
                   reads=[self.rZS], writes=[zb])
        fo_res = self.wstage[1]
        fo = fo_res[:].bitcast(BF16)
        for kp in range(8):
            acc = self.next_bank()
            for e in range(2):
                k1l = kp * 2 + e
                k1 = ch * 16 + k1l
                kb.op('pe', lambda e=e, k1=k1, k1l=k1l: nc.tensor.matmul(acc[0:64, e * 256:(e + 1) * 256], lhsT=self.g_bf[:, k1, :], rhs=z2[:, k1l, :],
                                                                         start=True, stop=True),
                      reads=[self.g_bf, zb], writes=[acc])
            if kp % 2 == 0:
                kb.op('act', lambda kp=kp: nc.scalar.copy(out=fo[0:64, kp * 512:(kp + 1) * 512], in_=acc[0:64, :]), reads=[acc], writes=[fo_res])
            else:
                kb.op('dve', lambda kp=kp: nc.vector.tensor_copy(out=fo[0:64, kp * 512:(kp + 1) * 512], in_=acc[0:64, :]), reads=[acc], writes=[fo_res])
        kb.dma('act', FSv[:, ch * 16:(ch + 1) * 16, :], fo[0:64, 0:4096].rearrange("p (k c) -> p k c", c=256), fo_res,
               reads=[fo_res], writes=self.rFS[0:8])
    for sq in range(2):
        base = TS + sq * 256
        vt = self.hT[sq % 2]
        vv = vt[:].rearrange("p a b -> p (a b)")[:, 0:1024].rearrange("p (k c) -> p k c", c=512)
        kb.dma('sp', vv, self.VS[base:base + 256, :].rearrange("(k p) c -> p k c", p=128), vt, reads=[self.rVS[8]], writes=[vt])
        for mt in range(2):
            acc = self.next_bank()
            n = 0
            for cs in range(2):
                for kt in range(2):
                    kb.op('pe', lambda cs=cs, kt=kt, n=n, mt=mt: nc.tensor.matmul(
                        acc[:, 0:256], lhsT=self.cs256_bf[:, cs, kt, mt * 128:(mt + 1) * 128], rhs=vv[:, kt, cs * 256:(cs + 1) * 256],
                        start=(n == 0), stop=(n == 3)), reads=[self.cs256_bf, vt], writes=[acc])
                    n += 1
            fb = self.tmstage[mt % 2]
            kb.op('dve', lambda: nc.vector.tensor_copy(out=fb[:, 0:256], in_=acc[:, 0:256]), reads=[acc], writes=[fb])
            kb.dma('act', self.FS[base + mt * 128:base + (mt + 1) * 128, :], fb[:, 0:256], fb, reads=[fb], writes=[self.rFS[8]])
    for g in range(NG):
        tsl = slice(g * 512, (g + 1) * 512)
        ftok = self.wkb[2]
        fT = [self.wkb[0], self.wkb[1]]
        for j in range(4):
            ft = self.wkb[2 + j % 2]
            tt = g * 4 + j
            kb.dma('sp', ft[:, 0:256], self.FS[tt * 128:(tt + 1) * 128, :], ft, reads=[self.rFS[g]], writes=[ft])
            pT = self.bank[j % 2]
            pTv = pT[:].bitcast(BF16)
            for ct in range(2):
                kb.op('pe', lambda ct=ct: nc.tensor.transpose(pTv[:, ct * 128:(ct + 1) * 128], ft[:, ct * 128:(ct + 1) * 128], self.ident[:]),
                      reads=[ft, self.ident], writes=[pT])
            for ct in range(2):
                kb.op('act', lambda ct=ct: nc.scalar.copy(out=fT[ct][:, j * 128:(j + 1) * 128], in_=pTv[:, ct * 128:(ct + 1) * 128]),
                      reads=[pT], writes=[fT[ct]])
        for dtile in range(2):
            zT, sz = self.wk[2 + dtile], self.wk[4 + dtile]
            kb.dma('sp', zT[:], self.PT[C_FTZ + dtile * 128:C_FTZ + (dtile + 1) * 128, tsl], zT, reads=[self.rPT[g]], writes=[zT])
            kb.op('act', lambda: nc.scalar.activation(out=sz[:], in_=zT[:], func=AF.Silu), reads=[zT], writes=[sz])
            acc = self.next_bank()
            for ct in range(2):
                kb.op('pe', lambda ct=ct: nc.tensor.matmul(acc[:], lhsT=self.ftw_bf[:, ct, dtile * 128:(dtile + 1) * 128], rhs=fT[ct][:],
                                                           start=(ct == 0), stop=(ct == 1)),
                      reads=[self.ftw_bf, fT[ct]], writes=[acc])
            yb = self.ybuf[self.ybuf_i % 2]
            self.ybuf_i += 1
            kb.op('dve', lambda: nc.vector.tensor_tensor(out=yb[:], in0=acc[:], in1=sz[:], op=ALU.mult), reads=[acc, sz], writes=[yb])
            self.store_y(yb, 768 + dtile * 128, g)


Model.ft_init = _ft_init
Model.ft_setup = _ft_setup
Model.ft_phase = ft_phase


S5_L = 256
MAGIC = 12582912.0
P_ARE, P_AIM, P_LDT, P_DT, P_MAG, P_TH, P_K, P_R, P_SN, P_CS, P_ABRE, P_ABIM, P_DEN, P_NR, P_CRE, P_CIM, P_T1, P_T2, P_NCIM = range(19)


def _s5_init(self):
    kb, nc = self.kb, self.nc
    L = DEPTH
    dt_in = lambda name, shape, dt=F32: nc.dram_tensor(name, list(shape), dt, kind="ExternalInput").ap()
    self.s5_a_reT = dt_in("s5_a_reT", [L, 128, 16])
    self.s5_a_imT = dt_in("s5_a_imT", [L, 128, 16])
    self.s5_logdtT = dt_in("s5_logdtT", [L, 128, 16])
    self.s5_b_re = dt_in("s5_b_re", [L, 2, 16, 64, 16])
    self.s5_b_im = dt_in("s5_b_im", [L, 2, 16, 64, 16])
    self.s5_c_re = dt_in("s5_c_re", [L, 2, 16, 16, 64])
    self.s5_c_im = dt_in("s5_c_im", [L, 2, 16, 16, 64])
    self.s5_dT = dt_in("s5_dT", [L, 128, 2])
    self.s5_glu_bT = dt_in("s5_glu_bT", [L, 128, 2])
    self.s5_glu_w = dt_in("s5_glu_w", [L, 256, 256])
    self.s5_s0T = dt_in("s5_s0T", [L, 128, 32])
    self.new_s5T = nc.dram_tensor("new_s5T", [L, 128, 64], F32, kind="ExternalOutput").ap()
    self.YS = nc.dram_tensor("YS_scr", [2, 256, NT], F32, kind="Internal").ap()
    self.YG = nc.dram_tensor("YG_scr", [256, NT], BF16, kind="Internal").ap()
    self.rYS = [[[Res(f"YS{d}_{c}_{k}") for k in range(NT // S5_L)] for c in range(2)] for d in range(2)]
    self.rYG = [[Res(f"YG{c}_{g}") for g in range(NG)] for c in range(2)]
    self.s5par = kb.sb("s5par", [128, 19, 16], F32)
    self.s5b = kb.sb("s5b", [128, 4, 16], F32)
    self.s5s0 = kb.sb("s5s0", [128, 32], F32)
    self.s5carry = kb.sb("s5carry", [128, 4, 2], F32)
    self.s5carry2 = kb.sb("s5carry2", [128, 2, 4], F32)
    self.s5ctmp = kb.sb("s5ctmp", [128, 2, 4], F32)
    self.s5carry2b = kb.sb("s5carry2b", [128, 2, 4], F32)
    self.s5ctmpb = kb.sb("s5ctmpb", [128, 2, 4], F32)
    self.s5fin = kb.sb("s5fin", [128, 64], F32)
    self.s5d = kb.sb("s5d", [128, 2], F32)
    self.s5gb = kb.sb("s5gb", [128, 2], F32)
    self.gluw_bf = kb.sb("gluw_bf", [128, 2, 256], BF16)
    self.hp_col = kb.sb("hp_col", [128, 1], F32)
    self.ident32 = kb.sb("ident32", [128, 128], F32)


def _s5_setup(self):
    kb, nc = self.kb, self.nc
    kb.op('dve', lambda: nc.vector.memset(self.hp_col[:], float(np.pi / 2)), writes=[self.hp_col])
    kb.op('dve', lambda: nc.vector.tensor_copy(out=self.ident32[:], in_=self.ident[:]), reads=[self.ident], writes=[self.ident32])


def s5_phase(self, l):
    kb, nc = self.kb, self.nc
    P = self.s5par
    V, A, G_ = 'dve', 'act', 'pool'

    def pc(i):
        return P[:, i, :]

    def dve_tt(o, a, b, op):
        kb.op(V, lambda: nc.vector.tensor_tensor(out=pc(o), in0=pc(a), in1=pc(b), op=op), reads=[P], writes=[P])

    def dve_ts(o, a, s1, s2, op0, op1=None):
        if op1 is None:
            kb.op(V, lambda: nc.vector.tensor_scalar(out=pc(o), in0=pc(a), scalar1=s1, scalar2=None, op0=op0), reads=[P], writes=[P])
        else:
            kb.op(V, lambda: nc.vector.tensor_scalar(out=pc(o), in0=pc(a), scalar1=s1, scalar2=s2, op0=op0, op1=op1), reads=[P], writes=[P])

    def act_f(o, a, func, **kw):
        rd = [P] + ([self.hp_col] if 'bias' in kw else [])
        kb.op(A, lambda: nc.scalar.activation(out=pc(o), in_=pc(a), func=func, **kw), reads=rd, writes=[P])

    kb.dma('sp', pc(P_ARE), self.s5_a_reT[l], P, writes=[P])
    kb.dma('sp', pc(P_AIM), self.s5_a_imT[l], P, writes=[P])
    kb.dma('sp', pc(P_LDT), self.s5_logdtT[l], P, writes=[P])
    kb.dma('sp', self.s5s0[:], self.s5_s0T[l], self.s5s0, writes=[self.s5s0])
    kb.dma('sp', self.s5d[:], self.s5_dT[l], self.s5d, writes=[self.s5d])
    kb.dma('sp', self.s5gb[:], self.s5_glu_bT[l], self.s5gb, writes=[self.s5gb])
    stg = self.wstage[self.wstage_i % 2]
    self.wstage_i += 1
    kb.dma('sp', stg[:, 0:512].rearrange("p (k d) -> p k d", k=2), self.s5_glu_w[l].rearrange("(k p) d -> p k d", p=128), stg, writes=[stg])
    kb.op(V, lambda: nc.vector.tensor_copy(out=self.gluw_bf[:].rearrange("p k d -> p (k d)"), in_=stg[:, 0:512]), reads=[stg], writes=[self.gluw_bf])
    act_f(P_DT, P_LDT, AF.Exp)
    dve_tt(P_T1, P_ARE, P_DT, ALU.mult)
    act_f(P_MAG, P_T1, AF.Exp)
    dve_tt(P_TH, P_AIM, P_DT, ALU.mult)
    dve_ts(P_K, P_TH, float(1 / (2 * np.pi)), MAGIC, ALU.mult, ALU.add)
    dve_ts(P_K, P_K, MAGIC, None, ALU.subtract)
    kb.op(V, lambda: nc.vector.scalar_tensor_tensor(out=pc(P_R), in0=pc(P_K), scalar=float(-2 * np.pi), in1=pc(P_TH), op0=ALU.mult, op1=ALU.add), reads=[P], writes=[P])
    act_f(P_SN, P_R, AF.Sin)
    act_f(P_T2, P_R, AF.Abs)
    act_f(P_CS, P_T2, AF.Sin, scale=-1.0, bias=self.hp_col[:])
    dve_tt(P_ABRE, P_MAG, P_CS, ALU.mult)
    dve_tt(P_ABIM, P_MAG, P_SN, ALU.mult)
    dve_tt(P_DEN, P_ARE, P_ARE, ALU.mult)
    dve_tt(P_T1, P_AIM, P_AIM, ALU.mult)
    dve_tt(P_DEN, P_DEN, P_T1, ALU.add)
    kb.op(V, lambda: nc.vector.reciprocal(out=pc(P_DEN), in_=pc(P_DEN)), reads=[P], writes=[P])
    dve_ts(P_NR, P_ABRE, -1.0, None, ALU.add)
    dve_tt(P_T1, P_NR, P_ARE, ALU.mult)
    dve_tt(P_T2, P_ABIM, P_AIM, ALU.mult)
    dve_tt(P_T1, P_T1, P_T2, ALU.add)
    dve_tt(P_CRE, P_T1, P_DEN, ALU.mult)
    dve_tt(P_T1, P_ABIM, P_ARE, ALU.mult)
    dve_tt(P_T2, P_NR, P_AIM, ALU.mult)
    dve_tt(P_T1, P_T1, P_T2, ALU.subtract)
    dve_tt(P_CIM, P_T1, P_DEN, ALU.mult)
    dve_ts(P_NCIM, P_CIM, -1.0, None, ALU.mult)

    if not hasattr(self, '_s5_bufs'):
        self._s5_bufs = self.s5_make_bufs()
    C0, C1 = self._s5_bufs
    for src_, al in self.s5_alias:
        kb.alias_begin(src_, al)
    kb.op(V, lambda: nc.vector.memset(self.otmp[0][:], 0.0), writes=[self.otmp[0]])
    kb.op(V, lambda: nc.vector.memset(self.otmp[1][:], 0.0), writes=[self.otmp[1]])
    kb.op(V, lambda: nc.vector.memset(self.s5fin[:], 0.0), writes=[self.s5fin])
    run_chains([self.s5_chain(l, 0, C0), self.s5_chain(l, 1, C1)], head_start=(self.cfg.get('s5_hs', 2), 0))
    for src_, al in self.s5_alias:
        kb.alias_end(src_, al)
    kb.dma('act', self.new_s5T[l], self.s5fin[:], self.s5fin, reads=[self.s5fin])
    Lc = S5_L
    for g in range(NG):
        tsl = slice(g * 512, (g + 1) * 512)
        yg = [self.wkb[2], self.wkb[3]]
        for ct in range(2):
            y0, y1, uf = self.wk[0], self.wk[1], self.wk[2]
            rows = slice(ct * 128, (ct + 1) * 128)
            kb.dma('sp', y0[:], self.YS[0, rows, tsl], y0, reads=[self.rYS[0][ct][2 * g], self.rYS[0][ct][2 * g + 1]], writes=[y0])
            kb.dma('sp', y1[:], self.YS[1, rows, tsl], y1, reads=[self.rYS[1][ct][2 * g], self.rYS[1][ct][2 * g + 1]], writes=[y1])
            kb.dma('sp', uf[:], self.PT[C_S5U + ct * 128:C_S5U + (ct + 1) * 128, tsl], uf, reads=[self.rPT[g]], writes=[uf])
            kb.op(V, lambda: nc.vector.tensor_tensor(out=y0[:], in0=y0[:], in1=y1[:], op=ALU.add), reads=[y0, y1], writes=[y0])
            kb.op(V, lambda: nc.vector.scalar_tensor_tensor(out=y0[:], in0=uf[:], scalar=self.s5d[:, ct:ct + 1], in1=y0[:], op0=ALU.mult, op1=ALU.add),
                  reads=[uf, self.s5d, y0], writes=[y0])
            g2 = self.wk[3]
            kb.op(V, lambda: nc.vector.tensor_tensor(out=g2[:], in0=y0[:], in1=y0[:], op=ALU.mult), reads=[y0], writes=[g2])
            kb.op(V, lambda: nc.vector.tensor_scalar(out=g2[:], in0=g2[:], scalar1=0.044715, scalar2=1.0, op0=ALU.mult, op1=ALU.add), reads=[g2], writes=[g2])
            kb.op(V, lambda: nc.vector.tensor_tensor(out=g2[:], in0=g2[:], in1=y0[:], op=ALU.mult), reads=[g2, y0], writes=[g2])
            kb.op(A, lambda: nc.scalar.activation(out=g2[:], in_=g2[:], func=AF.Sigmoid, scale=1.5957691216057308), reads=[g2], writes=[g2])
            kb.op(V, lambda ct=ct: nc.vector.tensor_tensor(out=yg[ct][:], in0=g2[:], in1=y0[:], op=ALU.mult), reads=[g2, y0], writes=[yg[ct]])
        for dtile in range(2):
            acc = self.next_bank()
            for ct in range(2):
                kb.op('pe', lambda ct=ct: nc.tensor.matmul(acc[:], lhsT=self.gluw_bf[:, ct, dtile * 128:(dtile + 1) * 128], rhs=yg[ct][:],
                                                           start=(ct == 0), stop=(ct == 1)), reads=[self.gluw_bf, yg[ct]], writes=[acc])
            sig, zT = self.wk[4], self.wk[5]
            kb.op(A, lambda: nc.scalar.activation(out=sig[:], in_=acc[:], func=AF.Sigmoid, bias=self.s5gb[:, dtile:dtile + 1]),
                  reads=[acc, self.s5gb], writes=[sig])
            kb.dma('sp', zT[:], self.PT[C_S5Z + dtile * 128:C_S5Z + (dtile + 1) * 128, tsl], zT, reads=[self.rPT[g]], writes=[zT])
            kb.op(A, lambda: nc.scalar.activation(out=zT[:], in_=zT[:], func=AF.Silu), reads=[zT], writes=[zT])
            kb.op(V, lambda: nc.vector.tensor_tensor(out=sig[:], in0=sig[:], in1=zT[:], op=ALU.mult), reads=[sig, zT], writes=[sig])
            yb = self.ybuf[self.ybuf_i % 2]
            self.ybuf_i += 1
            kb.op(V, lambda: nc.vector.tensor_tensor(out=yb[:], in0=sig[:], in1=yg[dtile][:], op=ALU.mult), reads=[sig, yg[dtile]], writes=[yb])
            self.store_y(yb, 512 + dtile * 128, g)


class S5Bufs:
    pass


def s5_make_bufs(self):
    hv = lambda r: r[:].rearrange("p a b -> p (a b)")
    wf = hv(self.w_in_bf)
    wo = hv(self.w_out_bf)
    C0, C1 = S5Bufs(), S5Bufs()
    slot = lambda n_, nm: Res(nm, wf[:, n_ * 4096:(n_ + 1) * 4096].bitcast(F32))
    C0.TCr, C0.TSr = self.xt[0], self.xt[1]
    C0.BTr, C0.CTr = self.xn[0], self.xn[1]
    C0.buR, C0.tabR, C0.zR = slot(0, "s5c0bu"), slot(1, "s5c0tab"), slot(2, "s5c0z")
    C0.sbR = self.hT[0]
    C0.carry, C0.ctmp = self.s5carry2, self.s5ctmp
    C0.uf, C0.ubr, C0.yst = self.wk[0], self.wkb[0], self.pstage[0]
    C0.t1r, C0.t2r = self.wk[2], self.wk[3]
    C0.bank_lo = 0
    C0.CTnR = self.junk
    C1.TCr = Res("s5c1TC", wo[:, 0:2048].bitcast(F32))
    C1.TSr = Res("s5c1TS", wo[:, 2048:4096].bitcast(F32))
    C1.BTr = Res("s5c1BT", wo[:, 4096:5120])
    C1.CTr = Res("s5c1CT", wo[:, 5120:6144])
    C1.buR, C1.tabR = slot(3, "s5c1bu"), slot(4, "s5c1tab")
    C1.zR = Res("s5c1z", self.wstage[0][:, 0:2048])
    C1.sbR = self.hT[1]
    C1.carry, C1.ctmp = self.s5carry2b, self.s5ctmpb
    C1.uf, C1.ubr, C1.yst = self.wk[1], self.wkb[1], self.pstage[1]
    C1.t1r, C1.t2r = self.wk[4], self.wk[5]
    C1.bank_lo = 4
    C1.CTnR = Res("s5c1CTn", wo[:, 6144:7168])
    self.s5_alias = [(self.w_in_bf, [C0.buR, C0.tabR, C0.zR, C1.buR, C1.tabR]),
                     (self.w_out_bf, [C1.TCr, C1.TSr, C1.BTr, C1.CTr, C1.CTnR]),
                     (self.wstage[0], [C1.zR])]
    return C0, C1


def s5_chain(self, l, d, C):
    kb, nc = self.kb, self.nc
    P = self.s5par
    V, A = 'dve', 'act'
    Lc = S5_L
    bst = [0]

    def nb():
        b = self.bank[C.bank_lo + bst[0] % 4]
        bst[0] += 1
        return b
    TCr, TSr, BTr, CTr = C.TCr, C.TSr, C.BTr, C.CTr
    TCv = TCr[:, :].rearrange("p (s t) -> p s t", s=4)
    TSv = TSr[:, :].rearrange("p (s t) -> p s t", s=4)
    BBr, XXr = self.otmp[0], self.otmp[1]
    BB = BBr[:].rearrange("p (v r c) -> p v r c", v=4, r=2)
    XX = XXr[:].rearrange("p (v r c) -> p v r c", v=4, r=2)
    BT = BTr[:, :].rearrange("p (s r c) -> p s r c", s=4, r=2)
    CT = CTr[:, :].rearrange("p (s r c) -> p s r c", s=4, r=2)
    CTn = C.CTnR[:, 0:512].rearrange("p (s c) -> p s c", s=4)
    bur, tabr, zR, sbr = C.buR, C.tabR, C.zR, C.sbR
    bu = bur[:, :].rearrange("p (s r t) -> p s r t", s=4, r=2)
    z = zR[:, :].rearrange("p (s r t) -> p s r t", s=4, r=2)
    tab = tabr[:, :].rearrange("p (r s t) -> p r s t", r=2, s=4)
    sbv = sbr[:].rearrange("p a b -> p (a b)").rearrange("p (r s t) -> p r s t", r=4, s=4)
    carry, ctmp = C.carry, C.ctmp
    uf, ubr = C.uf, C.ubr
    seqs = [(0, 0, TS // Lc)] + [(1 + q, (TS + q * 256) // Lc, 1) for q in range(2)]
    for ct in range(2):
        for s4 in range(4):
            s = ct * 4 + s4
            col = d * 8 + s
            b4 = self.s5b
            kb.dma('sp', b4[:, 0, :], self.s5_b_re[l, d, 2 * s:2 * s + 2].rearrange("g n p -> (g n) p"), b4, writes=[b4])
            kb.dma('sp', b4[:, 1, :], self.s5_b_im[l, d, 2 * s:2 * s + 2].rearrange("g n p -> (g n) p"), b4, writes=[b4])
            kb.op(V, lambda col=col: nc.vector.tensor_scalar(out=b4[:, 2, :], in0=b4[:, 1, :], scalar1=P[:, P_NCIM, col:col + 1], scalar2=None, op0=ALU.mult),
                  reads=[b4, P], writes=[b4])
            kb.op(V, lambda col=col: nc.vector.tensor_scalar(out=b4[:, 3, :], in0=b4[:, 0, :], scalar1=P[:, P_CIM, col:col + 1], scalar2=None, op0=ALU.mult),
                  reads=[b4, P], writes=[b4])
            for gl in range(2):
                pl = slice(gl * 64, (gl + 1) * 64)
                c0 = 32 * s4 + 16 * gl
                kb.op(V, lambda pl=pl, c0=c0, col=col, s4=s4: nc.vector.scalar_tensor_tensor(
                    out=BB[pl, s4, 0, c0:c0 + 16], in0=b4[pl, 0, :], scalar=P[pl, P_CRE, col:col + 1], in1=b4[pl, 2, :],
                    op0=ALU.mult, op1=ALU.add), reads=[b4, P], writes=[BBr])
                kb.op(V, lambda pl=pl, c0=c0, col=col, s4=s4: nc.vector.scalar_tensor_tensor(
                    out=BB[pl, s4, 1, c0:c0 + 16], in0=b4[pl, 1, :], scalar=P[pl, P_CRE, col:col + 1], in1=b4[pl, 3, :],
                    op0=ALU.mult, op1=ALU.add), reads=[b4, P], writes=[BBr])
                for ri, csrc in enumerate((self.s5_c_re, self.s5_c_im)):
                    kb.dma('sp', XX[c0:c0 + 16, s4, ri, 64 * gl:64 * gl + 64], csrc[l, d, 2 * s + gl], XXr, writes=[XXr])
            for ri in range(2):
                pt = nb()
                kb.op('pe', lambda ri=ri, s4=s4, pt=pt: nc.tensor.transpose(pt[:, 0:128], BB[:, s4, ri, :], self.ident32[:]),
                      reads=[BBr, self.ident32], writes=[pt])
                kb.op(A, lambda ri=ri, s4=s4, pt=pt: nc.scalar.copy(out=BT[:, s4, ri, :], in_=pt[:, 0:128]), reads=[pt], writes=[BTr])
                pt2 = nb()
                kb.op('pe', lambda ri=ri, s4=s4, pt2=pt2: nc.tensor.transpose(pt2[:, 0:128], XX[:, s4, ri, :], self.ident32[:]),
                      reads=[XXr, self.ident32], writes=[pt2])
                kb.op(A, lambda ri=ri, s4=s4, pt2=pt2: nc.scalar.activation(out=CT[:, s4, ri, :], in_=pt2[:, 0:128], func=AF.Identity,
                                                                        scale=(1.0 if ri == 0 else -1.0)), reads=[pt2], writes=[CTr])
                if ri == 0:
                    kb.op(A, lambda s4=s4, pt2=pt2: nc.scalar.activation(out=CTn[:, s4, :], in_=pt2[:, 0:128], func=AF.Identity, scale=-1.0),
                          reads=[pt2], writes=[C.CTnR])
            yield
        c0_ = d * 8 + ct * 4
        kb.op(V, lambda: nc.vector.tensor_copy(out=TCv[:, :, 0:1], in_=P[:, P_CS, c0_:c0_ + 4].unsqueeze(2)), reads=[P], writes=[TCr])
        kb.op(V, lambda: nc.vector.tensor_copy(out=TSv[:, :, 0:1], in_=P[:, P_SN, c0_:c0_ + 4].unsqueeze(2)), reads=[P], writes=[TSr])
        t1r, t2r = C.t1r, C.t2r
        m = 1
        while m < Lc:
            cb = TCv[:, :, m - 1:m].to_broadcast([128, 4, m])
            sbc = TSv[:, :, m - 1:m].to_broadcast([128, 4, m])
            t1 = t1r[:, 0:4 * m].rearrange("p (s t) -> p s t", s=4)
            t2 = t2r[:, 0:4 * m].rearrange("p (s t) -> p s t", s=4)
            kb.op(V, lambda m=m, sbc=sbc, t1=t1: nc.vector.tensor_tensor(out=t1, in0=TSv[:, :, 0:m], in1=sbc, op=ALU.mult), reads=[TSr], writes=[t1r])
            kb.op(V, lambda m=m, sbc=sbc, t2=t2: nc.vector.tensor_tensor(out=t2, in0=TCv[:, :, 0:m], in1=sbc, op=ALU.mult), reads=[TCr, TSr], writes=[t2r])
            kb.op(V, lambda m=m, cb=cb: nc.vector.tensor_tensor(out=TCv[:, :, m:2 * m], in0=TCv[:, :, 0:m], in1=cb, op=ALU.mult), reads=[TCr], writes=[TCr])
            kb.op(V, lambda m=m, cb=cb: nc.vector.tensor_tensor(out=TSv[:, :, m:2 * m], in0=TSv[:, :, 0:m], in1=cb, op=ALU.mult), reads=[TSr, TCr], writes=[TSr])
            kb.op(V, lambda m=m, t1=t1: nc.vector.tensor_tensor(out=TCv[:, :, m:2 * m], in0=TCv[:, :, m:2 * m], in1=t1, op=ALU.subtract), reads=[TCr, t1r], writes=[TCr])
            kb.op(V, lambda m=m, t2=t2: nc.vector.tensor_tensor(out=TSv[:, :, m:2 * m], in0=TSv[:, :, m:2 * m], in1=t2, op=ALU.add), reads=[TSr, t2r], writes=[TSr])
            m *= 2
            yield
        yield
        magb = P[:, P_MAG, c0_:c0_ + 4]
        for (sid, c_first, c_n) in seqs:
            order = list(range(c_first, c_first + c_n))
            if d == 1:
                order = order[::-1]
            for oi, ck in enumerate(order):
                t0 = ck * Lc
                tsl = slice(t0, t0 + Lc)
                g512 = t0 // 512
                kb.dma('sp', uf[:, 0:Lc], self.PT[C_S5U + ct * 128:C_S5U + (ct + 1) * 128, tsl], uf, reads=[self.rPT[g512]], writes=[uf])
                kb.op(A, lambda: nc.scalar.copy(out=ubr[:, 0:Lc], in_=uf[:, 0:Lc]), reads=[uf], writes=[ubr])
                rhs_u = ubr[:, 0:Lc] if d == 0 else ubr[:, Lc - 1::-1]
                if oi == 0:
                    if sid == 0:
                        s0v = self.s5s0[:, 2 * c0_:2 * c0_ + 8].rearrange("p (s r) -> p r s", r=2)
                        kb.op(V, lambda s0v=s0v: nc.vector.tensor_copy(out=carry[:], in_=s0v), reads=[self.s5s0], writes=[carry])
                    else:
                        kb.op(V, lambda: nc.vector.memset(carry[:], 0.0), writes=[carry])
                for s4 in range(4):
                    pbu = nb()
                    for ri in range(2):
                        kb.op('pe', lambda ri=ri, s4=s4, pbu=pbu: nc.tensor.matmul(pbu[:, ri * Lc:(ri + 1) * Lc], lhsT=BT[:, s4, ri, :], rhs=rhs_u, start=True, stop=True),
                              reads=[BTr, ubr], writes=[pbu])
                    kb.op(A, lambda s4=s4, pbu=pbu: nc.scalar.copy(out=bu[:, s4, :, :].rearrange("p r t -> p (r t)"), in_=pbu[:, 0:2 * Lc]), reads=[pbu], writes=[bur])
                yield
                bre, bim = bu[:, :, 0, :], bu[:, :, 1, :]
                tA, tB = tab[:, 0, :, :], tab[:, 1, :, :]
                zt0, zt1 = z[:, :, 0, :], z[:, :, 1, :]
                kb.op(V, lambda: nc.vector.tensor_tensor(out=tA, in0=bre, in1=TCv[:, :, :], op=ALU.mult), reads=[bur, TCr], writes=[tabr])
                kb.op(V, lambda: nc.vector.tensor_tensor(out=zt0, in0=bim, in1=TSv[:, :, :], op=ALU.mult), reads=[bur, TSr], writes=[zR])
                kb.op(V, lambda: nc.vector.tensor_tensor(out=tB, in0=bim, in1=TCv[:, :, :], op=ALU.mult), reads=[bur, TCr], writes=[tabr])
                kb.op(V, lambda: nc.vector.tensor_tensor(out=zt1, in0=bre, in1=TSv[:, :, :], op=ALU.mult), reads=[bur, TSr], writes=[zR])
                kb.op(V, lambda: nc.vector.tensor_tensor(out=tA, in0=tA, in1=zt0, op=ALU.add), reads=[tabr, zR], writes=[tabr])
                kb.op(V, lambda: nc.vector.tensor_tensor(out=tB, in0=tB, in1=zt1, op=ALU.subtract), reads=[tabr, zR], writes=[tabr])
                yield
                for s4 in range(4):
                    for ri in range(2):
                        kb.op(V, lambda ri=ri, s4=s4: nc.vector.tensor_tensor_scan(
                            out=z[:, s4, ri, :], data0=magb[:, s4:s4 + 1].to_broadcast([128, Lc]), data1=tab[:, ri, s4, :],
                            initial=carry[:, ri, s4:s4 + 1], op0=ALU.mult, op1=ALU.add), reads=[P, tabr, carry], writes=[zR])
                yield
                zre, zim = z[:, :, 0, :], z[:, :, 1, :]
                zl_re, zl_im = z[:, :, 0, Lc - 1:Lc], z[:, :, 1, Lc - 1:Lc]
                cl, sl_ = TCv[:, :, Lc - 1:Lc], TSv[:, :, Lc - 1:Lc]
                c_re, c_im = carry[:, 0, :].unsqueeze(2), carry[:, 1, :].unsqueeze(2)
                kb.op(V, lambda: nc.vector.tensor_tensor(out=ctmp[:, 0, :].unsqueeze(2), in0=zl_im, in1=sl_, op=ALU.mult), reads=[zR, TSr], writes=[ctmp])
                kb.op(V, lambda: nc.vector.tensor_tensor(out=ctmp[:, 1, :].unsqueeze(2), in0=zl_re, in1=sl_, op=ALU.mult), reads=[zR, TSr], writes=[ctmp])
                kb.op(V, lambda: nc.vector.tensor_tensor(out=c_re, in0=zl_re, in1=cl, op=ALU.mult), reads=[zR, TCr], writes=[carry])
                kb.op(V, lambda: nc.vector.tensor_tensor(out=c_im, in0=zl_im, in1=cl, op=ALU.mult), reads=[zR, TCr], writes=[carry])
                kb.op(V, lambda: nc.vector.tensor_tensor(out=carry[:, 0, :], in0=carry[:, 0, :], in1=ctmp[:, 0, :], op=ALU.subtract), reads=[carry, ctmp], writes=[carry])
                kb.op(V, lambda: nc.vector.tensor_tensor(out=carry[:, 1, :], in0=carry[:, 1, :], in1=ctmp[:, 1, :], op=ALU.add), reads=[carry, ctmp], writes=[carry])
                if sid > 0 and oi == len(order) - 1:
                    for s4 in range(4):
                        s = ct * 4 + s4
                        fi = (((sid - 1) * 2 + d) * 8 + s) * 2
                        kb.op(V, lambda s4=s4, fi=fi: nc.vector.tensor_copy(out=self.s5fin[:, fi:fi + 2], in_=carry[:, :, s4]),
                              reads=[carry], writes=[self.s5fin])
                kb.op(V, lambda: nc.vector.tensor_tensor(out=sbv[:, 0, :, :], in0=zre, in1=TCv[:, :, :], op=ALU.mult), reads=[zR, TCr], writes=[sbr])
                kb.op(V, lambda: nc.vector.tensor_tensor(out=sbv[:, 1, :, :], in0=zim, in1=TSv[:, :, :], op=ALU.mult), reads=[zR, TSr], writes=[sbr])
                kb.op(V, lambda: nc.vector.tensor_tensor(out=sbv[:, 2, :, :], in0=zre, in1=TSv[:, :, :], op=ALU.mult), reads=[zR, TSr], writes=[sbr])
                kb.op(V, lambda: nc.vector.tensor_tensor(out=sbv[:, 3, :, :], in0=zim, in1=TCv[:, :, :], op=ALU.mult), reads=[zR, TCr], writes=[sbr])
                yield
                acc_y = nb()
                for s4 in range(4):
                    lhs = [CT[:, s4, 0, :], CTn[:, s4, :], CT[:, s4, 1, :], CT[:, s4, 1, :]]
                    for pi in range(4):
                        kb.op('pe', lambda pi=pi, s4=s4, lhs=lhs: nc.tensor.matmul(acc_y[:, 0:Lc], lhsT=lhs[pi], rhs=sbv[:, pi, s4, :],
                                                                                   start=(s4 == 0 and pi == 0), stop=(s4 == 3 and pi == 3)),
                              reads=[CTr, C.CTnR, sbr], writes=[acc_y])
                yst = C.yst
                if d == 0:
                    kb.op(A, lambda: nc.scalar.copy(out=yst[:, 0:Lc], in_=acc_y[:, 0:Lc]), reads=[acc_y], writes=[yst])
                else:
                    kb.op(V, lambda: nc.vector.tensor_copy(out=yst[:, 0:Lc], in_=acc_y[:, Lc - 1::-1]), reads=[acc_y], writes=[yst])
                kb.dma('sp', self.YS[d, ct * 128:(ct + 1) * 128, tsl], yst[:, 0:Lc], yst, reads=[yst], writes=[self.rYS[d][ct][ck]])
                yield


Model.s5_make_bufs = s5_make_bufs
Model.s5_chain = s5_chain


def make_s5_inputs(inputs, k):
    f32 = lambda a: np.ascontiguousarray(np.asarray(a, dtype=np.float32))
    L = DEPTH

    def par_T(a):
        a = f32(a).reshape(L, 2, 8, 2, 64)
        return np.ascontiguousarray(a.transpose(0, 3, 4, 1, 2).reshape(L, 128, 16))
    ldt = np.broadcast_to(f32(inputs['s5_log_dt'])[:, :, :, None], (L, 2, 16, 64))
    sb = k // 4
    s0 = f32(inputs['state_s5'])[sb]
    s0 = s0.reshape(L, 2, 2, 8, 2, 64)
    s0T = np.ascontiguousarray(s0.transpose(0, 4, 5, 1, 3, 2).reshape(L, 128, 32))
    return dict(
        s5_a_reT=par_T(inputs['s5_a_re']), s5_a_imT=par_T(inputs['s5_a_im']), s5_logdtT=par_T(ldt),
        s5_b_re=f32(inputs['s5_b_re']), s5_b_im=f32(inputs['s5_b_im']),
        s5_c_re=f32(inputs['s5_c_re']), s5_c_im=f32(inputs['s5_c_im']),
        s5_dT=np.ascontiguousarray(f32(inputs['s5_d']).reshape(L, 2, 128).transpose(0, 2, 1)),
        s5_glu_bT=np.ascontiguousarray(f32(inputs['s5_glu_b']).reshape(L, 2, 128).transpose(0, 2, 1)),
        s5_glu_w=f32(inputs['s5_glu_w']), s5_s0T=s0T)


def unpack_new_s5(t):
    L = t.shape[0]
    a = t.reshape(L, 2, 64, 2, 2, 8, 2)
    a = a.transpose(3, 0, 4, 6, 5, 1, 2)
    return np.ascontiguousarray(a.reshape(2, L, 2, 2, 16, 64))


Model.s5_init = _s5_init
Model.s5_setup = _s5_setup
Model.s5_phase = s5_phase


def make_dn_consts():
    i = np.arange(64)
    U = [(i[:, None] <= i[None, :]), (i[:, None] >= i[None, :])]
    c = {}
    ud_blk = np.zeros((2, 128, 128), np.float32)
    ud2 = np.zeros((2, 128, 64), np.float32)
    mincl = np.zeros((2, 128, 64), np.float32)
    mstrict = np.zeros((2, 128, 64), np.float32)
    for d in range(2):
        for cc in range(2):
            ud_blk[d, cc * 64:(cc + 1) * 64, cc * 64:(cc + 1) * 64] = U[d]
            ud2[d, cc * 64:(cc + 1) * 64] = U[d]
            mincl[d, cc * 64:(cc + 1) * 64] = U[d].T
            mstrict[d, cc * 64:(cc + 1) * 64] = U[d].T & (i[:, None] != i[None, :])
    blk = np.zeros((128, 128), np.float32)
    blk[0:64, 0:64] = 1
    blk[64:128, 64:128] = 1
    eye2 = np.concatenate([np.eye(64), np.eye(64)], 0).astype(np.float32)
    packed = np.concatenate([ud_blk[0], ud_blk[1], ud2[0], ud2[1], blk, mincl[0], mincl[1], mstrict[0], mstrict[1], eye2], axis=1)
    return dict(dn_consts=np.ascontiguousarray(packed.astype(np.float32)))


DNC_UDBLK, DNC_UD2, DNC_BLK, DNC_MINCL, DNC_MSTR, DNC_EYE2 = 0, 256, 384, 512, 640, 768


def _dn_init(self):
    kb, nc = self.kb, self.nc
    L = DEPTH
    dt_in = lambda name, shape, dt=F32: nc.dram_tensor(name, list(shape), dt, kind="ExternalInput").ap()
    self.dn_consts_in = dt_in("dn_consts", [128, 832])
    self.dn_convT = dt_in("dn_convT", [L, 128, 6, 5])
    self.dn_cols = dt_in("dn_cols", [L, 16, 2])
    self.dn_norm_g = dt_in("dn_norm_g", [L, 64])
    self.dn_s0 = dt_in("dn_s0", [L, 2, 4, 64, 64])
    self.new_dn = nc.dram_tensor("new_dn", [2, L, 2, 4, 64, 64], F32, kind="ExternalOutput").ap()
    mk = lambda name, shape, dt: nc.dram_tensor(name, list(shape), dt, kind="Internal").ap()
    self.QT = mk("QT_scr", [256, NT], BF16)
    self.KT = mk("KT_scr", [256, NT], BF16)
    self.QTOK = mk("QTOK_scr", [NT, 256], BF16)
    self.KTOK = mk("KTOK_scr", [NT, 256], BF16)
    self.VTOK = mk("VTOK_scr", [NT, 256], BF16)
    self.BG = mk("BG_scr", [NT, 16], F32)
    self.OF = mk("OF_scr", [2, NT, 256], F32)
    self.ON = mk("ON_scr", [NT, 256], BF16)
    self.rDN0 = [Res(f"DN0_{t}") for t in range(NTT)]
    self.rOF = [[Res(f"OF{d}_{t}") for t in range(NTT)] for d in range(2)]
    self.rON = [Res(f"ON{t}") for t in range(NTT)]
    self.dnc = kb.sb("dnc", [128, 832], F32)
    self.dn_convw = kb.sb("dn_convw", [128, 6, 5], F32)
    self.dn_colsb_full = kb.sb("dn_colsb", [128, 4], F32)
    self.dn_ngbc = kb.sb("dn_ngbc", [128, 64], F32)
    self.dn_S = kb.sb("dn_S", [128, 4, 64], F32)
    self.dn_sm = kb.sb("dn_sm", [128, 80], F32)
    self.dn_bg2 = kb.sb("dn_bg2", [128, 2, 16], F32)
    self.dn_bg2b = kb.sb("dn_bg2b", [128, 2, 16], F32)
    self.dn_bg2c = kb.sb("dn_bg2c", [128, 2, 16], F32)
    self.dn_bg2d = kb.sb("dn_bg2d", [128, 2, 16], F32)
    self.dn_smb = kb.sb("dn_smb", [128, 80], F32)
    self.dn_Sb = kb.sb("dn_Sb", [128, 4, 64], F32)
    self.dn_xb = kb.sb("dn_xb", [128, 520], BF16)
    self.dn_diag = kb.sb("dn_diag", [128, 5, 128], BF16)
    self.dn_bg = kb.sb("dn_bg", [128, 16], F32)
    self.dn_qkb = kb.sb("dn_qkb", [128, 2, 2], F32)


def _dn_setup(self):
    kb, nc = self.kb, self.nc
    kb.dma('sp', self.dnc[:], self.dn_consts_in, self.dnc, writes=[self.dnc])
    kb.op('dve', lambda: nc.vector.memset(self.dn_qkb[:, 0, :], 64.0 * EPS), writes=[self.dn_qkb])
    kb.op('dve', lambda: nc.vector.memset(self.dn_qkb[:, 1, :], EPS), writes=[self.dn_qkb])


def dn_segments():
    segs = [(g * 512, 512, g > 0, g < 7) for g in range(8)]
    segs += [(TS, 256, False, False), (TS + 256, 256, False, False)]
    return segs


def dn_pre(self, l):
    self.dn_colsb = Res('dn_colsb_v', None)
    self.dn_colsb = self.dn_colsb_full
    kb, nc = self.kb, self.nc
    V, A, G_ = 'dve', 'act', 'dve'
    kb.dma('sp', self.dn_convw[:], self.dn_convT[l], self.dn_convw, writes=[self.dn_convw])
    kb.dma('sp', self.dn_colsb[0:16, 0:2], self.dn_cols[l], self.dn_colsb, writes=[self.dn_colsb])
    kb.dma('sp', self.dn_ngbc[:], self.dn_norm_g[l:l + 1, :].partition_broadcast(128), self.dn_ngbc, writes=[self.dn_ngbc])
    kb.op(A, lambda: nc.scalar.activation(out=self.dn_colsb[0:16, 2:3], in_=self.dn_colsb[0:16, 1:2], func=AF.Exp),
          reads=[self.dn_colsb], writes=[self.dn_colsb])
    kb.op(V, lambda: nc.vector.tensor_scalar(out=self.dn_colsb[0:16, 3:4], in0=self.dn_colsb[0:16, 2:3], scalar1=-1.0, scalar2=None, op0=ALU.mult),
          reads=[self.dn_colsb], writes=[self.dn_colsb])
    blkones = self.dnc[:, DNC_BLK:DNC_BLK + 128]
    for (t0, ln, hl, hr) in dn_segments():
        ntile = ln // 128
        for ct6 in range(self.cfg.get('dn_nct', 6)):
            xin = self.hT[ct6 % 2]
            xinf = xin[:].rearrange("p a b -> p (a b)").bitcast(F32)
            if not hl:
                kb.op(V, lambda: nc.vector.memset(xinf[:, 0:2], 0.0), writes=[xin])
            if not hr:
                kb.op(V, lambda: nc.vector.memset(xinf[:, ln + 2:ln + 4], 0.0), writes=[xin])
            a0 = t0 - (2 if hl else 0)
            a1 = t0 + ln + (2 if hr else 0)
            o0 = 0 if hl else 2
            rows = slice(C_QKV + ct6 * 128, C_QKV + (ct6 + 1) * 128)
            rds = [self.rPT[min(NG - 1, max(0, t // 512))] for t in (a0, a1 - 1)]
            kb.dma('sp', xinf[:, o0:o0 + (a1 - a0)], self.PT[rows, a0:a1], xin, reads=rds, writes=[xin])
            xbf = self.wkb[2 + ct6 % 2]
            xbf2 = self.dn_xb
            kb.op(A, lambda: nc.scalar.copy(out=xbf2[:, 0:ln + 4], in_=xinf[:, 0:ln + 4]), reads=[xin], writes=[xbf2])
            dg = self.dn_diag
            for k in range(5):
                kb.op(V, lambda k=k: nc.vector.tensor_scalar(out=dg[:, k, :], in0=self.ident[:], scalar1=self.dn_convw[:, ct6, k:k + 1], scalar2=None, op0=ALU.mult),
                      reads=[self.ident, self.dn_convw], writes=[dg])
            cps = self.next_bank()
            for k in range(5):
                kb.op('pe', lambda k=k: nc.tensor.matmul(cps[:, 0:ln], lhsT=dg[:, k, :], rhs=xbf2[:, k:k + ln], start=(k == 0), stop=(k == 4)),
                      reads=[dg, xbf2], writes=[cps])
            acc = cps[:, 0:ln]
            if ct6 < 4:
                xs = self.wk[0]
                kb.op(A, lambda: nc.scalar.activation(out=xs[:, 0:ln], in_=acc, func=AF.Silu), reads=[cps], writes=[xs])
                sq = self.wk[1]
                kb.op(V, lambda: nc.vector.tensor_tensor(out=sq[:, 0:ln], in0=xs[:, 0:ln], in1=xs[:, 0:ln], op=ALU.mult), reads=[xs], writes=[sq])
                ps = self.next_bank()
                kb.op('pe', lambda: nc.tensor.matmul(ps[:, 0:ln], lhsT=blkones, rhs=sq[:, 0:ln], start=True, stop=True),
                      reads=[self.dnc, sq], writes=[ps])
                isq = (ct6 < 2)
                rn = self.wk[2]
                kb.op(A, lambda: nc.scalar.activation(out=rn[:, 0:ln], in_=ps[:, 0:ln], func=AF.Sqrt, scale=(64.0 if isq else 1.0),
                                                      bias=self.dn_qkb[:, (0 if isq else 1), 0:1]), reads=[ps, self.dn_qkb], writes=[rn])
                kb.op(V, lambda: nc.vector.reciprocal(out=rn[:, 0:ln], in_=rn[:, 0:ln]), reads=[rn], writes=[rn])
                xb = self.wkb[ct6 % 2]
                kb.op(G_, lambda: nc.vector.tensor_tensor(out=xb[:, 0:ln], in0=xs[:, 0:ln], in1=rn[:, 0:ln], op=ALU.mult), reads=[xs, rn], writes=[xb])
                dstT = self.QT if isq else self.KT
                r0 = (ct6 % 2) * 128
                kb.dma('act', dstT[r0:r0 + 128, t0:t0 + ln], xb[:, 0:ln], xb, reads=[xb],
                       writes=[self.rDN0[t0 // 128 + j] for j in range(ntile)])
                dst_tok = self.QTOK if isq else self.KTOK
            else:
                xb = self.wkb[ct6 % 2]
                kb.op(A, lambda: nc.scalar.activation(out=xb[:, 0:ln], in_=acc, func=AF.Silu), reads=[cps], writes=[xb])
                dst_tok = self.VTOK
                r0 = (ct6 % 2) * 128
            for j in range(ntile):
                pT = self.bank[j % 2]
                pTv = pT[:].bitcast(BF16)
                kb.op('pe', lambda j=j: nc.tensor.transpose(pTv[:, 0:128], xb[:, j * 128:(j + 1) * 128], self.ident[:]),
                      reads=[xb, self.ident], writes=[pT])
                ts_ = self.tmstage[j % 2]
                kb.op(A if j % 2 == 0 else V, (lambda: nc.scalar.copy(out=ts_[:, 0:128], in_=pTv[:, 0:128])) if j % 2 == 0 else
                      (lambda: nc.vector.tensor_copy(out=ts_[:, 0:128], in_=pTv[:, 0:128])), reads=[pT], writes=[ts_])
                tt = t0 // 128 + j
                kb.dma('act', dst_tok[tt * 128:(tt + 1) * 128, r0:r0 + 128], ts_[:, 0:128], ts_, reads=[ts_], writes=[self.rDN0[tt]])
        if self.cfg.get('dn_nobg', 0):
            continue
        bgin = self.wk[3]
        rds = [self.rPT[min(NG - 1, t // 512)] for t in (t0, t0 + ln - 1)]
        kb.dma('sp', bgin[0:16, 0:ln], self.PT[C_BETA:C_BETA + 16, t0:t0 + ln], bgin, reads=rds, writes=[bgin])
        sg, gg = self.wk[4], self.wk[5]
        kb.op(A, lambda: nc.scalar.activation(out=sg[0:16, 0:ln], in_=bgin[0:16, 0:ln], func=AF.Sigmoid), reads=[bgin], writes=[sg])
        kb.op(A, lambda: nc.scalar.activation(out=gg[0:16, 0:ln], in_=bgin[0:16, 0:ln], func=AF.Exp, bias=self.dn_colsb[0:16, 0:1]),
              reads=[bgin, self.dn_colsb], writes=[gg])
        kb.op(V, lambda: nc.vector.tensor_scalar(out=gg[0:16, 0:ln], in0=gg[0:16, 0:ln], scalar1=1.0, scalar2=None, op0=ALU.add), reads=[gg], writes=[gg])
        kb.op(A, lambda: nc.scalar.activation(out=gg[0:16, 0:ln], in_=gg[0:16, 0:ln], func=AF.Ln), reads=[gg], writes=[gg])
        kb.op(V, lambda: nc.vector.tensor_scalar(out=gg[0:16, 0:ln], in0=gg[0:16, 0:ln], scalar1=self.dn_colsb[0:16, 3:4], scalar2=None, op0=ALU.mult),
              reads=[gg, self.dn_colsb], writes=[gg])
        for j in range(ntile):
            ps = self.next_bank()
            kb.op('pe', lambda j=j: nc.tensor.matmul(ps[:, 0:16], lhsT=sg[0:16, j * 128:(j + 1) * 128], rhs=self.ident32[0:16, 0:16], start=True, stop=True),
                  reads=[sg, self.ident32], writes=[ps])
            kb.op('pe', lambda j=j: nc.tensor.matmul(ps[:, 16:32], lhsT=gg[0:16, j * 128:(j + 1) * 128], rhs=self.ident32[0:16, 0:16], start=True, stop=True),
                  reads=[gg, self.ident32], writes=[ps])
            bgt = self.dn_bg
            kb.op(V, lambda: nc.vector.tensor_copy(out=bgt[:, 0:8], in_=ps[:, 0:8]), reads=[ps], writes=[bgt])
            kb.op(V, lambda: nc.vector.tensor_copy(out=bgt[:, 8:16], in_=ps[:, 24:32]), reads=[ps], writes=[bgt])
            tt = t0 // 128 + j
            kb.dma('act', self.BG[tt * 128:(tt + 1) * 128, :], bgt[:], bgt, reads=[bgt], writes=[self.rDN0[tt]])


def dn_chain(self, l, d, B):
    kb, nc = self.kb, self.nc
    bstate = [0]

    def nb():
        b = self.bank[B.bank_lo + bstate[0] % 4]
        bstate[0] += 1
        return b
    V, A, G_ = 'dve', 'act', 'dve'
    dnc = self.dnc
    Ud = dnc[0:64, DNC_UDBLK + d * 128:DNC_UDBLK + d * 128 + 64]
    ud2 = dnc[0:64, DNC_UD2 + d * 64:DNC_UD2 + (d + 1) * 64]
    ones = dnc[0:64, DNC_BLK:DNC_BLK + 64]
    mincl = dnc[0:64, DNC_MINCL + d * 64:DNC_MINCL + (d + 1) * 64]
    mstr = dnc[0:64, DNC_MSTR + d * 64:DNC_MSTR + (d + 1) * 64]
    eye = dnc[0:64, DNC_EYE2:DNC_EYE2 + 64]
    id64 = self.ident32[0:64, 0:64]
    bc8 = lambda ap: ap.unsqueeze(1).to_broadcast([64, 8, 64])
    bcj = lambda ap: ap.unsqueeze(2).to_broadcast([64, 8, 64])
    v8 = lambda ap: ap.rearrange("p (b j) -> p b j", b=8)
    S = B.S
    sm = B.sm
    KT4 = self.KT.rearrange("(h k) t -> k h t", k=64)
    QT4 = self.QT.rearrange("(h k) t -> k h t", k=64)
    w = B.w
    F = lambda r: r[0:64, :]
    xt0, xt1, ot0, ot1 = self.xt[0], self.xt[1], self.otmp[0], self.otmp[1]
    seqs = [(0, 0, 32), (1, 32, 2), (2, 34, 2)]
    items = []
    for (sid, tt0, ntl) in seqs:
        order = list(range(tt0, tt0 + ntl))
        if d == 1:
            order = order[::-1]
        for n_, tt in enumerate(order):
            items.append((sid, tt, n_ == 0, n_ == len(order) - 1))
    tokv = lambda dr, tsl: dr[tsl, :].rearrange("(c p) x -> p c x", p=64)
    c3 = lambda r: r[0:64, :].rearrange("p (c x) -> p c x", c=2)

    def issue_loads(n_):
        tt = items[n_][1]
        I = B.inp[n_ % 2]
        tsl = slice(tt * 128, (tt + 1) * 128)
        kb.dma('sp', I.kT, KT4[:, :, tsl], I.kTr, reads=[self.rDN0[tt]], writes=[I.kTr])
        kb.dma('sp', I.qT, QT4[:, :, tsl], I.qTr, reads=[self.rDN0[tt]], writes=[I.qTr])
        kb.dma('sp', c3(I.tkr), tokv(self.KTOK, tsl), I.tkr, reads=[self.rDN0[tt]], writes=[I.tkr])
        kb.dma('sp', c3(I.tvr), tokv(self.VTOK, tsl), I.tvr, reads=[self.rDN0[tt]], writes=[I.tvr])
        kb.dma('sp', c3(I.tqr), tokv(self.QTOK, tsl), I.tqr, reads=[self.rDN0[tt]], writes=[I.tqr])
        kb.dma('sp', I.bg2[0:64, :, :], self.BG[tsl, :].rearrange("(c p) x -> p c x", p=64), I.bg2, reads=[self.rDN0[tt]], writes=[I.bg2])

    issue_loads(0)
    for n_, (sid, tt, first, last) in enumerate(items):
        if True:
            if first:
                if sid == 0:
                    kb.dma('sp', S[0:64], self.dn_s0[l, d].rearrange("h k v -> k h v"), S, writes=[S])
                else:
                    kb.op(V, lambda: nc.vector.memset(S[0:64], 0.0), writes=[S])
            if n_ + 1 < len(items):
                issue_loads(n_ + 1)
            I = B.inp[n_ % 2]
            tsl = slice(tt * 128, (tt + 1) * 128)
            kTr, qTr, kT, qT = I.kTr, I.qTr, I.kT, I.qT
            tkr, tvr, tqr, kpr = I.tkr, I.tvr, I.tqr, B.kpr
            bg2 = I.bg2
            g8 = bg2[0:64, :, 8 + 4 * d:12 + 4 * d]
            b8 = bg2[0:64, :, 4 * d:4 * d + 4]
            g8j = g8.unsqueeze(3).to_broadcast([64, 2, 4, 64])
            b8j = b8.unsqueeze(3).to_broadcast([64, 2, 4, 64])
            v24 = lambda ap: ap.rearrange("p (c h j) -> p c h j", c=2, h=4)
            ktok8, vtok8, qtok8 = v8(F(tkr)), v8(F(tvr)), v8(F(tqr))
            yield
            X1, X2 = nb(), nb()
            for c in range(2):
                cs = slice(c * 64, (c + 1) * 64)
                for h in range(4):
                    bs = slice((c * 4 + h) * 64, (c * 4 + h + 1) * 64)
                    kb.op('pe', lambda cs=cs, h=h, bs=bs: nc.tensor.matmul(X1[0:64, bs], lhsT=kT[:, h, cs], rhs=kT[:, h, cs], start=True, stop=True),
                          reads=[kTr], writes=[X1])
                    kb.op('pe', lambda cs=cs, h=h, bs=bs: nc.tensor.matmul(X2[0:64, bs], lhsT=qT[:, h, cs], rhs=kT[:, h, cs], start=True, stop=True),
                          reads=[kTr, qTr], writes=[X2])
            yield
            G4b, NGU = F(w[0]), F(w[1])
            kb.op(V, lambda: nc.vector.tensor_copy(out=v24(G4b), in_=g8j), reads=[bg2], writes=[w[0]])
            kb.op(V, lambda: nc.vector.scalar_tensor_tensor(out=v8(NGU), in0=bc8(ud2), scalar=-1.0, in1=v8(G4b), op0=ALU.mult, op1=ALU.mult),
                  reads=[dnc, w[0]], writes=[w[1]])
            Y = nb()
            kb.op('pe', lambda: nc.tensor.matmul(Y[0:64, :], lhsT=Ud, rhs=G4b, start=True, stop=False), reads=[dnc, w[0]], writes=[Y])
            kb.op('pe', lambda: nc.tensor.matmul(Y[0:64, :], lhsT=ones, rhs=NGU, start=False, stop=True), reads=[dnc, w[1]], writes=[Y])
            kb.op(V, lambda: nc.vector.tensor_copy(out=sm[0:64, 72:80].rearrange("p (c h) -> p c h", c=2), in_=g8), reads=[bg2], writes=[sm])
            Z = nb()
            kb.op('pe', lambda: nc.tensor.matmul(Z[0:64, 0:8], lhsT=Ud, rhs=sm[0:64, 72:80], start=True, stop=True), reads=[dnc, sm], writes=[Z])
            kb.op('pe', lambda: nc.tensor.matmul(Z[0:64, 8:16], lhsT=ones, rhs=sm[0:64, 72:80], start=True, stop=True), reads=[dnc, sm], writes=[Z])
            kb.op(V, lambda: nc.vector.tensor_copy(out=sm[0:64, 0:8], in_=Z[0:64, 0:8]), reads=[Z], writes=[sm])
            kb.op(V, lambda: nc.vector.tensor_tensor(out=sm[0:64, 8:16], in0=Z[0:64, 8:16], in1=sm[0:64, 0:8], op=ALU.subtract), reads=[Z, sm], writes=[sm])
            kb.op(V, lambda: nc.vector.tensor_copy(out=sm[0:64, 16:24], in_=Z[0:64, 8:16]), reads=[Z], writes=[sm])
            kb.op(A, lambda: nc.scalar.activation(out=sm[0:64, 24:48], in_=sm[0:64, 0:24], func=AF.Exp), reads=[sm], writes=[sm])
            egc, ekl, egl = sm[0:64, 24:32], sm[0:64, 32:40], sm[0:64, 40:48]
            kb.op(V, lambda: nc.vector.tensor_tensor(out=sm[0:64, 48:56].rearrange("p (c h) -> p c h", c=2), in0=b8, in1=egc.rearrange("p (c h) -> p c h", c=2), op=ALU.mult),
                  reads=[bg2, sm], writes=[sm])
            be = sm[0:64, 48:56]
            kb.op(V, lambda: nc.vector.tensor_copy(out=sm[0:64, 56:64].rearrange("p (c h) -> p c h", c=2), in_=b8), reads=[bg2], writes=[sm])
            bt = sm[0:64, 56:64]
            yield
            dec = F(w[2])
            kb.op(V, lambda: nc.vector.tensor_scalar(out=dec, in0=Y[0:64, :], scalar1=0.0, scalar2=None, op0=ALU.min), reads=[Y], writes=[w[2]])
            kb.op(A, lambda: nc.scalar.activation(out=dec, in_=dec, func=AF.Exp), reads=[w[2]], writes=[w[2]])
            kb.op(G_, lambda: nc.vector.tensor_tensor(out=v8(dec), in0=v8(dec), in1=bc8(mincl), op=ALU.mult), reads=[w[2], dnc], writes=[w[2]])
            yield
            aqk, Nm = F(w[3]), F(w[4])
            kb.op(V, lambda: nc.vector.tensor_tensor(out=aqk, in0=X2[0:64, :], in1=dec, op=ALU.mult), reads=[X2, w[2]], writes=[w[3]])
            kb.op(V, lambda: nc.vector.tensor_tensor(out=Nm, in0=X1[0:64, :], in1=dec, op=ALU.mult), reads=[X1, w[2]], writes=[w[4]])
            kb.op(V, lambda: nc.vector.scalar_tensor_tensor(out=v8(Nm), in0=v8(Nm), scalar=-1.0, in1=bcj(bt), op0=ALU.mult, op1=ALU.mult),
                  reads=[w[4], sm], writes=[w[4]])
            kb.op(G_, lambda: nc.vector.tensor_tensor(out=v8(Nm), in0=v8(Nm), in1=bc8(mstr), op=ALU.mult), reads=[w[4], dnc], writes=[w[4]])
            yield
            T1a, T1b = nb(), nb()
            for b in range(8):
                bs = slice(b * 64, (b + 1) * 64)
                kb.op('pe', lambda bs=bs: nc.tensor.matmul(T1a[0:64, bs], lhsT=Nm[:, bs], rhs=id64, start=True, stop=True),
                      reads=[w[4], self.ident32], writes=[T1a])
                kb.op('pe', lambda bs=bs: nc.tensor.matmul(T1b[0:64, bs], lhsT=aqk[:, bs], rhs=id64, start=True, stop=True),
                      reads=[w[3], self.ident32], writes=[T1b])
            Pr, PTr = B.Pr, B.PTr
            ubr, aqr, Rr_, WUr_ = B.ubR, B.aqR, B.RR, B.WUR
            Ub, aqkTb = ubr[0:64, :], aqr[0:64, :]
            kb.op(A, lambda: nc.scalar.copy(out=F(PTr[0]), in_=T1a[0:64, :]), reads=[T1a], writes=[PTr[0]])
            kb.op(A, lambda: nc.scalar.copy(out=F(Pr[0]), in_=Nm), reads=[w[4]], writes=[Pr[0]])
            kb.op(A, lambda: nc.scalar.copy(out=aqkTb, in_=T1b[0:64, :]), reads=[T1b], writes=[aqr])
            kb.op(V, lambda: nc.vector.tensor_tensor(out=v8(Ub), in0=v8(T1a[0:64, :]), in1=bc8(eye), op=ALU.add), reads=[T1a, dnc], writes=[ubr])
            yield
            cur = 0
            for kstep in range(0, 6):
                Pc, PTc = F(Pr[cur]), F(PTr[cur])
                nxt = 1 - cur
                if kstep >= 1:
                    ubk = nb()
                    for b in range(8):
                        bs = slice(b * 64, (b + 1) * 64)
                        kb.op('pe', lambda bs=bs, Pc=Pc, ubk=ubk: nc.tensor.matmul(ubk[0:64, bs], lhsT=Pc[:, bs], rhs=Ub[:, bs], start=True, stop=True),
                              reads=[Pr[cur], ubr], writes=[ubk])
                if kstep < 5:
                    sq1, sq2 = nb(), nb()
                    for b in range(8):
                        bs = slice(b * 64, (b + 1) * 64)
                        kb.op('pe', lambda bs=bs, Pc=Pc, PTc=PTc, sq1=sq1: nc.tensor.matmul(sq1[0:64, bs], lhsT=PTc[:, bs], rhs=Pc[:, bs], start=True, stop=True),
                              reads=[Pr[cur], PTr[cur]], writes=[sq1])
                        kb.op('pe', lambda bs=bs, Pc=Pc, PTc=PTc, sq2=sq2: nc.tensor.matmul(sq2[0:64, bs], lhsT=Pc[:, bs], rhs=PTc[:, bs], start=True, stop=True),
                              reads=[Pr[cur], PTr[cur]], writes=[sq2])
                if kstep >= 1:
                    kb.op(V, lambda ubk=ubk: nc.vector.tensor_tensor(out=Ub, in0=ubk[0:64, :], in1=Ub, op=ALU.add), reads=[ubk, ubr], writes=[ubr])
                if kstep < 5:
                    kb.op(A, lambda nxt=nxt, sq1=sq1: nc.scalar.copy(out=F(Pr[nxt]), in_=sq1[0:64, :]), reads=[sq1], writes=[Pr[nxt]])
                    kb.op(A if kstep % 2 == 0 else V, (lambda nxt=nxt, sq2=sq2: nc.scalar.copy(out=F(PTr[nxt]), in_=sq2[0:64, :])) if kstep % 2 == 0 else
                          (lambda nxt=nxt, sq2=sq2: nc.vector.tensor_copy(out=F(PTr[nxt]), in_=sq2[0:64, :])), reads=[sq2], writes=[PTr[nxt]])
                    cur = nxt
                yield
            Rf = Rr_[0:64, :]
            R8 = Rf.rearrange("p (b x) -> p b x", b=8)
            kb.op(G_, lambda: nc.vector.tensor_tensor(out=R8[:, :, 0:64], in0=ktok8, in1=bcj(be), op=ALU.mult), reads=[tkr, sm], writes=[Rr_])
            kb.op(G_, lambda: nc.vector.tensor_tensor(out=R8[:, :, 64:128], in0=vtok8, in1=bcj(bt), op=ALU.mult), reads=[tvr, sm], writes=[Rr_])
            W1 = [nb(), nb()]
            for b in range(8):
                kb.op('pe', lambda b=b: nc.tensor.matmul(W1[b // 4][0:64, (b % 4) * 128:(b % 4 + 1) * 128], lhsT=Ub[:, b * 64:(b + 1) * 64], rhs=R8[:, b, :], start=True, stop=True),
                      reads=[ubr, Rr_], writes=[W1[b // 4]])
            WUf = WUr_[0:64, :]
            WU8 = WUf.rearrange("p (b x) -> p b x", b=8)
            for c in range(2):
                kb.op(A, lambda c=c: nc.scalar.copy(out=WUf[:, c * 512:(c + 1) * 512], in_=W1[c][0:64, :]), reads=[W1[c]], writes=[WUr_])
            yield
            W2 = [nb(), nb()]
            for b in range(8):
                kb.op('pe', lambda b=b: nc.tensor.matmul(W2[b // 4][0:64, (b % 4) * 128:(b % 4 + 1) * 128], lhsT=aqkTb[:, b * 64:(b + 1) * 64], rhs=WU8[:, b, :], start=True, stop=True),
                      reads=[aqr, WUr_], writes=[W2[b // 4]])
            Mm, ccs = F(w[5]), F(w[0])
            kb.op(G_, lambda: nc.vector.tensor_tensor(out=v8(Mm), in0=qtok8, in1=bcj(egc), op=ALU.mult), reads=[tqr, sm], writes=[w[5]])
            for c in range(2):
                W2v = W2[c][0:64, :].rearrange("p (h x) -> p h x", h=4)
                Mc = Mm[:, c * 256:(c + 1) * 256].rearrange("p (h j) -> p h j", h=4)
                Cc = ccs[:, c * 256:(c + 1) * 256].rearrange("p (h j) -> p h j", h=4)
                kb.op(V, lambda Mc=Mc, W2v=W2v: nc.vector.tensor_tensor(out=Mc, in0=Mc, in1=W2v[:, :, 0:64], op=ALU.subtract), reads=[w[5], W2[c]], writes=[w[5]])
                kb.op(A, lambda Cc=Cc, W2v=W2v: nc.scalar.copy(out=Cc, in_=W2v[:, :, 64:128]), reads=[W2[c]], writes=[w[0]])
            yield
            T2 = nb()
            for b in range(8):
                bs = slice(b * 64, (b + 1) * 64)
                kb.op('pe', lambda bs=bs: nc.tensor.matmul(T2[0:64, bs], lhsT=Mm[:, bs], rhs=id64, start=True, stop=True),
                      reads=[w[5], self.ident32], writes=[T2])
            MTs = F(w[1])
            kb.op(A, lambda: nc.scalar.copy(out=MTs, in_=T2[0:64, :]), reads=[T2], writes=[w[1]])
            yield
            Kp = F(kpr)
            Kp8 = v8(Kp)
            kb.op(G_, lambda: nc.vector.tensor_tensor(out=Kp8, in0=ktok8, in1=bcj(ekl), op=ALU.mult), reads=[tkr, sm], writes=[kpr])
            G1 = nb()
            for b in range(8):
                bs = slice(b * 64, (b + 1) * 64)
                kb.op('pe', lambda b=b, bs=bs: nc.tensor.matmul(G1[0:64, bs], lhsT=WU8[:, b, 0:64], rhs=Kp8[:, b, :], start=True, stop=True),
                      reads=[WUr_, kpr], writes=[G1])
            Gneg = F(w[2])
            kb.op(V, lambda: nc.vector.tensor_scalar(out=Gneg, in0=G1[0:64, :], scalar1=-1.0, scalar2=None, op0=ALU.mult), reads=[G1], writes=[w[2]])
            yield
            O1 = nb()
            corder = (0, 1) if d == 0 else (1, 0)
            for c in corder:
                for h in range(4):
                    b = c * 4 + h
                    bs = slice(b * 64, (b + 1) * 64)
                    kb.op('pe', lambda h=h, bs=bs: nc.tensor.matmul(O1[0:64, bs], lhsT=MTs[:, bs], rhs=S[0:64, h, :], start=True, stop=True),
                          reads=[w[1], S], writes=[O1])
                SS = nb()
                for h in range(4):
                    b = c * 4 + h
                    bs = slice(b * 64, (b + 1) * 64)
                    kb.op('pe', lambda h=h, b=b: nc.tensor.matmul(SS[0:64, h * 64:(h + 1) * 64], lhsT=Kp8[:, b, :], rhs=WU8[:, b, 64:128], start=True, stop=False),
                          reads=[kpr, WUr_], writes=[SS])
                    kb.op('pe', lambda h=h, bs=bs: nc.tensor.matmul(SS[0:64, h * 64:(h + 1) * 64], lhsT=Gneg[:, bs], rhs=S[0:64, h, :], start=False, stop=True),
                          reads=[w[2], S], writes=[SS])
                eglc = egl[:, c * 4:(c + 1) * 4].unsqueeze(2).to_broadcast([64, 4, 64])
                kb.op(V, lambda eglc=eglc: nc.vector.tensor_tensor(out=S[0:64], in0=S[0:64], in1=eglc, op=ALU.mult), reads=[S, sm], writes=[S])
                kb.op(V, lambda: nc.vector.tensor_tensor(out=S[0:64].rearrange("p h v -> p (h v)"), in0=S[0:64].rearrange("p h v -> p (h v)"), in1=SS[0:64, 0:256], op=ALU.add),
                      reads=[S, SS], writes=[S])
            od = F(w[3])
            kb.op(V, lambda: nc.vector.tensor_tensor(out=od, in0=O1[0:64, :], in1=ccs, op=ALU.add), reads=[O1, w[0]], writes=[w[3]])
            OFv = self.OF[d, tsl, :].rearrange("(c p) x -> p c x", p=64)
            od3 = od.rearrange("p (c x) -> p c x", c=2)
            kb.dma('sp', OFv, od3, w[3], reads=[w[3]], writes=[self.rOF[d][tt]])
            yield
            if last and sid > 0:
                kb.dma('sp', self.new_dn[sid - 1, l, d].rearrange("h k v -> k h v"), S[0:64], S, reads=[S])
                yield


def dn_final(self, l):
    kb, nc = self.kb, self.nc
    V, A = 'dve', 'act'
    v4 = lambda ap: ap.rearrange("p (h j) -> p h j", h=4)
    sm = self.dn_sm
    for g in range(NG):
        tsl = slice(g * 512, (g + 1) * 512)
        oT = [self.wkb[0], self.wkb[1]]
        for j in range(4):
            tt = g * 4 + j
            o0, o1 = self.pstage[0], self.pstage[1]
            kb.dma('sp', o0[:, 0:256], self.OF[0, tt * 128:(tt + 1) * 128, :], o0, reads=[self.rOF[0][tt]], writes=[o0])
            kb.dma('sp', o1[:, 0:256], self.OF[1, tt * 128:(tt + 1) * 128, :], o1, reads=[self.rOF[1][tt]], writes=[o1])
            kb.op(V, lambda: nc.vector.tensor_tensor(out=o0[:, 0:256], in0=o0[:, 0:256], in1=o1[:, 0:256], op=ALU.add), reads=[o0, o1], writes=[o0])
            kb.op(V, lambda: nc.vector.tensor_tensor(out=o0[:, 256:512], in0=o0[:, 0:256], in1=o0[:, 0:256], op=ALU.mult), reads=[o0], writes=[o0])
            kb.op(V, lambda: nc.vector.tensor_reduce(out=sm[:, 64:68], in_=v4(o0[:, 256:512]), axis=AX.X, op=ALU.add), reads=[o0], writes=[sm])
            kb.op(A, lambda: nc.scalar.activation(out=sm[:, 64:68], in_=sm[:, 64:68], func=AF.Sqrt, scale=1.0 / 64, bias=self.eps_col[:]),
                  reads=[sm, self.eps_col], writes=[sm])
            kb.op(V, lambda: nc.vector.reciprocal(out=sm[:, 64:68], in_=sm[:, 64:68]), reads=[sm], writes=[sm])
            kb.op(V, lambda: nc.vector.tensor_tensor(out=v4(o0[:, 0:256]), in0=v4(o0[:, 0:256]), in1=sm[:, 64:68].unsqueeze(2).to_broadcast([128, 4, 64]), op=ALU.mult),
                  reads=[o0, sm], writes=[o0])
            ont = self.wkb[2 + j % 2]
            kb.op(V, lambda: nc.vector.tensor_tensor(out=v4(ont[:, 0:256]), in0=v4(o0[:, 0:256]), in1=self.dn_ngbc[:].unsqueeze(1).to_broadcast([128, 4, 64]), op=ALU.mult),
                  reads=[o0, self.dn_ngbc], writes=[ont])
            pT = self.bank[j % 2]
            pTv = pT[:].bitcast(BF16)
            for ft in range(2):
                kb.op('pe', lambda ft=ft: nc.tensor.transpose(pTv[:, ft * 128:(ft + 1) * 128], ont[:, ft * 128:(ft + 1) * 128], self.ident[:]),
                      reads=[ont, self.ident], writes=[pT])
            for ft in range(2):
                kb.op('act', lambda ft=ft: nc.scalar.copy(out=oT[ft][:, j * 128:(j + 1) * 128], in_=pTv[:, ft * 128:(ft + 1) * 128]),
                      reads=[pT], writes=[oT[ft]])
        for ft in range(2):
            zT = self.wk[ft]
            kb.dma('sp', zT[:], self.PT[C_DNZ + ft * 128:C_DNZ + (ft + 1) * 128, tsl], zT, reads=[self.rPT[g]], writes=[zT])
            kb.op('act', lambda: nc.scalar.activation(out=zT[:], in_=zT[:], func=AF.Silu), reads=[zT], writes=[zT])
            yb = self.ybuf[self.ybuf_i % 2]
            self.ybuf_i += 1
            kb.op('dve', lambda: nc.vector.tensor_tensor(out=yb[:], in0=zT[:], in1=oT[ft][:], op=ALU.mult), reads=[zT, oT[ft]], writes=[yb])
            self.store_y(yb, 256 + ft * 128, g)


class DNBufs:
    pass


def dn_make_bufs(self):
    kb = self.kb
    hv = lambda r: r[:].rearrange("p a b -> p (a b)")
    B0, B1 = DNBufs(), DNBufs()
    B0.bank_lo, B1.bank_lo = 0, 4
    B0.kTr, B0.qTr = self.yT[0], self.yT[1]
    B0.kT, B0.qT = B0.kTr[0:64, 0:4, :], B0.qTr[0:64, 0:4, :]
    B0.tkr, B0.tvr, B0.tqr, B0.kpr = self.wkb
    B0.bg2, B0.sm, B0.S = self.dn_bg2, self.dn_sm, self.dn_S
    B0.w = list(self.wk)
    B0.Pr, B0.PTr = [self.ybuf[0], self.ybuf[1]], [self.tmstage[0], self.tmstage[1]]
    h0 = hv(self.hT[0])
    B0.ubR, B0.aqR = Res("dn_ub0", h0[:, 0:512]), Res("dn_aq0", h0[:, 512:1024])
    B0.RR, B0.WUR = Res("dn_R0", h0[:, 1024:2048]), Res("dn_WU0", h0[:, 2048:3072])
    self.dn_alias0 = (self.hT[0], [B0.ubR, B0.aqR, B0.RR, B0.WUR])
    wf = self.w_in_bf[:].rearrange("p a b -> p (a b)")
    off = [0]

    def take(n_bf16, name, f32=False):
        ap = wf[:, off[0]:off[0] + n_bf16]
        off[0] += n_bf16
        return Res(name, ap.bitcast(F32) if f32 else ap)
    B1.kTr, B1.qTr = take(512, "dn1_kT"), take(512, "dn1_qT")
    B1.kT = B1.kTr[0:64, :].rearrange("p (h t) -> p h t", h=4)
    B1.qT = B1.qTr[0:64, :].rearrange("p (h t) -> p h t", h=4)
    B1.tkr, B1.tvr, B1.tqr, B1.kpr = [take(512, f"dn1_tk{i}") for i in range(4)]
    B1.Pr = [take(512, f"dn1_P{i}") for i in range(2)]
    B1.PTr = [take(512, f"dn1_PT{i}") for i in range(2)]
    B1.w = [take(1024, f"dn1_w{i}", f32=True) for i in range(6)]
    B1.bg2, B1.sm, B1.S = self.dn_bg2b, self.dn_smb, self.dn_Sb
    h1 = hv(self.hT[1])
    B1.ubR, B1.aqR = Res("dn_ub1", h1[:, 0:512]), Res("dn_aq1", h1[:, 512:1024])
    B1.RR, B1.WUR = Res("dn_R1", h1[:, 1024:2048]), Res("dn_WU1", h1[:, 2048:3072])
    def mkset(kTr, qTr, tkr, tvr, tqr, bg2, yT_like):
        I = DNBufs()
        I.kTr, I.qTr, I.tkr, I.tvr, I.tqr, I.bg2 = kTr, qTr, tkr, tvr, tqr, bg2
        if yT_like:
            I.kT, I.qT = kTr[0:64, 0:4, :], qTr[0:64, 0:4, :]
        else:
            I.kT = kTr[0:64, :].rearrange("p (h t) -> p h t", h=4)
            I.qT = qTr[0:64, :].rearrange("p (h t) -> p h t", h=4)
        return I
    ex0 = [take(512, f"dn0x{i}") for i in range(5)]
    ex1 = [take(512, f"dn1x{i}") for i in range(5)]
    B0.inp = [mkset(B0.kTr, B0.qTr, B0.tkr, B0.tvr, B0.tqr, self.dn_bg2, True),
              mkset(ex0[0], ex0[1], ex0[2], ex0[3], ex0[4], self.dn_bg2c, False)]
    B1.inp = [mkset(B1.kTr, B1.qTr, B1.tkr, B1.tvr, B1.tqr, self.dn_bg2b, False),
              mkset(ex1[0], ex1[1], ex1[2], ex1[3], ex1[4], self.dn_bg2d, False)]
    scr1 = [B1.kTr, B1.qTr, B1.tkr, B1.tvr, B1.tqr, B1.kpr] + B1.Pr + B1.PTr + B1.w + ex0 + ex1
    self.dn_alias1 = [(self.w_in_bf, scr1), (self.hT[1], [B1.ubR, B1.aqR, B1.RR, B1.WUR])]
    return B0, B1


def run_chains(gens, head_start=()):
    active = list(gens)
    for g, n in zip(list(active), head_start):
        for _ in range(n):
            try:
                next(g)
            except StopIteration:
                active.remove(g)
                break
    while active:
        for g in list(active):
            try:
                next(g)
            except StopIteration:
                active.remove(g)


def dn_phase(self, l):
    kb = self.kb
    self.dn_pre(l)
    if not hasattr(self, '_dn_bufs'):
        self._dn_bufs = self.dn_make_bufs()
    B0, B1 = self._dn_bufs
    kb.alias_begin(*self.dn_alias0)
    for src_, al in self.dn_alias1:
        kb.alias_begin(src_, al)
    if self.cfg.get("dn_seq", 0):
        run_chains([self.dn_chain(l, 0, B0)])
        run_chains([self.dn_chain(l, 1, B1)])
    else:
        run_chains([self.dn_chain(l, 0, B0), self.dn_chain(l, 1, B1)], head_start=(self.cfg.get('dn_hs', 11), 0))
    kb.alias_end(*self.dn_alias0)
    for src_, al in self.dn_alias1:
        kb.alias_end(src_, al)
    self.dn_final(l)


def make_dn_inputs(inputs, k):
    f32 = lambda a: np.ascontiguousarray(np.asarray(a, dtype=np.float32))
    L = DEPTH
    conv = f32(inputs['dn_conv'])
    convT = np.ascontiguousarray(conv.reshape(L, 5, 6, 128).transpose(0, 3, 2, 1))
    cols = np.zeros((L, 16, 2), np.float32)
    cols[:, 8:16, 0] = f32(inputs['dn_dt_bias']).reshape(L, 8)
    cols[:, 8:16, 1] = f32(inputs['dn_a_log']).reshape(L, 8)
    sb = k // 4
    return dict(dn_convT=convT, dn_cols=cols, dn_norm_g=f32(inputs['dn_norm_g']),
                dn_s0=f32(inputs['state_delta'])[sb])


Model.dn_init = _dn_init
Model.dn_setup = _dn_setup
Model.dn_pre = dn_pre
Model.dn_chain = dn_chain
Model.dn_make_bufs = dn_make_bufs
Model.dn_final = dn_final
Model.dn_phase = dn_phase


def kernel(**inputs):
    nc = build_nc({})
    maps = make_in_maps(inputs, 8)
    res = run_bass_kernel_spmd(nc, maps, core_ids=list(range(8)))
    rs = res.results
    L = DEPTH
    y_prompt = np.zeros((16, 256, D), np.float32)
    y_sample = np.zeros((2, TS, D), np.float32)
    new_dn = np.zeros((16, L, 2, 4, 64, 64), np.float32)
    new_s5 = np.zeros((16, L, 2, 2, 16, 64), np.float32)
    for k in range(8):
        r = rs[k]
        ya = np.asarray(r["y_all"], dtype=np.float32)
        y_prompt[2 * k] = ya[TS:TS + 256]
        y_prompt[2 * k + 1] = ya[TS + 256:TS + 512]
        if k % 4 == 0:
            y_sample[k // 4] = ya[0:TS]
        new_dn[2 * k:2 * k + 2] = np.asarray(r["new_dn"], dtype=np.float32)
        new_s5[2 * k:2 * k + 2] = unpack_new_s5(np.asarray(r["new_s5T"], dtype=np.float32))
    return (y_prompt, y_sample, new_dn, new_s5)
```
